# Optimizing a Trainium2 kernel written in Bass

```python
import jax
import jax.numpy as jnp
from jax import lax
import numpy as np

D_MODEL = 2048
BATCH = 4
SEQ = 4096
DEPTH = 2

CTX_LEN = 256
GRID_W = 64
W_BRANCH = D_MODEL // 2
N_BRANCH = 3
NA_HEADS = 16
NA_HEAD_DIM = W_BRANCH // NA_HEADS
NA_WIN_H = 8
NA_WIN_W = 16
POOL_WINDOWS = (2, 4, 8, 16)
POOL_GROUPS = len(POOL_WINDOWS)
POOL_GROUP_DIM = W_BRANCH // POOL_GROUPS
RWKV_HEAD_DIM = 64
RWKV_HEADS = W_BRANCH // RWKV_HEAD_DIM
RWKV_LORA = 64
RMS_EPS = 1e-6
LNX_EPS = 64e-5
NEG_INF = -1e30
IN_SIZES = (W_BRANCH,) * 10 + (RWKV_LORA, RWKV_LORA, N_BRANCH * D_MODEL)
IN_SPLIT_POINTS = tuple(int(s) for s in np.cumsum(IN_SIZES)[:-1])
D_IN = int(sum(IN_SIZES))

kernel_name = "hybrid_natten_pool_rwkv7_prefix_block"


def rms_norm(x, g):
    xf = x.astype(jnp.float32)
    y = xf * lax.rsqrt(jnp.mean(xf * xf, axis=-1, keepdims=True) + RMS_EPS)
    return (y * g).astype(x.dtype)


def adaln_modulation(cond, w_mod, b_mod):
    m = jax.nn.silu(cond) @ w_mod + b_mod
    return jnp.split(m, 3, axis=-1)


def to_heads(z, n_heads):
    return z.reshape(z.shape[:-1] + (n_heads, z.shape[-1] // n_heads))


def neighbourhood_attention(q, k, v, k_ctx, v_ctx, rpb):
    B, T, H, Dh = q.shape
    rows = T // GRID_W
    win_h = min(NA_WIN_H, rows)
    scale = Dh ** -0.5
    qg = q.reshape(B, rows, GRID_W, H, Dh)
    kg = k.reshape(B, rows, GRID_W, H, Dh)
    vg = v.reshape(B, rows, GRID_W, H, Dh)
    col = jnp.arange(GRID_W)
    col_start = jnp.clip(col - NA_WIN_W // 2, 0, GRID_W - NA_WIN_W)
    col_mask = (col[None, :] >= col_start[:, None]) & (col[None, :] < col_start[:, None] + NA_WIN_W)
    col_off = jnp.clip(col[None, :] - col[:, None] + NA_WIN_W - 1, 0, 2 * NA_WIN_W - 2)
    n_loc = win_h * GRID_W

    def one_row(r):
        r0 = jnp.clip(r - win_h // 2, 0, rows - win_h)
        kb = lax.dynamic_slice_in_dim(kg, r0, win_h, axis=1)
        vb = lax.dynamic_slice_in_dim(vg, r0, win_h, axis=1)
        qr = lax.dynamic_index_in_dim(qg, r, axis=1, keepdims=False)
        row_off = r0 + jnp.arange(win_h) - r + NA_WIN_H - 1
        bias = jnp.take(jnp.take(rpb, row_off, axis=1), col_off, axis=2)
        bias = bias.transpose(0, 2, 1, 3).astype(jnp.float32)
        s_loc = jnp.einsum('bqhd,bkwhd->bhqkw', qr, kb, preferred_element_type=jnp.float32) * scale + bias[None]
        s_loc = jnp.where(col_mask[None, None, :, None, :], s_loc, NEG_INF)
        s_ctx = jnp.einsum('bqhd,bchd->bhqc', qr, k_ctx, preferred_element_type=jnp.float32) * scale
        s = jnp.concatenate([s_loc.reshape(B, H, GRID_W, n_loc), s_ctx], axis=-1)
        p = jax.nn.softmax(s, axis=-1).astype(v.dtype)
        p_loc = p[..., :n_loc].reshape(B, H, GRID_W, win_h, GRID_W)
        return (jnp.einsum('bhqkw,bkwhd->bqhd', p_loc, vb)
                + jnp.einsum('bhqc,bchd->bqhd', p[..., n_loc:], v_ctx))

    out = lax.map(one_row, jnp.arange(rows))
    return out.transpose(1, 0, 2, 3, 4).reshape(B, T, H * Dh)


def context_attention(q, k, v):
    s = jnp.einsum('bqhd,bkhd->bhqk', q, k, preferred_element_type=jnp.float32) * q.shape[-1] ** -0.5
    p = jax.nn.softmax(s, axis=-1).astype(v.dtype)
    o = jnp.einsum('bhqk,bkhd->bqhd', p, v)
    return o.reshape(o.shape[:2] + (-1,))


def multiscale_pool(u, pool_w, pool_scale):
    B, T, _ = u.shape
    uf = u.astype(jnp.float32).reshape(B, T, POOL_GROUPS, POOL_GROUP_DIM)
    csum = jnp.concatenate([jnp.zeros_like(uf[:, :1]), jnp.cumsum(uf, axis=1)], axis=1)
    t = jnp.arange(T)
    pooled = []
    for g, win in enumerate(POOL_WINDOWS):
        lo = jnp.maximum(t - win // 2, 0)
        hi = jnp.minimum(t + win // 2, T)
        cg = csum[:, :, g]
        pooled.append((cg[:, hi] - cg[:, lo]) / (hi - lo).astype(jnp.float32)[None, :, None])
    diff = (jnp.stack(pooled, axis=2) - uf).astype(u.dtype)
    y = jnp.einsum('btgc,gcd->btgd', diff, pool_w)
    return y.reshape(B, T, -1) * pool_scale


def centred_neighbour_mean(z):
    zp = jnp.pad(z, ((0, 0), (1, 1), (0, 0)))
    return 0.5 * (zp[:, :-2] + zp[:, 2:])


def rwkv_features(r0, k0, v0, lw, la, mu, w0, w2, a0, a2, k_k, k_a):
    f32 = jnp.float32
    r, k, v = (z + m * (centred_neighbour_mean(z) - z)
               for z, m in ((r0.astype(f32), mu[0]), (k0.astype(f32), mu[1]), (v0.astype(f32), mu[2])))
    kk = to_heads(k * k_k, RWKV_HEADS)
    kk = kk / jnp.maximum(jnp.sqrt(jnp.sum(kk * kk, axis=-1, keepdims=True)), 1e-12)
    lw_t = jnp.tanh(lw.astype(f32))
    la_f = la.astype(f32)
    dirs = []
    for d in range(2):
        w_log = -jax.nn.softplus(-(w0[d] + lw_t @ w2[d])) - 0.5
        decay = jnp.exp(-jnp.exp(w_log))
        a = jax.nn.sigmoid(a0[d] + la_f @ a2[d])
        k_d = k * (1 + (a - 1) * k_a)
        dirs.append((to_heads(decay, RWKV_HEADS), to_heads(k_d, RWKV_HEADS), -kk, kk * to_heads(a, RWKV_HEADS)))
    return to_heads(r, RWKV_HEADS), to_heads(v, RWKV_HEADS), dirs


def wkv_scan(r, v, decay, k, a_in, b_in, s0, reverse):
    def step(S, inp):
        rt, vt, wt, kt, at, bt = inp
        sa = jnp.einsum('bhij,bhj->bhi', S, at)
        S = S * wt[:, :, None, :] + sa[..., None] * bt[:, :, None, :] + vt[..., None] * kt[:, :, None, :]
        return S, jnp.einsum('bhij,bhj->bhi', S, rt)
    xs = tuple(jnp.moveaxis(z, 1, 0) for z in (r, v, decay, k, a_in, b_in))
    s_end, ys = lax.scan(step, s0, xs, reverse=reverse)
    return jnp.moveaxis(ys, 0, 1), s_end


def rwkv_readout(r, v, ys, dirs, r_k, g, b):
    y = ys[0] + ys[1]
    mu = jnp.mean(y, axis=-1, keepdims=True)
    var = jnp.mean(jnp.square(y - mu), axis=-1, keepdims=True)
    y = (y - mu) * lax.rsqrt(var + LNX_EPS) * to_heads(g, RWKV_HEADS) + to_heads(b, RWKV_HEADS)
    bonus = (jnp.sum(r * dirs[0][1] * r_k, axis=-1, keepdims=True)
             + jnp.sum(r * dirs[1][1] * r_k, axis=-1, keepdims=True)) * v
    out = y + bonus
    return out.reshape(out.shape[:2] + (-1,))


def hybrid_layer(x_lat, x_ctx, c, c_ctx, norm_g, w_mod, b_mod, w_in, na_rpb, pool_w, pool_scale,
                 rw_mu, rw_w0, rw_w2, rw_a0, rw_a2, rw_k_k, rw_k_a, rw_r_k, rw_lnx_g, rw_lnx_b,
                 w_branch, w_out, compute_ctx):
    n_ctx = x_ctx.shape[1]
    shift_l, scale_l, gate_l = adaln_modulation(c, w_mod, b_mod)
    shift_c, scale_c, gate_c = adaln_modulation(c_ctx, w_mod, b_mod)
    h_lat = rms_norm(x_lat, norm_g) * (1 + scale_l[:, None]) + shift_l[:, None]
    h_ctx = rms_norm(x_ctx, norm_g) * (1 + scale_c) + shift_c
    h = jnp.concatenate([h_ctx, h_lat], axis=1)
    (na_q, na_k, na_v, na_gate, pool_u, pool_gate, rw_r, rw_k, rw_v, rw_gate,
     rw_lw, rw_la, merge_logits) = jnp.split(h @ w_in, IN_SPLIT_POINTS, axis=-1)

    def cpart(z):
        return z[:, :n_ctx]

    def lpart(z):
        return z[:, n_ctx:]

    keep = (lambda z: z) if compute_ctx else lpart

    q, k, v = (to_heads(z, NA_HEADS) for z in (na_q, na_k, na_v))
    y_na = neighbourhood_attention(lpart(q), lpart(k), lpart(v), cpart(k), cpart(v), na_rpb)
    if compute_ctx:
        y_na = jnp.concatenate([context_attention(cpart(q), cpart(k), cpart(v)), y_na], axis=1)

    y_pool = multiscale_pool(lpart(pool_u), pool_w, pool_scale)
    if compute_ctx:
        y_pool = jnp.concatenate([multiscale_pool(cpart(pool_u), pool_w, pool_scale), y_pool], axis=1)

    rw_params = (rw_mu, rw_w0, rw_w2, rw_a0, rw_a2, rw_k_k, rw_k_a)
    r_c, v_c, dirs_c = rwkv_features(*(cpart(z) for z in (rw_r, rw_k, rw_v, rw_lw, rw_la)), *rw_params)
    r_l, v_l, dirs_l = rwkv_features(*(lpart(z) for z in (rw_r, rw_k, rw_v, rw_lw, rw_la)), *rw_params)
    ys_c, ys_l = [], []
    for d in range(2):
        s0 = jnp.zeros((h.shape[0], RWKV_HEADS, RWKV_HEAD_DIM, RWKV_HEAD_DIM), jnp.float32)
        y_c, s_ctx_end = wkv_scan(r_c, v_c, *dirs_c[d], s0, reverse=(d == 1))
        y_l, _ = wkv_scan(r_l, v_l, *dirs_l[d], s_ctx_end, reverse=(d == 1))
        ys_c.append(y_c)
        ys_l.append(y_l)
    y_rw = rwkv_readout(r_l, v_l, ys_l, dirs_l, rw_r_k, rw_lnx_g, rw_lnx_b)
    if compute_ctx:
        y_rw = jnp.concatenate([rwkv_readout(r_c, v_c, ys_c, dirs_c, rw_r_k, rw_lnx_g, rw_lnx_b), y_rw], axis=1)

    dt = h.dtype
    b_na = y_na.astype(dt) * jax.nn.silu(keep(na_gate))
    b_pool = y_pool.astype(dt) * jax.nn.silu(keep(pool_gate))
    b_rw = y_rw.astype(dt) * jax.nn.silu(keep(rw_gate))
    g_na, g_pool, g_rw = jnp.split(jax.nn.sigmoid(keep(merge_logits)), N_BRANCH, axis=-1)
    merged = g_na * (b_na @ w_branch[0]) + g_pool * (b_pool @ w_branch[1]) + g_rw * (b_rw @ w_branch[2])
    out = merged @ w_out
    if compute_ctx:
        return x_lat + gate_l[:, None] * lpart(out), x_ctx + gate_c * cpart(out)
    return x_lat + gate_l[:, None] * out, None


def setup_inputs(seed: int = 0) -> dict:
    key = jax.random.key(seed)
    ks = jax.random.split(key, 24)
    f32 = jnp.float32
    D, L, W, R = D_MODEL, DEPTH, W_BRANCH, RWKV_LORA

    def nrm(k, shape, std):
        return jax.random.normal(k, shape, f32) * std

    return {
        "x": nrm(ks[0], (BATCH, SEQ, D), 1.0),
        "c": nrm(ks[1], (BATCH, D), 1.0),
        "ctx": nrm(ks[2], (BATCH, CTX_LEN, D), 1.0),
        "c_ctx": nrm(ks[3], (D,), 1.0),
        "norm_g": 1.0 + nrm(ks[4], (L, D), 0.05),
        "w_mod": nrm(ks[5], (L, D, 3 * D), 0.5 * D ** -0.5),
        "b_mod": nrm(ks[6], (L, 3 * D), 0.02),
        "w_in": nrm(ks[7], (L, D, D_IN), D ** -0.5),
        "na_rpb": nrm(ks[8], (L, NA_HEADS, 2 * NA_WIN_H - 1, 2 * NA_WIN_W - 1), 0.1),
        "pool_w": nrm(ks[9], (L, POOL_GROUPS, POOL_GROUP_DIM, POOL_GROUP_DIM), POOL_GROUP_DIM ** -0.5),
        "pool_scale": 1.0 + nrm(ks[10], (L, W), 0.1),
        "rw_mu": jax.random.uniform(ks[11], (L, 3, W), f32),
        "rw_w0": jax.random.uniform(ks[12], (L, 2, W), f32, -6.0, -1.0),
        "rw_w2": nrm(ks[13], (L, 2, R, W), 0.5 * R ** -0.5),
        "rw_a0": nrm(ks[14], (L, 2, W), 0.5),
        "rw_a2": nrm(ks[15], (L, 2, R, W), 0.5 * R ** -0.5),
        "rw_k_k": 0.85 + nrm(ks[16], (L, W), 0.05),
        "rw_k_a": 1.0 + nrm(ks[17], (L, W), 0.05),
        "rw_r_k": nrm(ks[18], (L, RWKV_HEADS, RWKV_HEAD_DIM), 0.1),
        "rw_lnx_g": 1.0 + nrm(ks[19], (L, W), 0.05),
        "rw_lnx_b": nrm(ks[20], (L, W), 0.02),
        "w_branch": nrm(ks[21], (L, N_BRANCH, W, D), W ** -0.5),
        "w_out": nrm(ks[22], (L, D, D), D ** -0.5),
        "final_g": 1.0 + nrm(ks[23], (D,), 0.05),
    }


def reference(x, c, ctx, c_ctx, norm_g, w_mod, b_mod, w_in, na_rpb, pool_w, pool_scale,
              rw_mu, rw_w0, rw_w2, rw_a0, rw_a2, rw_k_k, rw_k_a, rw_r_k, rw_lnx_g, rw_lnx_b,
              w_branch, w_out, final_g):
    x_lat, x_ctx = x, ctx
    for layer in range(DEPTH):
        x_lat, x_ctx = hybrid_layer(
            x_lat, x_ctx, c, c_ctx, norm_g[layer], w_mod[layer], b_mod[layer], w_in[layer],
            na_rpb[layer], pool_w[layer], pool_scale[layer], rw_mu[layer], rw_w0[layer], rw_w2[layer],
            rw_a0[layer], rw_a2[layer], rw_k_k[layer], rw_k_a[layer], rw_r_k[layer], rw_lnx_g[layer],
            rw_lnx_b[layer], w_branch[layer], w_out[layer], compute_ctx=(layer < DEPTH - 1))
    return rms_norm(x_lat, final_g)
```

```python
import contextlib
import os
import numpy as np
import concourse.bass as bass
import concourse.mybir as mybir
from concourse.bass_utils import run_bass_kernel_spmd

F32 = mybir.dt.float32
BF16 = mybir.dt.bfloat16
AF = mybir.ActivationFunctionType
ALU = mybir.AluOpType
AX = mybir.AxisListType

D = 2048
SEQ = 4096
NCTX = 256
S = SEQ + NCTX
NT = S // 128
W = 1024
DEPTH = 2
D_IN = 16512
NCORES = 4
SAME_ENGINE_SYNC = True


class Ev:
    __slots__ = ("sem", "key", "val")

    def __init__(self, sem, key, val):
        self.sem, self.key, self.val = sem, key, val


class Buf:
    __slots__ = ("name", "w", "rs", "excl")

    def __init__(self, name, excl=False):
        self.name = name
        self.w = []
        self.rs = {}
        self.excl = excl


class Prog:
    N_LANES = {"sp": 24, "pool": 8, "act": 4}

    def __init__(self, nc, es):
        self.nc = nc
        self.es = es
        self.h = {"pe": nc.tensor, "act": nc.scalar, "dve": nc.vector, "pool": nc.gpsimd, "sp": nc.sync}
        self.sem = {e: es.enter_context(nc.semaphore("s_" + e)) for e in self.h}
        self.cnt = {e: 0 for e in self.h}
        self.seen = {e: {} for e in self.h}
        self.lanes = {}
        self.lane_rr = {}
        self.nsem = 0
        self.ninst = {e: 0 for e in self.h}

    def _lane(self, q):
        if q not in self.lanes:
            self.lanes[q] = []
            for i in range(self.N_LANES[q]):
                self.nsem += 1
                key = "d_%s%d" % (q, i)
                self.lanes[q].append([self.es.enter_context(self.nc.semaphore(key)), key, 0])
            self.lane_rr[q] = 0
        ln = self.lanes[q][self.lane_rr[q] % len(self.lanes[q])]
        self.lane_rr[q] += 1
        if ln[2] > 0:
            self._wait(q, Ev(ln[0], ln[1], ln[2]))
        return ln

    def _wait(self, eng, ev):
        if ev is None:
            return
        if ev.key == eng and not SAME_ENGINE_SYNC:
            return
        if ev.key == "pe" and eng == "pe":
            return
        if self.seen[eng].get(ev.key, 0) >= ev.val:
            return
        self.h[eng].wait_ge(ev.sem, ev.val)
        self.ninst[eng] += 1
        self.seen[eng][ev.key] = ev.val

    def _deps(self, eng, reads, writes):
        for b in reads:
            for ev in b.w:
                self._wait(eng, ev)
            if b.excl:
                for k, r in b.rs.items():
                    if k != eng:
                        self._wait(eng, r)
        for b in writes:
            for ev in b.w:
                self._wait(eng, ev)
            for r in b.rs.values():
                self._wait(eng, r)

    def op(self, eng, fn, reads=(), writes=()):
        self._deps(eng, reads, writes)
        inst = fn(self.h[eng])
        self.cnt[eng] += 1
        self.ninst[eng] += 1
        inst.then_inc(self.sem[eng], 1)
        ev = Ev(self.sem[eng], eng, self.cnt[eng])
        for b in reads:
            b.rs[eng] = ev
        for b in writes:
            b.w = [ev]
            b.rs = {}
        return ev

    def dma(self, q, out, in_, owner=None, reads=(), writes=(), nowait=False, **kw):
        if not nowait:
            self._deps(q, reads, writes)
        ln = self._lane(q)
        inst = self.h[q].dma_start(out=out, in_=in_, **kw)
        ln[2] += 16
        inst.then_inc(ln[0], 16)
        self.ninst[q] += 1
        ev = Ev(ln[0], ln[1], ln[2])
        for b in reads:
            b.rs[ln[1]] = ev
        for b in writes:
            if nowait:
                b.w = list(b.w) + [ev]
            else:
                b.w = [ev]
                b.rs = {}
        return ev

    def barrier(self):
        evs = [Ev(self.sem[e], e, self.cnt[e]) for e in self.h if self.cnt[e] > 0]
        for q, lanes in self.lanes.items():
            evs += [Ev(l[0], l[1], l[2]) for l in lanes if l[2] > 0]
        for e in self.h:
            for ev in evs:
                if ev.key == e:
                    if self.seen[e].get(e, 0) < ev.val and e != "sp":
                        self.h[e].wait_ge(ev.sem, ev.val)
                        self.seen[e][e] = ev.val
                    continue
                self._wait(e, ev)


_uid = [0]


def _sb(es, nc, name, shape, dt):
    _uid[0] += 1
    return es.enter_context(nc.sbuf_tensor("sb%d_%s" % (_uid[0], name), list(shape), dt))


def host_consts():
    idx = np.arange(128)
    masks = np.stack([(idx[:, None] < idx[None, :]), (idx[:, None] <= idx[None, :]),
                      (idx[:, None] > idx[None, :]), (idx[:, None] >= idx[None, :])]).astype(np.float32)
    blockones = (idx[:, None] // 64 == idx[None, :] // 64).astype(np.float32)
    resetmask = np.ones((128, 256), np.float32)
    resetmask[:, 0] = 0.0
    resetmask[:, 128] = 0.0
    headsel = np.zeros((128, 2), np.float32)
    headsel[:64, 0] = 1.0
    headsel[64:, 1] = 1.0
    sel = np.zeros((2, 2, 128), np.float32)
    sel[0, 0] = 1
    sel[1, 1] = 1
    return {"ident": np.eye(128, dtype=np.float32), "sel": sel, "masks": masks, "blockones": blockones,
            "resetmask": resetmask, "headsel": headsel}


def pack_rwpar(inp):
    cols = [inp["rw_mu"][:, 0], inp["rw_mu"][:, 1], inp["rw_mu"][:, 2], inp["rw_k_k"], inp["rw_k_a"],
            inp["rw_r_k"].reshape(DEPTH, W), inp["rw_w0"][:, 0], inp["rw_w0"][:, 1], inp["rw_a0"][:, 0], inp["rw_a0"][:, 1],
            inp["rw_lnx_g"], inp["rw_lnx_b"]]
    a = np.stack([np.asarray(c, np.float32) for c in cols], axis=-1)
    return np.ascontiguousarray(a.reshape(DEPTH, 8, 128, NPAR))


def setup_consts(G, es):
    nc, P = G.nc, G.P
    masks_in = G.din("masks", [4, 128, 128])
    bo_in = G.din("blockones", [128, 128])
    rm_in = G.din("resetmask", [128, 256])
    hs_in = G.din("headsel", [128, 2])
    G.masks = _sb(es, nc, "masks", [128, 4, 128], F32)
    G.B_masks = Buf("masks")
    for i in range(4):
        P.dma("sp", G.masks[:, i, :], masks_in[i], G.B_masks, writes=[G.B_masks], nowait=(i > 0))
    G.blockones = _sb(es, nc, "blockones", [128, 128], F32)
    G.B_bo = Buf("blockones")
    P.dma("sp", G.blockones[:], bo_in[:, :], G.B_bo, writes=[G.B_bo])
    G.resetmask = _sb(es, nc, "resetmask", [128, 256], F32)
    G.B_rm = Buf("resetmask")
    P.dma("sp", G.resetmask[:], rm_in[:, :], G.B_rm, writes=[G.B_rm])
    G.headsel = _sb(es, nc, "headsel", [128, 2], F32)
    G.B_hs = Buf("headsel")
    P.dma("sp", G.headsel[:], hs_in[:, :], G.B_hs, writes=[G.B_hs])


NPAR = 12
R_RWR, R_RWK, R_RWV, R_RWG, R_LW, R_LA = 3072, 4096, 5120, 6144, 7168, 7232
LOGW_SCALE = -0.6065306597126334


def phase_rwkv(G, layer, do_ctx_out=True):
    nc, P = G.nc, G.P
    restT, B_restT = G.restT, G.B_restT
    rwpar = G.rwpar
    rw_w2, rw_a2 = G.rw_w2, G.rw_a2
    lnx_g, lnx_b = G.lnx_g, G.lnx_b
    masks, B_masks = G.masks, G.B_masks
    M_lt, M_le, M_gt, M_ge = (masks[:, i, :] for i in range(4))

    def OP(eng, fn, reads=(), writes=()):
        return P.op(eng, fn, reads=reads, writes=writes)

    def mm(bk, o, lhsT, rhs, reads, start=True, stop=True):
        OP("pe", lambda e: e.matmul(o, lhsT=lhsT, rhs=rhs, start=start, stop=stop), reads=reads, writes=[bk])

    def tr(bk, o, in_, reads):
        OP("pe", lambda e: e.transpose(o, in_, G.ident[:]), reads=list(reads) + [G.B_ident], writes=[bk])

    def act(o, i, func, reads, writes, **kw):
        OP("act", lambda e: e.activation(out=o, in_=i, func=func, **kw), reads=reads, writes=writes)

    def tt(o, a, b, op, reads, writes, eng="dve"):
        OP(eng, lambda e: e.tensor_tensor(out=o, in0=a, in1=b, op=op), reads=reads, writes=writes)

    def ts(o, a, s1, s2, op0, op1, reads, writes, eng="dve"):
        if s2 is None:
            OP(eng, lambda e: e.tensor_scalar(out=o, in0=a, scalar1=s1, scalar2=None, op0=op0), reads=reads, writes=writes)
        else:
            OP(eng, lambda e: e.tensor_scalar(out=o, in0=a, scalar1=s1, scalar2=s2, op0=op0, op1=op1), reads=reads, writes=writes)

    def stt(o, a, sc, b, op0, op1, reads, writes):
        OP("dve", lambda e: e.scalar_tensor_tensor(out=o, in0=a, scalar=sc, in1=b, op0=op0, op1=op1), reads=reads, writes=writes)

    def cp(eng, o, i, reads, writes):
        if eng == "act":
            act(o, i, AF.Copy, reads, writes)
        else:
            OP(eng, lambda e: e.tensor_copy(out=o, in_=i), reads=reads, writes=writes)

    with contextlib.ExitStack() as ps:
        def sb(name, shape, dt=F32):
            return _sb(ps, nc, name, shape, dt)

        lwla = sb("lwla", [128, S])
        B_lwla = Buf("lwla")
        P.dma("sp", lwla[:], restT[R_LW:R_LW + 128, :], B_lwla, reads=[B_restT], writes=[B_lwla])
        act(lwla[0:64, :], lwla[0:64, :], AF.Tanh, [B_lwla], [B_lwla])

        rT = sb("r", [128, S]); B_r = Buf("r")
        kT = sb("k", [128, S]); B_k = Buf("k")
        vT = sb("vkkn", [128, S]); B_v = Buf("vkkn")
        ytok = sb("ytok", [128, NT, 128]); B_y = [Buf("ytok%d" % i) for i in range(NT)]
        bon = sb("bon", [128, NT, 2]); B_bon = [Buf("bon%d" % i) for i in range(NT)]
        vtok = sb("vtok", [128, NT, 128]); B_vt = [Buf("vtok%d" % i) for i in range(NT)]
        par = sb("par", [128, NPAR + 8]); B_par = Buf("par")
        w2t = sb("w2t", [128, 2, 128]); B_w2 = Buf("w2t")
        lnxg = sb("lnxg", [128, 128]); lnxb = sb("lnxb", [128, 128]); B_lnx = Buf("lnx")
        rkblk = sb("rkblk", [128, 2]); B_rkblk = Buf("rkblk")
        nsum = ytok[:].rearrange("p t c -> p (t c)")
        GW = 256
        gtmp = [[(sb("gt%d_%d" % (d, i), [128, GW]), Buf("gt%d_%d" % (d, i))) for i in range(6)] for d in range(2)]
        gout = [[[(sb("go%d_%d_%d" % (d, pz, i), [128, GW]), Buf("go%d_%d_%d" % (d, pz, i))) for i in range(5)]
                 for pz in range(2)] for d in range(2)]
        def ctile(name, w=128):
            return (sb(name, [128, w]), Buf(name))
        cper = [[[[{n: ctile("c%s%d%d%d%d" % (n, d, pz, c, h)) for n in ("Pm", "BmT", "RBT", "RKT")} for h in range(2)]
                  for c in range(2)] for pz in range(2)] for d in range(2)]
        cpair = [[[{n: ctile("c%s%d%d%d" % (n, d, pz, c)) for n in ("btok", "ktok")} for c in range(2)]
                  for pz in range(2)] for d in range(2)]
        ctmp = [[[{n: ctile("t%s%d%d%d" % (n, d, c, h)) for n in ("Xa", "XTa", "Xb", "XTb", "Pb")} for h in range(2)]
                 for c in range(2)] for d in range(2)]
        STs = [[ctile("ST%d%d" % (d, i), 64) for i in range(2)] for d in range(2)]
        S0dec = [ctile("S0dec%d" % d, 64) for d in range(2)]
        Gt = [ctile("G%d" % d) for d in range(2)]
        SAt = [ctile("SA%d" % d) for d in range(2)]
        yst = [ctmp[0][0][1]["Xa"], ctmp[0][0][1]["XTa"]]
        ost = [(sb("rwo%d" % i, [128, 128], BF16), Buf("rwo%d" % i)) for i in range(2)]
        gts = [ctmp[0][0][0]["Xa"], ctmp[0][0][0]["XTa"]]
        small = [ctile("sm%d" % i, 8) for i in range(4)]

        RW_STAGE = int(os.environ.get("RW_STAGE", "99"))
        RW_SUB = int(os.environ.get("RW_SUB", "99"))
        for hp in range(int(os.environ.get("RW_PAIRS", "8"))):
            ch0 = hp * 128
            P.dma("sp", par[:, 0:NPAR], rwpar[layer, hp], B_par, writes=[B_par])
            for d in range(2):
                P.dma("sp", w2t[0:64, d, :], rw_w2[layer, d, :, ch0:ch0 + 128], B_w2, writes=[B_w2], nowait=(d > 0))
                P.dma("sp", w2t[64:128, d, :], rw_a2[layer, d, :, ch0:ch0 + 128], B_w2, writes=[B_w2], nowait=True)
            P.dma("sp", lnxg[:], lnx_g[layer, ch0:ch0 + 128].partition_broadcast(128), B_lnx, writes=[B_lnx])
            P.dma("sp", lnxb[:], lnx_b[layer, ch0:ch0 + 128].partition_broadcast(128), B_lnx, writes=[B_lnx], nowait=True)
            ts(par[:, NPAR:NPAR + 3], par[:, 0:3], -1.0, 1.0, ALU.mult, ALU.add, [B_par], [B_par])
            ts(par[:, NPAR + 3:NPAR + 6], par[:, 0:3], 0.5, None, ALU.mult, None, [B_par], [B_par])
            ts(par[:, NPAR + 6:NPAR + 7], par[:, 4:5], -1.0, 1.0, ALU.mult, ALU.add, [B_par], [B_par])
            ts(rkblk[:], G.headsel[:], par[:, 5:6], None, ALU.mult, None, [B_par, G.B_hs], [B_rkblk])
            C_KK, C_KA, C_OMKA = par[:, 3:4], par[:, 4:5], par[:, NPAR + 6:NPAR + 7]

            for zi, (zt, B_z, row) in enumerate(((rT, B_r, R_RWR), (kT, B_k, R_RWK), (vT, B_v, R_RWV))):
                P.dma("sp", zt[:], restT[row + ch0:row + ch0 + 128, :], B_z, reads=[B_restT], writes=[B_z])
                tt(nsum[:, 1:S - 1], zt[:, 0:S - 2], zt[:, 2:S], ALU.add, [B_z], B_y)
                for (dst, srcc) in ((0, 1), (255, 254), (256, 257), (S - 1, S - 2)):
                    cp("dve", nsum[:, dst:dst + 1], zt[:, srcc:srcc + 1], [B_z], B_y)
                ts(nsum[:, :], nsum[:, :], par[:, NPAR + 3 + zi:NPAR + 4 + zi], None, ALU.mult, None, B_y + [B_par], B_y)
                stt(zt[:], zt[:], par[:, NPAR + zi:NPAR + 1 + zi], nsum[:, :], ALU.mult, ALU.add, [B_z, B_par] + B_y, [B_z])
            if RW_STAGE < 2:
                continue
            for ti in range(NT):
                bk, B_bk = G.next_bank()
                tr(B_bk, bk[:, 0:128], vT[:, ti * 128:(ti + 1) * 128], [B_v])
                cp("act" if ti % 2 == 0 else "dve", vtok[:, ti, :], bk[:, 0:128], [B_bk], [B_vt[ti]])
            act(nsum[:, :], kT[:], AF.Copy, [B_k, B_par], B_y, scale=C_KK)
            act(vT[:], nsum[:, :], AF.Square, B_y + B_vt, [B_v])
            for t0 in range(0, S, 512):
                tw = min(512, S - t0)
                bk, B_bk = G.next_bank()
                mm(B_bk, bk[:, 0:tw], G.blockones[:], vT[:, t0:t0 + tw], [G.B_bo, B_v])
                act(vT[:, t0:t0 + tw], bk[:, 0:tw], AF.Sqrt, [B_bk], [B_v])
            ts(vT[:], vT[:], 1e-12, None, ALU.max, None, [B_v], [B_v])
            OP("dve", lambda e: e.reciprocal(out=vT[:], in_=vT[:]), reads=[B_v], writes=[B_v])
            tt(vT[:], vT[:], nsum[:, :], ALU.mult, [B_v] + B_y, [B_v])
            kkn, B_kkn = vT, B_v

            if RW_STAGE < 3:
                continue
            order = {0: list(range(17)), 1: [0] + list(range(16, 0, -1))}
            st_idx = [0, 0]
            ywritten = set()
            bwritten = set()
            for d in range(2):
                if RW_SUB < -1:
                    break
                OP("pool", lambda e, d=d: e.memset(STs[d][0][0][:], 0.0), writes=[STs[d][0][1]])

            def prep_rounds(d, step):
                g = order[d][step]
                pz = step % 2
                t0 = g * GW
                (sg, B_sg), (cs, B_cs), (tmp, B_tmp), (ad, B_ad), (kd, B_kd), (en, B_en) = gtmp[d]
                ukd, B_ukd = sg, B_sg
                (Ep, B_Ep), (aTt, B_aT), (bTt, B_bT), (kTt, B_kT), (rTt, B_rT) = gout[d][pz]
                rounds = []

                def r0():
                    if RW_SUB < 0:
                        return
                    bk, B_bk = G.next_bank()
                    bk2, B_bk2 = G.next_bank()
                    mm(B_bk, bk[:, 0:GW], w2t[0:64, d, :], lwla[0:64, t0:t0 + GW], [B_w2, B_lwla])
                    mm(B_bk2, bk2[:, 0:GW], w2t[64:128, d, :], lwla[64:128, t0:t0 + GW], [B_w2, B_lwla])
                    act(sg[:], bk[:, 0:GW], AF.Sigmoid, [B_bk, B_par], [B_sg], bias=par[:, 6 + d:7 + d])
                    act(ad[:], bk2[:, 0:GW], AF.Sigmoid, [B_bk2, B_par], [B_ad], bias=par[:, 8 + d:9 + d])
                    if RW_SUB < 1:
                        return
                    OP("dve", lambda e: e.tensor_tensor_scan(out=cs[:], data0=G.resetmask[:], data1=sg[:], initial=0.0,
                                                            op0=ALU.mult, op1=ALU.add), reads=[B_sg, G.B_rm], writes=[B_cs])
                    if d == 1 and RW_SUB >= 2:
                        cs3 = cs[:].rearrange("p (c k) -> p c k", k=128)
                        tot = cs3[:, :, 127:128].to_broadcast([128, 2, 128])
                        tt(tmp[:].rearrange("p (c k) -> p c k", k=128), tot, cs3, ALU.subtract, [B_cs], [B_tmp])
                        tt(cs[:], tmp[:], sg[:], ALU.add, [B_tmp, B_sg], [B_cs])
                rounds.append(r0)
                if RW_SUB < 3:
                    return rounds

                def r1():
                    act(Ep[:], cs[:], AF.Exp, [B_cs], [B_Ep], scale=LOGW_SCALE)
                    act(en[:], cs[:], AF.Exp, [B_cs], [B_en], scale=-LOGW_SCALE)
                    tt(tmp[:], cs[:], sg[:], ALU.subtract, [B_cs, B_sg], [B_tmp])
                    act(tmp[:], tmp[:], AF.Exp, [B_tmp], [B_tmp], scale=LOGW_SCALE)
                    ts(kd[:], ad[:], C_KA, C_OMKA, ALU.mult, ALU.add, [B_ad, B_par], [B_kd])
                    tt(kd[:], kd[:], kT[:, t0:t0 + GW], ALU.mult, [B_kd, B_k], [B_kd])
                rounds.append(r1)
                if RW_SUB < 4:
                    return rounds

                def r2():
                    tt(ukd[:], rT[:, t0:t0 + GW], kd[:], ALU.mult, [B_r, B_kd], [B_ukd], eng="pool")
                    stt(aTt[:], kkn[:, t0:t0 + GW], -1.0, tmp[:], ALU.mult, ALU.mult, [B_kkn, B_tmp], [B_aT])
                    tt(bTt[:], kkn[:, t0:t0 + GW], ad[:], ALU.mult, [B_kkn, B_ad], [B_bT])
                    tt(bTt[:], bTt[:], en[:], ALU.mult, [B_bT, B_en], [B_bT])
                    tt(kTt[:], kd[:], en[:], ALU.mult, [B_kd, B_en], [B_kT])
                    tt(rTt[:], rT[:, t0:t0 + GW], Ep[:], ALU.mult, [B_r, B_Ep], [B_rT], eng="pool")
                rounds.append(r2)

                strict_st, incl_st = (M_lt, M_le) if d == 0 else (M_gt, M_ge)
                strict_ts = M_gt if d == 0 else M_lt

                def r3():
                    for c in range(2):
                        cs_ = slice(c * 128, (c + 1) * 128)
                        bkAs = [G.next_bank(), G.next_bank()]
                        for h in range(2):
                            hs = slice(h * 64, (h + 1) * 64)
                            bkA, B_A = bkAs[h]
                            mm(B_A, bkA[:, 0:128], bTt[hs, cs_], aTt[hs, cs_], [B_bT, B_aT])
                            mm(B_A, bkA[:, 128:256], aTt[hs, cs_], bTt[hs, cs_], [B_bT, B_aT])
                        for h in range(2):
                            T = ctmp[d][c][h]
                            bkA, B_A = bkAs[h]
                            tt(T["Xa"][0][:], bkA[:, 0:128], strict_st, ALU.mult,
                               [B_A, B_masks], [T["Xa"][1]])
                            tt(T["XTa"][0][:], bkA[:, 128:256], strict_ts, ALU.mult,
                               [B_A, B_masks], [T["XTa"][1]])
                            tt(T["Pb"][0][:], T["Xa"][0][:], G.ident[:], ALU.add, [T["Xa"][1], G.B_ident], [T["Pb"][1]], eng="pool")
                        for h in range(2):
                            hs = slice(h * 64, (h + 1) * 64)
                            bkB, B_B = G.next_bank()
                            Cp = cper[d][pz][c][h]
                            mm(B_B, bkB[:, 0:128], kTt[hs, cs_], aTt[hs, cs_], [B_kT, B_aT])
                            mm(B_B, bkB[:, 128:256], bTt[hs, cs_], rTt[hs, cs_], [B_bT, B_rT])
                            mm(B_B, bkB[:, 256:384], kTt[hs, cs_], rTt[hs, cs_], [B_kT, B_rT])
                            tt(Cp["BmT"][0][:], bkB[:, 0:128], strict_st, ALU.mult, [B_B, B_masks], [Cp["BmT"][1]])
                            tt(Cp["RBT"][0][:], bkB[:, 128:256], incl_st, ALU.mult, [B_B, B_masks], [Cp["RBT"][1]])
                            tt(Cp["RKT"][0][:], bkB[:, 256:384], incl_st, ALU.mult, [B_B, B_masks], [Cp["RKT"][1]])
                        bkC, B_C = G.next_bank()
                        tr(B_C, bkC[:, 0:128], bTt[:, cs_], [B_bT])
                        tr(B_C, bkC[:, 128:256], kTt[:, cs_], [B_kT])
                        cp("act", cpair[d][pz][c]["btok"][0][:], bkC[:, 0:128], [B_C], [cpair[d][pz][c]["btok"][1]])
                        cp("act", cpair[d][pz][c]["ktok"][0][:], bkC[:, 128:256], [B_C], [cpair[d][pz][c]["ktok"][1]])
                if RW_STAGE < 4:
                    return rounds
                rounds.append(r3)

                def r3b():
                    bk, B_bk = G.next_bank()
                    for c in range(2):
                        mm(B_bk, bk[:, 2 * c:2 * c + 2], ukd[:, c * 128:(c + 1) * 128], rkblk[:], [B_ukd, B_rkblk])
                    for c in range(2):
                        ti = g * 2 + c
                        if ti not in bwritten:
                            bwritten.add(ti)
                            cp("act", bon[:, ti, :], bk[:, 2 * c:2 * c + 2], [B_bk], [B_bon[ti]])
                        else:
                            tt(bon[:, ti, :], bk[:, 2 * c:2 * c + 2], bon[:, ti, :], ALU.add, [B_bk, B_bon[ti]], [B_bon[ti]])
                rounds.append(r3b)

                def make_level(lvl):
                    def rl():
                        src, dst = ("a", "b") if lvl % 2 == 1 else ("b", "a")
                        last = (lvl == 6)
                        banks_ = []
                        for c in range(2):
                            bk, B_bk = G.next_bank()
                            banks_.append((bk, B_bk))
                            for h in range(2):
                                T = ctmp[d][c][h]
                                X, B_X = T["X" + src]
                                XT, B_XT = T["XT" + src]
                                if not last:
                                    mm(B_bk, bk[:, (2 * h) * 128:(2 * h + 1) * 128], XT[:], X[:], [B_X, B_XT])
                                mm(B_bk, bk[:, (2 * h + 1) * 128:(2 * h + 2) * 128], X[:], XT[:], [B_X, B_XT])
                        RW_DBG = int(os.environ.get("RW_DBG", "0"))
                        for c in range(2):
                            if RW_DBG == 1:
                                break
                            bk, B_bk = banks_[c]
                            for h in range(2):
                                T = ctmp[d][c][h]
                                if not last:
                                    cp("act", T["X" + dst][0][:], bk[:, (2 * h) * 128:(2 * h + 1) * 128], [B_bk], [T["X" + dst][1]])
                                cp("act" if (h == 0 or RW_DBG == 2) else "dve", T["XT" + dst][0][:], bk[:, (2 * h + 1) * 128:(2 * h + 2) * 128],
                                   [B_bk], [T["XT" + dst][1]])
                        if RW_SUB < 11:
                            return
                        for c in range(2):
                            bk, B_bk = G.next_bank()
                            for h in range(2):
                                T = ctmp[d][c][h]
                                Cp = cper[d][pz][c][h]
                                Pold, B_Pold = T["Pb"] if lvl % 2 == 1 else Cp["Pm"]
                                mm(B_bk, bk[:, h * 128:(h + 1) * 128], T["XT" + dst][0][:], Pold[:], [T["XT" + dst][1], B_Pold])
                            for h in range(2):
                                T = ctmp[d][c][h]
                                Cp = cper[d][pz][c][h]
                                Pold, B_Pold = T["Pb"] if lvl % 2 == 1 else Cp["Pm"]
                                Pnew, B_Pnew = Cp["Pm"] if lvl % 2 == 1 else T["Pb"]
                                tt(Pnew[:], bk[:, h * 128:(h + 1) * 128], Pold[:], ALU.add, [B_bk, B_Pold], [B_Pnew])
                    return rl
                if RW_STAGE < 5:
                    return rounds
                for lvl in range(1, 1 + int(os.environ.get("RW_LVL", "6"))):
                    rounds.append(make_level(lvl))
                def rfin():
                    for c in range(2):
                        for h in range(2):
                            T = ctmp[d][c][h]
                            Cp = cper[d][pz][c][h]
                            cp("pool", Cp["Pm"][0][:], T["Pb"][0][:], [T["Pb"][1]], [Cp["Pm"][1]])
                if RW_SUB >= 12:
                    rounds.append(rfin)
                return rounds

            def chain_rounds(d, step):
                g = order[d][step]
                pz = step % 2
                (Ep, B_Ep), (aTt, B_aT), (bTt, B_bT), (kTt, B_kT), (rTt, B_rT) = gout[d][pz]
                rounds = []
                corder = (0, 1) if d == 0 else (1, 0)
                for c in corder:
                    ti = g * 2 + c
                    cs_ = slice(c * 128, (c + 1) * 128)
                    ecol = c * 128 + (127 if d == 0 else 0)
                    eLC = Ep[:, ecol:ecol + 1]

                    def b1(c=c, ti=ti, cs_=cs_, eLC=eLC):
                        ST, B_ST = STs[d][st_idx[d] % 2]
                        bk, B_bk = G.next_bank()
                        for h in range(2):
                            hs = slice(h * 64, (h + 1) * 64)
                            Cp = cper[d][pz][c][h]
                            o = bk[:, h * 64:(h + 1) * 64]
                            mm(B_bk, o, Cp["BmT"][0][:], vtok[:, ti, h * 64:(h + 1) * 64], [Cp["BmT"][1], B_vt[ti]], start=True, stop=False)
                            mm(B_bk, o, aTt[hs, cs_], ST[hs, :], [B_aT, B_ST], start=False, stop=True)
                        cp("act", Gt[d][0][:], bk[:, 0:128], [B_bk], [Gt[d][1]])
                        ts(S0dec[d][0][:], ST[:], eLC, None, ALU.mult, None, [B_ST, B_Ep], [S0dec[d][1]], eng="pool")
                    rounds.append(b1)

                    def b2(c=c):
                        bk, B_bk = G.next_bank()
                        for h in range(2):
                            Cp = cper[d][pz][c][h]
                            mm(B_bk, bk[:, h * 64:(h + 1) * 64], Cp["Pm"][0][:], Gt[d][0][:, h * 64:(h + 1) * 64], [Cp["Pm"][1], Gt[d][1]])
                        cp("dve", SAt[d][0][:], bk[:, 0:128], [B_bk], [SAt[d][1]])
                    rounds.append(b2)

                    def b3(c=c, ti=ti, cs_=cs_, eLC=eLC):
                        ST, B_ST = STs[d][st_idx[d] % 2]
                        STn, B_STn = STs[d][(st_idx[d] + 1) % 2]
                        st_idx[d] += 1
                        SA, B_SA = SAt[d]
                        bk, B_bk = G.next_bank()
                        for h in range(2):
                            hs = slice(h * 64, (h + 1) * 64)
                            Cp = cper[d][pz][c][h]
                            o = bk[:, h * 64:(h + 1) * 64]
                            mm(B_bk, o, rTt[hs, cs_], ST[hs, :], [B_rT, B_ST], start=True, stop=False)
                            mm(B_bk, o, Cp["RBT"][0][:], SA[:, h * 64:(h + 1) * 64], [Cp["RBT"][1], B_SA], start=False, stop=False)
                            mm(B_bk, o, Cp["RKT"][0][:], vtok[:, ti, h * 64:(h + 1) * 64], [Cp["RKT"][1], B_vt[ti]], start=False, stop=True)
                        bk2, B_bk2 = G.next_bank()
                        Cq = cpair[d][pz][c]
                        mm(B_bk2, bk2[:, 0:128], Cq["ktok"][0][:], vtok[:, ti, :], [Cq["ktok"][1], B_vt[ti]], start=True, stop=False)
                        mm(B_bk2, bk2[:, 0:128], Cq["btok"][0][:], SA[:], [Cq["btok"][1], B_SA], start=False, stop=True)
                        if ti not in ywritten:
                            ywritten.add(ti)
                            cp("act", ytok[:, ti, :], bk[:, 0:128], [B_bk], [B_y[ti]])
                        else:
                            tt(ytok[:, ti, :], bk[:, 0:128], ytok[:, ti, :], ALU.add, [B_bk, B_y[ti]], [B_y[ti]])
                        for h in range(2):
                            hs = slice(h * 64, (h + 1) * 64)
                            stt(STn[hs, :], bk2[hs, h * 64:(h + 1) * 64], eLC[hs, :], S0dec[d][0][hs, :], ALU.mult, ALU.add,
                                [B_bk2, B_Ep, S0dec[d][1]], [B_STn])
                    rounds.append(b3)
                return rounds

            def interleave(lists):
                n = max(len(l) for l in lists) if lists else 0
                for i in range(n):
                    for l in lists:
                        if i < len(l):
                            l[i]()

            nsteps = 17
            for step in range(nsteps + 1):
                lists = []
                for d in range(2):
                    if step < nsteps:
                        lists.append(prep_rounds(d, step))
                    if step >= 1 and RW_STAGE >= 6:
                        lists.append(chain_rounds(d, step - 1))
                interleave(lists)

            if RW_STAGE < 7:
                continue
            for ti in range(NT):
                t0 = ti * 128
                sm, B_sm = small[ti % 4]
                y, B_yy = ytok[:, ti, :], B_y[ti]
                yt, B_yt = yst[ti % 2]
                y3 = y.rearrange("p (h c) -> p h c", h=2)
                cp("act", sm[:, 6:8], bon[:, ti, :], [B_bon[ti]], [B_sm])
                OP("dve", lambda e, y3=y3, sm=sm: e.tensor_reduce(out=sm[:, 0:2], in_=y3, axis=AX.X, op=ALU.add), reads=[B_yy], writes=[B_sm])
                ts(sm[:, 0:2], sm[:, 0:2], 1.0 / 64, None, ALU.mult, None, [B_sm], [B_sm])
                for h in range(2):
                    ts(yt[:, h * 64:(h + 1) * 64], y[:, h * 64:(h + 1) * 64], sm[:, h:h + 1], None, ALU.subtract, None, [B_yy, B_sm], [B_yt])
                sq, B_sq = gts[ti % 2]
                act(sq[:], yt[:], AF.Square, [B_yt], [B_sq])
                OP("dve", lambda e, sq=sq, sm=sm: e.tensor_reduce(out=sm[:, 2:4], in_=sq[:].rearrange("p (h c) -> p h c", h=2), axis=AX.X,
                                                                op=ALU.add), reads=[B_sq], writes=[B_sm])
                ts(sm[:, 2:4], sm[:, 2:4], 1.0 / 64, 64e-5, ALU.mult, ALU.add, [B_sm], [B_sm])
                act(sm[:, 2:4], sm[:, 2:4], AF.Sqrt, [B_sm], [B_sm])
                OP("dve", lambda e, sm=sm: e.reciprocal(out=sm[:, 4:6], in_=sm[:, 2:4]), reads=[B_sm], writes=[B_sm])
                for h in range(2):
                    hsl = slice(h * 64, (h + 1) * 64)
                    stt(yt[:, hsl], yt[:, hsl], sm[:, 4 + h:5 + h], lnxg[:, hsl], ALU.mult, ALU.mult, [B_yt, B_sm, B_lnx], [B_yt])
                tt(yt[:], yt[:], lnxb[:], ALU.add, [B_yt, B_lnx], [B_yt])
                for h in range(2):
                    hsl = slice(h * 64, (h + 1) * 64)
                    stt(yt[:, hsl], vtok[:, ti, hsl], sm[:, 6 + h:7 + h], yt[:, hsl], ALU.mult, ALU.add, [B_vt[ti], B_sm, B_yt], [B_yt])
                bk, B_bk = G.next_bank()
                tr(B_bk, bk[:, 0:128], yt[:], [B_yt])
                gt, B_gt = gts[ti % 2]
                P.dma("sp", gt[:], restT[R_RWG + ch0:R_RWG + ch0 + 128, t0:t0 + 128], B_gt, reads=[B_restT], writes=[B_gt])
                act(gt[:], gt[:], AF.Silu, [B_gt], [B_gt])
                ot, B_ot = ost[ti % 2]
                tt(ot[:], bk[:, 0:128], gt[:], ALU.mult, [B_bk, B_gt], [B_ot])
                P.dma("sp", G.brT[2 * W + ch0:2 * W + ch0 + 128, t0:t0 + 128], ot[:], G.B_brT, reads=[B_ot])
        P.barrier()


def mk_helpers(G):
    P = G.P

    class H:
        pass
    H_ = H()

    def OP(eng, fn, reads=(), writes=()):
        return P.op(eng, fn, reads=reads, writes=writes)

    def mm(bk, o, lhsT, rhs, reads, start=True, stop=True):
        OP("pe", lambda e: e.matmul(o, lhsT=lhsT, rhs=rhs, start=start, stop=stop), reads=reads, writes=[bk])

    def tr(bk, o, in_, reads):
        OP("pe", lambda e: e.transpose(o, in_, G.ident[:]), reads=list(reads) + [G.B_ident], writes=[bk])

    def act(o, i, func, reads, writes, **kw):
        OP("act", lambda e: e.activation(out=o, in_=i, func=func, **kw), reads=reads, writes=writes)

    def tt(o, a, b, op, reads, writes, eng="dve"):
        OP(eng, lambda e: e.tensor_tensor(out=o, in0=a, in1=b, op=op), reads=reads, writes=writes)

    def ts(o, a, s1, s2, op0, op1, reads, writes, eng="dve"):
        if s2 is None:
            OP(eng, lambda e: e.tensor_scalar(out=o, in0=a, scalar1=s1, scalar2=None, op0=op0), reads=reads, writes=writes)
        else:
            OP(eng, lambda e: e.tensor_scalar(out=o, in0=a, scalar1=s1, scalar2=s2, op0=op0, op1=op1), reads=reads, writes=writes)

    def stt(o, a, sc, b, op0, op1, reads, writes):
        OP("dve", lambda e: e.scalar_tensor_tensor(out=o, in0=a, scalar=sc, in1=b, op0=op0, op1=op1), reads=reads, writes=writes)

    def cp(eng, o, i, reads, writes):
        if eng == "act":
            act(o, i, AF.Copy, reads, writes)
        else:
            OP(eng, lambda e: e.tensor_copy(out=o, in_=i), reads=reads, writes=writes)

    def memset(eng, o, val, writes):
        OP(eng, lambda e: e.memset(o, val), writes=writes)
    return OP, mm, tr, act, tt, ts, stt, cp, memset


R_NAG, R_PU, R_PG, R_MERGE = 0, 1024, 2048, 7296
PADW = 8 + 256 + 16 + 4096 + 16
OFF_C, OFF_L = 8, 8 + 256 + 16


def host_pool_invcnt():
    out = np.zeros((4, S), np.float32)
    for g, win in enumerate((2, 4, 8, 16)):
        for (o, T) in ((0, NCTX), (NCTX, SEQ)):
            t = np.arange(T)
            lo = np.maximum(t - win // 2, 0)
            hi = np.minimum(t + win // 2, T)
            out[g, o:o + T] = 1.0 / (hi - lo)
    return out


def phase_pool(G, layer):
    nc, P = G.nc, G.P
    OP, mm, tr, act, tt, ts, stt, cp, memset = mk_helpers(G)
    restT, B_restT = G.restT, G.B_restT
    with contextlib.ExitStack() as ps:
        def sb(name, shape, dt=F32):
            return _sb(ps, nc, name, shape, dt)
        A = sb("pA", [128, PADW]); B_A = Buf("pA")
        Bt = sb("pB", [128, PADW]); B_B = Buf("pB")
        Ct = sb("pC", [128, PADW]); B_C = Buf("pC")
        inv = sb("pinv", [128, S]); B_inv = Buf("pinv")
        diff = [(sb("pdiff%d" % i, [128, S], BF16), Buf("pdiff%d" % i)) for i in range(2)]
        gate = sb("pgate", [128, S]); B_gate = Buf("pgate")
        pw = sb("ppw", [128, 2, 256], BF16); B_pw = Buf("ppw")
        psc = sb("ppsc", [128, 2]); B_psc = Buf("ppsc")
        stg = [(sb("pstg%d" % i, [128, 512], BF16), Buf("pstg%d" % i)) for i in range(2)]
        for t_, b_ in ((A, B_A), (Bt, B_B), (Ct, B_C)):
            memset("pool", t_[:], 0.0, [b_])

        def zero_gaps(t_, b_):
            memset("pool", t_[:, 0:OFF_C], 0.0, [b_])
            memset("pool", t_[:, OFF_C + 256:OFF_L], 0.0, [b_])
            memset("pool", t_[:, OFF_L + 4096:PADW], 0.0, [b_])

        R0, R1 = 4, PADW - 8
        for g in range(4):
            win = (2, 4, 8, 16)[g]
            P.dma("sp", inv[:], G.pool_invcnt[g].partition_broadcast(128), None, writes=[B_inv])
            P.dma("pool", pw[:], G.pool_w[layer, g].rearrange("(k p) d -> p k d", p=128), None, writes=[B_pw])
            P.dma("sp", psc[:], G.pool_scale[layer, g * 256:(g + 1) * 256].rearrange("(k p) -> p k", p=128), None, writes=[B_psc])
            for cbi in range(2):
                cb = g * 2 + cbi
                row = R_PU + cb * 128
                P.dma("sp", A[:, OFF_C:OFF_C + 256], restT[row:row + 128, 0:256], None, reads=[B_restT], writes=[B_A])
                P.dma("sp", A[:, OFF_L:OFF_L + 4096], restT[row:row + 128, 256:S], None, reads=[B_restT], writes=[B_A], nowait=True)
                tt(Bt[:, R0:R1], A[:, R0 - 1:R1 - 1], A[:, R0:R1], ALU.add, [B_A], [B_B])
                cur, B_cur = Bt, B_B
                oth, B_oth = Ct, B_C
                sh = 1
                w_ = 2
                while w_ < win:
                    tt(oth[:, R0:R1], cur[:, R0 - sh:R1 - sh], cur[:, R0 + sh:R1 + sh], ALU.add, [B_cur], [B_oth])
                    cur, B_cur, oth, B_oth = oth, B_oth, cur, B_cur
                    sh *= 2
                    w_ *= 2
                dt_, B_d = diff[cbi]
                for (po, so, T) in ((OFF_C, 0, 256), (OFF_L, 256, 4096)):
                    tt(cur[:, po:po + T], cur[:, po:po + T], inv[:, so:so + T], ALU.mult, [B_cur, B_inv], [B_cur])
                    tt(dt_[:, so:so + T], cur[:, po:po + T], A[:, po:po + T], ALU.subtract, [B_cur, B_A], [B_d])
            for dch in range(2):
                row = R_PG + g * 256 + dch * 128
                P.dma("sp", gate[:], restT[row:row + 128, :], None, reads=[B_restT], writes=[B_gate])
                act(gate[:], gate[:], AF.Silu, [B_gate], [B_gate])
                for bi, t0 in enumerate(range(0, S, 512)):
                    tw = min(512, S - t0)
                    bk, B_bk = G.next_bank()
                    for k in range(2):
                        mm(B_bk, bk[:, 0:tw], pw[:, k, dch * 128:(dch + 1) * 128], diff[k][0][:, t0:t0 + tw], [B_pw, diff[k][1]],
                           start=(k == 0), stop=(k == 1))
                    st_, B_st = stg[bi % 2]
                    stt(st_[:, 0:tw], bk[:, 0:tw], psc[:, dch:dch + 1], gate[:, t0:t0 + tw], ALU.mult, ALU.mult,
                        [B_bk, B_psc, B_gate], [B_st])
                    orow = W + g * 256 + dch * 128
                    P.dma("sp", G.brT[orow:orow + 128, t0:t0 + tw], st_[:, 0:tw], None, reads=[B_st], writes=[])
        P.barrier()


def host_na_table(na_rpb):
    L = na_rpb.shape[0]
    NEG = np.float32(-30000.0)
    col = np.arange(64)
    cs = np.clip(col - 8, 0, 48)
    cmask = (col[:, None] >= cs[None, :]) & (col[:, None] < cs[None, :] + 16)
    coff = np.clip(col[:, None] - col[None, :] + 15, 0, 30)
    tab = np.full((L, 16, 2, 64, 18, 64), NEG, np.float32)
    for par in range(2):
        for j in range(16):
            ro = j - 1 + par
            if 0 <= ro <= 14:
                g = na_rpb[:, :, ro][:, :, coff]
                tab[:, :, par, :, j, :] = np.where(cmask[None, None], g, NEG)
    tab[:, :, 1, :, 16, :] = np.where(cmask[None, None], na_rpb[:, :, 3][:, :, coff], NEG)
    tab[:, :, 0, :, 17, :] = np.where(cmask[None, None], na_rpb[:, :, 10][:, :, coff], NEG)
    return np.ascontiguousarray(tab.reshape(L, 16, 128, 18, 64))


def phase_na(G, layer, do_ctx=True):
    nc, P = G.nc, G.P
    OP, mm, tr, act, tt, ts, stt, cp, memset = mk_helpers(G)
    restT, B_restT = G.restT, G.B_restT
    qkT, B_qkT, vaug, B_vaug = G.qkT, G.B_qkT, G.vaug, G.B_vaug
    with contextlib.ExitStack() as ps:
        def sb(name, shape, dt=F32):
            return _sb(ps, nc, name, shape, dt)
        V = sb("naV", [128, NT, 16 * 65], BF16); B_V = Buf("naV")
        vsrc = vaug.rearrange("(n p) h e -> p n (h e)", p=128)
        for i in range(0, NT, 6):
            j = min(NT, i + 6)
            P.dma("sp", V[:, i:j, :], vsrc[:, i:j, :], None, reads=[B_vaug], writes=[B_V], nowait=(i > 0))
        qs = [(sb("naq%d" % i, [64, S], BF16), Buf("naq%d" % i)) for i in range(2)]
        ks = [(sb("nak%d" % i, [64, S], BF16), Buf("nak%d" % i)) for i in range(2)]
        tabs = [(sb("natab%d" % i, [128, 18, 64]), Buf("natab%d" % i)) for i in range(2)]
        ytok = sb("naytok", [128, NT, 64]); B_yt = [Buf("nay%d" % i) for i in range(NT)]
        gate = sb("nagate", [64, S]); B_gate = Buf("nagate")
        sc = [(sb("nasc%d" % i, [128, 5, 64]), Buf("nasc%d" % i)) for i in range(2)]
        PT = [[(sb("naPT%d%d" % (p_, i), [128, 7, 128], BF16), Buf("naPT%d%d" % (p_, i))) for i in range(2)] for p_ in range(2)]
        for p_ in range(2):
            for i in range(2):
                memset("pool", PT[p_][i][0][:], 0.0, [PT[p_][i][1]])
        rden = [(sb("narden%d" % i, [128, 1]), Buf("narden%d" % i)) for i in range(2)]
        ost = [(sb("naost%d" % i, [64, 512], BF16), Buf("naost%d" % i)) for i in range(2)]
        ptc = [(sb("naptc%d" % i, [128, 2, 128], BF16), Buf("naptc%d" % i)) for i in range(2)]

        for h in range(16):
            q, B_q = qs[h % 2]
            k, B_k = ks[h % 2]
            tab, B_tab = tabs[h % 2]
            P.dma("sp", q[:], qkT[h * 64:(h + 1) * 64, :], None, reads=[B_qkT], writes=[B_q])
            P.dma("sp", k[:], qkT[W + h * 64:W + (h + 1) * 64, :], None, reads=[B_qkT], writes=[B_k])
            P.dma("sp", tab[:], G.na_tab[layer, h], None, writes=[B_tab])
            P.dma("sp", gate[:], restT[R_NAG + h * 64:R_NAG + (h + 1) * 64, :], None, reads=[B_restT], writes=[B_gate])
            act(gate[:], gate[:], AF.Silu, [B_gate], [B_gate])
            vh = slice(h * 65, (h + 1) * 65)
            if do_ctx:
                for qt in range(2):
                    bk, B_bk = G.next_bank()
                    for kt in range(2):
                        mm(B_bk, bk[:, kt * 128:(kt + 1) * 128], k[:, kt * 128:(kt + 1) * 128], q[:, qt * 128:(qt + 1) * 128], [B_k, B_q])
                    pc, B_pc = ptc[qt]
                    act(pc[:].rearrange("p a b -> p (a b)"), bk[:, 0:256], AF.Exp, [B_bk], [B_pc], scale=0.125)
                    bk2, B_bk2 = G.next_bank()
                    for kt in range(2):
                        mm(B_bk2, bk2[:, 0:65], pc[:, kt, :], V[:, kt, vh], [B_pc, B_V], start=(kt == 0), stop=(kt == 1))
                    rd, B_rd = rden[qt]
                    OP("dve", lambda e, rd=rd, bk2=bk2: e.reciprocal(out=rd[:], in_=bk2[:, 64:65]), reads=[B_bk2], writes=[B_rd])
                    ts(ytok[:, qt, :], bk2[:, 0:64], rd[:, 0:1], None, ALU.mult, None, [B_bk2, B_rd], [B_yt[qt]])
            for rp in range(32):
                bk2, B_bk2 = G.next_bank()
                nmm = 0
                plan = []
                for par in range(2):
                    r = 2 * rp + par
                    r0 = min(max(r - 4, 0), 56)
                    t_lo = r0 // 2
                    t_hi = (r0 + 7) // 2
                    ntl = t_hi - t_lo + 1
                    bk, B_bk = G.next_bank()
                    qsl = q[:, 256 + r * 64:256 + (r + 1) * 64]
                    for m in range(ntl):
                        tk = 256 + (t_lo + m) * 128
                        mm(B_bk, bk[:, m * 64:(m + 1) * 64], k[:, tk:tk + 128], qsl, [B_k, B_q])
                    for m in range(2):
                        mm(B_bk, bk[:, (5 + m) * 64:(6 + m) * 64], k[:, m * 128:(m + 1) * 128], qsl, [B_k, B_q])
                    s_, B_s = sc[par]
                    j0 = 2 * t_lo - r + 8
                    bk3 = bk[:, 0:ntl * 64].rearrange("p (a b) -> p a b", b=64)
                    if ntl == 4:
                        stt(s_[:, 0:4, :], bk3, 0.125, tab[:, j0:j0 + 7:2, :], ALU.mult, ALU.add, [B_bk, B_tab], [B_s])
                    else:
                        stt(s_[:, 0:1, :], bk3[:, 0:1, :], 0.125, tab[:, 16:17, :], ALU.mult, ALU.add, [B_bk, B_tab], [B_s])
                        stt(s_[:, 1:4, :], bk3[:, 1:4, :], 0.125, tab[:, j0 + 2:j0 + 7:2, :], ALU.mult, ALU.add, [B_bk, B_tab], [B_s])
                        stt(s_[:, 4:5, :], bk3[:, 4:5, :], 0.125, tab[:, 17:18, :], ALU.mult, ALU.add, [B_bk, B_tab], [B_s])
                    pt, B_pt = PT[par][rp % 2]
                    qo = par * 64
                    act(pt[:, 0:ntl, qo:qo + 64], s_[:, 0:ntl, :], AF.Exp, [B_s], [B_pt])
                    act(pt[:, 5:7, qo:qo + 64], bk[:, 320:448].rearrange("p (a b) -> p a b", b=64), AF.Exp, [B_bk], [B_pt], scale=0.125)
                    plan.append((pt, B_pt, t_lo, ntl))
                tot = sum(p_[3] + 2 for p_ in plan)
                for (pt, B_pt, t_lo, ntl) in plan:
                    for m in range(ntl):
                        mm(B_bk2, bk2[:, 0:65], pt[:, m, :], V[:, 2 + t_lo + m, vh], [B_pt, B_V], start=(nmm == 0), stop=(nmm == tot - 1))
                        nmm += 1
                    for m in range(2):
                        mm(B_bk2, bk2[:, 0:65], pt[:, 5 + m, :], V[:, m, vh], [B_pt, B_V], start=(nmm == 0), stop=(nmm == tot - 1))
                        nmm += 1
                rd, B_rd = rden[rp % 2]
                OP("dve", lambda e, rd=rd, bk2=bk2: e.reciprocal(out=rd[:], in_=bk2[:, 64:65]), reads=[B_bk2], writes=[B_rd])
                ts(ytok[:, 2 + rp, :], bk2[:, 0:64], rd[:, 0:1], None, ALU.mult, None, [B_bk2, B_rd], [B_yt[2 + rp]])
            t_first = 0 if do_ctx else 2
            for ti0 in range(t_first, NT, 4):
                n = min(4, NT - ti0)
                bk, B_bk = G.next_bank()
                for i in range(n):
                    tr(B_bk, bk[0:64, i * 128:(i + 1) * 128], ytok[:, ti0 + i, :], [B_yt[ti0 + i]])
                o_, B_o = ost[(ti0 // 4) % 2]
                tt(o_[:, 0:n * 128], bk[0:64, 0:n * 128], gate[:, ti0 * 128:(ti0 + n) * 128], ALU.mult, [B_bk, B_gate], [B_o])
                P.dma("sp", G.brT[h * 64:(h + 1) * 64, ti0 * 128:(ti0 + n) * 128], o_[:, 0:n * 128], None, reads=[B_o], writes=[])
        P.barrier()


def phase_out(G, layer, last):
    nc, P = G.nc, G.P
    OP, mm, tr, act, tt, ts, stt, cp, memset = mk_helpers(G)
    restT, B_restT = G.restT, G.B_restT
    mT, B_mT = G.mT, G.B_mT
    t_first = 2 if last else 0
    with contextlib.ExitStack() as ps:
        def sb(name, shape, dt=F32):
            return _sb(ps, nc, name, shape, dt)
        wbr = sb("wbr", [128, 24, D], BF16); B_wbr = Buf("wbr")
        wsrc = G.w_branch[layer].rearrange("b (k p) d -> p (b k) d", p=128)
        for i in range(0, 24, 4):
            P.dma("pool", wbr[:, i:i + 4, :], wsrc[:, i:i + 4, :], None, writes=[B_wbr], nowait=(i > 0))
        bts = [(sb("bT%d" % i, [128, 24, 512], BF16), Buf("bT%d" % i)) for i in range(2)]
        lgs = [(sb("lg%d" % i, [128, 512]), Buf("lg%d" % i)) for i in range(3)]
        macc = [(sb("macc%d" % i, [128, 512]), Buf("macc%d" % i)) for i in range(2)]
        mst = [(sb("mst%d" % i, [128, 512], BF16), Buf("mst%d" % i)) for i in range(2)]
        bsrc = G.brT.rearrange("(c p) t -> p c t", p=128)
        for bi, t0 in enumerate(range(t_first * 128, S, 512)):
            tw = min(512, S - t0)
            bt, B_bt = bts[bi % 2]
            P.dma("sp", bt[:, 0:12, 0:tw], bsrc[:, 0:12, t0:t0 + tw], None, reads=[G.B_brT], writes=[B_bt])
            P.dma("sp", bt[:, 12:24, 0:tw], bsrc[:, 12:24, t0:t0 + tw], None, reads=[G.B_brT], writes=[B_bt], nowait=True)
            for fo in range(16):
                ma, B_ma = macc[fo % 2]
                for kb in range(3):
                    lg, B_lg = lgs[kb]
                    row = R_MERGE + kb * D + fo * 128
                    P.dma("sp", lg[:, 0:tw], restT[row:row + 128, t0:t0 + tw], None, reads=[B_restT], writes=[B_lg])
                    act(lg[:, 0:tw], lg[:, 0:tw], AF.Sigmoid, [B_lg], [B_lg])
                    bk, B_bk = G.next_bank()
                    for kc in range(8):
                        mm(B_bk, bk[:, 0:tw], wbr[:, kb * 8 + kc, fo * 128:(fo + 1) * 128], bt[:, kb * 8 + kc, 0:tw], [B_wbr, B_bt],
                           start=(kc == 0), stop=(kc == 7))
                    if kb == 0:
                        tt(ma[:, 0:tw], bk[:, 0:tw], lg[:, 0:tw], ALU.mult, [B_bk, B_lg], [B_ma])
                    else:
                        tt(lg[:, 0:tw], bk[:, 0:tw], lg[:, 0:tw], ALU.mult, [B_bk, B_lg], [B_lg])
                        if kb == 1:
                            tt(ma[:, 0:tw], ma[:, 0:tw], lg[:, 0:tw], ALU.add, [B_ma, B_lg], [B_ma], eng="pool")
                        else:
                            ms, B_ms = mst[fo % 2]
                            tt(ms[:, 0:tw], ma[:, 0:tw], lg[:, 0:tw], ALU.add, [B_ma, B_lg], [B_ms], eng="pool")
                            P.dma("sp", mT[fo * 128:(fo + 1) * 128, t0:t0 + tw], ms[:, 0:tw], None, reads=[B_ms], writes=[])
        P.barrier()
    with contextlib.ExitStack() as ps:
        def sb(name, shape, dt=F32):
            return _sb(ps, nc, name, shape, dt)
        wo = sb("wo", [128, 16, D], BF16); B_wo = Buf("wo")
        wsrc = G.w_out[layer].rearrange("(k p) d -> p k d", p=128)
        for i in range(0, 16, 4):
            P.dma("pool", wo[:, i:i + 4, :], wsrc[:, i:i + 4, :], None, writes=[B_wo], nowait=(i > 0))
        gbc = sb("gbc", [128, 2, D]); B_gbc = Buf("gbc")
        P.dma("sp", gbc[:, 0, :], G.gate_d[layer, 0].partition_broadcast(128), None, reads=[G.B_gate_d], writes=[B_gbc])
        P.dma("sp", gbc[:, 1, :], G.gate_d[layer, 1].partition_broadcast(128), None, reads=[G.B_gate_d], writes=[B_gbc], nowait=True)
        if last:
            fg = sb("fg", [128, D]); B_fg = Buf("fg")
            P.dma("sp", fg[:], G.final_g.partition_broadcast(128), None, writes=[B_fg])
        mts = [(sb("mt%d" % i, [128, 16, 128], BF16), Buf("mt%d" % i)) for i in range(2)]
        xts = [(sb("xo%d" % i, [128, D]), Buf("xo%d" % i)) for i in range(2)]
        xns = [(sb("xnw%d" % i, [128, D]), Buf("xnw%d" % i)) for i in range(2)]
        junk = sb("ojunk", [128, D], BF16); B_junk = Buf("ojunk")
        stat = [(sb("ostat%d" % i, [128, 2]), Buf("ostat%d" % i)) for i in range(2)]
        msrc = mT.rearrange("(k p) t -> p k t", p=128)
        for ti in range(t_first, NT):
            mt, B_mt = mts[ti % 2]
            xt, B_xt = xts[ti % 2]
            xn, B_xn = xns[ti % 2]
            isctx = 1 if ti < 2 else 0
            P.dma("sp", mt[:], msrc[:, :, ti * 128:(ti + 1) * 128], None, reads=[B_mT], writes=[B_mt])
            P.dma("sp", xt[:], G.x_src(layer, ti), None, reads=[G.B_xs], writes=[B_xt])
            for cbk in range(4):
                bk, B_bk = G.next_bank()
                for kc in range(16):
                    mm(B_bk, bk[:, :], mt[:, kc, :], wo[:, kc, cbk * 512:(cbk + 1) * 512], [B_mt, B_wo], start=(kc == 0), stop=(kc == 15))
                csl = slice(cbk * 512, (cbk + 1) * 512)
                tt(xn[:, csl], bk[:, :], gbc[:, isctx, csl], ALU.mult, [B_bk, B_gbc], [B_xn])
                tt(xn[:, csl], xn[:, csl], xt[:, csl], ALU.add, [B_xn, B_xt], [B_xn], eng="pool")
            if not last:
                P.dma("sp", G.xs[ti * 128:(ti + 1) * 128, :], xn[:], None, reads=[B_xn], writes=[])
            else:
                st, B_st = stat[ti % 2]
                act(junk[:], xn[:], AF.Square, [B_xn], [B_junk, B_st], scale=float(D) ** -0.5, accum_out=st[:, 0:1])
                ts(st[:, 0:1], st[:, 0:1], 1e-6, None, ALU.add, None, [B_st], [B_st])
                act(st[:, 0:1], st[:, 0:1], AF.Sqrt, [B_st], [B_st])
                OP("dve", lambda e, st=st: e.reciprocal(out=st[:, 1:2], in_=st[:, 0:1]), reads=[B_st], writes=[B_st])
                stt(xn[:], xn[:], st[:, 1:2], fg[:], ALU.mult, ALU.mult, [B_xn, B_st, B_fg], [B_xn])
                P.dma("sp", G.out[(ti - 2) * 128:(ti - 1) * 128, :], xn[:], None, reads=[B_xn], writes=[])
        P.barrier()


ALL_PHASES = ("p0", "p1", "rw", "pool", "na", "p3")


def build_program(n_layers=DEPTH, debug_outs=(), phases=ALL_PHASES, ext_in=(), only_layer=None):
    nc = bass.Bass("TRN2", target_bir_lowering=False)
    es = contextlib.ExitStack()
    with es:
        _build(nc, es, n_layers, debug_outs, phases, ext_in, only_layer)
    return nc


def _build(nc, es, n_layers, debug_outs, phases, ext_in, only_layer=None):
    P = Prog(nc, es)
    allow = es.enter_context(nc.allow_non_contiguous_dma(reason="small strided param loads"))

    def din(name, shape, dt=F32):
        return nc.dram_tensor(name, list(shape), dt, kind="ExternalInput").ap()

    def dscr(name, shape, dt=F32):
        kind = "ExternalOutput" if name in debug_outs else ("ExternalInput" if name in ext_in else "Internal")
        return nc.dram_tensor(name, list(shape), dt, kind=kind).ap()

    if "p0" in phases or "p1" in phases:
        x_in = din("x", [SEQ, D])
        ctx_in = din("ctx", [NCTX, D])
        c_in = din("c", [D])
        cctx_in = din("c_ctx", [D])
        norm_g = din("norm_g", [DEPTH, D])
        w_mod = din("w_mod", [DEPTH, D, 3 * D])
        b_mod = din("b_mod", [DEPTH, 3 * D])
        w_in = din("w_in", [DEPTH, D, D_IN])
    ident_in = din("ident", [128, 128])
    sel_in = din("sel", [2, 2, 128])
    out = nc.dram_tensor("out", [SEQ, D], F32, kind="ExternalOutput").ap()

    qkT = dscr("qkT", [2048, S], BF16)
    vaug = dscr("vaug", [S, 16, 65], BF16)
    restT = dscr("restT", [D_IN - 3072, S], F32)
    B_qkT, B_vaug, B_restT = Buf("qkT"), Buf("vaug"), Buf("restT")

    banks = []
    for i in range(8):
        t = es.enter_context(nc.psum_tensor("bank%d" % i, [128, 512], F32))
        banks.append((t, Buf("bank%d" % i, excl=True)))
    bank_rr = [0]

    def next_bank():
        b = banks[bank_rr[0] % 8]
        bank_rr[0] += 1
        return b

    ident = _sb(es, nc, "ident", [128, 128], F32)
    B_ident = Buf("ident")
    P.dma("sp", ident[:], ident_in[:, :], B_ident, writes=[B_ident])
    sel = _sb(es, nc, "sel", [2, 2, 128], F32)
    B_sel = Buf("sel")
    P.dma("sp", sel[:], sel_in[:, :, :], B_sel, writes=[B_sel])
    gs_col = _sb(es, nc, "gs_col", [128, 16, 2], F32)
    sh_col = _sb(es, nc, "sh_col", [128, 16, 2], F32)
    B_gs, B_sh = Buf("gs"), Buf("sh")
    gate_d = dscr("gate_d", [DEPTH, 2, D], F32)
    B_gate_d = Buf("gate_d")

    G = type("Ctx", (), {})()
    G.nc, G.P, G.next_bank, G.ident, G.B_ident, G.restT, G.B_restT = nc, P, next_bank, ident, B_ident, restT, B_restT
    G.din, G.dscr = din, dscr
    G.phases = phases
    setup_consts(G, es)
    xs = dscr("xs", [S, D], F32)
    G.xs, G.B_xs = xs, Buf("xs")
    G.qkT, G.B_qkT, G.vaug, G.B_vaug = qkT, B_qkT, vaug, B_vaug
    G.gate_d, G.B_gate_d = gate_d, B_gate_d
    G.out = out
    G.mT, G.B_mT = dscr("mT", [D, S], BF16), Buf("mT")
    if "pool" in phases:
        G.pool_invcnt = din("pool_invcnt", [4, S])
        G.pool_w = din("pool_w", [DEPTH, 4, 256, 256])
        G.pool_scale = din("pool_scale", [DEPTH, W])
    if "na" in phases:
        G.na_tab = din("na_tab", [DEPTH, 16, 128, 18, 64])
    if "p3" in phases:
        G.w_branch = din("w_branch", [DEPTH, 3, W, D])
        G.w_out = din("w_out", [DEPTH, D, D])
        G.final_g = din("final_g", [D])
        if "p1" not in phases:
            x_in = din("x", [SEQ, D])
            ctx_in = din("ctx", [NCTX, D])

        def x_src(layer, ti):
            if layer == 0:
                return ctx_in[ti * 128:(ti + 1) * 128, :] if ti < 2 else x_in[(ti - 2) * 128:(ti - 1) * 128, :]
            return xs[ti * 128:(ti + 1) * 128, :]
        G.x_src = x_src
    if "rw" in phases:
        G.rwpar = din("rwpar", [DEPTH, 8, 128, NPAR])
        G.rw_w2 = din("rw_w2", [DEPTH, 2, 64, W])
        G.rw_a2 = din("rw_a2", [DEPTH, 2, 64, W])
        G.lnx_g = din("rw_lnx_g", [DEPTH, W])
        G.lnx_b = din("rw_lnx_b", [DEPTH, W])
    brT = dscr("brT", [3 * W, S], BF16)
    G.brT, G.B_brT = brT, Buf("brT")
    for layer in range(n_layers):
        if only_layer is not None and layer != only_layer:
            continue
        if "p0" in G.phases:
            with contextlib.ExitStack() as ps:
                condT = _sb(ps, nc, "condT", [128, 16, 2], F32)
                B_cond = Buf("condT")
                P.dma("sp", condT[:, :, 0], c_in.rearrange("(k p) -> p k", p=128), B_cond, writes=[B_cond])
                P.dma("sp", condT[:, :, 1], cctx_in.rearrange("(k p) -> p k", p=128), B_cond, writes=[B_cond], nowait=True)
                scond = _sb(ps, nc, "scond", [128, 16, 2], F32)
                B_scond = Buf("scond")
                P.op("act", lambda e: e.activation(out=scond[:], in_=condT[:], func=AF.Silu),
                     reads=[B_cond], writes=[B_scond])
                gcol = _sb(ps, nc, "gcol", [128, 16], F32)
                B_gcol = Buf("gcol")
                P.dma("sp", gcol[:], norm_g[layer].rearrange("(k p) -> p k", p=128), B_gcol, writes=[B_gcol])
                modrow = _sb(ps, nc, "modrow", [2, 3 * D], F32)
                B_modrow = Buf("modrow")
                bmod2 = _sb(ps, nc, "bmod2", [2, 3 * D], F32)
                B_bmod = Buf("bmod2")
                P.dma("sp", bmod2[0:1, :], b_mod[layer:layer + 1, :], B_bmod, writes=[B_bmod])
                P.dma("sp", bmod2[1:2, :], b_mod[layer:layer + 1, :], B_bmod, writes=[B_bmod], nowait=True)
                wbufs = []
                for i in range(2):
                    wbufs.append((_sb(ps, nc, "wmod%d" % i, [128, 16, 512], F32), Buf("wmod%d" % i)))
                for cb in range(12):
                    wt, B_w = wbufs[cb % 2]
                    src = w_mod[layer, :, cb * 512:(cb + 1) * 512].rearrange("(k p) c -> p k c", p=128)
                    P.dma("sp", wt[:, 0:8, :], src[:, 0:8, :], B_w, writes=[B_w])
                    P.dma("sp", wt[:, 8:16, :], src[:, 8:16, :], B_w, writes=[B_w], nowait=True)
                    bk, B_bk = next_bank()
                    for k in range(16):
                        P.op("pe", lambda e, k=k, wt=wt, bk=bk: e.matmul(bk[0:2, :], lhsT=scond[:, k, :], rhs=wt[:, k, :],
                                                                        start=(k == 0), stop=(k == 15)),
                             reads=[B_scond, B_w], writes=[B_bk])
                    P.op("dve", lambda e, bk=bk, cb=cb: e.tensor_tensor(out=modrow[:, cb * 512:(cb + 1) * 512], in0=bk[0:2, :],
                                                                       in1=bmod2[:, cb * 512:(cb + 1) * 512], op=ALU.add),
                         reads=[B_bk, B_bmod], writes=[B_modrow])
                bk, B_bk = next_bank()
                for which in range(2):
                    for k in range(16):
                        c0 = which * D + k * 128
                        o0 = (which * 16 + k) * 2
                        P.op("pe", lambda e, c0=c0, o0=o0, bk=bk: e.matmul(bk[:, o0:o0 + 2], lhsT=modrow[0:2, c0:c0 + 128],
                                                                          rhs=ident[0:2, 0:2], start=True, stop=True),
                             reads=[B_modrow, B_ident], writes=[B_bk])
                P.op("act", lambda e, bk=bk: e.activation(out=sh_col[:].rearrange("p k t -> p (k t)"), in_=bk[:, 0:32], func=AF.Copy),
                     reads=[B_bk], writes=[B_sh])
                tmpc = _sb(ps, nc, "tmpc", [128, 16, 2], F32)
                B_tmpc = Buf("tmpc")
                P.op("dve", lambda e, bk=bk: e.tensor_scalar(out=tmpc[:].rearrange("p k t -> p (k t)"), in0=bk[:, 32:64], scalar1=1.0,
                                                            scalar2=None, op0=ALU.add),
                     reads=[B_bk], writes=[B_tmpc])
                for t in range(2):
                    P.op("dve", lambda e, t=t: e.tensor_tensor(out=gs_col[:, :, t], in0=tmpc[:, :, t], in1=gcol[:], op=ALU.mult),
                         reads=[B_tmpc, B_gcol], writes=[B_gs])
                P.dma("sp", gate_d[layer], modrow[0:2, 2 * D:3 * D], B_gate_d, reads=[B_modrow])
                P.barrier()

        if "p1" in G.phases:
            with contextlib.ExitStack() as ps:
                hT = _sb(ps, nc, "hT", [128, 16, S], BF16)
                B_hT = [Buf("hT%d" % i) for i in range(NT)]
                ps_outer = ps
                ps = contextlib.ExitStack()
                ps.__enter__()
                xts = [(_sb(ps, nc, "xt%d" % i, [128, D], F32), Buf("xt%d" % i)) for i in range(2)]
                xns = [(_sb(ps, nc, "xn%d" % i, [128, D], F32), Buf("xn%d" % i)) for i in range(2)]
                junk = _sb(ps, nc, "junk", [128, D], BF16)
                B_junk = Buf("junk")
                stat = [(_sb(ps, nc, "stat%d" % i, [128, 2], F32), Buf("stat%d" % i)) for i in range(2)]
                for ti in range(NT):
                    xt, B_xt = xts[ti % 2]
                    xn, B_xn = xns[ti % 2]
                    st, B_st = stat[ti % 2]
                    isctx = 1 if ti < 2 else 0
                    if layer == 0:
                        src = ctx_in[ti * 128:(ti + 1) * 128, :] if ti < 2 else x_in[(ti - 2) * 128:(ti - 1) * 128, :]
                    else:
                        src = xs[ti * 128:(ti + 1) * 128, :]
                    P.dma("sp", xt[:], src, B_xt, writes=[B_xt])
                    P.op("act", lambda e, xt=xt, st=st: e.activation(out=junk[:], in_=xt[:], func=AF.Square, scale=float(D) ** -0.5,
                                                                   accum_out=st[:, 0:1]),
                         reads=[B_xt], writes=[B_junk, B_st])
                    P.op("dve", lambda e, st=st: e.tensor_scalar(out=st[:, 0:1], in0=st[:, 0:1], scalar1=1e-6, scalar2=None,
                                                               op0=ALU.add),
                         reads=[B_st], writes=[B_st])
                    P.op("act", lambda e, st=st: e.activation(out=st[:, 0:1], in_=st[:, 0:1], func=AF.Sqrt),
                         reads=[B_st], writes=[B_st])
                    P.op("dve", lambda e, st=st: e.reciprocal(out=st[:, 1:2], in_=st[:, 0:1]),
                         reads=[B_st], writes=[B_st])
                    P.op("act", lambda e, xt=xt, xn=xn, st=st: e.activation(out=xn[:], in_=xt[:], func=AF.Copy, scale=st[:, 1:2]),
                         reads=[B_xt, B_st], writes=[B_xn])
                    for g in range(4):
                        bk, B_bk = next_bank()
                        for j in range(4):
                            k = g * 4 + j
                            P.op("pe", lambda e, k=k, j=j, bk=bk, xn=xn: e.transpose(bk[:, j * 128:(j + 1) * 128],
                                                                                   xn[:, k * 128:(k + 1) * 128], ident[:]),
                                 reads=[B_xn, B_ident], writes=[B_bk])
                        for j in range(4):
                            k = g * 4 + j
                            eng = "act" if (j % 2 == 0) else "dve"
                            o = hT[:, k, ti * 128:(ti + 1) * 128]
                            i_ = bk[:, j * 128:(j + 1) * 128]
                            if eng == "act":
                                P.op("act", lambda e, o=o, i_=i_, k=k, isctx=isctx: e.activation(
                                    out=o, in_=i_, func=AF.Identity, scale=gs_col[:, k, isctx:isctx + 1],
                                    bias=sh_col[:, k, isctx:isctx + 1]),
                                    reads=[B_bk, B_gs, B_sh], writes=[B_hT[ti]])
                            else:
                                P.op("dve", lambda e, o=o, i_=i_, k=k, isctx=isctx: e.tensor_scalar(
                                    out=o, in0=i_, scalar1=gs_col[:, k, isctx:isctx + 1], scalar2=sh_col[:, k, isctx:isctx + 1],
                                    op0=ALU.mult, op1=ALU.add),
                                    reads=[B_bk, B_gs, B_sh], writes=[B_hT[ti]])

                P.barrier()
                ps.__exit__(None, None, None)
                ps = contextlib.ExitStack()
                ps.__enter__()
                wbs = [(_sb(ps, nc, "win%d" % i, [128, 16, 512], BF16), Buf("win%d" % i)) for i in range(2)]
                ost = [(_sb(ps, nc, "ost%d" % i, [128, 512], F32), Buf("ost%d" % i)) for i in range(4)]
                ostb = [(_sb(ps, nc, "ostb%d" % i, [128, 512], BF16), Buf("ostb%d" % i)) for i in range(4)]
                vst = [(_sb(ps, nc, "vst%d" % i, [128, 8, 65], BF16), Buf("vst%d" % i)) for i in range(2)]
                for i in range(2):
                    P.op("pool", lambda e, i=i: e.memset(vst[i][0][:], 1.0), writes=[vst[i][1]])
                tblocks = [(i * 512, 512) for i in range(8)] + [(4096, 256)]
                n_cb = (D_IN + 511) // 512
                evac_rr = [0]
                sub_rr = [0]
                for cb in range(n_cb):
                    c0 = cb * 512
                    cw = min(512, D_IN - c0)
                    wt, B_w = wbs[cb % 2]
                    src = w_in[layer, :, c0:c0 + cw].rearrange("(k p) c -> p k c", p=128)
                    P.dma("pool", wt[:, 0:8, 0:cw], src[:, 0:8, :], B_w, writes=[B_w])
                    P.dma("pool", wt[:, 8:16, 0:cw], src[:, 8:16, :], B_w, writes=[B_w], nowait=True)
                    if 2048 <= c0 < 3072:
                        hg = (c0 - 2048) // 512
                        for ti in range(NT):
                            bk, B_bk = next_bank()
                            for k in range(16):
                                P.op("pe", lambda e, k=k, ti=ti, bk=bk, wt=wt: e.matmul(
                                    bk[:, :], lhsT=hT[:, k, ti * 128:(ti + 1) * 128], rhs=wt[:, k, :], start=(k == 0), stop=(k == 15)),
                                    reads=[B_hT[ti], B_w], writes=[B_bk])
                            vs, B_vs = vst[ti % 2]
                            eng = "act" if evac_rr[0] % 2 == 0 else "dve"
                            evac_rr[0] += 1
                            o = vs[:, :, 0:64]
                            i_ = bk[:, :].rearrange("p (h d) -> p h d", h=8)
                            if eng == "act":
                                P.op("act", lambda e, o=o, i_=i_: e.activation(out=o, in_=i_, func=AF.Copy), reads=[B_bk], writes=[B_vs])
                            else:
                                P.op("dve", lambda e, o=o, i_=i_: e.tensor_copy(out=o, in_=i_), reads=[B_bk], writes=[B_vs])
                            P.dma("sp", vaug[ti * 128:(ti + 1) * 128, hg * 8:(hg + 1) * 8, :], vs[:], B_vaug, reads=[B_vs], writes=[])
                        continue
                    for j in range(cw // 128):
                        is_qk = c0 < 2048
                        col = c0 + j * 128
                        for tb, (t0, tw) in enumerate(tblocks):
                            bk, B_bk = next_bank()
                            for k in range(16):
                                P.op("pe", lambda e, k=k, j=j, bk=bk, wt=wt, t0=t0, tw=tw: e.matmul(
                                    bk[:, 0:tw], lhsT=wt[:, k, j * 128:(j + 1) * 128], rhs=hT[:, k, t0:t0 + tw],
                                    start=(k == 0), stop=(k == 15)),
                                    reads=[B_w] + B_hT[t0 // 128:(t0 + tw) // 128], writes=[B_bk])
                            stg, B_stg = (ostb if is_qk else ost)[sub_rr[0] % 4]
                            sub_rr[0] += 1
                            eng = "act" if evac_rr[0] % 2 == 0 else "dve"
                            evac_rr[0] += 1
                            o = stg[:, 0:tw]
                            i_ = bk[:, 0:tw]
                            if eng == "act":
                                P.op("act", lambda e, o=o, i_=i_: e.activation(out=o, in_=i_, func=AF.Copy), reads=[B_bk], writes=[B_stg])
                            else:
                                P.op("dve", lambda e, o=o, i_=i_: e.tensor_copy(out=o, in_=i_), reads=[B_bk], writes=[B_stg])
                            if is_qk:
                                P.dma("sp", qkT[col:col + 128, t0:t0 + tw], stg[:, 0:tw], B_qkT, reads=[B_stg])
                            else:
                                r0 = col - 3072
                                P.dma("sp", restT[r0:r0 + 128, t0:t0 + tw], stg[:, 0:tw], B_restT, reads=[B_stg])
                P.barrier()
                ps.__exit__(None, None, None)
                ps = ps_outer
                P.barrier()

        if "rw" in G.phases:
            phase_rwkv(G, layer)
        if "pool" in G.phases:
            phase_pool(G, layer)
        if "na" in G.phases:
            phase_na(G, layer, do_ctx=(layer < DEPTH - 1))
        if "p3" in G.phases:
            phase_out(G, layer, last=(layer == DEPTH - 1))

    P.barrier()
    print("inst counts", P.ninst, "nsem", P.nsem)


def kernel(**inputs):
    inp = {k: np.asarray(v) for k, v in inputs.items()}
    shared = dict(host_consts())
    for k in ("c_ctx", "norm_g", "w_mod", "b_mod", "w_in", "rw_w2", "rw_a2", "rw_lnx_g", "rw_lnx_b", "pool_w", "pool_scale",
              "w_branch", "w_out", "final_g"):
        shared[k] = np.ascontiguousarray(inp[k], dtype=np.float32)
    shared["rwpar"] = pack_rwpar(inp)
    shared["pool_invcnt"] = host_pool_invcnt()
    shared["na_tab"] = host_na_table(np.asarray(inp["na_rpb"], np.float32))
    nb = inp["x"].shape[0]
    in_maps = []
    for b in range(nb):
        m = dict(shared)
        m["x"] = np.ascontiguousarray(inp["x"][b], dtype=np.float32)
        m["ctx"] = np.ascontiguousarray(inp["ctx"][b], dtype=np.float32)
        m["c"] = np.ascontiguousarray(inp["c"][b], dtype=np.float32)
        in_maps.append(m)
    nc = build_program()
    res = run_bass_kernel_spmd(nc, in_maps, core_ids=list(range(nb)))
    return np.stack([np.asarray(res.results[b]["out"], dtype=np.float32) for b in range(nb)], axis=0)
```

```python
import contextlib
import os
import numpy as np
import concourse.bass as bass
import concourse.mybir as mybir
from concourse.bass_utils import run_bass_kernel_spmd

F32 = mybir.dt.float32
F32R = mybir.dt.float32r
BF16 = mybir.dt.bfloat16
AF = mybir.ActivationFunctionType
ALU = mybir.AluOpType
AX = mybir.AxisListType

D = 2048
SEQ = 4096
NCTX = 256
S = SEQ + NCTX
NT = S // 128
W = 1024
DEPTH = 2
D_IN = 16512
NCORES = 4
SAME_ENGINE_SYNC = bool(int(os.environ.get("SAME_ENGINE_SYNC", "1")))


class Ev:
    __slots__ = ("sem", "key", "val")

    def __init__(self, sem, key, val):
        self.sem, self.key, self.val = sem, key, val


class Buf:
    __slots__ = ("name", "w", "rs", "excl")

    def __init__(self, name, excl=False):
        self.name = name
        self.w = []
        self.rs = {}
        self.excl = excl


class Prog:
    N_LANES = {"sp": 24, "pool": 8, "act": 4}

    def __init__(self, nc, es):
        self.nc = nc
        self.es = es
        self.h = {"pe": nc.tensor, "act": nc.scalar, "dve": nc.vector, "pool": nc.gpsimd, "sp": nc.sync}
        self.sem = {e: es.enter_context(nc.semaphore("s_" + e)) for e in self.h}
        self.cnt = {e: 0 for e in self.h}
        self.seen = {e: {} for e in self.h}
        self.lanes = {}
        self.lane_rr = {}
        self.nsem = 0
        self.ninst = {e: 0 for e in self.h}

    def _lane(self, q):
        if q not in self.lanes:
            self.lanes[q] = []
            for i in range(self.N_LANES[q]):
                self.nsem += 1
                key = "d_%s%d" % (q, i)
                self.lanes[q].append([self.es.enter_context(self.nc.semaphore(key)), key, 0])
            self.lane_rr[q] = 0
        ln = self.lanes[q][self.lane_rr[q] % len(self.lanes[q])]
        self.lane_rr[q] += 1
        if ln[2] > 0:
            self._wait(q, Ev(ln[0], ln[1], ln[2]))
        return ln

    def _wait(self, eng, ev):
        if ev is None:
            return
        if ev.key == eng and not SAME_ENGINE_SYNC:
            return
        if ev.key == "pe" and eng == "pe":
            return
        if self.seen[eng].get(ev.key, 0) >= ev.val:
            return
        self.h[eng].wait_ge(ev.sem, ev.val)
        self.ninst[eng] += 1
        self.seen[eng][ev.key] = ev.val

    def _deps(self, eng, reads, writes):
        for b in reads:
            for ev in b.w:
                self._wait(eng, ev)
            if b.excl:
                for k, r in b.rs.items():
                    if k != eng:
                        self._wait(eng, r)
        for b in writes:
            for ev in b.w:
                self._wait(eng, ev)
            for r in b.rs.values():
                self._wait(eng, r)

    def op(self, eng, fn, reads=(), writes=()):
        self._deps(eng, reads, writes)
        inst = fn(self.h[eng])
        self.cnt[eng] += 1
        self.ninst[eng] += 1
        inst.then_inc(self.sem[eng], 1)
        ev = Ev(self.sem[eng], eng, self.cnt[eng])
        for b in reads:
            b.rs[eng] = ev
        for b in writes:
            b.w = [ev]
            b.rs = {}
        return ev

    def dma(self, q, out, in_, owner=None, reads=(), writes=(), nowait=False, **kw):
        if not nowait:
            self._deps(q, reads, writes)
        ln = self._lane(q)
        inst = self.h[q].dma_start(out=out, in_=in_, **kw)
        ln[2] += 16
        inst.then_inc(ln[0], 16)
        self.ninst[q] += 1
        ev = Ev(ln[0], ln[1], ln[2])
        for b in reads:
            b.rs[ln[1]] = ev
        for b in writes:
            if nowait:
                b.w = list(b.w) + [ev]
            else:
                b.w = [ev]
                b.rs = {}
        return ev

    def barrier(self):
        evs = [Ev(self.sem[e], e, self.cnt[e]) for e in self.h if self.cnt[e] > 0]
        for q, lanes in self.lanes.items():
            evs += [Ev(l[0], l[1], l[2]) for l in lanes if l[2] > 0]
        for e in self.h:
            for ev in evs:
                if ev.key == e:
                    if self.seen[e].get(e, 0) < ev.val and e != "sp":
                        self.h[e].wait_ge(ev.sem, ev.val)
                        self.seen[e][e] = ev.val
                    continue
                self._wait(e, ev)


_uid = [0]


def _rd(ap):
    try:
        if ap.dtype == F32R:
            return ap.bitcast(F32)
    except AttributeError:
        pass
    return ap


def _sb(es, nc, name, shape, dt):
    _uid[0] += 1
    return es.enter_context(nc.sbuf_tensor("sb%d_%s" % (_uid[0], name), list(shape), dt))


def host_consts():
    idx = np.arange(128)
    masks = np.stack([(idx[:, None] < idx[None, :]), (idx[:, None] <= idx[None, :]),
                      (idx[:, None] > idx[None, :]), (idx[:, None] >= idx[None, :])]).astype(np.float32)
    blockones = (idx[:, None] // 64 == idx[None, :] // 64).astype(np.float32)
    resetmask = np.ones((128, 256), np.float32)
    resetmask[:, 0] = 0.0
    resetmask[:, 128] = 0.0
    headsel = np.zeros((128, 2), np.float32)
    headsel[:64, 0] = 1.0
    headsel[64:, 1] = 1.0
    sel = np.zeros((2, 2, 128), np.float32)
    sel[0, 0] = 1
    sel[1, 1] = 1
    return {"ident": np.eye(128, dtype=np.float32), "sel": sel, "masks": masks, "blockones": blockones,
            "resetmask": resetmask, "headsel": headsel}


def pack_rwpar(inp):
    cols = [inp["rw_mu"][:, 0], inp["rw_mu"][:, 1], inp["rw_mu"][:, 2], inp["rw_k_k"], inp["rw_k_a"],
            inp["rw_r_k"].reshape(DEPTH, W), inp["rw_w0"][:, 0], inp["rw_w0"][:, 1], inp["rw_a0"][:, 0], inp["rw_a0"][:, 1],
            inp["rw_lnx_g"], inp["rw_lnx_b"]]
    a = np.stack([np.asarray(c, np.float32) for c in cols], axis=-1)
    return np.ascontiguousarray(a.reshape(DEPTH, 8, 128, NPAR))


def setup_consts(G, es):
    nc, P = G.nc, G.P
    masks_in = G.din("masks", [4, 128, 128])
    bo_in = G.din("blockones", [128, 128])
    rm_in = G.din("resetmask", [128, 256])
    hs_in = G.din("headsel", [128, 2])
    G.masks = _sb(es, nc, "masks", [128, 4, 128], F32)
    G.B_masks = Buf("masks")
    for i in range(4):
        P.dma("sp", G.masks[:, i, :], masks_in[i], G.B_masks, writes=[G.B_masks], nowait=(i > 0))
    G.blockones = _sb(es, nc, "blockones", [128, 128], F32)
    G.B_bo = Buf("blockones")
    P.dma("sp", G.blockones[:], bo_in[:, :], G.B_bo, writes=[G.B_bo])
    G.resetmask = _sb(es, nc, "resetmask", [128, 256], F32)
    G.B_rm = Buf("resetmask")
    P.dma("sp", G.resetmask[:], rm_in[:, :], G.B_rm, writes=[G.B_rm])
    G.headsel = _sb(es, nc, "headsel", [128, 2], F32)
    G.B_hs = Buf("headsel")
    P.dma("sp", G.headsel[:], hs_in[:, :], G.B_hs, writes=[G.B_hs])


NPAR = 12
R_RWR, R_RWK, R_RWV, R_RWG, R_LW, R_LA = 3072, 4096, 5120, 6144, 7168, 7232
LOGW_SCALE = -0.6065306597126334


def phase_rwkv(G, layer, do_ctx_out=True):
    nc, P = G.nc, G.P
    restT, B_restT = G.restT, G.B_restT
    rwpar = G.rwpar
    rw_w2, rw_a2 = G.rw_w2, G.rw_a2
    lnx_g, lnx_b = G.lnx_g, G.lnx_b
    masks, B_masks = G.masks, G.B_masks
    M_lt, M_le, M_gt, M_ge = (masks[:, i, :] for i in range(4))

    def OP(eng, fn, reads=(), writes=()):
        return P.op(eng, fn, reads=reads, writes=writes)

    def mm(bk, o, lhsT, rhs, reads, start=True, stop=True):
        OP("pe", lambda e: e.matmul(o, lhsT=lhsT, rhs=rhs, start=start, stop=stop), reads=reads, writes=[bk])

    def tr(bk, o, in_, reads):
        OP("pe", lambda e: e.transpose(o, _rd(in_), G.ident[:]), reads=list(reads) + [G.B_ident], writes=[bk])

    def act(o, i, func, reads, writes, **kw):
        kw = {k_: _rd(v_) for k_, v_ in kw.items()}
        OP("act", lambda e: e.activation(out=o, in_=_rd(i), func=func, **kw), reads=reads, writes=writes)

    def tt(o, a, b, op, reads, writes, eng="dve"):
        OP(eng, lambda e: e.tensor_tensor(out=o, in0=_rd(a), in1=_rd(b), op=op), reads=reads, writes=writes)

    def ts(o, a, s1, s2, op0, op1, reads, writes, eng="dve"):
        if s2 is None:
            OP(eng, lambda e: e.tensor_scalar(out=o, in0=_rd(a), scalar1=_rd(s1), scalar2=None, op0=op0), reads=reads, writes=writes)
        else:
            OP(eng, lambda e: e.tensor_scalar(out=o, in0=_rd(a), scalar1=_rd(s1), scalar2=_rd(s2), op0=op0, op1=op1), reads=reads, writes=writes)

    def stt(o, a, sc, b, op0, op1, reads, writes):
        OP("dve", lambda e: e.scalar_tensor_tensor(out=o, in0=_rd(a), scalar=_rd(sc), in1=_rd(b), op0=op0, op1=op1), reads=reads, writes=writes)

    def cp(eng, o, i, reads, writes):
        if eng == "act":
            act(o, i, AF.Copy, reads, writes)
        else:
            OP(eng, lambda e: e.tensor_copy(out=o, in_=_rd(i)), reads=reads, writes=writes)

    with contextlib.ExitStack() as ps:
        def sb(name, shape, dt=F32):
            return _sb(ps, nc, name, shape, dt)

        R32 = F32R if int(os.environ.get("RW_F32R", "0")) else F32
        lwla = sb("lwla", [128, S])
        B_lwla = Buf("lwla")
        P.dma("sp", lwla[:], restT[R_LW:R_LW + 128, :], B_lwla, reads=[B_restT], writes=[B_lwla])
        act(lwla[0:64, :], lwla[0:64, :], AF.Tanh, [B_lwla], [B_lwla])

        rT = sb("r", [128, S]); B_r = Buf("r")
        kT = sb("k", [128, S]); B_k = Buf("k")
        vT = sb("vkkn", [128, S]); B_v = Buf("vkkn")
        ytok = sb("ytok", [128, NT, 128]); B_y = [Buf("ytok%d" % i) for i in range(NT)]
        bon = sb("bon", [128, NT, 2]); B_bon = [Buf("bon%d" % i) for i in range(NT)]
        vtok = sb("vtok", [128, NT, 128], R32); B_vt = [Buf("vtok%d" % i) for i in range(NT)]
        par = sb("par", [128, NPAR + 8]); B_par = Buf("par")
        w2t = sb("w2t", [128, 2, 128]); B_w2 = Buf("w2t")
        lnxg = sb("lnxg", [128, 128]); lnxb = sb("lnxb", [128, 128]); B_lnx = Buf("lnx")
        rkblk = sb("rkblk", [128, 2], R32); B_rkblk = Buf("rkblk")
        nsum = ytok[:].rearrange("p t c -> p (t c)")
        GW = 256
        gtmp = [[(sb("gt%d_%d" % (d, i), [128, GW]), Buf("gt%d_%d" % (d, i))) for i in range(6)] for d in range(2)]
        gout = [[[(sb("go%d_%d_%d" % (d, pz, i), [128, GW], F32 if i == 0 else R32), Buf("go%d_%d_%d" % (d, pz, i))) for i in range(5)]
                 for pz in range(2)] for d in range(2)]
        ukds = [(sb("ukd%d" % d, [128, GW], R32), Buf("ukd%d" % d)) for d in range(2)]
        def ctile(name, w=128, dt=None):
            return (sb(name, [128, w], R32 if dt is None else dt), Buf(name))
        cper = [[[[{n: ctile("c%s%d%d%d%d" % (n, d, pz, c, h)) for n in ("Pm", "BmT", "RBT", "RKT")} for h in range(2)]
                  for c in range(2)] for pz in range(2)] for d in range(2)]
        cpair = [[[{n: ctile("c%s%d%d%d" % (n, d, pz, c)) for n in ("btok", "ktok")} for c in range(2)]
                  for pz in range(2)] for d in range(2)]
        ctmp = [[[{n: ctile("t%s%d%d%d" % (n, d, c, h)) for n in ("Xa", "XTa", "Xb", "XTb", "Pb")} for h in range(2)]
                 for c in range(2)] for d in range(2)]
        STs = [[ctile("ST%d%d" % (d, i), 64) for i in range(2)] for d in range(2)]
        S0dec = [ctile("S0dec%d" % d, 64, F32) for d in range(2)]
        Gt = [ctile("G%d" % d) for d in range(2)]
        SAt = [ctile("SA%d" % d) for d in range(2)]
        yst = [ctmp[0][0][1]["Xa"], ctmp[0][0][1]["XTa"]]
        ost = [(sb("rwo%d" % i, [128, 128], BF16), Buf("rwo%d" % i)) for i in range(2)]
        gts = [(gtmp[0][0][0][:, 0:128], gtmp[0][0][1]), (gtmp[0][1][0][:, 0:128], gtmp[0][1][1])]
        small = [ctile("sm%d" % i, 8, F32) for i in range(4)]

        RW_STAGE = int(os.environ.get("RW_STAGE", "99"))
        RW_SUB = int(os.environ.get("RW_SUB", "99"))
        for hp in range(int(os.environ.get("RW_PAIRS", "8"))):
            ch0 = hp * 128
            P.dma("sp", par[:, 0:NPAR], rwpar[layer, hp], B_par, writes=[B_par])
            for d in range(2):
                P.dma("sp", w2t[0:64, d, :], rw_w2[layer, d, :, ch0:ch0 + 128], B_w2, writes=[B_w2], nowait=(d > 0))
                P.dma("sp", w2t[64:128, d, :], rw_a2[layer, d, :, ch0:ch0 + 128], B_w2, writes=[B_w2], nowait=True)
            P.dma("sp", lnxg[:], lnx_g[layer, ch0:ch0 + 128].partition_broadcast(128), B_lnx, writes=[B_lnx])
            P.dma("sp", lnxb[:], lnx_b[layer, ch0:ch0 + 128].partition_broadcast(128), B_lnx, writes=[B_lnx], nowait=True)
            ts(par[:, NPAR:NPAR + 3], par[:, 0:3], -1.0, 1.0, ALU.mult, ALU.add, [B_par], [B_par])
            ts(par[:, NPAR + 3:NPAR + 6], par[:, 0:3], 0.5, None, ALU.mult, None, [B_par], [B_par])
            ts(par[:, NPAR + 6:NPAR + 7], par[:, 4:5], -1.0, 1.0, ALU.mult, ALU.add, [B_par], [B_par])
            ts(rkblk[:], G.headsel[:], par[:, 5:6], None, ALU.mult, None, [B_par, G.B_hs], [B_rkblk])
            C_KK, C_KA, C_OMKA = par[:, 3:4], par[:, 4:5], par[:, NPAR + 6:NPAR + 7]

            for zi, (zt, B_z, row) in enumerate(((rT, B_r, R_RWR), (kT, B_k, R_RWK), (vT, B_v, R_RWV))):
                P.dma("sp", zt[:], restT[row + ch0:row + ch0 + 128, :], B_z, reads=[B_restT], writes=[B_z])
                tt(nsum[:, 1:S - 1], zt[:, 0:S - 2], zt[:, 2:S], ALU.add, [B_z], B_y)
                for (dst, srcc) in ((0, 1), (255, 254), (256, 257), (S - 1, S - 2)):
                    cp("dve", nsum[:, dst:dst + 1], zt[:, srcc:srcc + 1], [B_z], B_y)
                ts(nsum[:, :], nsum[:, :], par[:, NPAR + 3 + zi:NPAR + 4 + zi], None, ALU.mult, None, B_y + [B_par], B_y)
                stt(zt[:], zt[:], par[:, NPAR + zi:NPAR + 1 + zi], nsum[:, :], ALU.mult, ALU.add, [B_z, B_par] + B_y, [B_z])
            if RW_STAGE < 2:
                continue
            for ti in range(NT):
                bk, B_bk = G.next_bank()
                tr(B_bk, bk[:, 0:128], vT[:, ti * 128:(ti + 1) * 128], [B_v])
                cp("act" if ti % 2 == 0 else "dve", vtok[:, ti, :], bk[:, 0:128], [B_bk], [B_vt[ti]])
            act(nsum[:, :], kT[:], AF.Copy, [B_k, B_par], B_y, scale=C_KK)
            act(vT[:], nsum[:, :], AF.Square, B_y + B_vt, [B_v])
            for t0 in range(0, S, 512):
                tw = min(512, S - t0)
                bk, B_bk = G.next_bank()
                mm(B_bk, bk[:, 0:tw], G.blockones[:], vT[:, t0:t0 + tw], [G.B_bo, B_v])
                act(vT[:, t0:t0 + tw], bk[:, 0:tw], AF.Sqrt, [B_bk], [B_v])
            ts(vT[:], vT[:], 1e-12, None, ALU.max, None, [B_v], [B_v])
            OP("dve", lambda e: e.reciprocal(out=vT[:], in_=vT[:]), reads=[B_v], writes=[B_v])
            tt(vT[:], vT[:], nsum[:, :], ALU.mult, [B_v] + B_y, [B_v])
            kkn, B_kkn = vT, B_v

            if RW_STAGE < 3:
                continue
            order = {0: list(range(17)), 1: [0] + list(range(16, 0, -1))}
            st_idx = [0, 0]
            ywritten = set()
            bwritten = set()
            for d in range(2):
                if RW_SUB < -1:
                    break
                ts(STs[d][0][0][:], G.ident[:, 0:64], 0.0, None, ALU.mult, None, [G.B_ident], [STs[d][0][1]], eng="pool")

            def prep_rounds(d, step):
                g = order[d][step]
                pz = step % 2
                t0 = g * GW
                (sg, B_sg), (cs, B_cs), (tmp, B_tmp), (ad, B_ad), (kd, B_kd), (en, B_en) = gtmp[d]
                ukd, B_ukd = ukds[d]
                (Ep, B_Ep), (aTt, B_aT), (bTt, B_bT), (kTt, B_kT), (rTt, B_rT) = gout[d][pz]
                rounds = []

                def r0():
                    if RW_SUB < 0:
                        return
                    bk, B_bk = G.next_bank()
                    bk2, B_bk2 = G.next_bank()
                    mm(B_bk, bk[:, 0:GW], w2t[0:64, d, :], lwla[0:64, t0:t0 + GW], [B_w2, B_lwla])
                    mm(B_bk2, bk2[:, 0:GW], w2t[64:128, d, :], lwla[64:128, t0:t0 + GW], [B_w2, B_lwla])
                    act(sg[:], bk[:, 0:GW], AF.Sigmoid, [B_bk, B_par], [B_sg], bias=par[:, 6 + d:7 + d])
                    act(ad[:], bk2[:, 0:GW], AF.Sigmoid, [B_bk2, B_par], [B_ad], bias=par[:, 8 + d:9 + d])
                    if RW_SUB < 1:
                        return
                    OP("dve", lambda e: e.tensor_tensor_scan(out=cs[:], data0=G.resetmask[:], data1=sg[:], initial=0.0,
                                                            op0=ALU.mult, op1=ALU.add), reads=[B_sg, G.B_rm], writes=[B_cs])
                    if d == 1 and RW_SUB >= 2:
                        cs3 = cs[:].rearrange("p (c k) -> p c k", k=128)
                        tot = cs3[:, :, 127:128].to_broadcast([128, 2, 128])
                        tt(tmp[:].rearrange("p (c k) -> p c k", k=128), tot, cs3, ALU.subtract, [B_cs], [B_tmp])
                        tt(cs[:], tmp[:], sg[:], ALU.add, [B_tmp, B_sg], [B_cs])
                rounds.append(r0)
                if RW_SUB < 3:
                    return rounds

                def r1():
                    act(Ep[:], cs[:], AF.Exp, [B_cs], [B_Ep], scale=LOGW_SCALE)
                    act(en[:], cs[:], AF.Exp, [B_cs], [B_en], scale=-LOGW_SCALE)
                    tt(tmp[:], cs[:], sg[:], ALU.subtract, [B_cs, B_sg], [B_tmp])
                    act(tmp[:], tmp[:], AF.Exp, [B_tmp], [B_tmp], scale=LOGW_SCALE)
                    ts(kd[:], ad[:], C_KA, C_OMKA, ALU.mult, ALU.add, [B_ad, B_par], [B_kd])
                    tt(kd[:], kd[:], kT[:, t0:t0 + GW], ALU.mult, [B_kd, B_k], [B_kd])
                rounds.append(r1)
                if RW_SUB < 4:
                    return rounds

                def r2():
                    tt(ukd[:], rT[:, t0:t0 + GW], kd[:], ALU.mult, [B_r, B_kd], [B_ukd], eng="pool")
                    stt(aTt[:], kkn[:, t0:t0 + GW], -1.0, tmp[:], ALU.mult, ALU.mult, [B_kkn, B_tmp], [B_aT])
                    tt(bTt[:], kkn[:, t0:t0 + GW], ad[:], ALU.mult, [B_kkn, B_ad], [B_bT])
                    tt(bTt[:], bTt[:], en[:], ALU.mult, [B_bT, B_en], [B_bT])
                    tt(kTt[:], kd[:], en[:], ALU.mult, [B_kd, B_en], [B_kT])
                    tt(rTt[:], rT[:, t0:t0 + GW], Ep[:], ALU.mult, [B_r, B_Ep], [B_rT], eng="pool")
                rounds.append(r2)

                strict_st, incl_st = (M_lt, M_le) if d == 0 else (M_gt, M_ge)
                strict_ts = M_gt if d == 0 else M_lt

                def r3():
                    for c in range(2):
                        cs_ = slice(c * 128, (c + 1) * 128)
                        bkAs = [G.next_bank(), G.next_bank()]
                        for h in range(2):
                            hs = slice(h * 64, (h + 1) * 64)
                            bkA, B_A = bkAs[h]
                            mm(B_A, bkA[:, 0:128], bTt[hs, cs_], aTt[hs, cs_], [B_bT, B_aT])
                            mm(B_A, bkA[:, 128:256], aTt[hs, cs_], bTt[hs, cs_], [B_bT, B_aT])
                        for h in range(2):
                            T = ctmp[d][c][h]
                            bkA, B_A = bkAs[h]
                            tt(T["Xa"][0][:], bkA[:, 0:128], strict_st, ALU.mult,
                               [B_A, B_masks], [T["Xa"][1]])
                            tt(T["XTa"][0][:], bkA[:, 128:256], strict_ts, ALU.mult,
                               [B_A, B_masks], [T["XTa"][1]])
                            tt(T["Pb"][0][:], T["Xa"][0][:], G.ident[:], ALU.add, [T["Xa"][1], G.B_ident], [T["Pb"][1]], eng="pool")
                        for h in range(2):
                            hs = slice(h * 64, (h + 1) * 64)
                            bkB, B_B = G.next_bank()
                            Cp = cper[d][pz][c][h]
                            mm(B_B, bkB[:, 0:128], kTt[hs, cs_], aTt[hs, cs_], [B_kT, B_aT])
                            mm(B_B, bkB[:, 128:256], bTt[hs, cs_], rTt[hs, cs_], [B_bT, B_rT])
                            mm(B_B, bkB[:, 256:384], kTt[hs, cs_], rTt[hs, cs_], [B_kT, B_rT])
                            tt(Cp["BmT"][0][:], bkB[:, 0:128], strict_st, ALU.mult, [B_B, B_masks], [Cp["BmT"][1]])
                            tt(Cp["RBT"][0][:], bkB[:, 128:256], incl_st, ALU.mult, [B_B, B_masks], [Cp["RBT"][1]])
                            tt(Cp["RKT"][0][:], bkB[:, 256:384], incl_st, ALU.mult, [B_B, B_masks], [Cp["RKT"][1]])
                        bkC, B_C = G.next_bank()
                        tr(B_C, bkC[:, 0:128], bTt[:, cs_], [B_bT])
                        tr(B_C, bkC[:, 128:256], kTt[:, cs_], [B_kT])
                        cp("act", cpair[d][pz][c]["btok"][0][:], bkC[:, 0:128], [B_C], [cpair[d][pz][c]["btok"][1]])
                        cp("act", cpair[d][pz][c]["ktok"][0][:], bkC[:, 128:256], [B_C], [cpair[d][pz][c]["ktok"][1]])
                if RW_STAGE < 4:
                    return rounds
                rounds.append(r3)

                def r3b():
                    bk, B_bk = G.next_bank()
                    for c in range(2):
                        mm(B_bk, bk[:, 2 * c:2 * c + 2], ukd[:, c * 128:(c + 1) * 128], rkblk[:], [B_ukd, B_rkblk])
                    for c in range(2):
                        ti = g * 2 + c
                        if ti not in bwritten:
                            bwritten.add(ti)
                            cp("act", bon[:, ti, :], bk[:, 2 * c:2 * c + 2], [B_bk], [B_bon[ti]])
                        else:
                            tt(bon[:, ti, :], bk[:, 2 * c:2 * c + 2], bon[:, ti, :], ALU.add, [B_bk, B_bon[ti]], [B_bon[ti]])
                rounds.append(r3b)

                def make_level(lvl):
                    def rl():
                        src, dst = ("a", "b") if lvl % 2 == 1 else ("b", "a")
                        last = (lvl == 6)
                        banks_ = []
                        for c in range(2):
                            bk, B_bk = G.next_bank()
                            banks_.append((bk, B_bk))
                            for h in range(2):
                                T = ctmp[d][c][h]
                                X, B_X = T["X" + src]
                                XT, B_XT = T["XT" + src]
                                if not last:
                                    mm(B_bk, bk[:, (2 * h) * 128:(2 * h + 1) * 128], XT[:], X[:], [B_X, B_XT])
                                mm(B_bk, bk[:, (2 * h + 1) * 128:(2 * h + 2) * 128], X[:], XT[:], [B_X, B_XT])
                        RW_DBG = int(os.environ.get("RW_DBG", "0"))
                        for c in range(2):
                            if RW_DBG == 1:
                                break
                            bk, B_bk = banks_[c]
                            for h in range(2):
                                T = ctmp[d][c][h]
                                if not last:
                                    cp("act", T["X" + dst][0][:], bk[:, (2 * h) * 128:(2 * h + 1) * 128], [B_bk], [T["X" + dst][1]])
                                cp("act" if (h == 0 or RW_DBG == 2) else "dve", T["XT" + dst][0][:], bk[:, (2 * h + 1) * 128:(2 * h + 2) * 128],
                                   [B_bk], [T["XT" + dst][1]])
                    def rl2():
                        src, dst = ("a", "b") if lvl % 2 == 1 else ("b", "a")
                        if RW_SUB < 11:
                            return
                        for c in range(2):
                            bk, B_bk = G.next_bank()
                            for h in range(2):
                                T = ctmp[d][c][h]
                                Cp = cper[d][pz][c][h]
                                Pold, B_Pold = T["Pb"] if lvl % 2 == 1 else Cp["Pm"]
                                mm(B_bk, bk[:, h * 128:(h + 1) * 128], T["XT" + dst][0][:], Pold[:], [T["XT" + dst][1], B_Pold])
                            for h in range(2):
                                T = ctmp[d][c][h]
                                Cp = cper[d][pz][c][h]
                                Pold, B_Pold = T["Pb"] if lvl % 2 == 1 else Cp["Pm"]
                                Pnew, B_Pnew = Cp["Pm"] if lvl % 2 == 1 else T["Pb"]
                                tt(Pnew[:], bk[:, h * 128:(h + 1) * 128], Pold[:], ALU.add, [B_bk, B_Pold], [B_Pnew])
                    return rl, rl2
                if RW_STAGE < 5:
                    return rounds
                for lvl in range(1, 1 + int(os.environ.get("RW_LVL", "6"))):
                    rounds.extend(make_level(lvl))
                def rfin():
                    for c in range(2):
                        for h in range(2):
                            T = ctmp[d][c][h]
                            Cp = cper[d][pz][c][h]
                            cp("pool", Cp["Pm"][0][:], T["Pb"][0][:], [T["Pb"][1]], [Cp["Pm"][1]])
                if RW_SUB >= 12:
                    rounds.append(rfin)
                return rounds

            def chain_rounds(d, step):
                g = order[d][step]
                pz = step % 2
                (Ep, B_Ep), (aTt, B_aT), (bTt, B_bT), (kTt, B_kT), (rTt, B_rT) = gout[d][pz]
                rounds = []
                corder = (0, 1) if d == 0 else (1, 0)
                for c in corder:
                    ti = g * 2 + c
                    cs_ = slice(c * 128, (c + 1) * 128)
                    ecol = c * 128 + (127 if d == 0 else 0)
                    eLC = Ep[:, ecol:ecol + 1]

                    def b1(c=c, ti=ti, cs_=cs_, eLC=eLC):
                        ST, B_ST = STs[d][st_idx[d] % 2]
                        bk, B_bk = G.next_bank()
                        for h in range(2):
                            hs = slice(h * 64, (h + 1) * 64)
                            Cp = cper[d][pz][c][h]
                            o = bk[:, h * 64:(h + 1) * 64]
                            mm(B_bk, o, Cp["BmT"][0][:], vtok[:, ti, h * 64:(h + 1) * 64], [Cp["BmT"][1], B_vt[ti]], start=True, stop=False)
                            mm(B_bk, o, aTt[hs, cs_], ST[hs, :], [B_aT, B_ST], start=False, stop=True)
                        cp("act", Gt[d][0][:], bk[:, 0:128], [B_bk], [Gt[d][1]])
                        ts(S0dec[d][0][:], ST[:], eLC, None, ALU.mult, None, [B_ST, B_Ep], [S0dec[d][1]], eng="pool")
                    rounds.append(b1)

                    def b2(c=c):
                        bk, B_bk = G.next_bank()
                        for h in range(2):
                            Cp = cper[d][pz][c][h]
                            mm(B_bk, bk[:, h * 64:(h + 1) * 64], Cp["Pm"][0][:], Gt[d][0][:, h * 64:(h + 1) * 64], [Cp["Pm"][1], Gt[d][1]])
                        cp("dve", SAt[d][0][:], bk[:, 0:128], [B_bk], [SAt[d][1]])
                    rounds.append(b2)

                    def b3(c=c, ti=ti, cs_=cs_, eLC=eLC):
                        ST, B_ST = STs[d][st_idx[d] % 2]
                        STn, B_STn = STs[d][(st_idx[d] + 1) % 2]
                        st_idx[d] += 1
                        SA, B_SA = SAt[d]
                        bk, B_bk = G.next_bank()
                        for h in range(2):
                            hs = slice(h * 64, (h + 1) * 64)
                            Cp = cper[d][pz][c][h]
                            o = bk[:, h * 64:(h + 1) * 64]
                            mm(B_bk, o, rTt[hs, cs_], ST[hs, :], [B_rT, B_ST], start=True, stop=False)
                            mm(B_bk, o, Cp["RBT"][0][:], SA[:, h * 64:(h + 1) * 64], [Cp["RBT"][1], B_SA], start=False, stop=False)
                            mm(B_bk, o, Cp["RKT"][0][:], vtok[:, ti, h * 64:(h + 1) * 64], [Cp["RKT"][1], B_vt[ti]], start=False, stop=True)
                        bk2, B_bk2 = G.next_bank()
                        Cq = cpair[d][pz][c]
                        mm(B_bk2, bk2[:, 0:128], Cq["ktok"][0][:], vtok[:, ti, :], [Cq["ktok"][1], B_vt[ti]], start=True, stop=False)
                        mm(B_bk2, bk2[:, 0:128], Cq["btok"][0][:], SA[:], [Cq["btok"][1], B_SA], start=False, stop=True)
                        if ti not in ywritten:
                            ywritten.add(ti)
                            cp("act", ytok[:, ti, :], bk[:, 0:128], [B_bk], [B_y[ti]])
                        else:
                            tt(ytok[:, ti, :], bk[:, 0:128], ytok[:, ti, :], ALU.add, [B_bk, B_y[ti]], [B_y[ti]])
                        for h in range(2):
                            hs = slice(h * 64, (h + 1) * 64)
                            stt(STn[hs, :], bk2[hs, h * 64:(h + 1) * 64], eLC[hs, :], S0dec[d][0][hs, :], ALU.mult, ALU.add,
                                [B_bk2, B_Ep, S0dec[d][1]], [B_STn])
                    rounds.append(b3)
                return rounds

            def interleave(lists):
                n = max(len(l) for l in lists) if lists else 0
                for i in range(n):
                    for l in lists:
                        if i < len(l):
                            l[i]()

            nsteps = 17
            for step in range(nsteps + 1):
                lists = []
                for d in range(2):
                    if step < nsteps:
                        lists.append(prep_rounds(d, step))
                    if step >= 1 and RW_STAGE >= 6:
                        lists.append(chain_rounds(d, step - 1))
                interleave(lists)

            if RW_STAGE < 7:
                continue
            for ti in range(NT):
                t0 = ti * 128
                sm, B_sm = small[ti % 4]
                y, B_yy = ytok[:, ti, :], B_y[ti]
                yt, B_yt = yst[ti % 2]
                y3 = y.rearrange("p (h c) -> p h c", h=2)
                cp("act", sm[:, 6:8], bon[:, ti, :], [B_bon[ti]], [B_sm])
                OP("dve", lambda e, y3=y3, sm=sm: e.tensor_reduce(out=sm[:, 0:2], in_=y3, axis=AX.X, op=ALU.add), reads=[B_yy], writes=[B_sm])
                ts(sm[:, 0:2], sm[:, 0:2], 1.0 / 64, None, ALU.mult, None, [B_sm], [B_sm])
                for h in range(2):
                    ts(yt[:, h * 64:(h + 1) * 64], y[:, h * 64:(h + 1) * 64], sm[:, h:h + 1], None, ALU.subtract, None, [B_yy, B_sm], [B_yt])
                sq, B_sq = gts[ti % 2]
                act(sq[:], yt[:], AF.Square, [B_yt], [B_sq])
                OP("dve", lambda e, sq=sq, sm=sm: e.tensor_reduce(out=sm[:, 2:4], in_=sq[:].rearrange("p (h c) -> p h c", h=2), axis=AX.X,
                                                                op=ALU.add), reads=[B_sq], writes=[B_sm])
                ts(sm[:, 2:4], sm[:, 2:4], 1.0 / 64, 64e-5, ALU.mult, ALU.add, [B_sm], [B_sm])
                act(sm[:, 2:4], sm[:, 2:4], AF.Sqrt, [B_sm], [B_sm])
                OP("dve", lambda e, sm=sm: e.reciprocal(out=sm[:, 4:6], in_=sm[:, 2:4]), reads=[B_sm], writes=[B_sm])
                for h in range(2):
                    hsl = slice(h * 64, (h + 1) * 64)
                    stt(yt[:, hsl], yt[:, hsl], sm[:, 4 + h:5 + h], lnxg[:, hsl], ALU.mult, ALU.mult, [B_yt, B_sm, B_lnx], [B_yt])
                tt(yt[:], yt[:], lnxb[:], ALU.add, [B_yt, B_lnx], [B_yt])
                for h in range(2):
                    hsl = slice(h * 64, (h + 1) * 64)
                    stt(yt[:, hsl], vtok[:, ti, hsl], sm[:, 6 + h:7 + h], yt[:, hsl], ALU.mult, ALU.add, [B_vt[ti], B_sm, B_yt], [B_yt])
                bk, B_bk = G.next_bank()
                tr(B_bk, bk[:, 0:128], yt[:], [B_yt])
                gt, B_gt = gts[ti % 2]
                P.dma("sp", gt[:], restT[R_RWG + ch0:R_RWG + ch0 + 128, t0:t0 + 128], B_gt, reads=[B_restT], writes=[B_gt])
                act(gt[:], gt[:], AF.Silu, [B_gt], [B_gt])
                ot, B_ot = ost[ti % 2]
                tt(ot[:], bk[:, 0:128], gt[:], ALU.mult, [B_bk, B_gt], [B_ot])
                P.dma("sp", G.brT[2 * W + ch0:2 * W + ch0 + 128, t0:t0 + 128], ot[:], G.B_brT, reads=[B_ot])
        P.barrier()


def mk_helpers(G):
    P = G.P

    class H:
        pass
    H_ = H()

    def OP(eng, fn, reads=(), writes=()):
        return P.op(eng, fn, reads=reads, writes=writes)

    def mm(bk, o, lhsT, rhs, reads, start=True, stop=True):
        OP("pe", lambda e: e.matmul(o, lhsT=lhsT, rhs=rhs, start=start, stop=stop), reads=reads, writes=[bk])

    def tr(bk, o, in_, reads):
        OP("pe", lambda e: e.transpose(o, in_, G.ident[:]), reads=list(reads) + [G.B_ident], writes=[bk])

    def act(o, i, func, reads, writes, **kw):
        OP("act", lambda e: e.activation(out=o, in_=i, func=func, **kw), reads=reads, writes=writes)

    def tt(o, a, b, op, reads, writes, eng="dve"):
        OP(eng, lambda e: e.tensor_tensor(out=o, in0=a, in1=b, op=op), reads=reads, writes=writes)

    def ts(o, a, s1, s2, op0, op1, reads, writes, eng="dve"):
        if s2 is None:
            OP(eng, lambda e: e.tensor_scalar(out=o, in0=a, scalar1=s1, scalar2=None, op0=op0), reads=reads, writes=writes)
        else:
            OP(eng, lambda e: e.tensor_scalar(out=o, in0=a, scalar1=s1, scalar2=s2, op0=op0, op1=op1), reads=reads, writes=writes)

    def stt(o, a, sc, b, op0, op1, reads, writes):
        OP("dve", lambda e: e.scalar_tensor_tensor(out=o, in0=a, scalar=sc, in1=b, op0=op0, op1=op1), reads=reads, writes=writes)

    def cp(eng, o, i, reads, writes):
        if eng == "act":
            act(o, i, AF.Copy, reads, writes)
        else:
            OP(eng, lambda e: e.tensor_copy(out=o, in_=i), reads=reads, writes=writes)

    def memset(eng, o, val, writes):
        OP(eng, lambda e: e.memset(o, val), writes=writes)
    return OP, mm, tr, act, tt, ts, stt, cp, memset


R_NAG, R_PU, R_PG, R_MERGE = 0, 1024, 2048, 7296
PADW = 8 + 256 + 16 + 4096 + 16
OFF_C, OFF_L = 8, 8 + 256 + 16


def host_pool_invcnt():
    out = np.zeros((4, S), np.float32)
    for g, win in enumerate((2, 4, 8, 16)):
        for (o, T) in ((0, NCTX), (NCTX, SEQ)):
            t = np.arange(T)
            lo = np.maximum(t - win // 2, 0)
            hi = np.minimum(t + win // 2, T)
            out[g, o:o + T] = 1.0 / (hi - lo)
    return out


def phase_pool(G, layer):
    nc, P = G.nc, G.P
    OP, mm, tr, act, tt, ts, stt, cp, memset = mk_helpers(G)
    restT, B_restT = G.restT, G.B_restT
    with contextlib.ExitStack() as ps:
        def sb(name, shape, dt=F32):
            return _sb(ps, nc, name, shape, dt)
        A = sb("pA", [128, PADW]); B_A = Buf("pA")
        Bt = sb("pB", [128, PADW]); B_B = Buf("pB")
        Ct = sb("pC", [128, PADW]); B_C = Buf("pC")
        inv = sb("pinv", [128, S]); B_inv = Buf("pinv")
        diff = [(sb("pdiff%d" % i, [128, S], BF16), Buf("pdiff%d" % i)) for i in range(2)]
        gate = sb("pgate", [128, S]); B_gate = Buf("pgate")
        pw = sb("ppw", [128, 2, 256], BF16); B_pw = Buf("ppw")
        psc = sb("ppsc", [128, 2]); B_psc = Buf("ppsc")
        stg = [(sb("pstg%d" % i, [128, 512], BF16), Buf("pstg%d" % i)) for i in range(2)]
        for t_, b_ in ((A, B_A), (Bt, B_B), (Ct, B_C)):
            memset("pool", t_[:], 0.0, [b_])

        def zero_gaps(t_, b_):
            memset("pool", t_[:, 0:OFF_C], 0.0, [b_])
            memset("pool", t_[:, OFF_C + 256:OFF_L], 0.0, [b_])
            memset("pool", t_[:, OFF_L + 4096:PADW], 0.0, [b_])

        R0, R1 = 4, PADW - 8
        for g in range(4):
            win = (2, 4, 8, 16)[g]
            P.dma("sp", inv[:], G.pool_invcnt[g].partition_broadcast(128), None, writes=[B_inv])
            P.dma("pool", pw[:], G.pool_w[layer, g].rearrange("(k p) d -> p k d", p=128), None, writes=[B_pw])
            P.dma("sp", psc[:], G.pool_scale[layer, g * 256:(g + 1) * 256].rearrange("(k p) -> p k", p=128), None, writes=[B_psc])
            for cbi in range(2):
                cb = g * 2 + cbi
                row = R_PU + cb * 128
                P.dma("sp", A[:, OFF_C:OFF_C + 256], restT[row:row + 128, 0:256], None, reads=[B_restT], writes=[B_A])
                P.dma("sp", A[:, OFF_L:OFF_L + 4096], restT[row:row + 128, 256:S], None, reads=[B_restT], writes=[B_A], nowait=True)
                tt(Bt[:, R0:R1], A[:, R0 - 1:R1 - 1], A[:, R0:R1], ALU.add, [B_A], [B_B])
                cur, B_cur = Bt, B_B
                oth, B_oth = Ct, B_C
                sh = 1
                w_ = 2
                while w_ < win:
                    tt(oth[:, R0:R1], cur[:, R0 - sh:R1 - sh], cur[:, R0 + sh:R1 + sh], ALU.add, [B_cur], [B_oth])
                    cur, B_cur, oth, B_oth = oth, B_oth, cur, B_cur
                    sh *= 2
                    w_ *= 2
                dt_, B_d = diff[cbi]
                for (po, so, T) in ((OFF_C, 0, 256), (OFF_L, 256, 4096)):
                    tt(cur[:, po:po + T], cur[:, po:po + T], inv[:, so:so + T], ALU.mult, [B_cur, B_inv], [B_cur])
                    tt(dt_[:, so:so + T], cur[:, po:po + T], A[:, po:po + T], ALU.subtract, [B_cur, B_A], [B_d])
            for dch in range(2):
                row = R_PG + g * 256 + dch * 128
                P.dma("sp", gate[:], restT[row:row + 128, :], None, reads=[B_restT], writes=[B_gate])
                act(gate[:], gate[:], AF.Silu, [B_gate], [B_gate])
                for bi, t0 in enumerate(range(0, S, 512)):
                    tw = min(512, S - t0)
                    bk, B_bk = G.next_bank()
                    for k in range(2):
                        mm(B_bk, bk[:, 0:tw], pw[:, k, dch * 128:(dch + 1) * 128], diff[k][0][:, t0:t0 + tw], [B_pw, diff[k][1]],
                           start=(k == 0), stop=(k == 1))
                    st_, B_st = stg[bi % 2]
                    stt(st_[:, 0:tw], bk[:, 0:tw], psc[:, dch:dch + 1], gate[:, t0:t0 + tw], ALU.mult, ALU.mult,
                        [B_bk, B_psc, B_gate], [B_st])
                    orow = W + g * 256 + dch * 128
                    P.dma("sp", G.brT[orow:orow + 128, t0:t0 + tw], st_[:, 0:tw], None, reads=[B_st], writes=[])
        P.barrier()


def host_na_table(na_rpb):
    L = na_rpb.shape[0]
    NEG = np.float32(-30000.0)
    col = np.arange(64)
    cs = np.clip(col - 8, 0, 48)
    cmask = (col[:, None] >= cs[None, :]) & (col[:, None] < cs[None, :] + 16)
    coff = np.clip(col[:, None] - col[None, :] + 15, 0, 30)
    tab = np.full((L, 16, 2, 64, 18, 64), NEG, np.float32)
    for par in range(2):
        for j in range(16):
            ro = j - 1 + par
            if 0 <= ro <= 14:
                g = na_rpb[:, :, ro][:, :, coff]
                tab[:, :, par, :, j, :] = np.where(cmask[None, None], g, NEG)
    tab[:, :, 1, :, 16, :] = np.where(cmask[None, None], na_rpb[:, :, 3][:, :, coff], NEG)
    tab[:, :, 0, :, 17, :] = np.where(cmask[None, None], na_rpb[:, :, 10][:, :, coff], NEG)
    return np.ascontiguousarray(tab.reshape(L, 16, 128, 18, 64))


def phase_na(G, layer, do_ctx=True):
    nc, P = G.nc, G.P
    OP, mm, tr, act, tt, ts, stt, cp, memset = mk_helpers(G)
    restT, B_restT = G.restT, G.B_restT
    qkT, B_qkT, vaug, B_vaug = G.qkT, G.B_qkT, G.vaug, G.B_vaug
    with contextlib.ExitStack() as ps:
        def sb(name, shape, dt=F32):
            return _sb(ps, nc, name, shape, dt)
        V = sb("naV", [128, NT, 16 * 65], BF16); B_V = Buf("naV")
        vsrc = vaug.rearrange("(n p) h e -> p n (h e)", p=128)
        for i in range(0, NT, 6):
            j = min(NT, i + 6)
            P.dma("sp", V[:, i:j, :], vsrc[:, i:j, :], None, reads=[B_vaug], writes=[B_V], nowait=(i > 0))
        qs = [(sb("naq%d" % i, [64, S], BF16), Buf("naq%d" % i)) for i in range(2)]
        ks = [(sb("nak%d" % i, [64, S], BF16), Buf("nak%d" % i)) for i in range(2)]
        tabs = [(sb("natab%d" % i, [128, 18, 64]), Buf("natab%d" % i)) for i in range(2)]
        ytok = sb("naytok", [128, NT, 64]); B_yt = [Buf("nay%d" % i) for i in range(NT)]
        gate = sb("nagate", [64, S]); B_gate = Buf("nagate")
        sc = [(sb("nasc%d" % i, [128, 5, 64]), Buf("nasc%d" % i)) for i in range(2)]
        PT = [[(sb("naPT%d%d" % (p_, i), [128, 7, 128], BF16), Buf("naPT%d%d" % (p_, i))) for i in range(2)] for p_ in range(2)]
        for p_ in range(2):
            for i in range(2):
                memset("pool", PT[p_][i][0][:], 0.0, [PT[p_][i][1]])
        rden = [(sb("narden%d" % i, [128, 1]), Buf("narden%d" % i)) for i in range(2)]
        ost = [(sb("naost%d" % i, [64, 512], BF16), Buf("naost%d" % i)) for i in range(2)]
        ptc = [(sb("naptc%d" % i, [128, 2, 128], BF16), Buf("naptc%d" % i)) for i in range(2)]

        for h in range(16):
            q, B_q = qs[h % 2]
            k, B_k = ks[h % 2]
            tab, B_tab = tabs[h % 2]
            P.dma("sp", q[:], qkT[h * 64:(h + 1) * 64, :], None, reads=[B_qkT], writes=[B_q])
            P.dma("sp", k[:], qkT[W + h * 64:W + (h + 1) * 64, :], None, reads=[B_qkT], writes=[B_k])
            P.dma("sp", tab[:], G.na_tab[layer, h], None, writes=[B_tab])
            P.dma("sp", gate[:], restT[R_NAG + h * 64:R_NAG + (h + 1) * 64, :], None, reads=[B_restT], writes=[B_gate])
            act(gate[:], gate[:], AF.Silu, [B_gate], [B_gate])
            vh = slice(h * 65, (h + 1) * 65)
            if do_ctx:
                for qt in range(2):
                    bk, B_bk = G.next_bank()
                    for kt in range(2):
                        mm(B_bk, bk[:, kt * 128:(kt + 1) * 128], k[:, kt * 128:(kt + 1) * 128], q[:, qt * 128:(qt + 1) * 128], [B_k, B_q])
                    pc, B_pc = ptc[qt]
                    act(pc[:].rearrange("p a b -> p (a b)"), bk[:, 0:256], AF.Exp, [B_bk], [B_pc], scale=0.125)
                    bk2, B_bk2 = G.next_bank()
                    for kt in range(2):
                        mm(B_bk2, bk2[:, 0:65], pc[:, kt, :], V[:, kt, vh], [B_pc, B_V], start=(kt == 0), stop=(kt == 1))
                    rd, B_rd = rden[qt]
                    OP("dve", lambda e, rd=rd, bk2=bk2: e.reciprocal(out=rd[:], in_=bk2[:, 64:65]), reads=[B_bk2], writes=[B_rd])
                    ts(ytok[:, qt, :], bk2[:, 0:64], rd[:, 0:1], None, ALU.mult, None, [B_bk2, B_rd], [B_yt[qt]])
            for rp in range(32):
                bk2, B_bk2 = G.next_bank()
                nmm = 0
                plan = []
                for par in range(2):
                    r = 2 * rp + par
                    r0 = min(max(r - 4, 0), 56)
                    t_lo = r0 // 2
                    t_hi = (r0 + 7) // 2
                    ntl = t_hi - t_lo + 1
                    bk, B_bk = G.next_bank()
                    qsl = q[:, 256 + r * 64:256 + (r + 1) * 64]
                    for m in range(ntl):
                        tk = 256 + (t_lo + m) * 128
                        mm(B_bk, bk[:, m * 64:(m + 1) * 64], k[:, tk:tk + 128], qsl, [B_k, B_q])
                    for m in range(2):
                        mm(B_bk, bk[:, (5 + m) * 64:(6 + m) * 64], k[:, m * 128:(m + 1) * 128], qsl, [B_k, B_q])
                    s_, B_s = sc[par]
                    j0 = 2 * t_lo - r + 8
                    bk3 = bk[:, 0:ntl * 64].rearrange("p (a b) -> p a b", b=64)
                    if ntl == 4:
                        stt(s_[:, 0:4, :], bk3, 0.125, tab[:, j0:j0 + 7:2, :], ALU.mult, ALU.add, [B_bk, B_tab], [B_s])
                    else:
                        stt(s_[:, 0:1, :], bk3[:, 0:1, :], 0.125, tab[:, 16:17, :], ALU.mult, ALU.add, [B_bk, B_tab], [B_s])
                        stt(s_[:, 1:4, :], bk3[:, 1:4, :], 0.125, tab[:, j0 + 2:j0 + 7:2, :], ALU.mult, ALU.add, [B_bk, B_tab], [B_s])
                        stt(s_[:, 4:5, :], bk3[:, 4:5, :], 0.125, tab[:, 17:18, :], ALU.mult, ALU.add, [B_bk, B_tab], [B_s])
                    pt, B_pt = PT[par][rp % 2]
                    qo = par * 64
                    act(pt[:, 0:ntl, qo:qo + 64], s_[:, 0:ntl, :], AF.Exp, [B_s], [B_pt])
                    act(pt[:, 5:7, qo:qo + 64], bk[:, 320:448].rearrange("p (a b) -> p a b", b=64), AF.Exp, [B_bk], [B_pt], scale=0.125)
                    plan.append((pt, B_pt, t_lo, ntl))
                tot = sum(p_[3] + 2 for p_ in plan)
                for (pt, B_pt, t_lo, ntl) in plan:
                    for m in range(ntl):
                        mm(B_bk2, bk2[:, 0:65], pt[:, m, :], V[:, 2 + t_lo + m, vh], [B_pt, B_V], start=(nmm == 0), stop=(nmm == tot - 1))
                        nmm += 1
                    for m in range(2):
                        mm(B_bk2, bk2[:, 0:65], pt[:, 5 + m, :], V[:, m, vh], [B_pt, B_V], start=(nmm == 0), stop=(nmm == tot - 1))
                        nmm += 1
                rd, B_rd = rden[rp % 2]
                OP("dve", lambda e, rd=rd, bk2=bk2: e.reciprocal(out=rd[:], in_=bk2[:, 64:65]), reads=[B_bk2], writes=[B_rd])
                ts(ytok[:, 2 + rp, :], bk2[:, 0:64], rd[:, 0:1], None, ALU.mult, None, [B_bk2, B_rd], [B_yt[2 + rp]])
            t_first = 0 if do_ctx else 2
            for ti0 in range(t_first, NT, 4):
                n = min(4, NT - ti0)
                bk, B_bk = G.next_bank()
                for i in range(n):
                    tr(B_bk, bk[0:64, i * 128:(i + 1) * 128], ytok[:, ti0 + i, :], [B_yt[ti0 + i]])
                o_, B_o = ost[(ti0 // 4) % 2]
                tt(o_[:, 0:n * 128], bk[0:64, 0:n * 128], gate[:, ti0 * 128:(ti0 + n) * 128], ALU.mult, [B_bk, B_gate], [B_o])
                P.dma("sp", G.brT[h * 64:(h + 1) * 64, ti0 * 128:(ti0 + n) * 128], o_[:, 0:n * 128], None, reads=[B_o], writes=[])
        P.barrier()


def phase_out(G, layer, last):
    nc, P = G.nc, G.P
    OP, mm, tr, act, tt, ts, stt, cp, memset = mk_helpers(G)
    restT, B_restT = G.restT, G.B_restT
    mT, B_mT = G.mT, G.B_mT
    t_first = 2 if last else 0
    with contextlib.ExitStack() as ps:
        def sb(name, shape, dt=F32):
            return _sb(ps, nc, name, shape, dt)
        wbr = sb("wbr", [128, 24, D], BF16); B_wbr = Buf("wbr")
        wsrc = G.w_branch[layer].rearrange("b (k p) d -> p (b k) d", p=128)
        for i in range(0, 24, 4):
            P.dma("pool", wbr[:, i:i + 4, :], wsrc[:, i:i + 4, :], None, writes=[B_wbr], nowait=(i > 0))
        bts = [(sb("bT%d" % i, [128, 24, 512], BF16), Buf("bT%d" % i)) for i in range(2)]
        lgs = [(sb("lg%d" % i, [128, 512]), Buf("lg%d" % i)) for i in range(3)]
        macc = [(sb("macc%d" % i, [128, 512]), Buf("macc%d" % i)) for i in range(2)]
        mst = [(sb("mst%d" % i, [128, 512], BF16), Buf("mst%d" % i)) for i in range(2)]
        bsrc = G.brT.rearrange("(c p) t -> p c t", p=128)
        for bi, t0 in enumerate(range(t_first * 128, S, 512)):
            tw = min(512, S - t0)
            bt, B_bt = bts[bi % 2]
            P.dma("sp", bt[:, 0:12, 0:tw], bsrc[:, 0:12, t0:t0 + tw], None, reads=[G.B_brT], writes=[B_bt])
            P.dma("sp", bt[:, 12:24, 0:tw], bsrc[:, 12:24, t0:t0 + tw], None, reads=[G.B_brT], writes=[B_bt], nowait=True)
            for fo in range(16):
                ma, B_ma = macc[fo % 2]
                for kb in range(3):
                    lg, B_lg = lgs[kb]
                    row = R_MERGE + kb * D + fo * 128
                    P.dma("sp", lg[:, 0:tw], restT[row:row + 128, t0:t0 + tw], None, reads=[B_restT], writes=[B_lg])
                    act(lg[:, 0:tw], lg[:, 0:tw], AF.Sigmoid, [B_lg], [B_lg])
                    bk, B_bk = G.next_bank()
                    for kc in range(8):
                        mm(B_bk, bk[:, 0:tw], wbr[:, kb * 8 + kc, fo * 128:(fo + 1) * 128], bt[:, kb * 8 + kc, 0:tw], [B_wbr, B_bt],
                           start=(kc == 0), stop=(kc == 7))
                    if kb == 0:
                        tt(ma[:, 0:tw], bk[:, 0:tw], lg[:, 0:tw], ALU.mult, [B_bk, B_lg], [B_ma])
                    else:
                        tt(lg[:, 0:tw], bk[:, 0:tw], lg[:, 0:tw], ALU.mult, [B_bk, B_lg], [B_lg])
                        if kb == 1:
                            tt(ma[:, 0:tw], ma[:, 0:tw], lg[:, 0:tw], ALU.add, [B_ma, B_lg], [B_ma], eng="pool")
                        else:
                            ms, B_ms = mst[fo % 2]
                            tt(ms[:, 0:tw], ma[:, 0:tw], lg[:, 0:tw], ALU.add, [B_ma, B_lg], [B_ms], eng="pool")
                            P.dma("sp", mT[fo * 128:(fo + 1) * 128, t0:t0 + tw], ms[:, 0:tw], None, reads=[B_ms], writes=[])
        P.barrier()
    with contextlib.ExitStack() as ps:
        def sb(name, shape, dt=F32):
            return _sb(ps, nc, name, shape, dt)
        wo = sb("wo", [128, 16, D], BF16); B_wo = Buf("wo")
        wsrc = G.w_out[layer].rearrange("(k p) d -> p k d", p=128)
        for i in range(0, 16, 4):
            P.dma("pool", wo[:, i:i + 4, :], wsrc[:, i:i + 4, :], None, writes=[B_wo], nowait=(i > 0))
        gbc = sb("gbc", [128, 2, D]); B_gbc = Buf("gbc")
        P.dma("sp", gbc[:, 0, :], G.gate_d[layer, 0].partition_broadcast(128), None, reads=[G.B_gate_d], writes=[B_gbc])
        P.dma("sp", gbc[:, 1, :], G.gate_d[layer, 1].partition_broadcast(128), None, reads=[G.B_gate_d], writes=[B_gbc], nowait=True)
        if last:
            fg = sb("fg", [128, D]); B_fg = Buf("fg")
            P.dma("sp", fg[:], G.final_g.partition_broadcast(128), None, writes=[B_fg])
        mts = [(sb("mt%d" % i, [128, 16, 128], BF16), Buf("mt%d" % i)) for i in range(2)]
        xts = [(sb("xo%d" % i, [128, D]), Buf("xo%d" % i)) for i in range(2)]
        xns = [(sb("xnw%d" % i, [128, D]), Buf("xnw%d" % i)) for i in range(2)]
        junk = sb("ojunk", [128, D], BF16); B_junk = Buf("ojunk")
        stat = [(sb("ostat%d" % i, [128, 2]), Buf("ostat%d" % i)) for i in range(2)]
        msrc = mT.rearrange("(k p) t -> p k t", p=128)
        for ti in range(t_first, NT):
            mt, B_mt = mts[ti % 2]
            xt, B_xt = xts[ti % 2]
            xn, B_xn = xns[ti % 2]
            isctx = 1 if ti < 2 else 0
            P.dma("sp", mt[:], msrc[:, :, ti * 128:(ti + 1) * 128], None, reads=[B_mT], writes=[B_mt])
            P.dma("sp", xt[:], G.x_src(layer, ti), None, reads=[G.B_xs], writes=[B_xt])
            for cbk in range(4):
                bk, B_bk = G.next_bank()
                for kc in range(16):
                    mm(B_bk, bk[:, :], mt[:, kc, :], wo[:, kc, cbk * 512:(cbk + 1) * 512], [B_mt, B_wo], start=(kc == 0), stop=(kc == 15))
                csl = slice(cbk * 512, (cbk + 1) * 512)
                tt(xn[:, csl], bk[:, :], gbc[:, isctx, csl], ALU.mult, [B_bk, B_gbc], [B_xn])
                tt(xn[:, csl], xn[:, csl], xt[:, csl], ALU.add, [B_xn, B_xt], [B_xn], eng="pool")
            if not last:
                P.dma("sp", G.xs[ti * 128:(ti + 1) * 128, :], xn[:], None, reads=[B_xn], writes=[])
            else:
                st, B_st = stat[ti % 2]
                act(junk[:], xn[:], AF.Square, [B_xn], [B_junk, B_st], scale=float(D) ** -0.5, accum_out=st[:, 0:1])
                ts(st[:, 0:1], st[:, 0:1], 1e-6, None, ALU.add, None, [B_st], [B_st])
                act(st[:, 0:1], st[:, 0:1], AF.Sqrt, [B_st], [B_st])
                OP("dve", lambda e, st=st: e.reciprocal(out=st[:, 1:2], in_=st[:, 0:1]), reads=[B_st], writes=[B_st])
                stt(xn[:], xn[:], st[:, 1:2], fg[:], ALU.mult, ALU.mult, [B_xn, B_st, B_fg], [B_xn])
                P.dma("sp", G.out[(ti - 2) * 128:(ti - 1) * 128, :], xn[:], None, reads=[B_xn], writes=[])
        P.barrier()


ALL_PHASES = ("p0", "p1", "rw", "pool", "na", "p3")


def build_program(n_layers=DEPTH, debug_outs=(), phases=ALL_PHASES, ext_in=(), only_layer=None):
    nc = bass.Bass("TRN2", target_bir_lowering=False)
    es = contextlib.ExitStack()
    with es:
        _build(nc, es, n_layers, debug_outs, phases, ext_in, only_layer)
    return nc


def _build(nc, es, n_layers, debug_outs, phases, ext_in, only_layer=None):
    P = Prog(nc, es)
    allow = es.enter_context(nc.allow_non_contiguous_dma(reason="small strided param loads"))

    def din(name, shape, dt=F32):
        return nc.dram_tensor(name, list(shape), dt, kind="ExternalInput").ap()

    def dscr(name, shape, dt=F32):
        kind = "ExternalOutput" if name in debug_outs else ("ExternalInput" if name in ext_in else "Internal")
        return nc.dram_tensor(name, list(shape), dt, kind=kind).ap()

    if "p0" in phases or "p1" in phases:
        x_in = din("x", [SEQ, D])
        ctx_in = din("ctx", [NCTX, D])
        c_in = din("c", [D])
        cctx_in = din("c_ctx", [D])
        norm_g = din("norm_g", [DEPTH, D])
        w_mod = din("w_mod", [DEPTH, D, 3 * D])
        b_mod = din("b_mod", [DEPTH, 3 * D])
        w_in = din("w_in", [DEPTH, D, D_IN])
    ident_in = din("ident", [128, 128])
    sel_in = din("sel", [2, 2, 128])
    out = nc.dram_tensor("out", [SEQ, D], F32, kind="ExternalOutput").ap()

    qkT = dscr("qkT", [2048, S], BF16)
    vaug = dscr("vaug", [S, 16, 65], BF16)
    restT = dscr("restT", [D_IN - 3072, S], F32)
    B_qkT, B_vaug, B_restT = Buf("qkT"), Buf("vaug"), Buf("restT")

    banks = []
    for i in range(8):
        t = es.enter_context(nc.psum_tensor("bank%d" % i, [128, 512], F32))
        banks.append((t, Buf("bank%d" % i, excl=True)))
    bank_rr = [0]

    def next_bank():
        b = banks[bank_rr[0] % 8]
        bank_rr[0] += 1
        return b

    ident = _sb(es, nc, "ident", [128, 128], F32)
    B_ident = Buf("ident")
    P.dma("sp", ident[:], ident_in[:, :], B_ident, writes=[B_ident])
    sel = _sb(es, nc, "sel", [2, 2, 128], F32)
    B_sel = Buf("sel")
    P.dma("sp", sel[:], sel_in[:, :, :], B_sel, writes=[B_sel])
    gs_col = _sb(es, nc, "gs_col", [128, 16, 2], F32)
    sh_col = _sb(es, nc, "sh_col", [128, 16, 2], F32)
    B_gs, B_sh = Buf("gs"), Buf("sh")
    gate_d = dscr("gate_d", [DEPTH, 2, D], F32)
    B_gate_d = Buf("gate_d")

    G = type("Ctx", (), {})()
    G.nc, G.P, G.next_bank, G.ident, G.B_ident, G.restT, G.B_restT = nc, P, next_bank, ident, B_ident, restT, B_restT
    G.din, G.dscr = din, dscr
    G.phases = phases
    setup_consts(G, es)
    xs = dscr("xs", [S, D], F32)
    G.xs, G.B_xs = xs, Buf("xs")
    G.qkT, G.B_qkT, G.vaug, G.B_vaug = qkT, B_qkT, vaug, B_vaug
    G.gate_d, G.B_gate_d = gate_d, B_gate_d
    G.out = out
    G.mT, G.B_mT = dscr("mT", [D, S], BF16), Buf("mT")
    if "pool" in phases:
        G.pool_invcnt = din("pool_invcnt", [4, S])
        G.pool_w = din("pool_w", [DEPTH, 4, 256, 256])
        G.pool_scale = din("pool_scale", [DEPTH, W])
    if "na" in phases:
        G.na_tab = din("na_tab", [DEPTH, 16, 128, 18, 64])
    if "p3" in phases:
        G.w_branch = din("w_branch", [DEPTH, 3, W, D])
        G.w_out = din("w_out", [DEPTH, D, D])
        G.final_g = din("final_g", [D])
        if "p1" not in phases:
            x_in = din("x", [SEQ, D])
            ctx_in = din("ctx", [NCTX, D])

        def x_src(layer, ti):
            if layer == 0:
                return ctx_in[ti * 128:(ti + 1) * 128, :] if ti < 2 else x_in[(ti - 2) * 128:(ti - 1) * 128, :]
            return xs[ti * 128:(ti + 1) * 128, :]
        G.x_src = x_src
    if "rw" in phases:
        G.rwpar = din("rwpar", [DEPTH, 8, 128, NPAR])
        G.rw_w2 = din("rw_w2", [DEPTH, 2, 64, W])
        G.rw_a2 = din("rw_a2", [DEPTH, 2, 64, W])
        G.lnx_g = din("rw_lnx_g", [DEPTH, W])
        G.lnx_b = din("rw_lnx_b", [DEPTH, W])
    brT = dscr("brT", [3 * W, S], BF16)
    G.brT, G.B_brT = brT, Buf("brT")
    for layer in range(n_layers):
        if only_layer is not None and layer != only_layer:
            continue
        if "p0" in G.phases:
            with contextlib.ExitStack() as ps:
                condT = _sb(ps, nc, "condT", [128, 16, 2], F32)
                B_cond = Buf("condT")
                P.dma("sp", condT[:, :, 0], c_in.rearrange("(k p) -> p k", p=128), B_cond, writes=[B_cond])
                P.dma("sp", condT[:, :, 1], cctx_in.rearrange("(k p) -> p k", p=128), B_cond, writes=[B_cond], nowait=True)
                scond = _sb(ps, nc, "scond", [128, 16, 2], F32)
                B_scond = Buf("scond")
                P.op("act", lambda e: e.activation(out=scond[:], in_=condT[:], func=AF.Silu),
                     reads=[B_cond], writes=[B_scond])
                gcol = _sb(ps, nc, "gcol", [128, 16], F32)
                B_gcol = Buf("gcol")
                P.dma("sp", gcol[:], norm_g[layer].rearrange("(k p) -> p k", p=128), B_gcol, writes=[B_gcol])
                modrow = _sb(ps, nc, "modrow", [2, 3 * D], F32)
                B_modrow = Buf("modrow")
                bmod2 = _sb(ps, nc, "bmod2", [2, 3 * D], F32)
                B_bmod = Buf("bmod2")
                P.dma("sp", bmod2[0:1, :], b_mod[layer:layer + 1, :], B_bmod, writes=[B_bmod])
                P.dma("sp", bmod2[1:2, :], b_mod[layer:layer + 1, :], B_bmod, writes=[B_bmod], nowait=True)
                wbufs = []
                for i in range(2):
                    wbufs.append((_sb(ps, nc, "wmod%d" % i, [128, 16, 512], F32), Buf("wmod%d" % i)))
                for cb in range(12):
                    wt, B_w = wbufs[cb % 2]
                    src = w_mod[layer, :, cb * 512:(cb + 1) * 512].rearrange("(k p) c -> p k c", p=128)
                    P.dma("sp", wt[:, 0:8, :], src[:, 0:8, :], B_w, writes=[B_w])
                    P.dma("sp", wt[:, 8:16, :], src[:, 8:16, :], B_w, writes=[B_w], nowait=True)
                    bk, B_bk = next_bank()
                    for k in range(16):
                        P.op("pe", lambda e, k=k, wt=wt, bk=bk: e.matmul(bk[0:2, :], lhsT=scond[:, k, :], rhs=wt[:, k, :],
                                                                        start=(k == 0), stop=(k == 15)),
                             reads=[B_scond, B_w], writes=[B_bk])
                    P.op("dve", lambda e, bk=bk, cb=cb: e.tensor_tensor(out=modrow[:, cb * 512:(cb + 1) * 512], in0=bk[0:2, :],
                                                                       in1=bmod2[:, cb * 512:(cb + 1) * 512], op=ALU.add),
                         reads=[B_bk, B_bmod], writes=[B_modrow])
                bk, B_bk = next_bank()
                for which in range(2):
                    for k in range(16):
                        c0 = which * D + k * 128
                        o0 = (which * 16 + k) * 2
                        P.op("pe", lambda e, c0=c0, o0=o0, bk=bk: e.matmul(bk[:, o0:o0 + 2], lhsT=modrow[0:2, c0:c0 + 128],
                                                                          rhs=ident[0:2, 0:2], start=True, stop=True),
                             reads=[B_modrow, B_ident], writes=[B_bk])
                P.op("act", lambda e, bk=bk: e.activation(out=sh_col[:].rearrange("p k t -> p (k t)"), in_=bk[:, 0:32], func=AF.Copy),
                     reads=[B_bk], writes=[B_sh])
                tmpc = _sb(ps, nc, "tmpc", [128, 16, 2], F32)
                B_tmpc = Buf("tmpc")
                P.op("dve", lambda e, bk=bk: e.tensor_scalar(out=tmpc[:].rearrange("p k t -> p (k t)"), in0=bk[:, 32:64], scalar1=1.0,
                                                            scalar2=None, op0=ALU.add),
                     reads=[B_bk], writes=[B_tmpc])
                for t in range(2):
                    P.op("dve", lambda e, t=t: e.tensor_tensor(out=gs_col[:, :, t], in0=tmpc[:, :, t], in1=gcol[:], op=ALU.mult),
                         reads=[B_tmpc, B_gcol], writes=[B_gs])
                P.dma("sp", gate_d[layer], modrow[0:2, 2 * D:3 * D], B_gate_d, reads=[B_modrow])
                P.barrier()

        if "p1" in G.phases:
            with contextlib.ExitStack() as ps:
                hT = _sb(ps, nc, "hT", [128, 16, S], BF16)
                B_hT = [Buf("hT%d" % i) for i in range(NT)]
                ps_outer = ps
                ps = contextlib.ExitStack()
                ps.__enter__()
                xts = [(_sb(ps, nc, "xt%d" % i, [128, D], F32), Buf("xt%d" % i)) for i in range(2)]
                xns = [(_sb(ps, nc, "xn%d" % i, [128, D], F32), Buf("xn%d" % i)) for i in range(2)]
                junk = _sb(ps, nc, "junk", [128, D], BF16)
                B_junk = Buf("junk")
                stat = [(_sb(ps, nc, "stat%d" % i, [128, 2], F32), Buf("stat%d" % i)) for i in range(2)]
                for ti in range(NT):
                    xt, B_xt = xts[ti % 2]
                    xn, B_xn = xns[ti % 2]
                    st, B_st = stat[ti % 2]
                    isctx = 1 if ti < 2 else 0
                    if layer == 0:
                        src = ctx_in[ti * 128:(ti + 1) * 128, :] if ti < 2 else x_in[(ti - 2) * 128:(ti - 1) * 128, :]
                    else:
                        src = xs[ti * 128:(ti + 1) * 128, :]
                    P.dma("sp", xt[:], src, B_xt, writes=[B_xt])
                    P.op("act", lambda e, xt=xt, st=st: e.activation(out=junk[:], in_=xt[:], func=AF.Square, scale=float(D) ** -0.5,
                                                                   accum_out=st[:, 0:1]),
                         reads=[B_xt], writes=[B_junk, B_st])
                    P.op("dve", lambda e, st=st: e.tensor_scalar(out=st[:, 0:1], in0=st[:, 0:1], scalar1=1e-6, scalar2=None,
                                                               op0=ALU.add),
                         reads=[B_st], writes=[B_st])
                    P.op("act", lambda e, st=st: e.activation(out=st[:, 0:1], in_=st[:, 0:1], func=AF.Sqrt),
                         reads=[B_st], writes=[B_st])
                    P.op("dve", lambda e, st=st: e.reciprocal(out=st[:, 1:2], in_=st[:, 0:1]),
                         reads=[B_st], writes=[B_st])
                    P.op("act", lambda e, xt=xt, xn=xn, st=st: e.activation(out=xn[:], in_=xt[:], func=AF.Copy, scale=st[:, 1:2]),
                         reads=[B_xt, B_st], writes=[B_xn])
                    for g in range(4):
                        bk, B_bk = next_bank()
                        for j in range(4):
                            k = g * 4 + j
                            P.op("pe", lambda e, k=k, j=j, bk=bk, xn=xn: e.transpose(bk[:, j * 128:(j + 1) * 128],
                                                                                   xn[:, k * 128:(k + 1) * 128], ident[:]),
                                 reads=[B_xn, B_ident], writes=[B_bk])
                        for j in range(4):
                            k = g * 4 + j
                            eng = "act" if (j % 2 == 0) else "dve"
                            o = hT[:, k, ti * 128:(ti + 1) * 128]
                            i_ = bk[:, j * 128:(j + 1) * 128]
                            if eng == "act":
                                P.op("act", lambda e, o=o, i_=i_, k=k, isctx=isctx: e.activation(
                                    out=o, in_=i_, func=AF.Identity, scale=gs_col[:, k, isctx:isctx + 1],
                                    bias=sh_col[:, k, isctx:isctx + 1]),
                                    reads=[B_bk, B_gs, B_sh], writes=[B_hT[ti]])
                            else:
                                P.op("dve", lambda e, o=o, i_=i_, k=k, isctx=isctx: e.tensor_scalar(
                                    out=o, in0=i_, scalar1=gs_col[:, k, isctx:isctx + 1], scalar2=sh_col[:, k, isctx:isctx + 1],
                                    op0=ALU.mult, op1=ALU.add),
                                    reads=[B_bk, B_gs, B_sh], writes=[B_hT[ti]])

                P.barrier()
                ps.__exit__(None, None, None)
                ps = contextlib.ExitStack()
                ps.__enter__()
                wbs = [(_sb(ps, nc, "win%d" % i, [128, 16, 512], BF16), Buf("win%d" % i)) for i in range(2)]
                ost = [(_sb(ps, nc, "ost%d" % i, [128, 512], F32), Buf("ost%d" % i)) for i in range(4)]
                ostb = [(_sb(ps, nc, "ostb%d" % i, [128, 512], BF16), Buf("ostb%d" % i)) for i in range(4)]
                vst = [(_sb(ps, nc, "vst%d" % i, [128, 8, 65], BF16), Buf("vst%d" % i)) for i in range(2)]
                for i in range(2):
                    P.op("pool", lambda e, i=i: e.memset(vst[i][0][:], 1.0), writes=[vst[i][1]])
                tblocks = [(i * 512, 512) for i in range(8)] + [(4096, 256)]
                n_cb = (D_IN + 511) // 512
                evac_rr = [0]
                sub_rr = [0]
                for cb in range(n_cb):
                    c0 = cb * 512
                    cw = min(512, D_IN - c0)
                    wt, B_w = wbs[cb % 2]
                    src = w_in[layer, :, c0:c0 + cw].rearrange("(k p) c -> p k c", p=128)
                    P.dma("pool", wt[:, 0:8, 0:cw], src[:, 0:8, :], B_w, writes=[B_w])
                    P.dma("pool", wt[:, 8:16, 0:cw], src[:, 8:16, :], B_w, writes=[B_w], nowait=True)
                    if 2048 <= c0 < 3072:
                        hg = (c0 - 2048) // 512
                        for ti in range(NT):
                            bk, B_bk = next_bank()
                            for k in range(16):
                                P.op("pe", lambda e, k=k, ti=ti, bk=bk, wt=wt: e.matmul(
                                    bk[:, :], lhsT=hT[:, k, ti * 128:(ti + 1) * 128], rhs=wt[:, k, :], start=(k == 0), stop=(k == 15)),
                                    reads=[B_hT[ti], B_w], writes=[B_bk])
                            vs, B_vs = vst[ti % 2]
                            eng = "act" if evac_rr[0] % 2 == 0 else "dve"
                            evac_rr[0] += 1
                            o = vs[:, :, 0:64]
                            i_ = bk[:, :].rearrange("p (h d) -> p h d", h=8)
                            if eng == "act":
                                P.op("act", lambda e, o=o, i_=i_: e.activation(out=o, in_=i_, func=AF.Copy), reads=[B_bk], writes=[B_vs])
                            else:
                                P.op("dve", lambda e, o=o, i_=i_: e.tensor_copy(out=o, in_=i_), reads=[B_bk], writes=[B_vs])
                            P.dma("sp", vaug[ti * 128:(ti + 1) * 128, hg * 8:(hg + 1) * 8, :], vs[:], B_vaug, reads=[B_vs], writes=[])
                        continue
                    for j in range(cw // 128):
                        is_qk = c0 < 2048
                        col = c0 + j * 128
                        for tb, (t0, tw) in enumerate(tblocks):
                            bk, B_bk = next_bank()
                            for k in range(16):
                                P.op("pe", lambda e, k=k, j=j, bk=bk, wt=wt, t0=t0, tw=tw: e.matmul(
                                    bk[:, 0:tw], lhsT=wt[:, k, j * 128:(j + 1) * 128], rhs=hT[:, k, t0:t0 + tw],
                                    start=(k == 0), stop=(k == 15)),
                                    reads=[B_w] + B_hT[t0 // 128:(t0 + tw) // 128], writes=[B_bk])
                            stg, B_stg = (ostb if is_qk else ost)[sub_rr[0] % 4]
                            sub_rr[0] += 1
                            eng = "act" if evac_rr[0] % 2 == 0 else "dve"
                            evac_rr[0] += 1
                            o = stg[:, 0:tw]
                            i_ = bk[:, 0:tw]
                            if eng == "act":
                                P.op("act", lambda e, o=o, i_=i_: e.activation(out=o, in_=i_, func=AF.Copy), reads=[B_bk], writes=[B_stg])
                            else:
                                P.op("dve", lambda e, o=o, i_=i_: e.tensor_copy(out=o, in_=i_), reads=[B_bk], writes=[B_stg])
                            if is_qk:
                                P.dma("sp", qkT[col:col + 128, t0:t0 + tw], stg[:, 0:tw], B_qkT, reads=[B_stg])
                            else:
                                r0 = col - 3072
                                P.dma("sp", restT[r0:r0 + 128, t0:t0 + tw], stg[:, 0:tw], B_restT, reads=[B_stg])
                P.barrier()
                ps.__exit__(None, None, None)
                ps = ps_outer
                P.barrier()

        if "rw" in G.phases:
            phase_rwkv(G, layer)
        if "pool" in G.phases:
            phase_pool(G, layer)
        if "na" in G.phases:
            phase_na(G, layer, do_ctx=(layer < DEPTH - 1))
        if "p3" in G.phases:
            phase_out(G, layer, last=(layer == DEPTH - 1))

    P.barrier()
    print("inst counts", P.ninst, "nsem", P.nsem)


def kernel(**inputs):
    inp = {k: np.asarray(v) for k, v in inputs.items()}
    shared = dict(host_consts())
    for k in ("c_ctx", "norm_g", "w_mod", "b_mod", "w_in", "rw_w2", "rw_a2", "rw_lnx_g", "rw_lnx_b", "pool_w", "pool_scale",
              "w_branch", "w_out", "final_g"):
        shared[k] = np.ascontiguousarray(inp[k], dtype=np.float32)
    shared["rwpar"] = pack_rwpar(inp)
    shared["pool_invcnt"] = host_pool_invcnt()
    shared["na_tab"] = host_na_table(np.asarray(inp["na_rpb"], np.float32))
    nb = inp["x"].shape[0]
    in_maps = []
    for b in range(nb):
        m = dict(shared)
        m["x"] = np.ascontiguousarray(inp["x"][b], dtype=np.float32)
        m["ctx"] = np.ascontiguousarray(inp["ctx"][b], dtype=np.float32)
        m["c"] = np.ascontiguousarray(inp["c"][b], dtype=np.float32)
        in_maps.append(m)
    nc = build_program()
    res = run_bass_kernel_spmd(nc, in_maps, core_ids=list(range(nb)))
    return np.stack([np.asarray(res.results[b]["out"], dtype=np.float32) for b in range(nb)], axis=0)
```

```python
import contextlib
import os
import numpy as np
import concourse.bass as bass
import concourse.mybir as mybir
from concourse.bass_utils import run_bass_kernel_spmd

F32 = mybir.dt.float32
F32R = mybir.dt.float32r
BF16 = mybir.dt.bfloat16
AF = mybir.ActivationFunctionType
ALU = mybir.AluOpType
AX = mybir.AxisListType

D = 2048
SEQ = 4096
NCTX = 256
S = SEQ + NCTX
NT = S // 128
W = 1024
DEPTH = 2
D_IN = 16512
NCORES = 4
SAME_ENGINE_SYNC = bool(int(os.environ.get("SAME_ENGINE_SYNC", "1")))


class Ev:
    __slots__ = ("sem", "key", "val")

    def __init__(self, sem, key, val):
        self.sem, self.key, self.val = sem, key, val


class Buf:
    __slots__ = ("name", "w", "rs", "excl")

    def __init__(self, name, excl=False):
        self.name = name
        self.w = []
        self.rs = {}
        self.excl = excl


class Prog:
    N_LANES = {"sp": 24, "pool": 8, "act": 4}

    def __init__(self, nc, es):
        self.nc = nc
        self.es = es
        self.h = {"pe": nc.tensor, "act": nc.scalar, "dve": nc.vector, "pool": nc.gpsimd, "sp": nc.sync}
        self.sem = {e: es.enter_context(nc.semaphore("s_" + e)) for e in self.h}
        self.cnt = {e: 0 for e in self.h}
        self.seen = {e: {} for e in self.h}
        self.lanes = {}
        self.lane_rr = {}
        self.nsem = 0
        self.ninst = {e: 0 for e in self.h}

    def _lane(self, q):
        if q not in self.lanes:
            self.lanes[q] = []
            for i in range(self.N_LANES[q]):
                self.nsem += 1
                key = "d_%s%d" % (q, i)
                self.lanes[q].append([self.es.enter_context(self.nc.semaphore(key)), key, 0])
            self.lane_rr[q] = 0
        ln = self.lanes[q][self.lane_rr[q] % len(self.lanes[q])]
        self.lane_rr[q] += 1
        if ln[2] > 0:
            self._wait(q, Ev(ln[0], ln[1], ln[2]))
        return ln

    def _wait(self, eng, ev):
        if ev is None:
            return
        if ev.key == eng and not SAME_ENGINE_SYNC:
            return
        if ev.key == "pe" and eng == "pe":
            return
        if self.seen[eng].get(ev.key, 0) >= ev.val:
            return
        self.h[eng].wait_ge(ev.sem, ev.val)
        self.ninst[eng] += 1
        self.seen[eng][ev.key] = ev.val

    def _deps(self, eng, reads, writes):
        for b in reads:
            for ev in b.w:
                self._wait(eng, ev)
            if b.excl:
                for k, r in b.rs.items():
                    if k != eng:
                        self._wait(eng, r)
        for b in writes:
            for ev in b.w:
                self._wait(eng, ev)
            for r in b.rs.values():
                self._wait(eng, r)

    def op(self, eng, fn, reads=(), writes=()):
        self._deps(eng, reads, writes)
        inst = fn(self.h[eng])
        self.cnt[eng] += 1
        self.ninst[eng] += 1
        inst.then_inc(self.sem[eng], 1)
        ev = Ev(self.sem[eng], eng, self.cnt[eng])
        for b in reads:
            b.rs[eng] = ev
        for b in writes:
            b.w = [ev]
            b.rs = {}
        return ev

    def dma(self, q, out, in_, owner=None, reads=(), writes=(), nowait=False, **kw):
        if not nowait:
            self._deps(q, reads, writes)
        ln = self._lane(q)
        inst = self.h[q].dma_start(out=out, in_=in_, **kw)
        ln[2] += 16
        inst.then_inc(ln[0], 16)
        self.ninst[q] += 1
        ev = Ev(ln[0], ln[1], ln[2])
        for b in reads:
            b.rs[ln[1]] = ev
        for b in writes:
            if nowait:
                b.w = list(b.w) + [ev]
            else:
                b.w = [ev]
                b.rs = {}
        return ev

    def barrier(self):
        evs = [Ev(self.sem[e], e, self.cnt[e]) for e in self.h if self.cnt[e] > 0]
        for q, lanes in self.lanes.items():
            evs += [Ev(l[0], l[1], l[2]) for l in lanes if l[2] > 0]
        for e in self.h:
            for ev in evs:
                if ev.key == e:
                    if self.seen[e].get(e, 0) < ev.val and e != "sp":
                        self.h[e].wait_ge(ev.sem, ev.val)
                        self.seen[e][e] = ev.val
                    continue
                self._wait(e, ev)


_uid = [0]


def _rd(ap):
    try:
        if ap.dtype == F32R:
            return ap.bitcast(F32)
    except AttributeError:
        pass
    return ap


def _sb(es, nc, name, shape, dt):
    _uid[0] += 1
    return es.enter_context(nc.sbuf_tensor("sb%d_%s" % (_uid[0], name), list(shape), dt))


def host_consts():
    idx = np.arange(128)
    masks = np.stack([(idx[:, None] < idx[None, :]), (idx[:, None] <= idx[None, :]),
                      (idx[:, None] > idx[None, :]), (idx[:, None] >= idx[None, :])]).astype(np.float32)
    blockones = (idx[:, None] // 64 == idx[None, :] // 64).astype(np.float32)
    resetmask = np.ones((128, 256), np.float32)
    resetmask[:, 0] = 0.0
    resetmask[:, 128] = 0.0
    headsel = np.zeros((128, 2), np.float32)
    headsel[:64, 0] = 1.0
    headsel[64:, 1] = 1.0
    sel = np.zeros((2, 2, 128), np.float32)
    sel[0, 0] = 1
    sel[1, 1] = 1
    return {"ident": np.eye(128, dtype=np.float32), "sel": sel, "masks": masks, "blockones": blockones,
            "resetmask": resetmask, "headsel": headsel}


def pack_rwpar(inp):
    cols = [inp["rw_mu"][:, 0], inp["rw_mu"][:, 1], inp["rw_mu"][:, 2], inp["rw_k_k"], inp["rw_k_a"],
            inp["rw_r_k"].reshape(DEPTH, W), inp["rw_w0"][:, 0], inp["rw_w0"][:, 1], inp["rw_a0"][:, 0], inp["rw_a0"][:, 1],
            inp["rw_lnx_g"], inp["rw_lnx_b"]]
    a = np.stack([np.asarray(c, np.float32) for c in cols], axis=-1)
    return np.ascontiguousarray(a.reshape(DEPTH, 8, 128, NPAR))


def setup_consts(G, es):
    nc, P = G.nc, G.P
    masks_in = G.din("masks", [4, 128, 128])
    bo_in = G.din("blockones", [128, 128])
    rm_in = G.din("resetmask", [128, 256])
    hs_in = G.din("headsel", [128, 2])
    G.masks = _sb(es, nc, "masks", [128, 4, 128], F32)
    G.B_masks = Buf("masks")
    for i in range(4):
        P.dma("sp", G.masks[:, i, :], masks_in[i], G.B_masks, writes=[G.B_masks], nowait=(i > 0))
    G.blockones = _sb(es, nc, "blockones", [128, 128], F32)
    G.B_bo = Buf("blockones")
    P.dma("sp", G.blockones[:], bo_in[:, :], G.B_bo, writes=[G.B_bo])
    G.resetmask = _sb(es, nc, "resetmask", [128, 256], F32)
    G.B_rm = Buf("resetmask")
    P.dma("sp", G.resetmask[:], rm_in[:, :], G.B_rm, writes=[G.B_rm])
    G.headsel = _sb(es, nc, "headsel", [128, 2], F32)
    G.B_hs = Buf("headsel")
    P.dma("sp", G.headsel[:], hs_in[:, :], G.B_hs, writes=[G.B_hs])


NPAR = 12
R_RWR, R_RWK, R_RWV, R_RWG, R_LW, R_LA = 3072, 4096, 5120, 6144, 7168, 7232
LOGW_SCALE = -0.6065306597126334


def phase_rwkv(G, layer, do_ctx_out=True):
    nc, P = G.nc, G.P
    restT, B_restT = G.restT, G.B_restT
    rwpar = G.rwpar
    rw_w2, rw_a2 = G.rw_w2, G.rw_a2
    lnx_g, lnx_b = G.lnx_g, G.lnx_b
    masks, B_masks = G.masks, G.B_masks
    M_lt, M_le, M_gt, M_ge = (masks[:, i, :] for i in range(4))

    def OP(eng, fn, reads=(), writes=()):
        return P.op(eng, fn, reads=reads, writes=writes)

    def mm(bk, o, lhsT, rhs, reads, start=True, stop=True):
        OP("pe", lambda e: e.matmul(o, lhsT=lhsT, rhs=rhs, start=start, stop=stop), reads=reads, writes=[bk])

    def tr(bk, o, in_, reads):
        OP("pe", lambda e: e.transpose(o, _rd(in_), G.ident[:]), reads=list(reads) + [G.B_ident], writes=[bk])

    def act(o, i, func, reads, writes, **kw):
        kw = {k_: _rd(v_) for k_, v_ in kw.items()}
        OP("act", lambda e: e.activation(out=o, in_=_rd(i), func=func, **kw), reads=reads, writes=writes)

    def tt(o, a, b, op, reads, writes, eng="dve"):
        OP(eng, lambda e: e.tensor_tensor(out=o, in0=_rd(a), in1=_rd(b), op=op), reads=reads, writes=writes)

    def ts(o, a, s1, s2, op0, op1, reads, writes, eng="dve"):
        if s2 is None:
            OP(eng, lambda e: e.tensor_scalar(out=o, in0=_rd(a), scalar1=_rd(s1), scalar2=None, op0=op0), reads=reads, writes=writes)
        else:
            OP(eng, lambda e: e.tensor_scalar(out=o, in0=_rd(a), scalar1=_rd(s1), scalar2=_rd(s2), op0=op0, op1=op1), reads=reads, writes=writes)

    def stt(o, a, sc, b, op0, op1, reads, writes):
        OP("dve", lambda e: e.scalar_tensor_tensor(out=o, in0=_rd(a), scalar=_rd(sc), in1=_rd(b), op0=op0, op1=op1), reads=reads, writes=writes)

    def cp(eng, o, i, reads, writes):
        if eng == "act":
            act(o, i, AF.Copy, reads, writes)
        else:
            OP(eng, lambda e: e.tensor_copy(out=o, in_=_rd(i)), reads=reads, writes=writes)

    with contextlib.ExitStack() as ps:
        def sb(name, shape, dt=F32):
            return _sb(ps, nc, name, shape, dt)

        R32 = F32R if int(os.environ.get("RW_F32R", "0")) else F32
        lwla = sb("lwla", [128, S])
        B_lwla = Buf("lwla")
        P.dma("sp", lwla[:], restT[R_LW:R_LW + 128, :], B_lwla, reads=[B_restT], writes=[B_lwla])
        act(lwla[0:64, :], lwla[0:64, :], AF.Tanh, [B_lwla], [B_lwla])

        rT = sb("r", [128, S]); B_r = Buf("r")
        kT = sb("k", [128, S]); B_k = Buf("k")
        vT = sb("vkkn", [128, S]); B_v = Buf("vkkn")
        ytok = sb("ytok", [128, NT, 128]); B_y = [Buf("ytok%d" % i) for i in range(NT)]
        bon = sb("bon", [128, NT, 2]); B_bon = [Buf("bon%d" % i) for i in range(NT)]
        vtok = sb("vtok", [128, NT, 128], R32); B_vt = [Buf("vtok%d" % i) for i in range(NT)]
        par = sb("par", [128, NPAR + 8]); B_par = Buf("par")
        w2t = sb("w2t", [128, 2, 128]); B_w2 = Buf("w2t")
        lnxg = sb("lnxg", [128, 128]); lnxb = sb("lnxb", [128, 128]); B_lnx = Buf("lnx")
        rkblk = sb("rkblk", [128, 2], R32); B_rkblk = Buf("rkblk")
        nsum = ytok[:].rearrange("p t c -> p (t c)")
        GW = 256
        gtmp = [[(sb("gt%d_%d" % (d, i), [128, GW]), Buf("gt%d_%d" % (d, i))) for i in range(6)] for d in range(2)]
        gout = [[[(sb("go%d_%d_%d" % (d, pz, i), [128, GW], F32 if i == 0 else R32), Buf("go%d_%d_%d" % (d, pz, i))) for i in range(5)]
                 for pz in range(2)] for d in range(2)]
        ukds = [(sb("ukd%d" % d, [128, GW], R32), Buf("ukd%d" % d)) for d in range(2)]
        def ctile(name, w=128, dt=None):
            return (sb(name, [128, w], R32 if dt is None else dt), Buf(name))
        cper = [[[[{n: ctile("c%s%d%d%d%d" % (n, d, pz, c, h)) for n in ("Pm", "BmT", "RBT", "RKT")} for h in range(2)]
                  for c in range(2)] for pz in range(2)] for d in range(2)]
        cpair = [[[{n: ctile("c%s%d%d%d" % (n, d, pz, c)) for n in ("btok", "ktok")} for c in range(2)]
                  for pz in range(2)] for d in range(2)]
        ctmp = [[[{n: ctile("t%s%d%d%d" % (n, d, c, h)) for n in ("Xa", "XTa", "Xb", "XTb", "Pb")} for h in range(2)]
                 for c in range(2)] for d in range(2)]
        STs = [[ctile("ST%d%d" % (d, i), 64) for i in range(2)] for d in range(2)]
        S0dec = [ctile("S0dec%d" % d, 64, F32) for d in range(2)]
        Gt = [ctile("G%d" % d) for d in range(2)]
        SAt = [ctile("SA%d" % d) for d in range(2)]
        yst = [ctmp[0][0][1]["Xa"], ctmp[0][0][1]["XTa"]]
        ost = [(sb("rwo%d" % i, [128, 128], BF16), Buf("rwo%d" % i)) for i in range(2)]
        gts = [(gtmp[0][0][0][:, 0:128], gtmp[0][0][1]), (gtmp[0][1][0][:, 0:128], gtmp[0][1][1])]
        small = [ctile("sm%d" % i, 8, F32) for i in range(4)]

        RW_STAGE = int(os.environ.get("RW_STAGE", "99"))
        RW_SUB = int(os.environ.get("RW_SUB", "99"))
        for hp in range(int(os.environ.get("RW_PAIRS", "8"))):
            ch0 = hp * 128
            P.dma("sp", par[:, 0:NPAR], rwpar[layer, hp], B_par, writes=[B_par])
            for d in range(2):
                P.dma("sp", w2t[0:64, d, :], rw_w2[layer, d, :, ch0:ch0 + 128], B_w2, writes=[B_w2], nowait=(d > 0))
                P.dma("sp", w2t[64:128, d, :], rw_a2[layer, d, :, ch0:ch0 + 128], B_w2, writes=[B_w2], nowait=True)
            P.dma("sp", lnxg[:], lnx_g[layer, ch0:ch0 + 128].partition_broadcast(128), B_lnx, writes=[B_lnx])
            P.dma("sp", lnxb[:], lnx_b[layer, ch0:ch0 + 128].partition_broadcast(128), B_lnx, writes=[B_lnx], nowait=True)
            ts(par[:, NPAR:NPAR + 3], par[:, 0:3], -1.0, 1.0, ALU.mult, ALU.add, [B_par], [B_par])
            ts(par[:, NPAR + 3:NPAR + 6], par[:, 0:3], 0.5, None, ALU.mult, None, [B_par], [B_par])
            ts(par[:, NPAR + 6:NPAR + 7], par[:, 4:5], -1.0, 1.0, ALU.mult, ALU.add, [B_par], [B_par])
            ts(rkblk[:], G.headsel[:], par[:, 5:6], None, ALU.mult, None, [B_par, G.B_hs], [B_rkblk])
            C_KK, C_KA, C_OMKA = par[:, 3:4], par[:, 4:5], par[:, NPAR + 6:NPAR + 7]

            for zi, (zt, B_z, row) in enumerate(((rT, B_r, R_RWR), (kT, B_k, R_RWK), (vT, B_v, R_RWV))):
                P.dma("sp", zt[:], restT[row + ch0:row + ch0 + 128, :], B_z, reads=[B_restT], writes=[B_z])
                tt(nsum[:, 1:S - 1], zt[:, 0:S - 2], zt[:, 2:S], ALU.add, [B_z], B_y)
                for (dst, srcc) in ((0, 1), (255, 254), (256, 257), (S - 1, S - 2)):
                    cp("dve", nsum[:, dst:dst + 1], zt[:, srcc:srcc + 1], [B_z], B_y)
                ts(nsum[:, :], nsum[:, :], par[:, NPAR + 3 + zi:NPAR + 4 + zi], None, ALU.mult, None, B_y + [B_par], B_y)
                stt(zt[:], zt[:], par[:, NPAR + zi:NPAR + 1 + zi], nsum[:, :], ALU.mult, ALU.add, [B_z, B_par] + B_y, [B_z])
            if RW_STAGE < 2:
                continue
            for ti in range(NT):
                bk, B_bk = G.next_bank()
                tr(B_bk, bk[:, 0:128], vT[:, ti * 128:(ti + 1) * 128], [B_v])
                cp("act" if ti % 2 == 0 else "dve", vtok[:, ti, :], bk[:, 0:128], [B_bk], [B_vt[ti]])
            act(nsum[:, :], kT[:], AF.Copy, [B_k, B_par], B_y, scale=C_KK)
            act(vT[:], nsum[:, :], AF.Square, B_y + B_vt, [B_v])
            for t0 in range(0, S, 512):
                tw = min(512, S - t0)
                bk, B_bk = G.next_bank()
                mm(B_bk, bk[:, 0:tw], G.blockones[:], vT[:, t0:t0 + tw], [G.B_bo, B_v])
                act(vT[:, t0:t0 + tw], bk[:, 0:tw], AF.Sqrt, [B_bk], [B_v])
            ts(vT[:], vT[:], 1e-12, None, ALU.max, None, [B_v], [B_v])
            OP("dve", lambda e: e.reciprocal(out=vT[:], in_=vT[:]), reads=[B_v], writes=[B_v])
            tt(vT[:], vT[:], nsum[:, :], ALU.mult, [B_v] + B_y, [B_v])
            kkn, B_kkn = vT, B_v

            if RW_STAGE < 3:
                continue
            order = {0: list(range(17)), 1: [0] + list(range(16, 0, -1))}
            st_idx = [0, 0]
            ywritten = set()
            bwritten = set()
            for d in range(2):
                if RW_SUB < -1:
                    break
                ts(STs[d][0][0][:], G.ident[:, 0:64], 0.0, None, ALU.mult, None, [G.B_ident], [STs[d][0][1]], eng="pool")

            def prep_rounds(d, step):
                g = order[d][step]
                pz = step % 2
                t0 = g * GW
                (sg, B_sg), (cs, B_cs), (tmp, B_tmp), (ad, B_ad), (kd, B_kd), (en, B_en) = gtmp[d]
                ukd, B_ukd = ukds[d]
                (Ep, B_Ep), (aTt, B_aT), (bTt, B_bT), (kTt, B_kT), (rTt, B_rT) = gout[d][pz]
                rounds = []

                def r0():
                    if RW_SUB < 0:
                        return
                    bk, B_bk = G.next_bank()
                    bk2, B_bk2 = G.next_bank()
                    mm(B_bk, bk[:, 0:GW], w2t[0:64, d, :], lwla[0:64, t0:t0 + GW], [B_w2, B_lwla])
                    mm(B_bk2, bk2[:, 0:GW], w2t[64:128, d, :], lwla[64:128, t0:t0 + GW], [B_w2, B_lwla])
                    act(sg[:], bk[:, 0:GW], AF.Sigmoid, [B_bk, B_par], [B_sg], bias=par[:, 6 + d:7 + d])
                    act(ad[:], bk2[:, 0:GW], AF.Sigmoid, [B_bk2, B_par], [B_ad], bias=par[:, 8 + d:9 + d])
                    if RW_SUB < 1:
                        return
                    OP("dve", lambda e: e.tensor_tensor_scan(out=cs[:], data0=G.resetmask[:], data1=sg[:], initial=0.0,
                                                            op0=ALU.mult, op1=ALU.add), reads=[B_sg, G.B_rm], writes=[B_cs])
                    if d == 1 and RW_SUB >= 2:
                        cs3 = cs[:].rearrange("p (c k) -> p c k", k=128)
                        tot = cs3[:, :, 127:128].to_broadcast([128, 2, 128])
                        tt(tmp[:].rearrange("p (c k) -> p c k", k=128), tot, cs3, ALU.subtract, [B_cs], [B_tmp])
                        tt(cs[:], tmp[:], sg[:], ALU.add, [B_tmp, B_sg], [B_cs])
                rounds.append(r0)
                if RW_SUB < 3:
                    return rounds

                def r1():
                    act(Ep[:], cs[:], AF.Exp, [B_cs], [B_Ep], scale=LOGW_SCALE)
                    act(en[:], cs[:], AF.Exp, [B_cs], [B_en], scale=-LOGW_SCALE)
                    tt(tmp[:], cs[:], sg[:], ALU.subtract, [B_cs, B_sg], [B_tmp])
                    act(tmp[:], tmp[:], AF.Exp, [B_tmp], [B_tmp], scale=LOGW_SCALE)
                    ts(kd[:], ad[:], C_KA, C_OMKA, ALU.mult, ALU.add, [B_ad, B_par], [B_kd])
                    tt(kd[:], kd[:], kT[:, t0:t0 + GW], ALU.mult, [B_kd, B_k], [B_kd])
                rounds.append(r1)
                if RW_SUB < 4:
                    return rounds

                def r2():
                    tt(ukd[:], rT[:, t0:t0 + GW], kd[:], ALU.mult, [B_r, B_kd], [B_ukd], eng="pool")
                    stt(aTt[:], kkn[:, t0:t0 + GW], -1.0, tmp[:], ALU.mult, ALU.mult, [B_kkn, B_tmp], [B_aT])
                    tt(bTt[:], kkn[:, t0:t0 + GW], ad[:], ALU.mult, [B_kkn, B_ad], [B_bT])
                    tt(bTt[:], bTt[:], en[:], ALU.mult, [B_bT, B_en], [B_bT])
                    tt(kTt[:], kd[:], en[:], ALU.mult, [B_kd, B_en], [B_kT])
                    tt(rTt[:], rT[:, t0:t0 + GW], Ep[:], ALU.mult, [B_r, B_Ep], [B_rT], eng="pool")
                rounds.append(r2)

                strict_st, incl_st = (M_lt, M_le) if d == 0 else (M_gt, M_ge)
                strict_ts = M_gt if d == 0 else M_lt

                def r3():
                    for c in range(2):
                        cs_ = slice(c * 128, (c + 1) * 128)
                        bkAs = [G.next_bank(), G.next_bank()]
                        for h in range(2):
                            hs = slice(h * 64, (h + 1) * 64)
                            bkA, B_A = bkAs[h]
                            mm(B_A, bkA[:, 0:128], bTt[hs, cs_], aTt[hs, cs_], [B_bT, B_aT])
                        for h in range(2):
                            T = ctmp[d][c][h]
                            bkA, B_A = bkAs[h]
                            tt(T["Xa"][0][:], bkA[:, 0:128], strict_st, ALU.mult,
                               [B_A, B_masks], [T["Xa"][1]])
                            tt(T["Pb"][0][:], T["Xa"][0][:], G.ident[:], ALU.add, [T["Xa"][1], G.B_ident], [T["Pb"][1]], eng="pool")
                        for h in range(2):
                            hs = slice(h * 64, (h + 1) * 64)
                            bkB, B_B = G.next_bank()
                            Cp = cper[d][pz][c][h]
                            mm(B_B, bkB[:, 0:128], kTt[hs, cs_], aTt[hs, cs_], [B_kT, B_aT])
                            mm(B_B, bkB[:, 128:256], bTt[hs, cs_], rTt[hs, cs_], [B_bT, B_rT])
                            mm(B_B, bkB[:, 256:384], kTt[hs, cs_], rTt[hs, cs_], [B_kT, B_rT])
                            tt(Cp["BmT"][0][:], bkB[:, 0:128], strict_st, ALU.mult, [B_B, B_masks], [Cp["BmT"][1]])
                            tt(Cp["RBT"][0][:], bkB[:, 128:256], incl_st, ALU.mult, [B_B, B_masks], [Cp["RBT"][1]])
                            tt(Cp["RKT"][0][:], bkB[:, 256:384], incl_st, ALU.mult, [B_B, B_masks], [Cp["RKT"][1]])
                        bkC, B_C = G.next_bank()
                        tr(B_C, bkC[:, 0:128], bTt[:, cs_], [B_bT])
                        tr(B_C, bkC[:, 128:256], kTt[:, cs_], [B_kT])
                        for h in range(2):
                            T = ctmp[d][c][h]
                            tr(B_C, bkC[:, (2 + h) * 128:(3 + h) * 128], T["Xa"][0][:], [T["Xa"][1]])
                        cp("act", cpair[d][pz][c]["btok"][0][:], bkC[:, 0:128], [B_C], [cpair[d][pz][c]["btok"][1]])
                        cp("act", cpair[d][pz][c]["ktok"][0][:], bkC[:, 128:256], [B_C], [cpair[d][pz][c]["ktok"][1]])
                        for h in range(2):
                            T = ctmp[d][c][h]
                            cp("act", T["XTa"][0][:], bkC[:, (2 + h) * 128:(3 + h) * 128], [B_C], [T["XTa"][1]])
                if RW_STAGE < 4:
                    return rounds
                rounds.append(r3)

                def r3b():
                    bk, B_bk = G.next_bank()
                    for c in range(2):
                        mm(B_bk, bk[:, 2 * c:2 * c + 2], ukd[:, c * 128:(c + 1) * 128], rkblk[:], [B_ukd, B_rkblk])
                    for c in range(2):
                        ti = g * 2 + c
                        if ti not in bwritten:
                            bwritten.add(ti)
                            cp("act", bon[:, ti, :], bk[:, 2 * c:2 * c + 2], [B_bk], [B_bon[ti]])
                        else:
                            tt(bon[:, ti, :], bk[:, 2 * c:2 * c + 2], bon[:, ti, :], ALU.add, [B_bk, B_bon[ti]], [B_bon[ti]])
                rounds.append(r3b)

                def make_level(lvl):
                    def rl():
                        src, dst = ("a", "b") if lvl % 2 == 1 else ("b", "a")
                        last = (lvl == 6)
                        banks_ = []
                        for c in range(2):
                            bk, B_bk = G.next_bank()
                            banks_.append((bk, B_bk))
                            for h in range(2):
                                T = ctmp[d][c][h]
                                X, B_X = T["X" + src]
                                XT, B_XT = T["XT" + src]
                                if not last:
                                    mm(B_bk, bk[:, h * 128:(h + 1) * 128], XT[:], X[:], [B_X, B_XT])
                                else:
                                    mm(B_bk, bk[:, h * 128:(h + 1) * 128], X[:], XT[:], [B_X, B_XT])
                        for c in range(2):
                            bk, B_bk = banks_[c]
                            for h in range(2):
                                T = ctmp[d][c][h]
                                nm = ("X" if not last else "XT") + dst
                                cp("act" if c == 0 else "dve", T[nm][0][:], bk[:, h * 128:(h + 1) * 128], [B_bk], [T[nm][1]])

                    def rl1():
                        src, dst = ("a", "b") if lvl % 2 == 1 else ("b", "a")
                        if lvl == 6:
                            return
                        for c in range(2):
                            bk, B_bk = G.next_bank()
                            for h in range(2):
                                T = ctmp[d][c][h]
                                tr(B_bk, bk[:, h * 128:(h + 1) * 128], T["X" + dst][0][:], [T["X" + dst][1]])
                            for h in range(2):
                                T = ctmp[d][c][h]
                                cp("dve" if c == 0 else "act", T["XT" + dst][0][:], bk[:, h * 128:(h + 1) * 128], [B_bk], [T["XT" + dst][1]])

                    def rl2():
                        src, dst = ("a", "b") if lvl % 2 == 1 else ("b", "a")
                        if RW_SUB < 11:
                            return
                        for c in range(2):
                            bk, B_bk = G.next_bank()
                            for h in range(2):
                                T = ctmp[d][c][h]
                                Cp = cper[d][pz][c][h]
                                Pold, B_Pold = T["Pb"] if lvl % 2 == 1 else Cp["Pm"]
                                mm(B_bk, bk[:, h * 128:(h + 1) * 128], T["XT" + dst][0][:], Pold[:], [T["XT" + dst][1], B_Pold])
                            for h in range(2):
                                T = ctmp[d][c][h]
                                Cp = cper[d][pz][c][h]
                                Pold, B_Pold = T["Pb"] if lvl % 2 == 1 else Cp["Pm"]
                                Pnew, B_Pnew = Cp["Pm"] if lvl % 2 == 1 else T["Pb"]
                                tt(Pnew[:], bk[:, h * 128:(h + 1) * 128], Pold[:], ALU.add, [B_bk, B_Pold], [B_Pnew])
                    return (rl, rl1, rl2) if lvl < 6 else (rl, rl2)
                if RW_STAGE < 5:
                    return rounds
                for lvl in range(1, 1 + int(os.environ.get("RW_LVL", "6"))):
                    rounds.extend(make_level(lvl))
                def rfin():
                    for c in range(2):
                        for h in range(2):
                            T = ctmp[d][c][h]
                            Cp = cper[d][pz][c][h]
                            cp("pool", Cp["Pm"][0][:], T["Pb"][0][:], [T["Pb"][1]], [Cp["Pm"][1]])
                if RW_SUB >= 12:
                    rounds.append(rfin)
                return rounds

            def chain_rounds(d, step):
                g = order[d][step]
                pz = step % 2
                (Ep, B_Ep), (aTt, B_aT), (bTt, B_bT), (kTt, B_kT), (rTt, B_rT) = gout[d][pz]
                rounds = []
                corder = (0, 1) if d == 0 else (1, 0)
                for c in corder:
                    ti = g * 2 + c
                    cs_ = slice(c * 128, (c + 1) * 128)
                    ecol = c * 128 + (127 if d == 0 else 0)
                    eLC = Ep[:, ecol:ecol + 1]

                    def b1(c=c, ti=ti, cs_=cs_, eLC=eLC):
                        ST, B_ST = STs[d][st_idx[d] % 2]
                        bk, B_bk = G.next_bank()
                        for h in range(2):
                            hs = slice(h * 64, (h + 1) * 64)
                            Cp = cper[d][pz][c][h]
                            o = bk[:, h * 64:(h + 1) * 64]
                            mm(B_bk, o, Cp["BmT"][0][:], vtok[:, ti, h * 64:(h + 1) * 64], [Cp["BmT"][1], B_vt[ti]], start=True, stop=False)
                            mm(B_bk, o, aTt[hs, cs_], ST[hs, :], [B_aT, B_ST], start=False, stop=True)
                        cp("act", Gt[d][0][:], bk[:, 0:128], [B_bk], [Gt[d][1]])
                        ts(S0dec[d][0][:], ST[:], eLC, None, ALU.mult, None, [B_ST, B_Ep], [S0dec[d][1]], eng="pool")
                    rounds.append(b1)

                    def b2(c=c):
                        bk, B_bk = G.next_bank()
                        for h in range(2):
                            Cp = cper[d][pz][c][h]
                            mm(B_bk, bk[:, h * 64:(h + 1) * 64], Cp["Pm"][0][:], Gt[d][0][:, h * 64:(h + 1) * 64], [Cp["Pm"][1], Gt[d][1]])
                        cp("dve", SAt[d][0][:], bk[:, 0:128], [B_bk], [SAt[d][1]])
                    rounds.append(b2)

                    def b3(c=c, ti=ti, cs_=cs_, eLC=eLC):
                        ST, B_ST = STs[d][st_idx[d] % 2]
                        STn, B_STn = STs[d][(st_idx[d] + 1) % 2]
                        st_idx[d] += 1
                        SA, B_SA = SAt[d]
                        bk, B_bk = G.next_bank()
                        for h in range(2):
                            hs = slice(h * 64, (h + 1) * 64)
                            Cp = cper[d][pz][c][h]
                            o = bk[:, h * 64:(h + 1) * 64]
                            mm(B_bk, o, rTt[hs, cs_], ST[hs, :], [B_rT, B_ST], start=True, stop=False)
                            mm(B_bk, o, Cp["RBT"][0][:], SA[:, h * 64:(h + 1) * 64], [Cp["RBT"][1], B_SA], start=False, stop=False)
                            mm(B_bk, o, Cp["RKT"][0][:], vtok[:, ti, h * 64:(h + 1) * 64], [Cp["RKT"][1], B_vt[ti]], start=False, stop=True)
                        bk2, B_bk2 = G.next_bank()
                        Cq = cpair[d][pz][c]
                        mm(B_bk2, bk2[:, 0:128], Cq["ktok"][0][:], vtok[:, ti, :], [Cq["ktok"][1], B_vt[ti]], start=True, stop=False)
                        mm(B_bk2, bk2[:, 0:128], Cq["btok"][0][:], SA[:], [Cq["btok"][1], B_SA], start=False, stop=True)
                        if ti not in ywritten:
                            ywritten.add(ti)
                            cp("act", ytok[:, ti, :], bk[:, 0:128], [B_bk], [B_y[ti]])
                        else:
                            tt(ytok[:, ti, :], bk[:, 0:128], ytok[:, ti, :], ALU.add, [B_bk, B_y[ti]], [B_y[ti]])
                        for h in range(2):
                            hs = slice(h * 64, (h + 1) * 64)
                            stt(STn[hs, :], bk2[hs, h * 64:(h + 1) * 64], eLC[hs, :], S0dec[d][0][hs, :], ALU.mult, ALU.add,
                                [B_bk2, B_Ep, S0dec[d][1]], [B_STn])
                    rounds.append(b3)
                return rounds

            def interleave(lists):
                n = max(len(l) for l in lists) if lists else 0
                for i in range(n):
                    for l in lists:
                        if i < len(l):
                            l[i]()

            nsteps = 17
            for step in range(nsteps + 1):
                lists = []
                for d in range(2):
                    if step < nsteps:
                        lists.append(prep_rounds(d, step))
                    if step >= 1 and RW_STAGE >= 6:
                        lists.append(chain_rounds(d, step - 1))
                interleave(lists)

            if RW_STAGE < 7:
                continue
            for ti in range(NT):
                t0 = ti * 128
                sm, B_sm = small[ti % 4]
                y, B_yy = ytok[:, ti, :], B_y[ti]
                yt, B_yt = yst[ti % 2]
                y3 = y.rearrange("p (h c) -> p h c", h=2)
                cp("act", sm[:, 6:8], bon[:, ti, :], [B_bon[ti]], [B_sm])
                OP("dve", lambda e, y3=y3, sm=sm: e.tensor_reduce(out=sm[:, 0:2], in_=y3, axis=AX.X, op=ALU.add), reads=[B_yy], writes=[B_sm])
                ts(sm[:, 0:2], sm[:, 0:2], 1.0 / 64, None, ALU.mult, None, [B_sm], [B_sm])
                for h in range(2):
                    ts(yt[:, h * 64:(h + 1) * 64], y[:, h * 64:(h + 1) * 64], sm[:, h:h + 1], None, ALU.subtract, None, [B_yy, B_sm], [B_yt])
                sq, B_sq = gts[ti % 2]
                act(sq[:], yt[:], AF.Square, [B_yt], [B_sq])
                OP("dve", lambda e, sq=sq, sm=sm: e.tensor_reduce(out=sm[:, 2:4], in_=sq[:].rearrange("p (h c) -> p h c", h=2), axis=AX.X,
                                                                op=ALU.add), reads=[B_sq], writes=[B_sm])
                ts(sm[:, 2:4], sm[:, 2:4], 1.0 / 64, 64e-5, ALU.mult, ALU.add, [B_sm], [B_sm])
                act(sm[:, 2:4], sm[:, 2:4], AF.Sqrt, [B_sm], [B_sm])
                OP("dve", lambda e, sm=sm: e.reciprocal(out=sm[:, 4:6], in_=sm[:, 2:4]), reads=[B_sm], writes=[B_sm])
                for h in range(2):
                    hsl = slice(h * 64, (h + 1) * 64)
                    stt(yt[:, hsl], yt[:, hsl], sm[:, 4 + h:5 + h], lnxg[:, hsl], ALU.mult, ALU.mult, [B_yt, B_sm, B_lnx], [B_yt])
                tt(yt[:], yt[:], lnxb[:], ALU.add, [B_yt, B_lnx], [B_yt])
                for h in range(2):
                    hsl = slice(h * 64, (h + 1) * 64)
                    stt(yt[:, hsl], vtok[:, ti, hsl], sm[:, 6 + h:7 + h], yt[:, hsl], ALU.mult, ALU.add, [B_vt[ti], B_sm, B_yt], [B_yt])
                bk, B_bk = G.next_bank()
                tr(B_bk, bk[:, 0:128], yt[:], [B_yt])
                gt, B_gt = gts[ti % 2]
                P.dma("sp", gt[:], restT[R_RWG + ch0:R_RWG + ch0 + 128, t0:t0 + 128], B_gt, reads=[B_restT], writes=[B_gt])
                act(gt[:], gt[:], AF.Silu, [B_gt], [B_gt])
                ot, B_ot = ost[ti % 2]
                tt(ot[:], bk[:, 0:128], gt[:], ALU.mult, [B_bk, B_gt], [B_ot])
                P.dma("sp", G.brT[2 * W + ch0:2 * W + ch0 + 128, t0:t0 + 128], ot[:], G.B_brT, reads=[B_ot])
        P.barrier()


def mk_helpers(G):
    P = G.P

    class H:
        pass
    H_ = H()

    def OP(eng, fn, reads=(), writes=()):
        return P.op(eng, fn, reads=reads, writes=writes)

    def mm(bk, o, lhsT, rhs, reads, start=True, stop=True):
        OP("pe", lambda e: e.matmul(o, lhsT=lhsT, rhs=rhs, start=start, stop=stop), reads=reads, writes=[bk])

    def tr(bk, o, in_, reads):
        OP("pe", lambda e: e.transpose(o, in_, G.ident[:]), reads=list(reads) + [G.B_ident], writes=[bk])

    def act(o, i, func, reads, writes, **kw):
        OP("act", lambda e: e.activation(out=o, in_=i, func=func, **kw), reads=reads, writes=writes)

    def tt(o, a, b, op, reads, writes, eng="dve"):
        OP(eng, lambda e: e.tensor_tensor(out=o, in0=a, in1=b, op=op), reads=reads, writes=writes)

    def ts(o, a, s1, s2, op0, op1, reads, writes, eng="dve"):
        if s2 is None:
            OP(eng, lambda e: e.tensor_scalar(out=o, in0=a, scalar1=s1, scalar2=None, op0=op0), reads=reads, writes=writes)
        else:
            OP(eng, lambda e: e.tensor_scalar(out=o, in0=a, scalar1=s1, scalar2=s2, op0=op0, op1=op1), reads=reads, writes=writes)

    def stt(o, a, sc, b, op0, op1, reads, writes):
        OP("dve", lambda e: e.scalar_tensor_tensor(out=o, in0=a, scalar=sc, in1=b, op0=op0, op1=op1), reads=reads, writes=writes)

    def cp(eng, o, i, reads, writes):
        if eng == "act":
            act(o, i, AF.Copy, reads, writes)
        else:
            OP(eng, lambda e: e.tensor_copy(out=o, in_=i), reads=reads, writes=writes)

    def memset(eng, o, val, writes):
        OP(eng, lambda e: e.memset(o, val), writes=writes)
    return OP, mm, tr, act, tt, ts, stt, cp, memset


R_NAG, R_PU, R_PG, R_MERGE = 0, 1024, 2048, 7296
PADW = 8 + 256 + 16 + 4096 + 16
OFF_C, OFF_L = 8, 8 + 256 + 16


def host_pool_invcnt():
    out = np.zeros((4, S), np.float32)
    for g, win in enumerate((2, 4, 8, 16)):
        for (o, T) in ((0, NCTX), (NCTX, SEQ)):
            t = np.arange(T)
            lo = np.maximum(t - win // 2, 0)
            hi = np.minimum(t + win // 2, T)
            out[g, o:o + T] = 1.0 / (hi - lo)
    return out


def phase_pool(G, layer):
    nc, P = G.nc, G.P
    OP, mm, tr, act, tt, ts, stt, cp, memset = mk_helpers(G)
    restT, B_restT = G.restT, G.B_restT
    with contextlib.ExitStack() as ps:
        def sb(name, shape, dt=F32):
            return _sb(ps, nc, name, shape, dt)
        A = sb("pA", [128, PADW]); B_A = Buf("pA")
        Bt = sb("pB", [128, PADW]); B_B = Buf("pB")
        Ct = sb("pC", [128, PADW]); B_C = Buf("pC")
        inv = sb("pinv", [128, S]); B_inv = Buf("pinv")
        diff = [(sb("pdiff%d" % i, [128, S], BF16), Buf("pdiff%d" % i)) for i in range(2)]
        gate = sb("pgate", [128, S]); B_gate = Buf("pgate")
        pw = sb("ppw", [128, 2, 256], BF16); B_pw = Buf("ppw")
        psc = sb("ppsc", [128, 2]); B_psc = Buf("ppsc")
        stg = [(sb("pstg%d" % i, [128, 512], BF16), Buf("pstg%d" % i)) for i in range(2)]
        for t_, b_ in ((A, B_A), (Bt, B_B), (Ct, B_C)):
            memset("pool", t_[:], 0.0, [b_])

        def zero_gaps(t_, b_):
            memset("pool", t_[:, 0:OFF_C], 0.0, [b_])
            memset("pool", t_[:, OFF_C + 256:OFF_L], 0.0, [b_])
            memset("pool", t_[:, OFF_L + 4096:PADW], 0.0, [b_])

        R0, R1 = 4, PADW - 8
        for g in range(4):
            win = (2, 4, 8, 16)[g]
            P.dma("sp", inv[:], G.pool_invcnt[g].partition_broadcast(128), None, writes=[B_inv])
            P.dma("pool", pw[:], G.pool_w[layer, g].rearrange("(k p) d -> p k d", p=128), None, writes=[B_pw])
            P.dma("sp", psc[:], G.pool_scale[layer, g * 256:(g + 1) * 256].rearrange("(k p) -> p k", p=128), None, writes=[B_psc])
            for cbi in range(2):
                cb = g * 2 + cbi
                row = R_PU + cb * 128
                P.dma("sp", A[:, OFF_C:OFF_C + 256], restT[row:row + 128, 0:256], None, reads=[B_restT], writes=[B_A])
                P.dma("sp", A[:, OFF_L:OFF_L + 4096], restT[row:row + 128, 256:S], None, reads=[B_restT], writes=[B_A], nowait=True)
                tt(Bt[:, R0:R1], A[:, R0 - 1:R1 - 1], A[:, R0:R1], ALU.add, [B_A], [B_B])
                cur, B_cur = Bt, B_B
                oth, B_oth = Ct, B_C
                sh = 1
                w_ = 2
                while w_ < win:
                    tt(oth[:, R0:R1], cur[:, R0 - sh:R1 - sh], cur[:, R0 + sh:R1 + sh], ALU.add, [B_cur], [B_oth])
                    cur, B_cur, oth, B_oth = oth, B_oth, cur, B_cur
                    sh *= 2
                    w_ *= 2
                dt_, B_d = diff[cbi]
                for (po, so, T) in ((OFF_C, 0, 256), (OFF_L, 256, 4096)):
                    tt(cur[:, po:po + T], cur[:, po:po + T], inv[:, so:so + T], ALU.mult, [B_cur, B_inv], [B_cur])
                    tt(dt_[:, so:so + T], cur[:, po:po + T], A[:, po:po + T], ALU.subtract, [B_cur, B_A], [B_d])
            for dch in range(2):
                row = R_PG + g * 256 + dch * 128
                P.dma("sp", gate[:], restT[row:row + 128, :], None, reads=[B_restT], writes=[B_gate])
                act(gate[:], gate[:], AF.Silu, [B_gate], [B_gate])
                for bi, t0 in enumerate(range(0, S, 512)):
                    tw = min(512, S - t0)
                    bk, B_bk = G.next_bank()
                    for k in range(2):
                        mm(B_bk, bk[:, 0:tw], pw[:, k, dch * 128:(dch + 1) * 128], diff[k][0][:, t0:t0 + tw], [B_pw, diff[k][1]],
                           start=(k == 0), stop=(k == 1))
                    st_, B_st = stg[bi % 2]
                    stt(st_[:, 0:tw], bk[:, 0:tw], psc[:, dch:dch + 1], gate[:, t0:t0 + tw], ALU.mult, ALU.mult,
                        [B_bk, B_psc, B_gate], [B_st])
                    orow = W + g * 256 + dch * 128
                    P.dma("sp", G.brT[orow:orow + 128, t0:t0 + tw], st_[:, 0:tw], None, reads=[B_st], writes=[])
        P.barrier()


def host_na_table(na_rpb):
    L = na_rpb.shape[0]
    NEG = np.float32(-30000.0)
    col = np.arange(64)
    cs = np.clip(col - 8, 0, 48)
    cmask = (col[:, None] >= cs[None, :]) & (col[:, None] < cs[None, :] + 16)
    coff = np.clip(col[:, None] - col[None, :] + 15, 0, 30)
    tab = np.full((L, 16, 2, 64, 18, 64), NEG, np.float32)
    for par in range(2):
        for j in range(16):
            ro = j - 1 + par
            if 0 <= ro <= 14:
                g = na_rpb[:, :, ro][:, :, coff]
                tab[:, :, par, :, j, :] = np.where(cmask[None, None], g, NEG)
    tab[:, :, 1, :, 16, :] = np.where(cmask[None, None], na_rpb[:, :, 3][:, :, coff], NEG)
    tab[:, :, 0, :, 17, :] = np.where(cmask[None, None], na_rpb[:, :, 10][:, :, coff], NEG)
    return np.ascontiguousarray(tab.reshape(L, 16, 128, 18, 64))


def phase_na(G, layer, do_ctx=True):
    nc, P = G.nc, G.P
    OP, mm, tr, act, tt, ts, stt, cp, memset = mk_helpers(G)
    restT, B_restT = G.restT, G.B_restT
    qkT, B_qkT, vaug, B_vaug = G.qkT, G.B_qkT, G.vaug, G.B_vaug
    with contextlib.ExitStack() as ps:
        def sb(name, shape, dt=F32):
            return _sb(ps, nc, name, shape, dt)
        V = sb("naV", [128, NT, 16 * 65], BF16); B_V = Buf("naV")
        vsrc = vaug.rearrange("(n p) h e -> p n (h e)", p=128)
        for i in range(0, NT, 6):
            j = min(NT, i + 6)
            P.dma("sp", V[:, i:j, :], vsrc[:, i:j, :], None, reads=[B_vaug], writes=[B_V], nowait=(i > 0))
        qs = [(sb("naq%d" % i, [64, S], BF16), Buf("naq%d" % i)) for i in range(2)]
        ks = [(sb("nak%d" % i, [64, S], BF16), Buf("nak%d" % i)) for i in range(2)]
        tabs = [(sb("natab%d" % i, [128, 18, 64]), Buf("natab%d" % i)) for i in range(2)]
        ytok = sb("naytok", [128, NT, 64]); B_yt = [Buf("nay%d" % i) for i in range(NT)]
        gate = sb("nagate", [64, S]); B_gate = Buf("nagate")
        sc = [(sb("nasc%d" % i, [128, 5, 64]), Buf("nasc%d" % i)) for i in range(2)]
        PT = [[(sb("naPT%d%d" % (p_, i), [128, 7, 128], BF16), Buf("naPT%d%d" % (p_, i))) for i in range(2)] for p_ in range(2)]
        for p_ in range(2):
            for i in range(2):
                memset("pool", PT[p_][i][0][:], 0.0, [PT[p_][i][1]])
        rden = [(sb("narden%d" % i, [128, 1]), Buf("narden%d" % i)) for i in range(2)]
        ost = [(sb("naost%d" % i, [64, 512], BF16), Buf("naost%d" % i)) for i in range(2)]
        ptc = [(sb("naptc%d" % i, [128, 2, 128], BF16), Buf("naptc%d" % i)) for i in range(2)]

        for h in range(16):
            q, B_q = qs[h % 2]
            k, B_k = ks[h % 2]
            tab, B_tab = tabs[h % 2]
            P.dma("sp", q[:], qkT[h * 64:(h + 1) * 64, :], None, reads=[B_qkT], writes=[B_q])
            P.dma("sp", k[:], qkT[W + h * 64:W + (h + 1) * 64, :], None, reads=[B_qkT], writes=[B_k])
            P.dma("sp", tab[:], G.na_tab[layer, h], None, writes=[B_tab])
            P.dma("sp", gate[:], restT[R_NAG + h * 64:R_NAG + (h + 1) * 64, :], None, reads=[B_restT], writes=[B_gate])
            act(gate[:], gate[:], AF.Silu, [B_gate], [B_gate])
            vh = slice(h * 65, (h + 1) * 65)
            if do_ctx:
                for qt in range(2):
                    bk, B_bk = G.next_bank()
                    for kt in range(2):
                        mm(B_bk, bk[:, kt * 128:(kt + 1) * 128], k[:, kt * 128:(kt + 1) * 128], q[:, qt * 128:(qt + 1) * 128], [B_k, B_q])
                    pc, B_pc = ptc[qt]
                    act(pc[:].rearrange("p a b -> p (a b)"), bk[:, 0:256], AF.Exp, [B_bk], [B_pc], scale=0.125)
                    bk2, B_bk2 = G.next_bank()
                    for kt in range(2):
                        mm(B_bk2, bk2[:, 0:65], pc[:, kt, :], V[:, kt, vh], [B_pc, B_V], start=(kt == 0), stop=(kt == 1))
                    rd, B_rd = rden[qt]
                    OP("dve", lambda e, rd=rd, bk2=bk2: e.reciprocal(out=rd[:], in_=bk2[:, 64:65]), reads=[B_bk2], writes=[B_rd])
                    ts(ytok[:, qt, :], bk2[:, 0:64], rd[:, 0:1], None, ALU.mult, None, [B_bk2, B_rd], [B_yt[qt]])
            def emit_scores(rp):
                plan = []
                for par in range(2):
                    r = 2 * rp + par
                    r0 = min(max(r - 4, 0), 56)
                    t_lo = r0 // 2
                    t_hi = (r0 + 7) // 2
                    ntl = t_hi - t_lo + 1
                    bk, B_bk = G.next_bank()
                    qsl = q[:, 256 + r * 64:256 + (r + 1) * 64]
                    for m in range(ntl):
                        tk = 256 + (t_lo + m) * 128
                        mm(B_bk, bk[:, m * 64:(m + 1) * 64], k[:, tk:tk + 128], qsl, [B_k, B_q])
                    for m in range(2):
                        mm(B_bk, bk[:, (5 + m) * 64:(6 + m) * 64], k[:, m * 128:(m + 1) * 128], qsl, [B_k, B_q])
                    s_, B_s = sc[par]
                    j0 = 2 * t_lo - r + 8
                    bk3 = bk[:, 0:ntl * 64].rearrange("p (a b) -> p a b", b=64)
                    if ntl == 4:
                        stt(s_[:, 0:4, :], bk3, 0.125, tab[:, j0:j0 + 7:2, :], ALU.mult, ALU.add, [B_bk, B_tab], [B_s])
                    else:
                        stt(s_[:, 0:1, :], bk3[:, 0:1, :], 0.125, tab[:, 16:17, :], ALU.mult, ALU.add, [B_bk, B_tab], [B_s])
                        stt(s_[:, 1:4, :], bk3[:, 1:4, :], 0.125, tab[:, j0 + 2:j0 + 7:2, :], ALU.mult, ALU.add, [B_bk, B_tab], [B_s])
                        stt(s_[:, 4:5, :], bk3[:, 4:5, :], 0.125, tab[:, 17:18, :], ALU.mult, ALU.add, [B_bk, B_tab], [B_s])
                    pt, B_pt = PT[par][rp % 2]
                    qo = par * 64
                    act(pt[:, 0:ntl, qo:qo + 64], s_[:, 0:ntl, :], AF.Exp, [B_s], [B_pt])
                    act(pt[:, 5:7, qo:qo + 64], bk[:, 320:448].rearrange("p (a b) -> p a b", b=64), AF.Exp, [B_bk], [B_pt], scale=0.125)
                    plan.append((pt, B_pt, t_lo, ntl))
                return plan

            def emit_pv(rp, plan):
                bk2, B_bk2 = G.next_bank()
                nmm = 0
                tot = sum(p_[3] + 2 for p_ in plan)
                for (pt, B_pt, t_lo, ntl) in plan:
                    for m in range(ntl):
                        mm(B_bk2, bk2[:, 0:65], pt[:, m, :], V[:, 2 + t_lo + m, vh], [B_pt, B_V], start=(nmm == 0), stop=(nmm == tot - 1))
                        nmm += 1
                    for m in range(2):
                        mm(B_bk2, bk2[:, 0:65], pt[:, 5 + m, :], V[:, m, vh], [B_pt, B_V], start=(nmm == 0), stop=(nmm == tot - 1))
                        nmm += 1
                rd, B_rd = rden[rp % 2]
                OP("dve", lambda e, rd=rd, bk2=bk2: e.reciprocal(out=rd[:], in_=bk2[:, 64:65]), reads=[B_bk2], writes=[B_rd])
                ts(ytok[:, 2 + rp, :], bk2[:, 0:64], rd[:, 0:1], None, ALU.mult, None, [B_bk2, B_rd], [B_yt[2 + rp]])

            plans = {0: emit_scores(0)}
            for rp in range(32):
                if rp + 1 < 32:
                    plans[rp + 1] = emit_scores(rp + 1)
                emit_pv(rp, plans.pop(rp))
            t_first = 0 if do_ctx else 2
            for ti0 in range(t_first, NT, 4):
                n = min(4, NT - ti0)
                bk, B_bk = G.next_bank()
                for i in range(n):
                    tr(B_bk, bk[0:64, i * 128:(i + 1) * 128], ytok[:, ti0 + i, :], [B_yt[ti0 + i]])
                o_, B_o = ost[(ti0 // 4) % 2]
                tt(o_[:, 0:n * 128], bk[0:64, 0:n * 128], gate[:, ti0 * 128:(ti0 + n) * 128], ALU.mult, [B_bk, B_gate], [B_o])
                P.dma("sp", G.brT[h * 64:(h + 1) * 64, ti0 * 128:(ti0 + n) * 128], o_[:, 0:n * 128], None, reads=[B_o], writes=[])
        P.barrier()


def phase_out(G, layer, last):
    nc, P = G.nc, G.P
    OP, mm, tr, act, tt, ts, stt, cp, memset = mk_helpers(G)
    restT, B_restT = G.restT, G.B_restT
    mT, B_mT = G.mT, G.B_mT
    t_first = 2 if last else 0
    with contextlib.ExitStack() as ps:
        def sb(name, shape, dt=F32):
            return _sb(ps, nc, name, shape, dt)
        wbr = sb("wbr", [128, 24, D], BF16); B_wbr = Buf("wbr")
        wsrc = G.w_branch[layer].rearrange("b (k p) d -> p (b k) d", p=128)
        for i in range(0, 24, 4):
            P.dma("pool", wbr[:, i:i + 4, :], wsrc[:, i:i + 4, :], None, writes=[B_wbr], nowait=(i > 0))
        bts = [(sb("bT%d" % i, [128, 24, 512], BF16), Buf("bT%d" % i)) for i in range(2)]
        lgs = [(sb("lg%d" % i, [128, 512]), Buf("lg%d" % i)) for i in range(3)]
        macc = [(sb("macc%d" % i, [128, 512]), Buf("macc%d" % i)) for i in range(2)]
        mst = [(sb("mst%d" % i, [128, 512], BF16), Buf("mst%d" % i)) for i in range(2)]
        bsrc = G.brT.rearrange("(c p) t -> p c t", p=128)
        for bi, t0 in enumerate(range(t_first * 128, S, 512)):
            tw = min(512, S - t0)
            bt, B_bt = bts[bi % 2]
            P.dma("sp", bt[:, 0:12, 0:tw], bsrc[:, 0:12, t0:t0 + tw], None, reads=[G.B_brT], writes=[B_bt])
            P.dma("sp", bt[:, 12:24, 0:tw], bsrc[:, 12:24, t0:t0 + tw], None, reads=[G.B_brT], writes=[B_bt], nowait=True)
            for fo in range(16):
                ma, B_ma = macc[fo % 2]
                for kb in range(3):
                    lg, B_lg = lgs[kb]
                    row = R_MERGE + kb * D + fo * 128
                    P.dma("sp", lg[:, 0:tw], restT[row:row + 128, t0:t0 + tw], None, reads=[B_restT], writes=[B_lg])
                    act(lg[:, 0:tw], lg[:, 0:tw], AF.Sigmoid, [B_lg], [B_lg])
                    bk, B_bk = G.next_bank()
                    for kc in range(8):
                        mm(B_bk, bk[:, 0:tw], wbr[:, kb * 8 + kc, fo * 128:(fo + 1) * 128], bt[:, kb * 8 + kc, 0:tw], [B_wbr, B_bt],
                           start=(kc == 0), stop=(kc == 7))
                    if kb == 0:
                        tt(ma[:, 0:tw], bk[:, 0:tw], lg[:, 0:tw], ALU.mult, [B_bk, B_lg], [B_ma])
                    else:
                        tt(lg[:, 0:tw], bk[:, 0:tw], lg[:, 0:tw], ALU.mult, [B_bk, B_lg], [B_lg])
                        if kb == 1:
                            tt(ma[:, 0:tw], ma[:, 0:tw], lg[:, 0:tw], ALU.add, [B_ma, B_lg], [B_ma], eng="pool")
                        else:
                            ms, B_ms = mst[fo % 2]
                            tt(ms[:, 0:tw], ma[:, 0:tw], lg[:, 0:tw], ALU.add, [B_ma, B_lg], [B_ms], eng="pool")
                            P.dma("sp", mT[fo * 128:(fo + 1) * 128, t0:t0 + tw], ms[:, 0:tw], None, reads=[B_ms], writes=[])
        P.barrier()
    with contextlib.ExitStack() as ps:
        def sb(name, shape, dt=F32):
            return _sb(ps, nc, name, shape, dt)
        wo = sb("wo", [128, 16, D], BF16); B_wo = Buf("wo")
        wsrc = G.w_out[layer].rearrange("(k p) d -> p k d", p=128)
        for i in range(0, 16, 4):
            P.dma("pool", wo[:, i:i + 4, :], wsrc[:, i:i + 4, :], None, writes=[B_wo], nowait=(i > 0))
        gbc = sb("gbc", [128, 2, D]); B_gbc = Buf("gbc")
        P.dma("sp", gbc[:, 0, :], G.gate_d[layer, 0].partition_broadcast(128), None, reads=[G.B_gate_d], writes=[B_gbc])
        P.dma("sp", gbc[:, 1, :], G.gate_d[layer, 1].partition_broadcast(128), None, reads=[G.B_gate_d], writes=[B_gbc], nowait=True)
        if last:
            fg = sb("fg", [128, D]); B_fg = Buf("fg")
            P.dma("sp", fg[:], G.final_g.partition_broadcast(128), None, writes=[B_fg])
        mts = [(sb("mt%d" % i, [128, 16, 128], BF16), Buf("mt%d" % i)) for i in range(2)]
        xts = [(sb("xo%d" % i, [128, D]), Buf("xo%d" % i)) for i in range(2)]
        xns = [(sb("xnw%d" % i, [128, D]), Buf("xnw%d" % i)) for i in range(2)]
        junk = sb("ojunk", [128, D], BF16); B_junk = Buf("ojunk")
        stat = [(sb("ostat%d" % i, [128, 2]), Buf("ostat%d" % i)) for i in range(2)]
        msrc = mT.rearrange("(k p) t -> p k t", p=128)
        for ti in range(t_first, NT):
            mt, B_mt = mts[ti % 2]
            xt, B_xt = xts[ti % 2]
            xn, B_xn = xns[ti % 2]
            isctx = 1 if ti < 2 else 0
            P.dma("sp", mt[:], msrc[:, :, ti * 128:(ti + 1) * 128], None, reads=[B_mT], writes=[B_mt])
            P.dma("sp", xt[:], G.x_src(layer, ti), None, reads=[G.B_xs], writes=[B_xt])
            for cbk in range(4):
                bk, B_bk = G.next_bank()
                for kc in range(16):
                    mm(B_bk, bk[:, :], mt[:, kc, :], wo[:, kc, cbk * 512:(cbk + 1) * 512], [B_mt, B_wo], start=(kc == 0), stop=(kc == 15))
                csl = slice(cbk * 512, (cbk + 1) * 512)
                tt(xn[:, csl], bk[:, :], gbc[:, isctx, csl], ALU.mult, [B_bk, B_gbc], [B_xn])
                tt(xn[:, csl], xn[:, csl], xt[:, csl], ALU.add, [B_xn, B_xt], [B_xn], eng="pool")
            if not last:
                P.dma("sp", G.xs[ti * 128:(ti + 1) * 128, :], xn[:], None, reads=[B_xn], writes=[])
            else:
                st, B_st = stat[ti % 2]
                act(junk[:], xn[:], AF.Square, [B_xn], [B_junk, B_st], scale=float(D) ** -0.5, accum_out=st[:, 0:1])
                ts(st[:, 0:1], st[:, 0:1], 1e-6, None, ALU.add, None, [B_st], [B_st])
                act(st[:, 0:1], st[:, 0:1], AF.Sqrt, [B_st], [B_st])
                OP("dve", lambda e, st=st: e.reciprocal(out=st[:, 1:2], in_=st[:, 0:1]), reads=[B_st], writes=[B_st])
                stt(xn[:], xn[:], st[:, 1:2], fg[:], ALU.mult, ALU.mult, [B_xn, B_st, B_fg], [B_xn])
                P.dma("sp", G.out[(ti - 2) * 128:(ti - 1) * 128, :], xn[:], None, reads=[B_xn], writes=[])
        P.barrier()


ALL_PHASES = ("p0", "p1", "rw", "pool", "na", "p3")


def build_program(n_layers=DEPTH, debug_outs=(), phases=ALL_PHASES, ext_in=(), only_layer=None):
    nc = bass.Bass("TRN2", target_bir_lowering=False)
    es = contextlib.ExitStack()
    with es:
        _build(nc, es, n_layers, debug_outs, phases, ext_in, only_layer)
    return nc


def _build(nc, es, n_layers, debug_outs, phases, ext_in, only_layer=None):
    P = Prog(nc, es)
    allow = es.enter_context(nc.allow_non_contiguous_dma(reason="small strided param loads"))

    def din(name, shape, dt=F32):
        return nc.dram_tensor(name, list(shape), dt, kind="ExternalInput").ap()

    def dscr(name, shape, dt=F32):
        kind = "ExternalOutput" if name in debug_outs else ("ExternalInput" if name in ext_in else "Internal")
        return nc.dram_tensor(name, list(shape), dt, kind=kind).ap()

    if "p0" in phases or "p1" in phases:
        x_in = din("x", [SEQ, D])
        ctx_in = din("ctx", [NCTX, D])
        c_in = din("c", [D])
        cctx_in = din("c_ctx", [D])
        norm_g = din("norm_g", [DEPTH, D])
        w_mod = din("w_mod", [DEPTH, D, 3 * D])
        b_mod = din("b_mod", [DEPTH, 3 * D])
        w_in = din("w_in", [DEPTH, D, D_IN])
    ident_in = din("ident", [128, 128])
    sel_in = din("sel", [2, 2, 128])
    out = nc.dram_tensor("out", [SEQ, D], F32, kind="ExternalOutput").ap()

    qkT = dscr("qkT", [2048, S], BF16)
    vaug = dscr("vaug", [S, 16, 65], BF16)
    restT = dscr("restT", [D_IN - 3072, S], F32)
    B_qkT, B_vaug, B_restT = Buf("qkT"), Buf("vaug"), Buf("restT")

    banks = []
    for i in range(8):
        t = es.enter_context(nc.psum_tensor("bank%d" % i, [128, 512], F32))
        banks.append((t, Buf("bank%d" % i, excl=True)))
    bank_rr = [0]

    def next_bank():
        b = banks[bank_rr[0] % 8]
        bank_rr[0] += 1
        return b

    ident = _sb(es, nc, "ident", [128, 128], F32)
    B_ident = Buf("ident")
    P.dma("sp", ident[:], ident_in[:, :], B_ident, writes=[B_ident])
    sel = _sb(es, nc, "sel", [2, 2, 128], F32)
    B_sel = Buf("sel")
    P.dma("sp", sel[:], sel_in[:, :, :], B_sel, writes=[B_sel])
    gs_col = _sb(es, nc, "gs_col", [128, 16, 2], F32)
    sh_col = _sb(es, nc, "sh_col", [128, 16, 2], F32)
    B_gs, B_sh = Buf("gs"), Buf("sh")
    gate_d = dscr("gate_d", [DEPTH, 2, D], F32)
    B_gate_d = Buf("gate_d")

    G = type("Ctx", (), {})()
    G.nc, G.P, G.next_bank, G.ident, G.B_ident, G.restT, G.B_restT = nc, P, next_bank, ident, B_ident, restT, B_restT
    G.din, G.dscr = din, dscr
    G.phases = phases
    setup_consts(G, es)
    xs = dscr("xs", [S, D], F32)
    G.xs, G.B_xs = xs, Buf("xs")
    G.qkT, G.B_qkT, G.vaug, G.B_vaug = qkT, B_qkT, vaug, B_vaug
    G.gate_d, G.B_gate_d = gate_d, B_gate_d
    G.out = out
    G.mT, G.B_mT = dscr("mT", [D, S], BF16), Buf("mT")
    if "pool" in phases:
        G.pool_invcnt = din("pool_invcnt", [4, S])
        G.pool_w = din("pool_w", [DEPTH, 4, 256, 256])
        G.pool_scale = din("pool_scale", [DEPTH, W])
    if "na" in phases:
        G.na_tab = din("na_tab", [DEPTH, 16, 128, 18, 64])
    if "p3" in phases:
        G.w_branch = din("w_branch", [DEPTH, 3, W, D])
        G.w_out = din("w_out", [DEPTH, D, D])
        G.final_g = din("final_g", [D])
        if "p1" not in phases:
            x_in = din("x", [SEQ, D])
            ctx_in = din("ctx", [NCTX, D])

        def x_src(layer, ti):
            if layer == 0:
                return ctx_in[ti * 128:(ti + 1) * 128, :] if ti < 2 else x_in[(ti - 2) * 128:(ti - 1) * 128, :]
            return xs[ti * 128:(ti + 1) * 128, :]
        G.x_src = x_src
    if "rw" in phases:
        G.rwpar = din("rwpar", [DEPTH, 8, 128, NPAR])
        G.rw_w2 = din("rw_w2", [DEPTH, 2, 64, W])
        G.rw_a2 = din("rw_a2", [DEPTH, 2, 64, W])
        G.lnx_g = din("rw_lnx_g", [DEPTH, W])
        G.lnx_b = din("rw_lnx_b", [DEPTH, W])
    brT = dscr("brT", [3 * W, S], BF16)
    G.brT, G.B_brT = brT, Buf("brT")
    for layer in range(n_layers):
        if only_layer is not None and layer != only_layer:
            continue
        if "p0" in G.phases:
            with contextlib.ExitStack() as ps:
                condT = _sb(ps, nc, "condT", [128, 16, 2], F32)
                B_cond = Buf("condT")
                P.dma("sp", condT[:, :, 0], c_in.rearrange("(k p) -> p k", p=128), B_cond, writes=[B_cond])
                P.dma("sp", condT[:, :, 1], cctx_in.rearrange("(k p) -> p k", p=128), B_cond, writes=[B_cond], nowait=True)
                scond = _sb(ps, nc, "scond", [128, 16, 2], F32)
                B_scond = Buf("scond")
                P.op("act", lambda e: e.activation(out=scond[:], in_=condT[:], func=AF.Silu),
                     reads=[B_cond], writes=[B_scond])
                gcol = _sb(ps, nc, "gcol", [128, 16], F32)
                B_gcol = Buf("gcol")
                P.dma("sp", gcol[:], norm_g[layer].rearrange("(k p) -> p k", p=128), B_gcol, writes=[B_gcol])
                modrow = _sb(ps, nc, "modrow", [2, 3 * D], F32)
                B_modrow = Buf("modrow")
                bmod2 = _sb(ps, nc, "bmod2", [2, 3 * D], F32)
                B_bmod = Buf("bmod2")
                P.dma("sp", bmod2[0:1, :], b_mod[layer:layer + 1, :], B_bmod, writes=[B_bmod])
                P.dma("sp", bmod2[1:2, :], b_mod[layer:layer + 1, :], B_bmod, writes=[B_bmod], nowait=True)
                wbufs = []
                for i in range(2):
                    wbufs.append((_sb(ps, nc, "wmod%d" % i, [128, 16, 512], F32), Buf("wmod%d" % i)))
                for cb in range(12):
                    wt, B_w = wbufs[cb % 2]
                    src = w_mod[layer, :, cb * 512:(cb + 1) * 512].rearrange("(k p) c -> p k c", p=128)
                    P.dma("sp", wt[:, 0:8, :], src[:, 0:8, :], B_w, writes=[B_w])
                    P.dma("sp", wt[:, 8:16, :], src[:, 8:16, :], B_w, writes=[B_w], nowait=True)
                    bk, B_bk = next_bank()
                    for k in range(16):
                        P.op("pe", lambda e, k=k, wt=wt, bk=bk: e.matmul(bk[0:2, :], lhsT=scond[:, k, :], rhs=wt[:, k, :],
                                                                        start=(k == 0), stop=(k == 15)),
                             reads=[B_scond, B_w], writes=[B_bk])
                    P.op("dve", lambda e, bk=bk, cb=cb: e.tensor_tensor(out=modrow[:, cb * 512:(cb + 1) * 512], in0=bk[0:2, :],
                                                                       in1=bmod2[:, cb * 512:(cb + 1) * 512], op=ALU.add),
                         reads=[B_bk, B_bmod], writes=[B_modrow])
                bk, B_bk = next_bank()
                for which in range(2):
                    for k in range(16):
                        c0 = which * D + k * 128
                        o0 = (which * 16 + k) * 2
                        P.op("pe", lambda e, c0=c0, o0=o0, bk=bk: e.matmul(bk[:, o0:o0 + 2], lhsT=modrow[0:2, c0:c0 + 128],
                                                                          rhs=ident[0:2, 0:2], start=True, stop=True),
                             reads=[B_modrow, B_ident], writes=[B_bk])
                P.op("act", lambda e, bk=bk: e.activation(out=sh_col[:].rearrange("p k t -> p (k t)"), in_=bk[:, 0:32], func=AF.Copy),
                     reads=[B_bk], writes=[B_sh])
                tmpc = _sb(ps, nc, "tmpc", [128, 16, 2], F32)
                B_tmpc = Buf("tmpc")
                P.op("dve", lambda e, bk=bk: e.tensor_scalar(out=tmpc[:].rearrange("p k t -> p (k t)"), in0=bk[:, 32:64], scalar1=1.0,
                                                            scalar2=None, op0=ALU.add),
                     reads=[B_bk], writes=[B_tmpc])
                for t in range(2):
                    P.op("dve", lambda e, t=t: e.tensor_tensor(out=gs_col[:, :, t], in0=tmpc[:, :, t], in1=gcol[:], op=ALU.mult),
                         reads=[B_tmpc, B_gcol], writes=[B_gs])
                P.dma("sp", gate_d[layer], modrow[0:2, 2 * D:3 * D], B_gate_d, reads=[B_modrow])
                P.barrier()

        if "p1" in G.phases:
            with contextlib.ExitStack() as ps:
                hT = _sb(ps, nc, "hT", [128, 16, S], BF16)
                B_hT = [Buf("hT%d" % i) for i in range(NT)]
                ps_outer = ps
                ps = contextlib.ExitStack()
                ps.__enter__()
                xts = [(_sb(ps, nc, "xt%d" % i, [128, D], F32), Buf("xt%d" % i)) for i in range(2)]
                xns = [(_sb(ps, nc, "xn%d" % i, [128, D], F32), Buf("xn%d" % i)) for i in range(2)]
                junk = _sb(ps, nc, "junk", [128, D], BF16)
                B_junk = Buf("junk")
                stat = [(_sb(ps, nc, "stat%d" % i, [128, 2], F32), Buf("stat%d" % i)) for i in range(2)]
                for ti in range(NT):
                    xt, B_xt = xts[ti % 2]
                    xn, B_xn = xns[ti % 2]
                    st, B_st = stat[ti % 2]
                    isctx = 1 if ti < 2 else 0
                    if layer == 0:
                        src = ctx_in[ti * 128:(ti + 1) * 128, :] if ti < 2 else x_in[(ti - 2) * 128:(ti - 1) * 128, :]
                    else:
                        src = xs[ti * 128:(ti + 1) * 128, :]
                    P.dma("sp", xt[:], src, B_xt, writes=[B_xt])
                    P.op("act", lambda e, xt=xt, st=st: e.activation(out=junk[:], in_=xt[:], func=AF.Square, scale=float(D) ** -0.5,
                                                                   accum_out=st[:, 0:1]),
                         reads=[B_xt], writes=[B_junk, B_st])
                    P.op("dve", lambda e, st=st: e.tensor_scalar(out=st[:, 0:1], in0=st[:, 0:1], scalar1=1e-6, scalar2=None,
                                                               op0=ALU.add),
                         reads=[B_st], writes=[B_st])
                    P.op("act", lambda e, st=st: e.activation(out=st[:, 0:1], in_=st[:, 0:1], func=AF.Sqrt),
                         reads=[B_st], writes=[B_st])
                    P.op("dve", lambda e, st=st: e.reciprocal(out=st[:, 1:2], in_=st[:, 0:1]),
                         reads=[B_st], writes=[B_st])
                    P.op("act", lambda e, xt=xt, xn=xn, st=st: e.activation(out=xn[:], in_=xt[:], func=AF.Copy, scale=st[:, 1:2]),
                         reads=[B_xt, B_st], writes=[B_xn])
                    for g in range(4):
                        bk, B_bk = next_bank()
                        for j in range(4):
                            k = g * 4 + j
                            P.op("pe", lambda e, k=k, j=j, bk=bk, xn=xn: e.transpose(bk[:, j * 128:(j + 1) * 128],
                                                                                   xn[:, k * 128:(k + 1) * 128], ident[:]),
                                 reads=[B_xn, B_ident], writes=[B_bk])
                        for j in range(4):
                            k = g * 4 + j
                            eng = "act" if (j % 2 == 0) else "dve"
                            o = hT[:, k, ti * 128:(ti + 1) * 128]
                            i_ = bk[:, j * 128:(j + 1) * 128]
                            if eng == "act":
                                P.op("act", lambda e, o=o, i_=i_, k=k, isctx=isctx: e.activation(
                                    out=o, in_=i_, func=AF.Identity, scale=gs_col[:, k, isctx:isctx + 1],
                                    bias=sh_col[:, k, isctx:isctx + 1]),
                                    reads=[B_bk, B_gs, B_sh], writes=[B_hT[ti]])
                            else:
                                P.op("dve", lambda e, o=o, i_=i_, k=k, isctx=isctx: e.tensor_scalar(
                                    out=o, in0=i_, scalar1=gs_col[:, k, isctx:isctx + 1], scalar2=sh_col[:, k, isctx:isctx + 1],
                                    op0=ALU.mult, op1=ALU.add),
                                    reads=[B_bk, B_gs, B_sh], writes=[B_hT[ti]])

                P.barrier()
                ps.__exit__(None, None, None)
                ps = contextlib.ExitStack()
                ps.__enter__()
                wbs = [(_sb(ps, nc, "win%d" % i, [128, 16, 512], BF16), Buf("win%d" % i)) for i in range(2)]
                ost = [(_sb(ps, nc, "ost%d" % i, [128, 512], F32), Buf("ost%d" % i)) for i in range(4)]
                ostb = [(_sb(ps, nc, "ostb%d" % i, [128, 512], BF16), Buf("ostb%d" % i)) for i in range(4)]
                vst = [(_sb(ps, nc, "vst%d" % i, [128, 8, 65], BF16), Buf("vst%d" % i)) for i in range(2)]
                for i in range(2):
                    P.op("pool", lambda e, i=i: e.memset(vst[i][0][:], 1.0), writes=[vst[i][1]])
                tblocks = [(i * 512, 512) for i in range(8)] + [(4096, 256)]
                n_cb = (D_IN + 511) // 512
                evac_rr = [0]
                sub_rr = [0]
                for cb in range(n_cb):
                    c0 = cb * 512
                    cw = min(512, D_IN - c0)
                    wt, B_w = wbs[cb % 2]
                    src = w_in[layer, :, c0:c0 + cw].rearrange("(k p) c -> p k c", p=128)
                    P.dma("pool", wt[:, 0:8, 0:cw], src[:, 0:8, :], B_w, writes=[B_w])
                    P.dma("pool", wt[:, 8:16, 0:cw], src[:, 8:16, :], B_w, writes=[B_w], nowait=True)
                    if 2048 <= c0 < 3072:
                        hg = (c0 - 2048) // 512
                        for ti in range(NT):
                            bk, B_bk = next_bank()
                            for k in range(16):
                                P.op("pe", lambda e, k=k, ti=ti, bk=bk, wt=wt: e.matmul(
                                    bk[:, :], lhsT=hT[:, k, ti * 128:(ti + 1) * 128], rhs=wt[:, k, :], start=(k == 0), stop=(k == 15)),
                                    reads=[B_hT[ti], B_w], writes=[B_bk])
                            vs, B_vs = vst[ti % 2]
                            eng = "act" if evac_rr[0] % 2 == 0 else "dve"
                            evac_rr[0] += 1
                            o = vs[:, :, 0:64]
                            i_ = bk[:, :].rearrange("p (h d) -> p h d", h=8)
                            if eng == "act":
                                P.op("act", lambda e, o=o, i_=i_: e.activation(out=o, in_=i_, func=AF.Copy), reads=[B_bk], writes=[B_vs])
                            else:
                                P.op("dve", lambda e, o=o, i_=i_: e.tensor_copy(out=o, in_=i_), reads=[B_bk], writes=[B_vs])
                            P.dma("sp", vaug[ti * 128:(ti + 1) * 128, hg * 8:(hg + 1) * 8, :], vs[:], B_vaug, reads=[B_vs], writes=[])
                        continue
                    for j in range(cw // 128):
                        is_qk = c0 < 2048
                        col = c0 + j * 128
                        for tb, (t0, tw) in enumerate(tblocks):
                            bk, B_bk = next_bank()
                            for k in range(16):
                                P.op("pe", lambda e, k=k, j=j, bk=bk, wt=wt, t0=t0, tw=tw: e.matmul(
                                    bk[:, 0:tw], lhsT=wt[:, k, j * 128:(j + 1) * 128], rhs=hT[:, k, t0:t0 + tw],
                                    start=(k == 0), stop=(k == 15)),
                                    reads=[B_w] + B_hT[t0 // 128:(t0 + tw) // 128], writes=[B_bk])
                            stg, B_stg = (ostb if is_qk else ost)[sub_rr[0] % 4]
                            sub_rr[0] += 1
                            eng = "act" if evac_rr[0] % 2 == 0 else "dve"
                            evac_rr[0] += 1
                            o = stg[:, 0:tw]
                            i_ = bk[:, 0:tw]
                            if eng == "act":
                                P.op("act", lambda e, o=o, i_=i_: e.activation(out=o, in_=i_, func=AF.Copy), reads=[B_bk], writes=[B_stg])
                            else:
                                P.op("dve", lambda e, o=o, i_=i_: e.tensor_copy(out=o, in_=i_), reads=[B_bk], writes=[B_stg])
                            if is_qk:
                                P.dma("sp", qkT[col:col + 128, t0:t0 + tw], stg[:, 0:tw], B_qkT, reads=[B_stg])
                            else:
                                r0 = col - 3072
                                P.dma("sp", restT[r0:r0 + 128, t0:t0 + tw], stg[:, 0:tw], B_restT, reads=[B_stg])
                P.barrier()
                ps.__exit__(None, None, None)
                ps = ps_outer
                P.barrier()

        if "rw" in G.phases:
            phase_rwkv(G, layer)
        if "pool" in G.phases:
            phase_pool(G, layer)
        if "na" in G.phases:
            phase_na(G, layer, do_ctx=(layer < DEPTH - 1))
        if "p3" in G.phases:
            phase_out(G, layer, last=(layer == DEPTH - 1))

    P.barrier()
    print("inst counts", P.ninst, "nsem", P.nsem)


def kernel(**inputs):
    inp = {k: np.asarray(v) for k, v in inputs.items()}
    shared = dict(host_consts())
    for k in ("c_ctx", "norm_g", "w_mod", "b_mod", "w_in", "rw_w2", "rw_a2", "rw_lnx_g", "rw_lnx_b", "pool_w", "pool_scale",
              "w_branch", "w_out", "final_g"):
        shared[k] = np.ascontiguousarray(inp[k], dtype=np.float32)
    shared["rwpar"] = pack_rwpar(inp)
    shared["pool_invcnt"] = host_pool_invcnt()
    shared["na_tab"] = host_na_table(np.asarray(inp["na_rpb"], np.float32))
    nb = inp["x"].shape[0]
    in_maps = []
    for b in range(nb):
        m = dict(shared)
        m["x"] = np.ascontiguousarray(inp["x"][b], dtype=np.float32)
        m["ctx"] = np.ascontiguousarray(inp["ctx"][b], dtype=np.float32)
        m["c"] = np.ascontiguousarray(inp["c"][b], dtype=np.float32)
        in_maps.append(m)
    nc = build_program()
    res = run_bass_kernel_spmd(nc, in_maps, core_ids=list(range(nb)))
    return np.stack([np.asarray(res.results[b]["out"], dtype=np.float32) for b in range(nb)], axis=0)
```

```python
import contextlib
import os
import numpy as np
import concourse.bass as bass
import concourse.mybir as mybir
from concourse.bass_utils import run_bass_kernel_spmd

F32 = mybir.dt.float32
F32R = mybir.dt.float32r
BF16 = mybir.dt.bfloat16
AF = mybir.ActivationFunctionType
ALU = mybir.AluOpType
AX = mybir.AxisListType

D = 2048
SEQ = 4096
NCTX = 256
S = SEQ + NCTX
NT = S // 128
W = 1024
DEPTH = 2
D_IN = 16512
NCORES = 4
SAME_ENGINE_SYNC = bool(int(os.environ.get("SAME_ENGINE_SYNC", "1")))


class Ev:
    __slots__ = ("sem", "key", "val")

    def __init__(self, sem, key, val):
        self.sem, self.key, self.val = sem, key, val


class Buf:
    __slots__ = ("name", "w", "rs", "excl")

    def __init__(self, name, excl=False):
        self.name = name
        self.w = []
        self.rs = {}
        self.excl = excl


class Prog:
    N_LANES = {"sp": 24, "pool": 8, "act": 4}

    def __init__(self, nc, es):
        self.nc = nc
        self.es = es
        self.h = {"pe": nc.tensor, "act": nc.scalar, "dve": nc.vector, "pool": nc.gpsimd, "sp": nc.sync}
        self.sem = {e: es.enter_context(nc.semaphore("s_" + e)) for e in self.h}
        self.cnt = {e: 0 for e in self.h}
        self.seen = {e: {} for e in self.h}
        self.lanes = {}
        self.lane_rr = {}
        self.nsem = 0
        self.ninst = {e: 0 for e in self.h}

    def _lane(self, q):
        if q not in self.lanes:
            self.lanes[q] = []
            for i in range(self.N_LANES[q]):
                self.nsem += 1
                key = "d_%s%d" % (q, i)
                self.lanes[q].append([self.es.enter_context(self.nc.semaphore(key)), key, 0])
            self.lane_rr[q] = 0
        ln = self.lanes[q][self.lane_rr[q] % len(self.lanes[q])]
        self.lane_rr[q] += 1
        if ln[2] > 0:
            self._wait(q, Ev(ln[0], ln[1], ln[2]))
        return ln

    def _wait(self, eng, ev):
        if ev is None:
            return
        if ev.key == eng and not SAME_ENGINE_SYNC:
            return
        if ev.key == "pe" and eng == "pe":
            return
        if self.seen[eng].get(ev.key, 0) >= ev.val:
            return
        self.h[eng].wait_ge(ev.sem, ev.val)
        self.ninst[eng] += 1
        self.seen[eng][ev.key] = ev.val

    def _deps(self, eng, reads, writes):
        for b in reads:
            for ev in b.w:
                self._wait(eng, ev)
            if b.excl:
                for k, r in b.rs.items():
                    if k != eng:
                        self._wait(eng, r)
        for b in writes:
            for ev in b.w:
                self._wait(eng, ev)
            for r in b.rs.values():
                self._wait(eng, r)

    def op(self, eng, fn, reads=(), writes=()):
        self._deps(eng, reads, writes)
        inst = fn(self.h[eng])
        self.cnt[eng] += 1
        self.ninst[eng] += 1
        inst.then_inc(self.sem[eng], 1)
        ev = Ev(self.sem[eng], eng, self.cnt[eng])
        for b in reads:
            b.rs[eng] = ev
        for b in writes:
            b.w = [ev]
            b.rs = {}
        return ev

    def dma(self, q, out, in_, owner=None, reads=(), writes=(), nowait=False, **kw):
        if not nowait:
            self._deps(q, reads, writes)
        ln = self._lane(q)
        inst = self.h[q].dma_start(out=out, in_=in_, **kw)
        ln[2] += 16
        inst.then_inc(ln[0], 16)
        self.ninst[q] += 1
        ev = Ev(ln[0], ln[1], ln[2])
        for b in reads:
            b.rs[ln[1]] = ev
        for b in writes:
            if nowait:
                b.w = list(b.w) + [ev]
            else:
                b.w = [ev]
                b.rs = {}
        return ev

    def barrier(self):
        evs = [Ev(self.sem[e], e, self.cnt[e]) for e in self.h if self.cnt[e] > 0]
        for q, lanes in self.lanes.items():
            evs += [Ev(l[0], l[1], l[2]) for l in lanes if l[2] > 0]
        for e in self.h:
            for ev in evs:
                if ev.key == e:
                    if self.seen[e].get(e, 0) < ev.val and e != "sp":
                        self.h[e].wait_ge(ev.sem, ev.val)
                        self.seen[e][e] = ev.val
                    continue
                self._wait(e, ev)


_uid = [0]


def _rd(ap):
    try:
        if ap.dtype == F32R:
            return ap.bitcast(F32)
    except AttributeError:
        pass
    return ap


def _sb(es, nc, name, shape, dt):
    _uid[0] += 1
    return es.enter_context(nc.sbuf_tensor("sb%d_%s" % (_uid[0], name), list(shape), dt))


def host_consts():
    idx = np.arange(128)
    masks = np.stack([(idx[:, None] < idx[None, :]), (idx[:, None] <= idx[None, :]),
                      (idx[:, None] > idx[None, :]), (idx[:, None] >= idx[None, :])]).astype(np.float32)
    blockones = (idx[:, None] // 64 == idx[None, :] // 64).astype(np.float32)
    resetmask = np.ones((128, 256), np.float32)
    resetmask[:, 0] = 0.0
    resetmask[:, 128] = 0.0
    headsel = np.zeros((128, 2), np.float32)
    headsel[:64, 0] = 1.0
    headsel[64:, 1] = 1.0
    sel = np.zeros((2, 2, 128), np.float32)
    sel[0, 0] = 1
    sel[1, 1] = 1
    return {"ident": np.eye(128, dtype=np.float32), "sel": sel, "masks": masks, "blockones": blockones,
            "resetmask": resetmask, "headsel": headsel}


def pack_rwpar(inp):
    cols = [inp["rw_mu"][:, 0], inp["rw_mu"][:, 1], inp["rw_mu"][:, 2], inp["rw_k_k"], inp["rw_k_a"],
            inp["rw_r_k"].reshape(DEPTH, W), inp["rw_w0"][:, 0], inp["rw_w0"][:, 1], inp["rw_a0"][:, 0], inp["rw_a0"][:, 1],
            inp["rw_lnx_g"], inp["rw_lnx_b"]]
    a = np.stack([np.asarray(c, np.float32) for c in cols], axis=-1)
    return np.ascontiguousarray(a.reshape(DEPTH, 8, 128, NPAR))


def setup_consts(G, es):
    nc, P = G.nc, G.P
    masks_in = G.din("masks", [4, 128, 128])
    bo_in = G.din("blockones", [128, 128])
    rm_in = G.din("resetmask", [128, 256])
    hs_in = G.din("headsel", [128, 2])
    G.masks = _sb(es, nc, "masks", [128, 4, 128], F32)
    G.B_masks = Buf("masks")
    for i in range(4):
        P.dma("sp", G.masks[:, i, :], masks_in[i], G.B_masks, writes=[G.B_masks], nowait=(i > 0))
    G.blockones = _sb(es, nc, "blockones", [128, 128], F32)
    G.B_bo = Buf("blockones")
    P.dma("sp", G.blockones[:], bo_in[:, :], G.B_bo, writes=[G.B_bo])
    G.resetmask = _sb(es, nc, "resetmask", [128, 256], F32)
    G.B_rm = Buf("resetmask")
    P.dma("sp", G.resetmask[:], rm_in[:, :], G.B_rm, writes=[G.B_rm])
    G.headsel = _sb(es, nc, "headsel", [128, 2], F32)
    G.B_hs = Buf("headsel")
    P.dma("sp", G.headsel[:], hs_in[:, :], G.B_hs, writes=[G.B_hs])


NPAR = 12
R_RWR, R_RWK, R_RWV, R_RWG, R_LW, R_LA = 3072, 4096, 5120, 6144, 7168, 7232
LOGW_SCALE = -0.6065306597126334


def phase_rwkv(G, layer, do_ctx_out=True):
    nc, P = G.nc, G.P
    restT, B_restT = G.restT, G.B_restT
    rwpar = G.rwpar
    rw_w2, rw_a2 = G.rw_w2, G.rw_a2
    lnx_g, lnx_b = G.lnx_g, G.lnx_b
    masks, B_masks = G.masks, G.B_masks
    M_lt, M_le, M_gt, M_ge = (masks[:, i, :] for i in range(4))

    def OP(eng, fn, reads=(), writes=()):
        return P.op(eng, fn, reads=reads, writes=writes)

    def mm(bk, o, lhsT, rhs, reads, start=True, stop=True):
        OP("pe", lambda e: e.matmul(o, lhsT=lhsT, rhs=rhs, start=start, stop=stop), reads=reads, writes=[bk])

    def tr(bk, o, in_, reads):
        OP("pe", lambda e: e.transpose(o, _rd(in_), G.ident[:]), reads=list(reads) + [G.B_ident], writes=[bk])

    def act(o, i, func, reads, writes, **kw):
        kw = {k_: _rd(v_) for k_, v_ in kw.items()}
        OP("act", lambda e: e.activation(out=o, in_=_rd(i), func=func, **kw), reads=reads, writes=writes)

    def tt(o, a, b, op, reads, writes, eng="dve"):
        OP(eng, lambda e: e.tensor_tensor(out=o, in0=_rd(a), in1=_rd(b), op=op), reads=reads, writes=writes)

    def ts(o, a, s1, s2, op0, op1, reads, writes, eng="dve"):
        if s2 is None:
            OP(eng, lambda e: e.tensor_scalar(out=o, in0=_rd(a), scalar1=_rd(s1), scalar2=None, op0=op0), reads=reads, writes=writes)
        else:
            OP(eng, lambda e: e.tensor_scalar(out=o, in0=_rd(a), scalar1=_rd(s1), scalar2=_rd(s2), op0=op0, op1=op1), reads=reads, writes=writes)

    def stt(o, a, sc, b, op0, op1, reads, writes):
        OP("dve", lambda e: e.scalar_tensor_tensor(out=o, in0=_rd(a), scalar=_rd(sc), in1=_rd(b), op0=op0, op1=op1), reads=reads, writes=writes)

    def cp(eng, o, i, reads, writes):
        if eng == "act":
            act(o, i, AF.Copy, reads, writes)
        else:
            OP(eng, lambda e: e.tensor_copy(out=o, in_=_rd(i)), reads=reads, writes=writes)

    with contextlib.ExitStack() as ps:
        def sb(name, shape, dt=F32):
            return _sb(ps, nc, name, shape, dt)

        R32 = F32R if int(os.environ.get("RW_F32R", "0")) else F32
        lwla = sb("lwla", [128, S])
        B_lwla = Buf("lwla")
        P.dma("sp", lwla[:], restT[R_LW:R_LW + 128, :], B_lwla, reads=[B_restT], writes=[B_lwla])
        act(lwla[0:64, :], lwla[0:64, :], AF.Tanh, [B_lwla], [B_lwla])

        rT = sb("r", [128, S]); B_r = Buf("r")
        kT = sb("k", [128, S]); B_k = Buf("k")
        vT = sb("vkkn", [128, S]); B_v = Buf("vkkn")
        ytok = sb("ytok", [128, NT, 128]); B_y = [Buf("ytok%d" % i) for i in range(NT)]
        bon = sb("bon", [128, NT, 2]); B_bon = [Buf("bon%d" % i) for i in range(NT)]
        vtok = sb("vtok", [128, NT, 128], R32); B_vt = [Buf("vtok%d" % i) for i in range(NT)]
        par = sb("par", [128, NPAR + 8]); B_par = Buf("par")
        w2t = sb("w2t", [128, 2, 128]); B_w2 = Buf("w2t")
        lnxg = sb("lnxg", [128, 128]); lnxb = sb("lnxb", [128, 128]); B_lnx = Buf("lnx")
        rkblk = sb("rkblk", [128, 2], R32); B_rkblk = Buf("rkblk")
        nsum = ytok[:].rearrange("p t c -> p (t c)")
        GW = 256
        gtmp = [[(sb("gt%d_%d" % (d, i), [128, GW]), Buf("gt%d_%d" % (d, i))) for i in range(6)] for d in range(2)]
        gout = [[[(sb("go%d_%d_%d" % (d, pz, i), [128, GW], F32 if i == 0 else R32), Buf("go%d_%d_%d" % (d, pz, i))) for i in range(5)]
                 for pz in range(2)] for d in range(2)]
        ukds = [(sb("ukd%d" % d, [128, GW], R32), Buf("ukd%d" % d)) for d in range(2)]
        def ctile(name, w=128, dt=None):
            return (sb(name, [128, w], R32 if dt is None else dt), Buf(name))
        cper = [[[[{n: ctile("c%s%d%d%d%d" % (n, d, pz, c, h)) for n in ("Pm", "BmT", "RBT", "RKT")} for h in range(2)]
                  for c in range(2)] for pz in range(2)] for d in range(2)]
        cpair = [[[{n: ctile("c%s%d%d%d" % (n, d, pz, c)) for n in ("btok", "ktok")} for c in range(2)]
                  for pz in range(2)] for d in range(2)]
        ctmp = [[[{n: ctile("t%s%d%d%d" % (n, d, c, h)) for n in ("Xa", "XTa", "Xb", "XTb", "Pb")} for h in range(2)]
                 for c in range(2)] for d in range(2)]
        STs = [[ctile("ST%d%d" % (d, i), 64) for i in range(2)] for d in range(2)]
        S0dec = [ctile("S0dec%d" % d, 64, F32) for d in range(2)]
        Gt = [ctile("G%d" % d) for d in range(2)]
        SAt = [ctile("SA%d" % d) for d in range(2)]
        yst = [ctmp[0][0][1]["Xa"], ctmp[0][0][1]["XTa"]]
        ost = [(sb("rwo%d" % i, [128, 128], BF16), Buf("rwo%d" % i)) for i in range(2)]
        gts = [(gtmp[0][0][0][:, 0:128], gtmp[0][0][1]), (gtmp[0][1][0][:, 0:128], gtmp[0][1][1])]
        small = [ctile("sm%d" % i, 8, F32) for i in range(4)]

        RW_STAGE = int(os.environ.get("RW_STAGE", "99"))
        RW_SUB = int(os.environ.get("RW_SUB", "99"))
        for hp in range(int(os.environ.get("RW_PAIRS", "8"))):
            ch0 = hp * 128
            P.dma("sp", par[:, 0:NPAR], rwpar[layer, hp], B_par, writes=[B_par])
            for d in range(2):
                P.dma("sp", w2t[0:64, d, :], rw_w2[layer, d, :, ch0:ch0 + 128], B_w2, writes=[B_w2], nowait=(d > 0))
                P.dma("sp", w2t[64:128, d, :], rw_a2[layer, d, :, ch0:ch0 + 128], B_w2, writes=[B_w2], nowait=True)
            P.dma("sp", lnxg[:], lnx_g[layer, ch0:ch0 + 128].partition_broadcast(128), B_lnx, writes=[B_lnx])
            P.dma("sp", lnxb[:], lnx_b[layer, ch0:ch0 + 128].partition_broadcast(128), B_lnx, writes=[B_lnx], nowait=True)
            ts(par[:, NPAR:NPAR + 3], par[:, 0:3], -1.0, 1.0, ALU.mult, ALU.add, [B_par], [B_par])
            ts(par[:, NPAR + 3:NPAR + 6], par[:, 0:3], 0.5, None, ALU.mult, None, [B_par], [B_par])
            ts(par[:, NPAR + 6:NPAR + 7], par[:, 4:5], -1.0, 1.0, ALU.mult, ALU.add, [B_par], [B_par])
            ts(rkblk[:], G.headsel[:], par[:, 5:6], None, ALU.mult, None, [B_par, G.B_hs], [B_rkblk])
            C_KK, C_KA, C_OMKA = par[:, 3:4], par[:, 4:5], par[:, NPAR + 6:NPAR + 7]

            for zi, (zt, B_z, row) in enumerate(((rT, B_r, R_RWR), (kT, B_k, R_RWK), (vT, B_v, R_RWV))):
                P.dma("sp", zt[:], restT[row + ch0:row + ch0 + 128, :], B_z, reads=[B_restT], writes=[B_z])
                tt(nsum[:, 1:S - 1], zt[:, 0:S - 2], zt[:, 2:S], ALU.add, [B_z], B_y)
                for (dst, srcc) in ((0, 1), (255, 254), (256, 257), (S - 1, S - 2)):
                    cp("dve", nsum[:, dst:dst + 1], zt[:, srcc:srcc + 1], [B_z], B_y)
                ts(nsum[:, :], nsum[:, :], par[:, NPAR + 3 + zi:NPAR + 4 + zi], None, ALU.mult, None, B_y + [B_par], B_y)
                stt(zt[:], zt[:], par[:, NPAR + zi:NPAR + 1 + zi], nsum[:, :], ALU.mult, ALU.add, [B_z, B_par] + B_y, [B_z])
            if RW_STAGE < 2:
                continue
            for ti in range(NT):
                bk, B_bk = G.next_bank()
                tr(B_bk, bk[:, 0:128], vT[:, ti * 128:(ti + 1) * 128], [B_v])
                cp("act" if ti % 2 == 0 else "dve", vtok[:, ti, :], bk[:, 0:128], [B_bk], [B_vt[ti]])
            act(nsum[:, :], kT[:], AF.Copy, [B_k, B_par], B_y, scale=C_KK)
            act(vT[:], nsum[:, :], AF.Square, B_y + B_vt, [B_v])
            for t0 in range(0, S, 512):
                tw = min(512, S - t0)
                bk, B_bk = G.next_bank()
                mm(B_bk, bk[:, 0:tw], G.blockones[:], vT[:, t0:t0 + tw], [G.B_bo, B_v])
                act(vT[:, t0:t0 + tw], bk[:, 0:tw], AF.Sqrt, [B_bk], [B_v])
            ts(vT[:], vT[:], 1e-12, None, ALU.max, None, [B_v], [B_v])
            OP("dve", lambda e: e.reciprocal(out=vT[:], in_=vT[:]), reads=[B_v], writes=[B_v])
            tt(vT[:], vT[:], nsum[:, :], ALU.mult, [B_v] + B_y, [B_v])
            kkn, B_kkn = vT, B_v

            if RW_STAGE < 3:
                continue
            order = {0: list(range(17)), 1: [0] + list(range(16, 0, -1))}
            st_idx = [0, 0]
            ywritten = set()
            bwritten = set()
            for d in range(2):
                if RW_SUB < -1:
                    break
                ts(STs[d][0][0][:], G.ident[:, 0:64], 0.0, None, ALU.mult, None, [G.B_ident], [STs[d][0][1]], eng="pool")

            def prep_rounds(d, step):
                g = order[d][step]
                pz = step % 2
                t0 = g * GW
                (sg, B_sg), (cs, B_cs), (tmp, B_tmp), (ad, B_ad), (kd, B_kd), (en, B_en) = gtmp[d]
                ukd, B_ukd = ukds[d]
                (Ep, B_Ep), (aTt, B_aT), (bTt, B_bT), (kTt, B_kT), (rTt, B_rT) = gout[d][pz]
                rounds = []

                def r0():
                    if RW_SUB < 0:
                        return
                    bk, B_bk = G.next_bank()
                    bk2, B_bk2 = G.next_bank()
                    mm(B_bk, bk[:, 0:GW], w2t[0:64, d, :], lwla[0:64, t0:t0 + GW], [B_w2, B_lwla])
                    mm(B_bk2, bk2[:, 0:GW], w2t[64:128, d, :], lwla[64:128, t0:t0 + GW], [B_w2, B_lwla])
                    act(sg[:], bk[:, 0:GW], AF.Sigmoid, [B_bk, B_par], [B_sg], bias=par[:, 6 + d:7 + d])
                    act(ad[:], bk2[:, 0:GW], AF.Sigmoid, [B_bk2, B_par], [B_ad], bias=par[:, 8 + d:9 + d])
                    if RW_SUB < 1:
                        return
                    OP("dve", lambda e: e.tensor_tensor_scan(out=cs[:], data0=G.resetmask[:], data1=sg[:], initial=0.0,
                                                            op0=ALU.mult, op1=ALU.add), reads=[B_sg, G.B_rm], writes=[B_cs])
                    if d == 1 and RW_SUB >= 2:
                        cs3 = cs[:].rearrange("p (c k) -> p c k", k=128)
                        tot = cs3[:, :, 127:128].to_broadcast([128, 2, 128])
                        tt(tmp[:].rearrange("p (c k) -> p c k", k=128), tot, cs3, ALU.subtract, [B_cs], [B_tmp])
                        tt(cs[:], tmp[:], sg[:], ALU.add, [B_tmp, B_sg], [B_cs])
                rounds.append(r0)
                if RW_SUB < 3:
                    return rounds

                def r1():
                    act(Ep[:], cs[:], AF.Exp, [B_cs], [B_Ep], scale=LOGW_SCALE)
                    act(en[:], cs[:], AF.Exp, [B_cs], [B_en], scale=-LOGW_SCALE)
                    tt(tmp[:], cs[:], sg[:], ALU.subtract, [B_cs, B_sg], [B_tmp])
                    act(tmp[:], tmp[:], AF.Exp, [B_tmp], [B_tmp], scale=LOGW_SCALE)
                    ts(kd[:], ad[:], C_KA, C_OMKA, ALU.mult, ALU.add, [B_ad, B_par], [B_kd])
                    tt(kd[:], kd[:], kT[:, t0:t0 + GW], ALU.mult, [B_kd, B_k], [B_kd])
                rounds.append(r1)
                if RW_SUB < 4:
                    return rounds

                def r2():
                    tt(ukd[:], rT[:, t0:t0 + GW], kd[:], ALU.mult, [B_r, B_kd], [B_ukd], eng="pool")
                    stt(aTt[:], kkn[:, t0:t0 + GW], -1.0, tmp[:], ALU.mult, ALU.mult, [B_kkn, B_tmp], [B_aT])
                    tt(bTt[:], kkn[:, t0:t0 + GW], ad[:], ALU.mult, [B_kkn, B_ad], [B_bT])
                    tt(bTt[:], bTt[:], en[:], ALU.mult, [B_bT, B_en], [B_bT])
                    tt(kTt[:], kd[:], en[:], ALU.mult, [B_kd, B_en], [B_kT])
                    tt(rTt[:], rT[:, t0:t0 + GW], Ep[:], ALU.mult, [B_r, B_Ep], [B_rT], eng="pool")
                rounds.append(r2)

                strict_st, incl_st = (M_lt, M_le) if d == 0 else (M_gt, M_ge)
                strict_ts = M_gt if d == 0 else M_lt

                def r3():
                    for c in range(2):
                        cs_ = slice(c * 128, (c + 1) * 128)
                        bkAs = [G.next_bank(), G.next_bank()]
                        for h in range(2):
                            hs = slice(h * 64, (h + 1) * 64)
                            bkA, B_A = bkAs[h]
                            mm(B_A, bkA[:, 0:128], bTt[hs, cs_], aTt[hs, cs_], [B_bT, B_aT])
                        for h in range(2):
                            T = ctmp[d][c][h]
                            bkA, B_A = bkAs[h]
                            tt(T["Xa"][0][:], bkA[:, 0:128], strict_st, ALU.mult,
                               [B_A, B_masks], [T["Xa"][1]])
                            tt(T["Pb"][0][:], T["Xa"][0][:], G.ident[:], ALU.add, [T["Xa"][1], G.B_ident], [T["Pb"][1]], eng="pool")
                        for h in range(2):
                            hs = slice(h * 64, (h + 1) * 64)
                            bkB, B_B = G.next_bank()
                            Cp = cper[d][pz][c][h]
                            mm(B_B, bkB[:, 0:128], kTt[hs, cs_], aTt[hs, cs_], [B_kT, B_aT])
                            mm(B_B, bkB[:, 128:256], bTt[hs, cs_], rTt[hs, cs_], [B_bT, B_rT])
                            mm(B_B, bkB[:, 256:384], kTt[hs, cs_], rTt[hs, cs_], [B_kT, B_rT])
                            tt(Cp["BmT"][0][:], bkB[:, 0:128], strict_st, ALU.mult, [B_B, B_masks], [Cp["BmT"][1]])
                            tt(Cp["RBT"][0][:], bkB[:, 128:256], incl_st, ALU.mult, [B_B, B_masks], [Cp["RBT"][1]])
                            tt(Cp["RKT"][0][:], bkB[:, 256:384], incl_st, ALU.mult, [B_B, B_masks], [Cp["RKT"][1]])
                        bkC, B_C = G.next_bank()
                        tr(B_C, bkC[:, 0:128], bTt[:, cs_], [B_bT])
                        tr(B_C, bkC[:, 128:256], kTt[:, cs_], [B_kT])
                        for h in range(2):
                            T = ctmp[d][c][h]
                            tr(B_C, bkC[:, (2 + h) * 128:(3 + h) * 128], T["Xa"][0][:], [T["Xa"][1]])
                        cp("act", cpair[d][pz][c]["btok"][0][:], bkC[:, 0:128], [B_C], [cpair[d][pz][c]["btok"][1]])
                        cp("act", cpair[d][pz][c]["ktok"][0][:], bkC[:, 128:256], [B_C], [cpair[d][pz][c]["ktok"][1]])
                        for h in range(2):
                            T = ctmp[d][c][h]
                            cp("act", T["XTa"][0][:], bkC[:, (2 + h) * 128:(3 + h) * 128], [B_C], [T["XTa"][1]])
                if RW_STAGE < 4:
                    return rounds
                rounds.append(r3)

                def r3b():
                    bk, B_bk = G.next_bank()
                    for c in range(2):
                        mm(B_bk, bk[:, 2 * c:2 * c + 2], ukd[:, c * 128:(c + 1) * 128], rkblk[:], [B_ukd, B_rkblk])
                    for c in range(2):
                        ti = g * 2 + c
                        if ti not in bwritten:
                            bwritten.add(ti)
                            cp("act", bon[:, ti, :], bk[:, 2 * c:2 * c + 2], [B_bk], [B_bon[ti]])
                        else:
                            tt(bon[:, ti, :], bk[:, 2 * c:2 * c + 2], bon[:, ti, :], ALU.add, [B_bk, B_bon[ti]], [B_bon[ti]])
                rounds.append(r3b)

                def make_level(lvl):
                    def rl():
                        src, dst = ("a", "b") if lvl % 2 == 1 else ("b", "a")
                        last = (lvl == 6)
                        banks_ = []
                        for c in range(2):
                            bk, B_bk = G.next_bank()
                            banks_.append((bk, B_bk))
                            for h in range(2):
                                T = ctmp[d][c][h]
                                X, B_X = T["X" + src]
                                XT, B_XT = T["XT" + src]
                                if not last:
                                    mm(B_bk, bk[:, h * 128:(h + 1) * 128], XT[:], X[:], [B_X, B_XT])
                                else:
                                    mm(B_bk, bk[:, h * 128:(h + 1) * 128], X[:], XT[:], [B_X, B_XT])
                        for c in range(2):
                            bk, B_bk = banks_[c]
                            for h in range(2):
                                T = ctmp[d][c][h]
                                nm = ("X" if not last else "XT") + dst
                                cp("act" if c == 0 else "dve", T[nm][0][:], bk[:, h * 128:(h + 1) * 128], [B_bk], [T[nm][1]])

                    def rl1():
                        src, dst = ("a", "b") if lvl % 2 == 1 else ("b", "a")
                        if lvl == 6:
                            return
                        for c in range(2):
                            bk, B_bk = G.next_bank()
                            for h in range(2):
                                T = ctmp[d][c][h]
                                tr(B_bk, bk[:, h * 128:(h + 1) * 128], T["X" + dst][0][:], [T["X" + dst][1]])
                            for h in range(2):
                                T = ctmp[d][c][h]
                                cp("dve" if c == 0 else "act", T["XT" + dst][0][:], bk[:, h * 128:(h + 1) * 128], [B_bk], [T["XT" + dst][1]])

                    def rl2():
                        src, dst = ("a", "b") if lvl % 2 == 1 else ("b", "a")
                        if RW_SUB < 11:
                            return
                        for c in range(2):
                            bk, B_bk = G.next_bank()
                            for h in range(2):
                                T = ctmp[d][c][h]
                                Cp = cper[d][pz][c][h]
                                Pold, B_Pold = T["Pb"] if lvl % 2 == 1 else Cp["Pm"]
                                mm(B_bk, bk[:, h * 128:(h + 1) * 128], T["XT" + dst][0][:], Pold[:], [T["XT" + dst][1], B_Pold])
                            for h in range(2):
                                T = ctmp[d][c][h]
                                Cp = cper[d][pz][c][h]
                                Pold, B_Pold = T["Pb"] if lvl % 2 == 1 else Cp["Pm"]
                                Pnew, B_Pnew = Cp["Pm"] if lvl % 2 == 1 else T["Pb"]
                                tt(Pnew[:], bk[:, h * 128:(h + 1) * 128], Pold[:], ALU.add, [B_bk, B_Pold], [B_Pnew])
                    return (rl, rl1, rl2) if lvl < 6 else (rl, rl2)
                if RW_STAGE < 5:
                    return rounds
                for lvl in range(1, 1 + int(os.environ.get("RW_LVL", "6"))):
                    rounds.extend(make_level(lvl))
                def rfin():
                    for c in range(2):
                        for h in range(2):
                            T = ctmp[d][c][h]
                            Cp = cper[d][pz][c][h]
                            cp("pool", Cp["Pm"][0][:], T["Pb"][0][:], [T["Pb"][1]], [Cp["Pm"][1]])
                if RW_SUB >= 12:
                    rounds.append(rfin)
                return rounds

            def chain_rounds(d, step):
                g = order[d][step]
                pz = step % 2
                (Ep, B_Ep), (aTt, B_aT), (bTt, B_bT), (kTt, B_kT), (rTt, B_rT) = gout[d][pz]
                rounds = []
                corder = (0, 1) if d == 0 else (1, 0)
                for c in corder:
                    ti = g * 2 + c
                    cs_ = slice(c * 128, (c + 1) * 128)
                    ecol = c * 128 + (127 if d == 0 else 0)
                    eLC = Ep[:, ecol:ecol + 1]

                    def b1(c=c, ti=ti, cs_=cs_, eLC=eLC):
                        ST, B_ST = STs[d][st_idx[d] % 2]
                        bk, B_bk = G.next_bank()
                        for h in range(2):
                            hs = slice(h * 64, (h + 1) * 64)
                            Cp = cper[d][pz][c][h]
                            o = bk[:, h * 64:(h + 1) * 64]
                            mm(B_bk, o, Cp["BmT"][0][:], vtok[:, ti, h * 64:(h + 1) * 64], [Cp["BmT"][1], B_vt[ti]], start=True, stop=False)
                            mm(B_bk, o, aTt[hs, cs_], ST[hs, :], [B_aT, B_ST], start=False, stop=True)
                        cp("act", Gt[d][0][:], bk[:, 0:128], [B_bk], [Gt[d][1]])
                        ts(S0dec[d][0][:], ST[:], eLC, None, ALU.mult, None, [B_ST, B_Ep], [S0dec[d][1]], eng="pool")
                    rounds.append(b1)

                    def b2(c=c):
                        bk, B_bk = G.next_bank()
                        for h in range(2):
                            Cp = cper[d][pz][c][h]
                            mm(B_bk, bk[:, h * 64:(h + 1) * 64], Cp["Pm"][0][:], Gt[d][0][:, h * 64:(h + 1) * 64], [Cp["Pm"][1], Gt[d][1]])
                        cp("dve", SAt[d][0][:], bk[:, 0:128], [B_bk], [SAt[d][1]])
                    rounds.append(b2)

                    def b3(c=c, ti=ti, cs_=cs_, eLC=eLC):
                        ST, B_ST = STs[d][st_idx[d] % 2]
                        STn, B_STn = STs[d][(st_idx[d] + 1) % 2]
                        st_idx[d] += 1
                        SA, B_SA = SAt[d]
                        bk, B_bk = G.next_bank()
                        for h in range(2):
                            hs = slice(h * 64, (h + 1) * 64)
                            Cp = cper[d][pz][c][h]
                            o = bk[:, h * 64:(h + 1) * 64]
                            mm(B_bk, o, rTt[hs, cs_], ST[hs, :], [B_rT, B_ST], start=True, stop=False)
                            mm(B_bk, o, Cp["RBT"][0][:], SA[:, h * 64:(h + 1) * 64], [Cp["RBT"][1], B_SA], start=False, stop=False)
                            mm(B_bk, o, Cp["RKT"][0][:], vtok[:, ti, h * 64:(h + 1) * 64], [Cp["RKT"][1], B_vt[ti]], start=False, stop=True)
                        bk2, B_bk2 = G.next_bank()
                        Cq = cpair[d][pz][c]
                        mm(B_bk2, bk2[:, 0:128], Cq["ktok"][0][:], vtok[:, ti, :], [Cq["ktok"][1], B_vt[ti]], start=True, stop=False)
                        mm(B_bk2, bk2[:, 0:128], Cq["btok"][0][:], SA[:], [Cq["btok"][1], B_SA], start=False, stop=True)
                        if ti not in ywritten:
                            ywritten.add(ti)
                            cp("act", ytok[:, ti, :], bk[:, 0:128], [B_bk], [B_y[ti]])
                        else:
                            tt(ytok[:, ti, :], bk[:, 0:128], ytok[:, ti, :], ALU.add, [B_bk, B_y[ti]], [B_y[ti]])
                        for h in range(2):
                            hs = slice(h * 64, (h + 1) * 64)
                            stt(STn[hs, :], bk2[hs, h * 64:(h + 1) * 64], eLC[hs, :], S0dec[d][0][hs, :], ALU.mult, ALU.add,
                                [B_bk2, B_Ep, S0dec[d][1]], [B_STn])
                    rounds.append(b3)
                return rounds

            def interleave(lists):
                items = []
                for li, l in enumerate(lists):
                    for i, fn in enumerate(l):
                        items.append(((i + 0.5) / len(l), li, i, fn))
                items.sort(key=lambda t: (t[0], t[1]))
                for _, _, _, fn in items:
                    fn()

            nsteps = 17
            for step in range(nsteps + 1):
                lists = []
                for d in range(2):
                    if step < nsteps:
                        lists.append(prep_rounds(d, step))
                    if step >= 1 and RW_STAGE >= 6:
                        lists.append(chain_rounds(d, step - 1))
                interleave(lists)

            if RW_STAGE < 7:
                continue
            for ti in range(NT):
                t0 = ti * 128
                sm, B_sm = small[ti % 4]
                y, B_yy = ytok[:, ti, :], B_y[ti]
                yt, B_yt = yst[ti % 2]
                y3 = y.rearrange("p (h c) -> p h c", h=2)
                cp("act", sm[:, 6:8], bon[:, ti, :], [B_bon[ti]], [B_sm])
                OP("dve", lambda e, y3=y3, sm=sm: e.tensor_reduce(out=sm[:, 0:2], in_=y3, axis=AX.X, op=ALU.add), reads=[B_yy], writes=[B_sm])
                ts(sm[:, 0:2], sm[:, 0:2], 1.0 / 64, None, ALU.mult, None, [B_sm], [B_sm])
                for h in range(2):
                    ts(yt[:, h * 64:(h + 1) * 64], y[:, h * 64:(h + 1) * 64], sm[:, h:h + 1], None, ALU.subtract, None, [B_yy, B_sm], [B_yt])
                sq, B_sq = gts[ti % 2]
                act(sq[:], yt[:], AF.Square, [B_yt], [B_sq])
                OP("dve", lambda e, sq=sq, sm=sm: e.tensor_reduce(out=sm[:, 2:4], in_=sq[:].rearrange("p (h c) -> p h c", h=2), axis=AX.X,
                                                                op=ALU.add), reads=[B_sq], writes=[B_sm])
                ts(sm[:, 2:4], sm[:, 2:4], 1.0 / 64, 64e-5, ALU.mult, ALU.add, [B_sm], [B_sm])
                act(sm[:, 2:4], sm[:, 2:4], AF.Sqrt, [B_sm], [B_sm])
                OP("dve", lambda e, sm=sm: e.reciprocal(out=sm[:, 4:6], in_=sm[:, 2:4]), reads=[B_sm], writes=[B_sm])
                for h in range(2):
                    hsl = slice(h * 64, (h + 1) * 64)
                    stt(yt[:, hsl], yt[:, hsl], sm[:, 4 + h:5 + h], lnxg[:, hsl], ALU.mult, ALU.mult, [B_yt, B_sm, B_lnx], [B_yt])
                tt(yt[:], yt[:], lnxb[:], ALU.add, [B_yt, B_lnx], [B_yt])
                for h in range(2):
                    hsl = slice(h * 64, (h + 1) * 64)
                    stt(yt[:, hsl], vtok[:, ti, hsl], sm[:, 6 + h:7 + h], yt[:, hsl], ALU.mult, ALU.add, [B_vt[ti], B_sm, B_yt], [B_yt])
                bk, B_bk = G.next_bank()
                tr(B_bk, bk[:, 0:128], yt[:], [B_yt])
                gt, B_gt = gts[ti % 2]
                P.dma("sp", gt[:], restT[R_RWG + ch0:R_RWG + ch0 + 128, t0:t0 + 128], B_gt, reads=[B_restT], writes=[B_gt])
                act(gt[:], gt[:], AF.Silu, [B_gt], [B_gt])
                ot, B_ot = ost[ti % 2]
                tt(ot[:], bk[:, 0:128], gt[:], ALU.mult, [B_bk, B_gt], [B_ot])
                P.dma("pool", G.brT[2 * W + ch0:2 * W + ch0 + 128, t0:t0 + 128], ot[:], G.B_brT, reads=[B_ot])
        P.barrier()


def mk_helpers(G):
    P = G.P

    class H:
        pass
    H_ = H()

    def OP(eng, fn, reads=(), writes=()):
        return P.op(eng, fn, reads=reads, writes=writes)

    def mm(bk, o, lhsT, rhs, reads, start=True, stop=True):
        OP("pe", lambda e: e.matmul(o, lhsT=lhsT, rhs=rhs, start=start, stop=stop), reads=reads, writes=[bk])

    def tr(bk, o, in_, reads):
        OP("pe", lambda e: e.transpose(o, in_, G.ident[:]), reads=list(reads) + [G.B_ident], writes=[bk])

    def act(o, i, func, reads, writes, **kw):
        OP("act", lambda e: e.activation(out=o, in_=i, func=func, **kw), reads=reads, writes=writes)

    def tt(o, a, b, op, reads, writes, eng="dve"):
        OP(eng, lambda e: e.tensor_tensor(out=o, in0=a, in1=b, op=op), reads=reads, writes=writes)

    def ts(o, a, s1, s2, op0, op1, reads, writes, eng="dve"):
        if s2 is None:
            OP(eng, lambda e: e.tensor_scalar(out=o, in0=a, scalar1=s1, scalar2=None, op0=op0), reads=reads, writes=writes)
        else:
            OP(eng, lambda e: e.tensor_scalar(out=o, in0=a, scalar1=s1, scalar2=s2, op0=op0, op1=op1), reads=reads, writes=writes)

    def stt(o, a, sc, b, op0, op1, reads, writes):
        OP("dve", lambda e: e.scalar_tensor_tensor(out=o, in0=a, scalar=sc, in1=b, op0=op0, op1=op1), reads=reads, writes=writes)

    def cp(eng, o, i, reads, writes):
        if eng == "act":
            act(o, i, AF.Copy, reads, writes)
        else:
            OP(eng, lambda e: e.tensor_copy(out=o, in_=i), reads=reads, writes=writes)

    def memset(eng, o, val, writes):
        OP(eng, lambda e: e.memset(o, val), writes=writes)
    return OP, mm, tr, act, tt, ts, stt, cp, memset


R_NAG, R_PU, R_PG, R_MERGE = 0, 1024, 2048, 7296
PADW = 8 + 256 + 16 + 4096 + 16
OFF_C, OFF_L = 8, 8 + 256 + 16


def host_pool_invcnt():
    out = np.zeros((4, S), np.float32)
    for g, win in enumerate((2, 4, 8, 16)):
        for (o, T) in ((0, NCTX), (NCTX, SEQ)):
            t = np.arange(T)
            lo = np.maximum(t - win // 2, 0)
            hi = np.minimum(t + win // 2, T)
            out[g, o:o + T] = 1.0 / (hi - lo)
    return out


def phase_pool(G, layer):
    nc, P = G.nc, G.P
    OP, mm, tr, act, tt, ts, stt, cp, memset = mk_helpers(G)
    restT, B_restT = G.restT, G.B_restT
    with contextlib.ExitStack() as ps:
        def sb(name, shape, dt=F32):
            return _sb(ps, nc, name, shape, dt)
        A = sb("pA", [128, PADW]); B_A = Buf("pA")
        Bt = sb("pB", [128, PADW]); B_B = Buf("pB")
        Ct = sb("pC", [128, PADW]); B_C = Buf("pC")
        inv = sb("pinv", [128, S]); B_inv = Buf("pinv")
        diff = [(sb("pdiff%d" % i, [128, S], BF16), Buf("pdiff%d" % i)) for i in range(2)]
        gate = sb("pgate", [128, S]); B_gate = Buf("pgate")
        pw = sb("ppw", [128, 2, 256], BF16); B_pw = Buf("ppw")
        psc = sb("ppsc", [128, 2]); B_psc = Buf("ppsc")
        stg = [(sb("pstg%d" % i, [128, 512], BF16), Buf("pstg%d" % i)) for i in range(2)]
        for t_, b_ in ((A, B_A), (Bt, B_B), (Ct, B_C)):
            memset("pool", t_[:], 0.0, [b_])

        def zero_gaps(t_, b_):
            memset("pool", t_[:, 0:OFF_C], 0.0, [b_])
            memset("pool", t_[:, OFF_C + 256:OFF_L], 0.0, [b_])
            memset("pool", t_[:, OFF_L + 4096:PADW], 0.0, [b_])

        R0, R1 = 4, PADW - 8
        for g in range(4):
            win = (2, 4, 8, 16)[g]
            P.dma("sp", inv[:], G.pool_invcnt[g].partition_broadcast(128), None, writes=[B_inv])
            P.dma("pool", pw[:], G.pool_w[layer, g].rearrange("(k p) d -> p k d", p=128), None, writes=[B_pw])
            P.dma("sp", psc[:], G.pool_scale[layer, g * 256:(g + 1) * 256].rearrange("(k p) -> p k", p=128), None, writes=[B_psc])
            for cbi in range(2):
                cb = g * 2 + cbi
                row = R_PU + cb * 128
                P.dma("sp", A[:, OFF_C:OFF_C + 256], restT[row:row + 128, 0:256], None, reads=[B_restT], writes=[B_A])
                P.dma("sp", A[:, OFF_L:OFF_L + 4096], restT[row:row + 128, 256:S], None, reads=[B_restT], writes=[B_A], nowait=True)
                tt(Bt[:, R0:R1], A[:, R0 - 1:R1 - 1], A[:, R0:R1], ALU.add, [B_A], [B_B])
                cur, B_cur = Bt, B_B
                oth, B_oth = Ct, B_C
                sh = 1
                w_ = 2
                while w_ < win:
                    tt(oth[:, R0:R1], cur[:, R0 - sh:R1 - sh], cur[:, R0 + sh:R1 + sh], ALU.add, [B_cur], [B_oth])
                    cur, B_cur, oth, B_oth = oth, B_oth, cur, B_cur
                    sh *= 2
                    w_ *= 2
                dt_, B_d = diff[cbi]
                for (po, so, T) in ((OFF_C, 0, 256), (OFF_L, 256, 4096)):
                    tt(cur[:, po:po + T], cur[:, po:po + T], inv[:, so:so + T], ALU.mult, [B_cur, B_inv], [B_cur])
                    tt(dt_[:, so:so + T], cur[:, po:po + T], A[:, po:po + T], ALU.subtract, [B_cur, B_A], [B_d])
            for dch in range(2):
                row = R_PG + g * 256 + dch * 128
                P.dma("sp", gate[:], restT[row:row + 128, :], None, reads=[B_restT], writes=[B_gate])
                act(gate[:], gate[:], AF.Silu, [B_gate], [B_gate])
                for bi, t0 in enumerate(range(0, S, 512)):
                    tw = min(512, S - t0)
                    bk, B_bk = G.next_bank()
                    for k in range(2):
                        mm(B_bk, bk[:, 0:tw], pw[:, k, dch * 128:(dch + 1) * 128], diff[k][0][:, t0:t0 + tw], [B_pw, diff[k][1]],
                           start=(k == 0), stop=(k == 1))
                    st_, B_st = stg[bi % 2]
                    stt(st_[:, 0:tw], bk[:, 0:tw], psc[:, dch:dch + 1], gate[:, t0:t0 + tw], ALU.mult, ALU.mult,
                        [B_bk, B_psc, B_gate], [B_st])
                    orow = W + g * 256 + dch * 128
                    P.dma("pool", G.brT[orow:orow + 128, t0:t0 + tw], st_[:, 0:tw], None, reads=[B_st], writes=[])
        P.barrier()


def host_na_table(na_rpb):
    L = na_rpb.shape[0]
    NEG = np.float32(-30000.0)
    col = np.arange(64)
    cs = np.clip(col - 8, 0, 48)
    cmask = (col[:, None] >= cs[None, :]) & (col[:, None] < cs[None, :] + 16)
    coff = np.clip(col[:, None] - col[None, :] + 15, 0, 30)
    tab = np.full((L, 16, 2, 64, 18, 64), NEG, np.float32)
    for par in range(2):
        for j in range(16):
            ro = j - 1 + par
            if 0 <= ro <= 14:
                g = na_rpb[:, :, ro][:, :, coff]
                tab[:, :, par, :, j, :] = np.where(cmask[None, None], g, NEG)
    tab[:, :, 1, :, 16, :] = np.where(cmask[None, None], na_rpb[:, :, 3][:, :, coff], NEG)
    tab[:, :, 0, :, 17, :] = np.where(cmask[None, None], na_rpb[:, :, 10][:, :, coff], NEG)
    return np.ascontiguousarray(tab.reshape(L, 16, 128, 18, 64))


def phase_na(G, layer, do_ctx=True):
    nc, P = G.nc, G.P
    OP, mm, tr, act, tt, ts, stt, cp, memset = mk_helpers(G)
    restT, B_restT = G.restT, G.B_restT
    qkT, B_qkT, vaug, B_vaug = G.qkT, G.B_qkT, G.vaug, G.B_vaug
    with contextlib.ExitStack() as ps:
        def sb(name, shape, dt=F32):
            return _sb(ps, nc, name, shape, dt)
        V = sb("naV", [128, NT, 16 * 65], BF16); B_V = Buf("naV")
        vsrc = vaug.rearrange("(n p) h e -> p n (h e)", p=128)
        for i in range(0, NT, 6):
            j = min(NT, i + 6)
            P.dma("sp", V[:, i:j, :], vsrc[:, i:j, :], None, reads=[B_vaug], writes=[B_V], nowait=(i > 0))
        qs = [(sb("naq%d" % i, [64, S], BF16), Buf("naq%d" % i)) for i in range(2)]
        ks = [(sb("nak%d" % i, [64, S], BF16), Buf("nak%d" % i)) for i in range(2)]
        tabs = [(sb("natab%d" % i, [128, 18, 64]), Buf("natab%d" % i)) for i in range(2)]
        ytok = sb("naytok", [128, NT, 64]); B_yt = [Buf("nay%d" % i) for i in range(NT)]
        gate = sb("nagate", [64, S]); B_gate = Buf("nagate")
        sc = [(sb("nasc%d" % i, [128, 5, 64]), Buf("nasc%d" % i)) for i in range(2)]
        PT = [[(sb("naPT%d%d" % (p_, i), [128, 7, 128], BF16), Buf("naPT%d%d" % (p_, i))) for i in range(2)] for p_ in range(2)]
        for p_ in range(2):
            for i in range(2):
                memset("pool", PT[p_][i][0][:], 0.0, [PT[p_][i][1]])
        rden = [(sb("narden%d" % i, [128, 1]), Buf("narden%d" % i)) for i in range(2)]
        ost = [(sb("naost%d" % i, [64, 512], BF16), Buf("naost%d" % i)) for i in range(2)]
        ptc = [(sb("naptc%d" % i, [128, 2, 128], BF16), Buf("naptc%d" % i)) for i in range(2)]

        for h in range(16):
            q, B_q = qs[h % 2]
            k, B_k = ks[h % 2]
            tab, B_tab = tabs[h % 2]
            P.dma("sp", q[:], qkT[h * 64:(h + 1) * 64, :], None, reads=[B_qkT], writes=[B_q])
            P.dma("sp", k[:], qkT[W + h * 64:W + (h + 1) * 64, :], None, reads=[B_qkT], writes=[B_k])
            P.dma("sp", tab[:], G.na_tab[layer, h], None, writes=[B_tab])
            P.dma("sp", gate[:], restT[R_NAG + h * 64:R_NAG + (h + 1) * 64, :], None, reads=[B_restT], writes=[B_gate])
            act(gate[:], gate[:], AF.Silu, [B_gate], [B_gate])
            vh = slice(h * 65, (h + 1) * 65)
            if do_ctx:
                for qt in range(2):
                    bk, B_bk = G.next_bank()
                    for kt in range(2):
                        mm(B_bk, bk[:, kt * 128:(kt + 1) * 128], k[:, kt * 128:(kt + 1) * 128], q[:, qt * 128:(qt + 1) * 128], [B_k, B_q])
                    pc, B_pc = ptc[qt]
                    act(pc[:].rearrange("p a b -> p (a b)"), bk[:, 0:256], AF.Exp, [B_bk], [B_pc], scale=0.125)
                    bk2, B_bk2 = G.next_bank()
                    for kt in range(2):
                        mm(B_bk2, bk2[:, 0:65], pc[:, kt, :], V[:, kt, vh], [B_pc, B_V], start=(kt == 0), stop=(kt == 1))
                    rd, B_rd = rden[qt]
                    OP("dve", lambda e, rd=rd, bk2=bk2: e.reciprocal(out=rd[:], in_=bk2[:, 64:65]), reads=[B_bk2], writes=[B_rd])
                    ts(ytok[:, qt, :], bk2[:, 0:64], rd[:, 0:1], None, ALU.mult, None, [B_bk2, B_rd], [B_yt[qt]])
            def emit_scores(rp):
                plan = []
                for par in range(2):
                    r = 2 * rp + par
                    r0 = min(max(r - 4, 0), 56)
                    t_lo = r0 // 2
                    t_hi = (r0 + 7) // 2
                    ntl = t_hi - t_lo + 1
                    bk, B_bk = G.next_bank()
                    qsl = q[:, 256 + r * 64:256 + (r + 1) * 64]
                    for m in range(ntl):
                        tk = 256 + (t_lo + m) * 128
                        mm(B_bk, bk[:, m * 64:(m + 1) * 64], k[:, tk:tk + 128], qsl, [B_k, B_q])
                    for m in range(2):
                        mm(B_bk, bk[:, (5 + m) * 64:(6 + m) * 64], k[:, m * 128:(m + 1) * 128], qsl, [B_k, B_q])
                    s_, B_s = sc[par]
                    j0 = 2 * t_lo - r + 8
                    bk3 = bk[:, 0:ntl * 64].rearrange("p (a b) -> p a b", b=64)
                    if ntl == 4:
                        stt(s_[:, 0:4, :], bk3, 0.125, tab[:, j0:j0 + 7:2, :], ALU.mult, ALU.add, [B_bk, B_tab], [B_s])
                    else:
                        stt(s_[:, 0:1, :], bk3[:, 0:1, :], 0.125, tab[:, 16:17, :], ALU.mult, ALU.add, [B_bk, B_tab], [B_s])
                        stt(s_[:, 1:4, :], bk3[:, 1:4, :], 0.125, tab[:, j0 + 2:j0 + 7:2, :], ALU.mult, ALU.add, [B_bk, B_tab], [B_s])
                        stt(s_[:, 4:5, :], bk3[:, 4:5, :], 0.125, tab[:, 17:18, :], ALU.mult, ALU.add, [B_bk, B_tab], [B_s])
                    pt, B_pt = PT[par][rp % 2]
                    qo = par * 64
                    act(pt[:, 0:ntl, qo:qo + 64], s_[:, 0:ntl, :], AF.Exp, [B_s], [B_pt])
                    act(pt[:, 5:7, qo:qo + 64], bk[:, 320:448].rearrange("p (a b) -> p a b", b=64), AF.Exp, [B_bk], [B_pt], scale=0.125)
                    plan.append((pt, B_pt, t_lo, ntl))
                return plan

            def emit_pv(rp, plan):
                bk2, B_bk2 = G.next_bank()
                nmm = 0
                tot = sum(p_[3] + 2 for p_ in plan)
                for (pt, B_pt, t_lo, ntl) in plan:
                    for m in range(ntl):
                        mm(B_bk2, bk2[:, 0:65], pt[:, m, :], V[:, 2 + t_lo + m, vh], [B_pt, B_V], start=(nmm == 0), stop=(nmm == tot - 1))
                        nmm += 1
                    for m in range(2):
                        mm(B_bk2, bk2[:, 0:65], pt[:, 5 + m, :], V[:, m, vh], [B_pt, B_V], start=(nmm == 0), stop=(nmm == tot - 1))
                        nmm += 1
                rd, B_rd = rden[rp % 2]
                OP("dve", lambda e, rd=rd, bk2=bk2: e.reciprocal(out=rd[:], in_=bk2[:, 64:65]), reads=[B_bk2], writes=[B_rd])
                ts(ytok[:, 2 + rp, :], bk2[:, 0:64], rd[:, 0:1], None, ALU.mult, None, [B_bk2, B_rd], [B_yt[2 + rp]])

            plans = {0: emit_scores(0)}
            for rp in range(32):
                if rp + 1 < 32:
                    plans[rp + 1] = emit_scores(rp + 1)
                emit_pv(rp, plans.pop(rp))
            t_first = 0 if do_ctx else 2
            for ti0 in range(t_first, NT, 4):
                n = min(4, NT - ti0)
                bk, B_bk = G.next_bank()
                for i in range(n):
                    tr(B_bk, bk[0:64, i * 128:(i + 1) * 128], ytok[:, ti0 + i, :], [B_yt[ti0 + i]])
                o_, B_o = ost[(ti0 // 4) % 2]
                tt(o_[:, 0:n * 128], bk[0:64, 0:n * 128], gate[:, ti0 * 128:(ti0 + n) * 128], ALU.mult, [B_bk, B_gate], [B_o])
                P.dma("pool", G.brT[h * 64:(h + 1) * 64, ti0 * 128:(ti0 + n) * 128], o_[:, 0:n * 128], None, reads=[B_o], writes=[])
        P.barrier()


def phase_out(G, layer, last):
    nc, P = G.nc, G.P
    OP, mm, tr, act, tt, ts, stt, cp, memset = mk_helpers(G)
    restT, B_restT = G.restT, G.B_restT
    mT, B_mT = G.mT, G.B_mT
    t_first = 2 if last else 0
    with contextlib.ExitStack() as ps:
        def sb(name, shape, dt=F32):
            return _sb(ps, nc, name, shape, dt)
        wbr = sb("wbr", [128, 24, D], BF16); B_wbrs = [Buf("wbr%d" % i) for i in range(4)]
        wsrc = G.w_branch[layer].rearrange("b (k p) d -> p (b k) d", p=128)
        for cb4 in range(4):
            csl = slice(cb4 * 512, (cb4 + 1) * 512)
            P.dma("pool", wbr[:, 0:12, csl], wsrc[:, 0:12, csl], None, writes=[B_wbrs[cb4]])
            P.dma("pool", wbr[:, 12:24, csl], wsrc[:, 12:24, csl], None, writes=[B_wbrs[cb4]], nowait=True)
        bts = [(sb("bT%d" % i, [128, 24, 512], BF16), Buf("bT%d" % i)) for i in range(2)]
        lgs = [(sb("lg%d" % i, [128, 512]), Buf("lg%d" % i)) for i in range(12)]
        macc = [(sb("macc%d" % i, [128, 512]), Buf("macc%d" % i)) for i in range(2)]
        mst = [(sb("mst%d" % i, [128, 512], BF16), Buf("mst%d" % i)) for i in range(2)]
        bsrc = G.brT.rearrange("(c p) t -> p c t", p=128)
        for bi, t0 in enumerate(range(t_first * 128, S, 512)):
            tw = min(512, S - t0)
            bt, B_bt = bts[bi % 2]
            P.dma("sp", bt[:, 0:12, 0:tw], bsrc[:, 0:12, t0:t0 + tw], None, reads=[G.B_brT], writes=[B_bt])
            P.dma("sp", bt[:, 12:24, 0:tw], bsrc[:, 12:24, t0:t0 + tw], None, reads=[G.B_brT], writes=[B_bt], nowait=True)
            for fo in range(16):
                ma, B_ma = macc[fo % 2]
                for kb in range(3):
                    lg, B_lg = lgs[(fo * 3 + kb) % 12]
                    row = R_MERGE + kb * D + fo * 128
                    P.dma("sp", lg[:, 0:tw], restT[row:row + 128, t0:t0 + tw], None, reads=[B_restT], writes=[B_lg])
                    act(lg[:, 0:tw], lg[:, 0:tw], AF.Sigmoid, [B_lg], [B_lg])
                    bk, B_bk = G.next_bank()
                    for kc in range(8):
                        mm(B_bk, bk[:, 0:tw], wbr[:, kb * 8 + kc, fo * 128:(fo + 1) * 128], bt[:, kb * 8 + kc, 0:tw], [B_wbrs[fo // 4], B_bt],
                           start=(kc == 0), stop=(kc == 7))
                    if kb == 0:
                        tt(ma[:, 0:tw], bk[:, 0:tw], lg[:, 0:tw], ALU.mult, [B_bk, B_lg], [B_ma])
                    else:
                        tt(lg[:, 0:tw], bk[:, 0:tw], lg[:, 0:tw], ALU.mult, [B_bk, B_lg], [B_lg])
                        if kb == 1:
                            tt(ma[:, 0:tw], ma[:, 0:tw], lg[:, 0:tw], ALU.add, [B_ma, B_lg], [B_ma], eng="pool")
                        else:
                            ms, B_ms = mst[fo % 2]
                            tt(ms[:, 0:tw], ma[:, 0:tw], lg[:, 0:tw], ALU.add, [B_ma, B_lg], [B_ms], eng="pool")
                            P.dma("pool", mT[fo * 128:(fo + 1) * 128, t0:t0 + tw], ms[:, 0:tw], None, reads=[B_ms], writes=[])
        P.barrier()
    with contextlib.ExitStack() as ps:
        def sb(name, shape, dt=F32):
            return _sb(ps, nc, name, shape, dt)
        wo = sb("wo", [128, 16, D], BF16); B_wos = [Buf("wo%d" % i) for i in range(4)]
        wsrc = G.w_out[layer].rearrange("(k p) d -> p k d", p=128)
        for cb4 in range(4):
            csl = slice(cb4 * 512, (cb4 + 1) * 512)
            P.dma("pool", wo[:, :, csl], wsrc[:, :, csl], None, writes=[B_wos[cb4]])
        gbc = sb("gbc", [128, 2, D]); B_gbc = Buf("gbc")
        P.dma("sp", gbc[:, 0, :], G.gate_d[layer, 0].partition_broadcast(128), None, reads=[G.B_gate_d], writes=[B_gbc])
        P.dma("sp", gbc[:, 1, :], G.gate_d[layer, 1].partition_broadcast(128), None, reads=[G.B_gate_d], writes=[B_gbc], nowait=True)
        if last:
            fg = sb("fg", [128, D]); B_fg = Buf("fg")
            P.dma("sp", fg[:], G.final_g.partition_broadcast(128), None, writes=[B_fg])
        mts = [(sb("mt%d" % i, [128, 16, 128], BF16), Buf("mt%d" % i)) for i in range(2)]
        xts = [(sb("xo%d" % i, [128, D]), Buf("xo%d" % i)) for i in range(2)]
        xns = [(sb("xnw%d" % i, [128, D]), Buf("xnw%d" % i)) for i in range(2)]
        junk = sb("ojunk", [128, D], BF16); B_junk = Buf("ojunk")
        stat = [(sb("ostat%d" % i, [128, 2]), Buf("ostat%d" % i)) for i in range(2)]
        msrc = mT.rearrange("(k p) t -> p k t", p=128)
        for ti in range(t_first, NT):
            mt, B_mt = mts[ti % 2]
            xt, B_xt = xts[ti % 2]
            xn, B_xn = xns[ti % 2]
            isctx = 1 if ti < 2 else 0
            P.dma("sp", mt[:], msrc[:, :, ti * 128:(ti + 1) * 128], None, reads=[B_mT], writes=[B_mt])
            P.dma("sp", xt[:], G.x_src(layer, ti), None, reads=[G.B_xs], writes=[B_xt])
            for cbk in range(4):
                bk, B_bk = G.next_bank()
                for kc in range(16):
                    mm(B_bk, bk[:, :], mt[:, kc, :], wo[:, kc, cbk * 512:(cbk + 1) * 512], [B_mt, B_wos[cbk]], start=(kc == 0), stop=(kc == 15))
                csl = slice(cbk * 512, (cbk + 1) * 512)
                tt(xn[:, csl], bk[:, :], gbc[:, isctx, csl], ALU.mult, [B_bk, B_gbc], [B_xn])
                tt(xn[:, csl], xn[:, csl], xt[:, csl], ALU.add, [B_xn, B_xt], [B_xn], eng="pool")
            if not last:
                P.dma("pool", G.xs[ti * 128:(ti + 1) * 128, :], xn[:], None, reads=[B_xn], writes=[])
            else:
                st, B_st = stat[ti % 2]
                act(junk[:], xn[:], AF.Square, [B_xn], [B_junk, B_st], scale=float(D) ** -0.5, accum_out=st[:, 0:1])
                ts(st[:, 0:1], st[:, 0:1], 1e-6, None, ALU.add, None, [B_st], [B_st])
                act(st[:, 0:1], st[:, 0:1], AF.Sqrt, [B_st], [B_st])
                OP("dve", lambda e, st=st: e.reciprocal(out=st[:, 1:2], in_=st[:, 0:1]), reads=[B_st], writes=[B_st])
                stt(xn[:], xn[:], st[:, 1:2], fg[:], ALU.mult, ALU.mult, [B_xn, B_st, B_fg], [B_xn])
                P.dma("pool", G.out[(ti - 2) * 128:(ti - 1) * 128, :], xn[:], None, reads=[B_xn], writes=[])
        P.barrier()


ALL_PHASES = ("p0", "p1", "rw", "pool", "na", "p3")


def build_program(n_layers=DEPTH, debug_outs=(), phases=ALL_PHASES, ext_in=(), only_layer=None):
    nc = bass.Bass("TRN2", target_bir_lowering=False)
    es = contextlib.ExitStack()
    with es:
        _build(nc, es, n_layers, debug_outs, phases, ext_in, only_layer)
    return nc


def _build(nc, es, n_layers, debug_outs, phases, ext_in, only_layer=None):
    P = Prog(nc, es)
    allow = es.enter_context(nc.allow_non_contiguous_dma(reason="small strided param loads"))

    def din(name, shape, dt=F32):
        return nc.dram_tensor(name, list(shape), dt, kind="ExternalInput").ap()

    def dscr(name, shape, dt=F32):
        kind = "ExternalOutput" if name in debug_outs else ("ExternalInput" if name in ext_in else "Internal")
        return nc.dram_tensor(name, list(shape), dt, kind=kind).ap()

    if "p0" in phases or "p1" in phases:
        x_in = din("x", [SEQ, D])
        ctx_in = din("ctx", [NCTX, D])
        c_in = din("c", [D])
        cctx_in = din("c_ctx", [D])
        norm_g = din("norm_g", [DEPTH, D])
        w_mod = din("w_mod", [DEPTH, D, 3 * D])
        b_mod = din("b_mod", [DEPTH, 3 * D])
        w_in = din("w_in", [DEPTH, D, D_IN])
    ident_in = din("ident", [128, 128])
    sel_in = din("sel", [2, 2, 128])
    out = nc.dram_tensor("out", [SEQ, D], F32, kind="ExternalOutput").ap()

    qkT = dscr("qkT", [2048, S], BF16)
    vaug = dscr("vaug", [S, 16, 65], BF16)
    restT = dscr("restT", [D_IN - 3072, S], F32)
    B_qkT, B_vaug, B_restT = Buf("qkT"), Buf("vaug"), Buf("restT")

    banks = []
    for i in range(8):
        t = es.enter_context(nc.psum_tensor("bank%d" % i, [128, 512], F32))
        banks.append((t, Buf("bank%d" % i, excl=True)))
    bank_rr = [0]

    def next_bank():
        b = banks[bank_rr[0] % 8]
        bank_rr[0] += 1
        return b

    ident = _sb(es, nc, "ident", [128, 128], F32)
    B_ident = Buf("ident")
    P.dma("sp", ident[:], ident_in[:, :], B_ident, writes=[B_ident])
    sel = _sb(es, nc, "sel", [2, 2, 128], F32)
    B_sel = Buf("sel")
    P.dma("sp", sel[:], sel_in[:, :, :], B_sel, writes=[B_sel])
    gs_col = _sb(es, nc, "gs_col", [128, 16, 2], F32)
    sh_col = _sb(es, nc, "sh_col", [128, 16, 2], F32)
    B_gs, B_sh = Buf("gs"), Buf("sh")
    gate_d = dscr("gate_d", [DEPTH, 2, D], F32)
    B_gate_d = Buf("gate_d")

    G = type("Ctx", (), {})()
    G.nc, G.P, G.next_bank, G.ident, G.B_ident, G.restT, G.B_restT = nc, P, next_bank, ident, B_ident, restT, B_restT
    G.din, G.dscr = din, dscr
    G.phases = phases
    setup_consts(G, es)
    xs = dscr("xs", [S, D], F32)
    G.xs, G.B_xs = xs, Buf("xs")
    G.qkT, G.B_qkT, G.vaug, G.B_vaug = qkT, B_qkT, vaug, B_vaug
    G.gate_d, G.B_gate_d = gate_d, B_gate_d
    G.out = out
    G.mT, G.B_mT = dscr("mT", [D, S], BF16), Buf("mT")
    if "pool" in phases:
        G.pool_invcnt = din("pool_invcnt", [4, S])
        G.pool_w = din("pool_w", [DEPTH, 4, 256, 256])
        G.pool_scale = din("pool_scale", [DEPTH, W])
    if "na" in phases:
        G.na_tab = din("na_tab", [DEPTH, 16, 128, 18, 64])
    if "p3" in phases:
        G.w_branch = din("w_branch", [DEPTH, 3, W, D])
        G.w_out = din("w_out", [DEPTH, D, D])
        G.final_g = din("final_g", [D])
        if "p1" not in phases:
            x_in = din("x", [SEQ, D])
            ctx_in = din("ctx", [NCTX, D])

        def x_src(layer, ti):
            if layer == 0:
                return ctx_in[ti * 128:(ti + 1) * 128, :] if ti < 2 else x_in[(ti - 2) * 128:(ti - 1) * 128, :]
            return xs[ti * 128:(ti + 1) * 128, :]
        G.x_src = x_src
    if "rw" in phases:
        G.rwpar = din("rwpar", [DEPTH, 8, 128, NPAR])
        G.rw_w2 = din("rw_w2", [DEPTH, 2, 64, W])
        G.rw_a2 = din("rw_a2", [DEPTH, 2, 64, W])
        G.lnx_g = din("rw_lnx_g", [DEPTH, W])
        G.lnx_b = din("rw_lnx_b", [DEPTH, W])
    brT = dscr("brT", [3 * W, S], BF16)
    G.brT, G.B_brT = brT, Buf("brT")
    for layer in range(n_layers):
        if only_layer is not None and layer != only_layer:
            continue
        if "p0" in G.phases:
            with contextlib.ExitStack() as ps:
                condT = _sb(ps, nc, "condT", [128, 16, 2], F32)
                B_cond = Buf("condT")
                P.dma("sp", condT[:, :, 0], c_in.rearrange("(k p) -> p k", p=128), B_cond, writes=[B_cond])
                P.dma("sp", condT[:, :, 1], cctx_in.rearrange("(k p) -> p k", p=128), B_cond, writes=[B_cond], nowait=True)
                scond = _sb(ps, nc, "scond", [128, 16, 2], F32)
                B_scond = Buf("scond")
                P.op("act", lambda e: e.activation(out=scond[:], in_=condT[:], func=AF.Silu),
                     reads=[B_cond], writes=[B_scond])
                gcol = _sb(ps, nc, "gcol", [128, 16], F32)
                B_gcol = Buf("gcol")
                P.dma("sp", gcol[:], norm_g[layer].rearrange("(k p) -> p k", p=128), B_gcol, writes=[B_gcol])
                modrow = _sb(ps, nc, "modrow", [2, 3 * D], F32)
                B_modrow = Buf("modrow")
                bmod2 = _sb(ps, nc, "bmod2", [2, 3 * D], F32)
                B_bmod = Buf("bmod2")
                P.dma("sp", bmod2[0:1, :], b_mod[layer:layer + 1, :], B_bmod, writes=[B_bmod])
                P.dma("sp", bmod2[1:2, :], b_mod[layer:layer + 1, :], B_bmod, writes=[B_bmod], nowait=True)
                wbufs = []
                for i in range(2):
                    wbufs.append((_sb(ps, nc, "wmod%d" % i, [128, 16, 512], F32), Buf("wmod%d" % i)))
                for cb in range(12):
                    wt, B_w = wbufs[cb % 2]
                    src = w_mod[layer, :, cb * 512:(cb + 1) * 512].rearrange("(k p) c -> p k c", p=128)
                    P.dma("sp", wt[:, 0:8, :], src[:, 0:8, :], B_w, writes=[B_w])
                    P.dma("sp", wt[:, 8:16, :], src[:, 8:16, :], B_w, writes=[B_w], nowait=True)
                    bk, B_bk = next_bank()
                    for k in range(16):
                        P.op("pe", lambda e, k=k, wt=wt, bk=bk: e.matmul(bk[0:2, :], lhsT=scond[:, k, :], rhs=wt[:, k, :],
                                                                        start=(k == 0), stop=(k == 15)),
                             reads=[B_scond, B_w], writes=[B_bk])
                    P.op("dve", lambda e, bk=bk, cb=cb: e.tensor_tensor(out=modrow[:, cb * 512:(cb + 1) * 512], in0=bk[0:2, :],
                                                                       in1=bmod2[:, cb * 512:(cb + 1) * 512], op=ALU.add),
                         reads=[B_bk, B_bmod], writes=[B_modrow])
                bk, B_bk = next_bank()
                for which in range(2):
                    for k in range(16):
                        c0 = which * D + k * 128
                        o0 = (which * 16 + k) * 2
                        P.op("pe", lambda e, c0=c0, o0=o0, bk=bk: e.matmul(bk[:, o0:o0 + 2], lhsT=modrow[0:2, c0:c0 + 128],
                                                                          rhs=ident[0:2, 0:2], start=True, stop=True),
                             reads=[B_modrow, B_ident], writes=[B_bk])
                P.op("act", lambda e, bk=bk: e.activation(out=sh_col[:].rearrange("p k t -> p (k t)"), in_=bk[:, 0:32], func=AF.Copy),
                     reads=[B_bk], writes=[B_sh])
                tmpc = _sb(ps, nc, "tmpc", [128, 16, 2], F32)
                B_tmpc = Buf("tmpc")
                P.op("dve", lambda e, bk=bk: e.tensor_scalar(out=tmpc[:].rearrange("p k t -> p (k t)"), in0=bk[:, 32:64], scalar1=1.0,
                                                            scalar2=None, op0=ALU.add),
                     reads=[B_bk], writes=[B_tmpc])
                for t in range(2):
                    P.op("dve", lambda e, t=t: e.tensor_tensor(out=gs_col[:, :, t], in0=tmpc[:, :, t], in1=gcol[:], op=ALU.mult),
                         reads=[B_tmpc, B_gcol], writes=[B_gs])
                P.dma("sp", gate_d[layer], modrow[0:2, 2 * D:3 * D], B_gate_d, reads=[B_modrow])
                P.barrier()

        if "p1" in G.phases:
            with contextlib.ExitStack() as ps:
                hT = _sb(ps, nc, "hT", [128, 16, S], BF16)
                B_hT = [Buf("hT%d" % i) for i in range(NT)]
                ps_outer = ps
                ps = contextlib.ExitStack()
                ps.__enter__()
                xts = [(_sb(ps, nc, "xt%d" % i, [128, D], F32), Buf("xt%d" % i)) for i in range(2)]
                xns = [(_sb(ps, nc, "xn%d" % i, [128, D], F32), Buf("xn%d" % i)) for i in range(2)]
                junk = _sb(ps, nc, "junk", [128, D], BF16)
                B_junk = Buf("junk")
                stat = [(_sb(ps, nc, "stat%d" % i, [128, 2], F32), Buf("stat%d" % i)) for i in range(2)]
                for ti in range(NT):
                    xt, B_xt = xts[ti % 2]
                    xn, B_xn = xns[ti % 2]
                    st, B_st = stat[ti % 2]
                    isctx = 1 if ti < 2 else 0
                    if layer == 0:
                        src = ctx_in[ti * 128:(ti + 1) * 128, :] if ti < 2 else x_in[(ti - 2) * 128:(ti - 1) * 128, :]
                    else:
                        src = xs[ti * 128:(ti + 1) * 128, :]
                    P.dma("sp", xt[:], src, B_xt, writes=[B_xt])
                    P.op("act", lambda e, xt=xt, st=st: e.activation(out=junk[:], in_=xt[:], func=AF.Square, scale=float(D) ** -0.5,
                                                                   accum_out=st[:, 0:1]),
                         reads=[B_xt], writes=[B_junk, B_st])
                    P.op("dve", lambda e, st=st: e.tensor_scalar(out=st[:, 0:1], in0=st[:, 0:1], scalar1=1e-6, scalar2=None,
                                                               op0=ALU.add),
                         reads=[B_st], writes=[B_st])
                    P.op("act", lambda e, st=st: e.activation(out=st[:, 0:1], in_=st[:, 0:1], func=AF.Sqrt),
                         reads=[B_st], writes=[B_st])
                    P.op("dve", lambda e, st=st: e.reciprocal(out=st[:, 1:2], in_=st[:, 0:1]),
                         reads=[B_st], writes=[B_st])
                    P.op("act", lambda e, xt=xt, xn=xn, st=st: e.activation(out=xn[:], in_=xt[:], func=AF.Copy, scale=st[:, 1:2]),
                         reads=[B_xt, B_st], writes=[B_xn])
                    for g in range(4):
                        bk, B_bk = next_bank()
                        for j in range(4):
                            k = g * 4 + j
                            P.op("pe", lambda e, k=k, j=j, bk=bk, xn=xn: e.transpose(bk[:, j * 128:(j + 1) * 128],
                                                                                   xn[:, k * 128:(k + 1) * 128], ident[:]),
                                 reads=[B_xn, B_ident], writes=[B_bk])
                        for j in range(4):
                            k = g * 4 + j
                            eng = "act" if (j % 2 == 0) else "dve"
                            o = hT[:, k, ti * 128:(ti + 1) * 128]
                            i_ = bk[:, j * 128:(j + 1) * 128]
                            if eng == "act":
                                P.op("act", lambda e, o=o, i_=i_, k=k, isctx=isctx: e.activation(
                                    out=o, in_=i_, func=AF.Identity, scale=gs_col[:, k, isctx:isctx + 1],
                                    bias=sh_col[:, k, isctx:isctx + 1]),
                                    reads=[B_bk, B_gs, B_sh], writes=[B_hT[ti]])
                            else:
                                P.op("dve", lambda e, o=o, i_=i_, k=k, isctx=isctx: e.tensor_scalar(
                                    out=o, in0=i_, scalar1=gs_col[:, k, isctx:isctx + 1], scalar2=sh_col[:, k, isctx:isctx + 1],
                                    op0=ALU.mult, op1=ALU.add),
                                    reads=[B_bk, B_gs, B_sh], writes=[B_hT[ti]])

                P.barrier()
                ps.__exit__(None, None, None)
                ps = contextlib.ExitStack()
                ps.__enter__()
                wbs = [(_sb(ps, nc, "win%d" % i, [128, 16, 512], BF16), Buf("win%d" % i)) for i in range(2)]
                ost = [(_sb(ps, nc, "ost%d" % i, [128, 512], F32), Buf("ost%d" % i)) for i in range(4)]
                ostb = [(_sb(ps, nc, "ostb%d" % i, [128, 512], BF16), Buf("ostb%d" % i)) for i in range(4)]
                vst = [(_sb(ps, nc, "vst%d" % i, [128, 8, 65], BF16), Buf("vst%d" % i)) for i in range(2)]
                for i in range(2):
                    P.op("pool", lambda e, i=i: e.memset(vst[i][0][:], 1.0), writes=[vst[i][1]])
                tblocks = [(i * 512, 512) for i in range(8)] + [(4096, 256)]
                n_cb = (D_IN + 511) // 512
                evac_rr = [0]
                sub_rr = [0]
                for cb in range(n_cb):
                    c0 = cb * 512
                    cw = min(512, D_IN - c0)
                    wt, B_w = wbs[cb % 2]
                    src = w_in[layer, :, c0:c0 + cw].rearrange("(k p) c -> p k c", p=128)
                    P.dma("pool", wt[:, 0:8, 0:cw], src[:, 0:8, :], B_w, writes=[B_w])
                    P.dma("pool", wt[:, 8:16, 0:cw], src[:, 8:16, :], B_w, writes=[B_w], nowait=True)
                    if 2048 <= c0 < 3072:
                        hg = (c0 - 2048) // 512
                        for ti in range(NT):
                            bk, B_bk = next_bank()
                            for k in range(16):
                                P.op("pe", lambda e, k=k, ti=ti, bk=bk, wt=wt: e.matmul(
                                    bk[:, :], lhsT=hT[:, k, ti * 128:(ti + 1) * 128], rhs=wt[:, k, :], start=(k == 0), stop=(k == 15)),
                                    reads=[B_hT[ti], B_w], writes=[B_bk])
                            vs, B_vs = vst[ti % 2]
                            eng = "act" if evac_rr[0] % 2 == 0 else "dve"
                            evac_rr[0] += 1
                            o = vs[:, :, 0:64]
                            i_ = bk[:, :].rearrange("p (h d) -> p h d", h=8)
                            if eng == "act":
                                P.op("act", lambda e, o=o, i_=i_: e.activation(out=o, in_=i_, func=AF.Copy), reads=[B_bk], writes=[B_vs])
                            else:
                                P.op("dve", lambda e, o=o, i_=i_: e.tensor_copy(out=o, in_=i_), reads=[B_bk], writes=[B_vs])
                            P.dma("sp", vaug[ti * 128:(ti + 1) * 128, hg * 8:(hg + 1) * 8, :], vs[:], B_vaug, reads=[B_vs], writes=[])
                        continue
                    for j in range(cw // 128):
                        is_qk = c0 < 2048
                        col = c0 + j * 128
                        for tb, (t0, tw) in enumerate(tblocks):
                            bk, B_bk = next_bank()
                            for k in range(16):
                                P.op("pe", lambda e, k=k, j=j, bk=bk, wt=wt, t0=t0, tw=tw: e.matmul(
                                    bk[:, 0:tw], lhsT=wt[:, k, j * 128:(j + 1) * 128], rhs=hT[:, k, t0:t0 + tw],
                                    start=(k == 0), stop=(k == 15)),
                                    reads=[B_w] + B_hT[t0 // 128:(t0 + tw) // 128], writes=[B_bk])
                            stg, B_stg = (ostb if is_qk else ost)[sub_rr[0] % 4]
                            sub_rr[0] += 1
                            eng = "act" if evac_rr[0] % 2 == 0 else "dve"
                            evac_rr[0] += 1
                            o = stg[:, 0:tw]
                            i_ = bk[:, 0:tw]
                            if eng == "act":
                                P.op("act", lambda e, o=o, i_=i_: e.activation(out=o, in_=i_, func=AF.Copy), reads=[B_bk], writes=[B_stg])
                            else:
                                P.op("dve", lambda e, o=o, i_=i_: e.tensor_copy(out=o, in_=i_), reads=[B_bk], writes=[B_stg])
                            if is_qk:
                                P.dma("sp", qkT[col:col + 128, t0:t0 + tw], stg[:, 0:tw], B_qkT, reads=[B_stg])
                            else:
                                r0 = col - 3072
                                P.dma("sp", restT[r0:r0 + 128, t0:t0 + tw], stg[:, 0:tw], B_restT, reads=[B_stg])
                P.barrier()
                ps.__exit__(None, None, None)
                ps = ps_outer
                P.barrier()

        if "rw" in G.phases:
            phase_rwkv(G, layer)
        if "pool" in G.phases:
            phase_pool(G, layer)
        if "na" in G.phases:
            phase_na(G, layer, do_ctx=(layer < DEPTH - 1))
        if "p3" in G.phases:
            phase_out(G, layer, last=(layer == DEPTH - 1))

    P.barrier()
    print("inst counts", P.ninst, "nsem", P.nsem)


def kernel(**inputs):
    inp = {k: np.asarray(v) for k, v in inputs.items()}
    shared = dict(host_consts())
    for k in ("c_ctx", "norm_g", "w_mod", "b_mod", "w_in", "rw_w2", "rw_a2", "rw_lnx_g", "rw_lnx_b", "pool_w", "pool_scale",
              "w_branch", "w_out", "final_g"):
        shared[k] = np.ascontiguousarray(inp[k], dtype=np.float32)
    shared["rwpar"] = pack_rwpar(inp)
    shared["pool_invcnt"] = host_pool_invcnt()
    shared["na_tab"] = host_na_table(np.asarray(inp["na_rpb"], np.float32))
    nb = inp["x"].shape[0]
    in_maps = []
    for b in range(nb):
        m = dict(shared)
        m["x"] = np.ascontiguousarray(inp["x"][b], dtype=np.float32)
        m["ctx"] = np.ascontiguousarray(inp["ctx"][b], dtype=np.float32)
        m["c"] = np.ascontiguousarray(inp["c"][b], dtype=np.float32)
        in_maps.append(m)
    nc = build_program()
    res = run_bass_kernel_spmd(nc, in_maps, core_ids=list(range(nb)))
    return np.stack([np.asarray(res.results[b]["out"], dtype=np.float32) for b in range(nb)], axis=0)
```

```python
import contextlib
import os
import numpy as np
import concourse.bass as bass
import concourse.mybir as mybir
from concourse.bass_utils import run_bass_kernel_spmd

F32 = mybir.dt.float32
F32R = mybir.dt.float32r
BF16 = mybir.dt.bfloat16
AF = mybir.ActivationFunctionType
ALU = mybir.AluOpType
AX = mybir.AxisListType

D = 2048
SEQ = 4096
NCTX = 256
S = SEQ + NCTX
NT = S // 128
W = 1024
DEPTH = 2
D_IN = 16512
NCORES = 4
SAME_ENGINE_SYNC = bool(int(os.environ.get("SAME_ENGINE_SYNC", "1")))


class Ev:
    __slots__ = ("sem", "key", "val")

    def __init__(self, sem, key, val):
        self.sem, self.key, self.val = sem, key, val


class Buf:
    __slots__ = ("name", "w", "rs", "excl")

    def __init__(self, name, excl=False):
        self.name = name
        self.w = []
        self.rs = {}
        self.excl = excl


class Prog:
    N_LANES = {"sp": 24, "pool": 8, "act": 4}

    def __init__(self, nc, es):
        self.nc = nc
        self.es = es
        self.h = {"pe": nc.tensor, "act": nc.scalar, "dve": nc.vector, "pool": nc.gpsimd, "sp": nc.sync}
        self.sem = {e: es.enter_context(nc.semaphore("s_" + e)) for e in self.h}
        self.cnt = {e: 0 for e in self.h}
        self.seen = {e: {} for e in self.h}
        self.lanes = {}
        self.lane_rr = {}
        self.nsem = 0
        self.ninst = {e: 0 for e in self.h}

    def _lane(self, q):
        if q not in self.lanes:
            self.lanes[q] = []
            for i in range(self.N_LANES[q]):
                self.nsem += 1
                key = "d_%s%d" % (q, i)
                self.lanes[q].append([self.es.enter_context(self.nc.semaphore(key)), key, 0])
            self.lane_rr[q] = 0
        ln = self.lanes[q][self.lane_rr[q] % len(self.lanes[q])]
        self.lane_rr[q] += 1
        if ln[2] > 0:
            self._wait(q, Ev(ln[0], ln[1], ln[2]))
        return ln

    def _wait(self, eng, ev):
        if ev is None:
            return
        if ev.key == eng and not SAME_ENGINE_SYNC:
            return
        if ev.key == "pe" and eng == "pe":
            return
        if self.seen[eng].get(ev.key, 0) >= ev.val:
            return
        self.h[eng].wait_ge(ev.sem, ev.val)
        self.ninst[eng] += 1
        self.seen[eng][ev.key] = ev.val

    def _deps(self, eng, reads, writes):
        for b in reads:
            for ev in b.w:
                self._wait(eng, ev)
            if b.excl:
                for k, r in b.rs.items():
                    if k != eng:
                        self._wait(eng, r)
        for b in writes:
            for ev in b.w:
                self._wait(eng, ev)
            for r in b.rs.values():
                self._wait(eng, r)

    def op(self, eng, fn, reads=(), writes=()):
        self._deps(eng, reads, writes)
        inst = fn(self.h[eng])
        self.cnt[eng] += 1
        self.ninst[eng] += 1
        inst.then_inc(self.sem[eng], 1)
        ev = Ev(self.sem[eng], eng, self.cnt[eng])
        for b in reads:
            b.rs[eng] = ev
        for b in writes:
            b.w = [ev]
            b.rs = {}
        return ev

    def dma(self, q, out, in_, owner=None, reads=(), writes=(), nowait=False, **kw):
        if not nowait:
            self._deps(q, reads, writes)
        ln = self._lane(q)
        inst = self.h[q].dma_start(out=out, in_=in_, **kw)
        ln[2] += 16
        inst.then_inc(ln[0], 16)
        self.ninst[q] += 1
        ev = Ev(ln[0], ln[1], ln[2])
        for b in reads:
            b.rs[ln[1]] = ev
        for b in writes:
            if nowait:
                b.w = list(b.w) + [ev]
            else:
                b.w = [ev]
                b.rs = {}
        return ev

    def barrier(self):
        evs = [Ev(self.sem[e], e, self.cnt[e]) for e in self.h if self.cnt[e] > 0]
        for q, lanes in self.lanes.items():
            evs += [Ev(l[0], l[1], l[2]) for l in lanes if l[2] > 0]
        for e in self.h:
            for ev in evs:
                if ev.key == e:
                    if self.seen[e].get(e, 0) < ev.val and e != "sp":
                        self.h[e].wait_ge(ev.sem, ev.val)
                        self.seen[e][e] = ev.val
                    continue
                self._wait(e, ev)


_uid = [0]


def _rd(ap):
    try:
        if ap.dtype == F32R:
            return ap.bitcast(F32)
    except AttributeError:
        pass
    return ap


def _sb(es, nc, name, shape, dt):
    _uid[0] += 1
    return es.enter_context(nc.sbuf_tensor("sb%d_%s" % (_uid[0], name), list(shape), dt))


def host_consts():
    idx = np.arange(128)
    masks = np.stack([(idx[:, None] < idx[None, :]), (idx[:, None] <= idx[None, :]),
                      (idx[:, None] > idx[None, :]), (idx[:, None] >= idx[None, :])]).astype(np.float32)
    blockones = (idx[:, None] // 64 == idx[None, :] // 64).astype(np.float32)
    resetmask = np.ones((128, 256), np.float32)
    resetmask[:, 0] = 0.0
    resetmask[:, 128] = 0.0
    headsel = np.zeros((128, 2), np.float32)
    headsel[:64, 0] = 1.0
    headsel[64:, 1] = 1.0
    sel = np.zeros((2, 2, 128), np.float32)
    sel[0, 0] = 1
    sel[1, 1] = 1
    return {"ident": np.eye(128, dtype=np.float32), "sel": sel, "masks": masks, "blockones": blockones,
            "resetmask": resetmask, "headsel": headsel}


def pack_rwpar(inp):
    cols = [inp["rw_mu"][:, 0], inp["rw_mu"][:, 1], inp["rw_mu"][:, 2], inp["rw_k_k"], inp["rw_k_a"],
            inp["rw_r_k"].reshape(DEPTH, W), inp["rw_w0"][:, 0], inp["rw_w0"][:, 1], inp["rw_a0"][:, 0], inp["rw_a0"][:, 1],
            inp["rw_lnx_g"], inp["rw_lnx_b"]]
    a = np.stack([np.asarray(c, np.float32) for c in cols], axis=-1)
    return np.ascontiguousarray(a.reshape(DEPTH, 8, 128, NPAR))


def setup_consts(G, es):
    nc, P = G.nc, G.P
    masks_in = G.din("masks", [4, 128, 128])
    bo_in = G.din("blockones", [128, 128])
    rm_in = G.din("resetmask", [128, 256])
    hs_in = G.din("headsel", [128, 2])
    G.masks = _sb(es, nc, "masks", [128, 4, 128], F32)
    G.B_masks = Buf("masks")
    for i in range(4):
        P.dma("sp", G.masks[:, i, :], masks_in[i], G.B_masks, writes=[G.B_masks], nowait=(i > 0))
    G.blockones = _sb(es, nc, "blockones", [128, 128], F32)
    G.B_bo = Buf("blockones")
    P.dma("sp", G.blockones[:], bo_in[:, :], G.B_bo, writes=[G.B_bo])
    G.resetmask = _sb(es, nc, "resetmask", [128, 256], F32)
    G.B_rm = Buf("resetmask")
    P.dma("sp", G.resetmask[:], rm_in[:, :], G.B_rm, writes=[G.B_rm])
    G.headsel = _sb(es, nc, "headsel", [128, 2], F32)
    G.B_hs = Buf("headsel")
    P.dma("sp", G.headsel[:], hs_in[:, :], G.B_hs, writes=[G.B_hs])


NPAR = 12
R_RWR, R_RWK, R_RWV, R_RWG, R_LW, R_LA = 3072, 4096, 5120, 6144, 7168, 7232
LOGW_SCALE = -0.6065306597126334


def phase_rwkv(G, layer, do_ctx_out=True):
    nc, P = G.nc, G.P
    restT, B_restT = G.restT, G.B_restT
    rwpar = G.rwpar
    rw_w2, rw_a2 = G.rw_w2, G.rw_a2
    lnx_g, lnx_b = G.lnx_g, G.lnx_b
    masks, B_masks = G.masks, G.B_masks
    M_lt, M_le, M_gt, M_ge = (masks[:, i, :] for i in range(4))

    def OP(eng, fn, reads=(), writes=()):
        return P.op(eng, fn, reads=reads, writes=writes)

    def mm(bk, o, lhsT, rhs, reads, start=True, stop=True):
        OP("pe", lambda e: e.matmul(o, lhsT=lhsT, rhs=rhs, start=start, stop=stop), reads=reads, writes=[bk])

    def tr(bk, o, in_, reads):
        OP("pe", lambda e: e.transpose(o, _rd(in_), G.ident[:]), reads=list(reads) + [G.B_ident], writes=[bk])

    def act(o, i, func, reads, writes, **kw):
        kw = {k_: _rd(v_) for k_, v_ in kw.items()}
        OP("act", lambda e: e.activation(out=o, in_=_rd(i), func=func, **kw), reads=reads, writes=writes)

    def tt(o, a, b, op, reads, writes, eng="dve"):
        OP(eng, lambda e: e.tensor_tensor(out=o, in0=_rd(a), in1=_rd(b), op=op), reads=reads, writes=writes)

    def ts(o, a, s1, s2, op0, op1, reads, writes, eng="dve"):
        if s2 is None:
            OP(eng, lambda e: e.tensor_scalar(out=o, in0=_rd(a), scalar1=_rd(s1), scalar2=None, op0=op0), reads=reads, writes=writes)
        else:
            OP(eng, lambda e: e.tensor_scalar(out=o, in0=_rd(a), scalar1=_rd(s1), scalar2=_rd(s2), op0=op0, op1=op1), reads=reads, writes=writes)

    def stt(o, a, sc, b, op0, op1, reads, writes):
        OP("dve", lambda e: e.scalar_tensor_tensor(out=o, in0=_rd(a), scalar=_rd(sc), in1=_rd(b), op0=op0, op1=op1), reads=reads, writes=writes)

    def cp(eng, o, i, reads, writes):
        if eng == "act":
            act(o, i, AF.Copy, reads, writes)
        else:
            OP(eng, lambda e: e.tensor_copy(out=o, in_=_rd(i)), reads=reads, writes=writes)

    with contextlib.ExitStack() as ps:
        def sb(name, shape, dt=F32):
            return _sb(ps, nc, name, shape, dt)

        R32 = F32R if int(os.environ.get("RW_F32R", "0")) else F32
        lwla = sb("lwla", [128, S])
        B_lwla = Buf("lwla")
        P.dma("sp", lwla[:], restT[R_LW:R_LW + 128, :], B_lwla, reads=[B_restT], writes=[B_lwla])
        act(lwla[0:64, :], lwla[0:64, :], AF.Tanh, [B_lwla], [B_lwla])

        rT = sb("r", [128, S]); B_r = Buf("r")
        kT = sb("k", [128, S]); B_k = Buf("k")
        vT = sb("vkkn", [128, S]); B_v = Buf("vkkn")
        ytok = sb("ytok", [128, NT, 128]); B_y = [Buf("ytok%d" % i) for i in range(NT)]
        bon = sb("bon", [128, NT, 2]); B_bon = [Buf("bon%d" % i) for i in range(NT)]
        vtok = sb("vtok", [128, NT, 128], R32); B_vt = [Buf("vtok%d" % i) for i in range(NT)]
        par = sb("par", [128, NPAR + 8]); B_par = Buf("par")
        w2t = sb("w2t", [128, 2, 128]); B_w2 = Buf("w2t")
        lnxg = sb("lnxg", [128, 128]); lnxb = sb("lnxb", [128, 128]); B_lnx = Buf("lnx")
        rkblk = sb("rkblk", [128, 2], R32); B_rkblk = Buf("rkblk")
        nsum = ytok[:].rearrange("p t c -> p (t c)")
        GW = 256
        gtmp = [[(sb("gt%d_%d" % (d, i), [128, GW]), Buf("gt%d_%d" % (d, i))) for i in range(6)] for d in range(2)]
        gout = [[[(sb("go%d_%d_%d" % (d, pz, i), [128, GW], F32 if i == 0 else R32), Buf("go%d_%d_%d" % (d, pz, i))) for i in range(5)]
                 for pz in range(2)] for d in range(2)]
        ukds = [(sb("ukd%d" % d, [128, GW], R32), Buf("ukd%d" % d)) for d in range(2)]
        def ctile(name, w=128, dt=None):
            return (sb(name, [128, w], R32 if dt is None else dt), Buf(name))
        cper = [[[[{n: ctile("c%s%d%d%d%d" % (n, d, pz, c, h)) for n in ("Pm", "BmT", "RBT", "RKT")} for h in range(2)]
                  for c in range(2)] for pz in range(2)] for d in range(2)]
        cpair = [[[{n: ctile("c%s%d%d%d" % (n, d, pz, c)) for n in ("btok", "ktok")} for c in range(2)]
                  for pz in range(2)] for d in range(2)]
        ctmp = [[[{n: ctile("t%s%d%d%d" % (n, d, c, h)) for n in ("Xa", "XTa", "Xb", "XTb", "Pb")} for h in range(2)]
                 for c in range(2)] for d in range(2)]
        STs = [[ctile("ST%d%d" % (d, i), 64) for i in range(2)] for d in range(2)]
        S0dec = [ctile("S0dec%d" % d, 64, F32) for d in range(2)]
        Gt = [ctile("G%d" % d) for d in range(2)]
        SAt = [ctile("SA%d" % d) for d in range(2)]
        yst = [ctmp[0][0][1]["Xa"], ctmp[0][0][1]["XTa"]]
        ost = [(sb("rwo%d" % i, [128, 128], BF16), Buf("rwo%d" % i)) for i in range(2)]
        gts = [(gtmp[0][0][0][:, 0:128], gtmp[0][0][1]), (gtmp[0][1][0][:, 0:128], gtmp[0][1][1])]
        small = [ctile("sm%d" % i, 8, F32) for i in range(4)]

        RW_STAGE = int(os.environ.get("RW_STAGE", "99"))
        RW_SUB = int(os.environ.get("RW_SUB", "99"))
        for hp in range(int(os.environ.get("RW_PAIRS", "8"))):
            ch0 = hp * 128
            P.dma("sp", par[:, 0:NPAR], rwpar[layer, hp], B_par, writes=[B_par])
            for d in range(2):
                P.dma("sp", w2t[0:64, d, :], rw_w2[layer, d, :, ch0:ch0 + 128], B_w2, writes=[B_w2], nowait=(d > 0))
                P.dma("sp", w2t[64:128, d, :], rw_a2[layer, d, :, ch0:ch0 + 128], B_w2, writes=[B_w2], nowait=True)
            P.dma("sp", lnxg[:], lnx_g[layer, ch0:ch0 + 128].partition_broadcast(128), B_lnx, writes=[B_lnx])
            P.dma("sp", lnxb[:], lnx_b[layer, ch0:ch0 + 128].partition_broadcast(128), B_lnx, writes=[B_lnx], nowait=True)
            ts(par[:, NPAR:NPAR + 3], par[:, 0:3], -1.0, 1.0, ALU.mult, ALU.add, [B_par], [B_par])
            ts(par[:, NPAR + 3:NPAR + 6], par[:, 0:3], 0.5, None, ALU.mult, None, [B_par], [B_par])
            ts(par[:, NPAR + 6:NPAR + 7], par[:, 4:5], -1.0, 1.0, ALU.mult, ALU.add, [B_par], [B_par])
            ts(rkblk[:], G.headsel[:], par[:, 5:6], None, ALU.mult, None, [B_par, G.B_hs], [B_rkblk])
            C_KK, C_KA, C_OMKA = par[:, 3:4], par[:, 4:5], par[:, NPAR + 6:NPAR + 7]

            for zi, (zt, B_z, row) in enumerate(((rT, B_r, R_RWR), (kT, B_k, R_RWK), (vT, B_v, R_RWV))):
                P.dma("sp", zt[:], restT[row + ch0:row + ch0 + 128, :], B_z, reads=[B_restT], writes=[B_z])
                tt(nsum[:, 1:S - 1], zt[:, 0:S - 2], zt[:, 2:S], ALU.add, [B_z], B_y)
                for (dst, srcc) in ((0, 1), (255, 254), (256, 257), (S - 1, S - 2)):
                    cp("dve", nsum[:, dst:dst + 1], zt[:, srcc:srcc + 1], [B_z], B_y)
                ts(nsum[:, :], nsum[:, :], par[:, NPAR + 3 + zi:NPAR + 4 + zi], None, ALU.mult, None, B_y + [B_par], B_y)
                stt(zt[:], zt[:], par[:, NPAR + zi:NPAR + 1 + zi], nsum[:, :], ALU.mult, ALU.add, [B_z, B_par] + B_y, [B_z])
            if RW_STAGE < 2:
                continue
            for ti in range(NT):
                bk, B_bk = G.next_bank()
                tr(B_bk, bk[:, 0:128], vT[:, ti * 128:(ti + 1) * 128], [B_v])
                cp("act" if ti % 2 == 0 else "dve", vtok[:, ti, :], bk[:, 0:128], [B_bk], [B_vt[ti]])
            act(nsum[:, :], kT[:], AF.Copy, [B_k, B_par], B_y, scale=C_KK)
            act(vT[:], nsum[:, :], AF.Square, B_y + B_vt, [B_v])
            for t0 in range(0, S, 512):
                tw = min(512, S - t0)
                bk, B_bk = G.next_bank()
                mm(B_bk, bk[:, 0:tw], G.blockones[:], vT[:, t0:t0 + tw], [G.B_bo, B_v])
                act(vT[:, t0:t0 + tw], bk[:, 0:tw], AF.Sqrt, [B_bk], [B_v])
            ts(vT[:], vT[:], 1e-12, None, ALU.max, None, [B_v], [B_v])
            OP("dve", lambda e: e.reciprocal(out=vT[:], in_=vT[:]), reads=[B_v], writes=[B_v])
            tt(vT[:], vT[:], nsum[:, :], ALU.mult, [B_v] + B_y, [B_v])
            kkn, B_kkn = vT, B_v

            if RW_STAGE < 3:
                continue
            order = {0: list(range(17)), 1: [0] + list(range(16, 0, -1))}
            st_idx = [0, 0]
            ywritten = set()
            bwritten = set()
            for d in range(2):
                if RW_SUB < -1:
                    break
                ts(STs[d][0][0][:], G.ident[:, 0:64], 0.0, None, ALU.mult, None, [G.B_ident], [STs[d][0][1]], eng="pool")

            def prep_rounds(d, step):
                g = order[d][step]
                pz = step % 2
                t0 = g * GW
                (sg, B_sg), (cs, B_cs), (tmp, B_tmp), (ad, B_ad), (kd, B_kd), (en, B_en) = gtmp[d]
                ukd, B_ukd = ukds[d]
                (Ep, B_Ep), (aTt, B_aT), (bTt, B_bT), (kTt, B_kT), (rTt, B_rT) = gout[d][pz]
                rounds = []

                def r0():
                    if RW_SUB < 0:
                        return
                    bk, B_bk = G.next_bank()
                    bk2, B_bk2 = G.next_bank()
                    mm(B_bk, bk[:, 0:GW], w2t[0:64, d, :], lwla[0:64, t0:t0 + GW], [B_w2, B_lwla])
                    mm(B_bk2, bk2[:, 0:GW], w2t[64:128, d, :], lwla[64:128, t0:t0 + GW], [B_w2, B_lwla])
                    act(sg[:], bk[:, 0:GW], AF.Sigmoid, [B_bk, B_par], [B_sg], bias=par[:, 6 + d:7 + d])
                    act(ad[:], bk2[:, 0:GW], AF.Sigmoid, [B_bk2, B_par], [B_ad], bias=par[:, 8 + d:9 + d])
                    if RW_SUB < 1:
                        return
                    OP("dve", lambda e: e.tensor_tensor_scan(out=cs[:], data0=G.resetmask[:], data1=sg[:], initial=0.0,
                                                            op0=ALU.mult, op1=ALU.add), reads=[B_sg, G.B_rm], writes=[B_cs])
                    if d == 1 and RW_SUB >= 2:
                        cs3 = cs[:].rearrange("p (c k) -> p c k", k=128)
                        tot = cs3[:, :, 127:128].to_broadcast([128, 2, 128])
                        tt(tmp[:].rearrange("p (c k) -> p c k", k=128), tot, cs3, ALU.subtract, [B_cs], [B_tmp])
                        tt(cs[:], tmp[:], sg[:], ALU.add, [B_tmp, B_sg], [B_cs])
                rounds.append(r0)
                if RW_SUB < 3:
                    return rounds

                def r1():
                    act(Ep[:], cs[:], AF.Exp, [B_cs], [B_Ep], scale=LOGW_SCALE)
                    act(en[:], cs[:], AF.Exp, [B_cs], [B_en], scale=-LOGW_SCALE)
                    tt(tmp[:], cs[:], sg[:], ALU.subtract, [B_cs, B_sg], [B_tmp])
                    act(tmp[:], tmp[:], AF.Exp, [B_tmp], [B_tmp], scale=LOGW_SCALE)
                    ts(kd[:], ad[:], C_KA, C_OMKA, ALU.mult, ALU.add, [B_ad, B_par], [B_kd])
                    tt(kd[:], kd[:], kT[:, t0:t0 + GW], ALU.mult, [B_kd, B_k], [B_kd])
                rounds.append(r1)
                if RW_SUB < 4:
                    return rounds

                def r2():
                    tt(ukd[:], rT[:, t0:t0 + GW], kd[:], ALU.mult, [B_r, B_kd], [B_ukd], eng="pool")
                    stt(aTt[:], kkn[:, t0:t0 + GW], -1.0, tmp[:], ALU.mult, ALU.mult, [B_kkn, B_tmp], [B_aT])
                    tt(bTt[:], kkn[:, t0:t0 + GW], ad[:], ALU.mult, [B_kkn, B_ad], [B_bT])
                    tt(bTt[:], bTt[:], en[:], ALU.mult, [B_bT, B_en], [B_bT])
                    tt(kTt[:], kd[:], en[:], ALU.mult, [B_kd, B_en], [B_kT])
                    tt(rTt[:], rT[:, t0:t0 + GW], Ep[:], ALU.mult, [B_r, B_Ep], [B_rT], eng="pool")
                rounds.append(r2)

                strict_st, incl_st = (M_lt, M_le) if d == 0 else (M_gt, M_ge)
                strict_ts = M_gt if d == 0 else M_lt

                def r3():
                    for c in range(2):
                        cs_ = slice(c * 128, (c + 1) * 128)
                        bkAs = [G.next_bank(), G.next_bank()]
                        for h in range(2):
                            hs = slice(h * 64, (h + 1) * 64)
                            bkA, B_A = bkAs[h]
                            mm(B_A, bkA[:, 0:128], bTt[hs, cs_], aTt[hs, cs_], [B_bT, B_aT])
                        for h in range(2):
                            T = ctmp[d][c][h]
                            bkA, B_A = bkAs[h]
                            tt(T["Xa"][0][:], bkA[:, 0:128], strict_st, ALU.mult,
                               [B_A, B_masks], [T["Xa"][1]])
                            tt(T["Pb"][0][:], T["Xa"][0][:], G.ident[:], ALU.add, [T["Xa"][1], G.B_ident], [T["Pb"][1]], eng="pool")
                        for h in range(2):
                            hs = slice(h * 64, (h + 1) * 64)
                            bkB, B_B = G.next_bank()
                            Cp = cper[d][pz][c][h]
                            mm(B_B, bkB[:, 0:128], kTt[hs, cs_], aTt[hs, cs_], [B_kT, B_aT])
                            mm(B_B, bkB[:, 128:256], bTt[hs, cs_], rTt[hs, cs_], [B_bT, B_rT])
                            mm(B_B, bkB[:, 256:384], kTt[hs, cs_], rTt[hs, cs_], [B_kT, B_rT])
                            tt(Cp["BmT"][0][:], bkB[:, 0:128], strict_st, ALU.mult, [B_B, B_masks], [Cp["BmT"][1]])
                            tt(Cp["RBT"][0][:], bkB[:, 128:256], incl_st, ALU.mult, [B_B, B_masks], [Cp["RBT"][1]])
                            tt(Cp["RKT"][0][:], bkB[:, 256:384], incl_st, ALU.mult, [B_B, B_masks], [Cp["RKT"][1]])
                        bkC, B_C = G.next_bank()
                        tr(B_C, bkC[:, 0:128], bTt[:, cs_], [B_bT])
                        tr(B_C, bkC[:, 128:256], kTt[:, cs_], [B_kT])
                        for h in range(2):
                            T = ctmp[d][c][h]
                            tr(B_C, bkC[:, (2 + h) * 128:(3 + h) * 128], T["Xa"][0][:], [T["Xa"][1]])
                        cp("act", cpair[d][pz][c]["btok"][0][:], bkC[:, 0:128], [B_C], [cpair[d][pz][c]["btok"][1]])
                        cp("act", cpair[d][pz][c]["ktok"][0][:], bkC[:, 128:256], [B_C], [cpair[d][pz][c]["ktok"][1]])
                        for h in range(2):
                            T = ctmp[d][c][h]
                            cp("act", T["XTa"][0][:], bkC[:, (2 + h) * 128:(3 + h) * 128], [B_C], [T["XTa"][1]])
                if RW_STAGE < 4:
                    return rounds
                rounds.append(r3)

                def r3b():
                    bk, B_bk = G.next_bank()
                    for c in range(2):
                        mm(B_bk, bk[:, 2 * c:2 * c + 2], ukd[:, c * 128:(c + 1) * 128], rkblk[:], [B_ukd, B_rkblk])
                    for c in range(2):
                        ti = g * 2 + c
                        if ti not in bwritten:
                            bwritten.add(ti)
                            cp("act", bon[:, ti, :], bk[:, 2 * c:2 * c + 2], [B_bk], [B_bon[ti]])
                        else:
                            tt(bon[:, ti, :], bk[:, 2 * c:2 * c + 2], bon[:, ti, :], ALU.add, [B_bk, B_bon[ti]], [B_bon[ti]])
                rounds.append(r3b)

                def make_level(lvl):
                    def rl():
                        src, dst = ("a", "b") if lvl % 2 == 1 else ("b", "a")
                        last = (lvl == 6)
                        banks_ = []
                        for c in range(2):
                            bk, B_bk = G.next_bank()
                            banks_.append((bk, B_bk))
                            for h in range(2):
                                T = ctmp[d][c][h]
                                X, B_X = T["X" + src]
                                XT, B_XT = T["XT" + src]
                                if not last:
                                    mm(B_bk, bk[:, h * 128:(h + 1) * 128], XT[:], X[:], [B_X, B_XT])
                                else:
                                    mm(B_bk, bk[:, h * 128:(h + 1) * 128], X[:], XT[:], [B_X, B_XT])
                        for c in range(2):
                            bk, B_bk = banks_[c]
                            for h in range(2):
                                T = ctmp[d][c][h]
                                nm = ("X" if not last else "XT") + dst
                                cp("act" if c == 0 else "dve", T[nm][0][:], bk[:, h * 128:(h + 1) * 128], [B_bk], [T[nm][1]])

                    def rl1():
                        src, dst = ("a", "b") if lvl % 2 == 1 else ("b", "a")
                        if lvl == 6:
                            return
                        for c in range(2):
                            bk, B_bk = G.next_bank()
                            for h in range(2):
                                T = ctmp[d][c][h]
                                tr(B_bk, bk[:, h * 128:(h + 1) * 128], T["X" + dst][0][:], [T["X" + dst][1]])
                            for h in range(2):
                                T = ctmp[d][c][h]
                                cp("dve" if c == 0 else "act", T["XT" + dst][0][:], bk[:, h * 128:(h + 1) * 128], [B_bk], [T["XT" + dst][1]])

                    def rl2():
                        src, dst = ("a", "b") if lvl % 2 == 1 else ("b", "a")
                        if RW_SUB < 11:
                            return
                        for c in range(2):
                            bk, B_bk = G.next_bank()
                            for h in range(2):
                                T = ctmp[d][c][h]
                                Cp = cper[d][pz][c][h]
                                Pold, B_Pold = T["Pb"] if lvl % 2 == 1 else Cp["Pm"]
                                mm(B_bk, bk[:, h * 128:(h + 1) * 128], T["XT" + dst][0][:], Pold[:], [T["XT" + dst][1], B_Pold])
                            for h in range(2):
                                T = ctmp[d][c][h]
                                Cp = cper[d][pz][c][h]
                                Pold, B_Pold = T["Pb"] if lvl % 2 == 1 else Cp["Pm"]
                                Pnew, B_Pnew = Cp["Pm"] if lvl % 2 == 1 else T["Pb"]
                                tt(Pnew[:], bk[:, h * 128:(h + 1) * 128], Pold[:], ALU.add, [B_bk, B_Pold], [B_Pnew])
                    return (rl, rl1, rl2) if lvl < 6 else (rl, rl2)
                if RW_STAGE < 5:
                    return rounds
                for lvl in range(1, 1 + int(os.environ.get("RW_LVL", "6"))):
                    rounds.extend(make_level(lvl))
                def rfin():
                    for c in range(2):
                        for h in range(2):
                            T = ctmp[d][c][h]
                            Cp = cper[d][pz][c][h]
                            cp("pool", Cp["Pm"][0][:], T["Pb"][0][:], [T["Pb"][1]], [Cp["Pm"][1]])
                if RW_SUB >= 12:
                    rounds.append(rfin)
                return rounds

            def chain_rounds(d, step):
                g = order[d][step]
                pz = step % 2
                (Ep, B_Ep), (aTt, B_aT), (bTt, B_bT), (kTt, B_kT), (rTt, B_rT) = gout[d][pz]
                rounds = []
                corder = (0, 1) if d == 0 else (1, 0)
                for c in corder:
                    ti = g * 2 + c
                    cs_ = slice(c * 128, (c + 1) * 128)
                    ecol = c * 128 + (127 if d == 0 else 0)
                    eLC = Ep[:, ecol:ecol + 1]

                    def b1(c=c, ti=ti, cs_=cs_, eLC=eLC):
                        ST, B_ST = STs[d][st_idx[d] % 2]
                        bk, B_bk = G.next_bank()
                        for h in range(2):
                            hs = slice(h * 64, (h + 1) * 64)
                            Cp = cper[d][pz][c][h]
                            o = bk[:, h * 64:(h + 1) * 64]
                            mm(B_bk, o, Cp["BmT"][0][:], vtok[:, ti, h * 64:(h + 1) * 64], [Cp["BmT"][1], B_vt[ti]], start=True, stop=False)
                            mm(B_bk, o, aTt[hs, cs_], ST[hs, :], [B_aT, B_ST], start=False, stop=True)
                        cp("act", Gt[d][0][:], bk[:, 0:128], [B_bk], [Gt[d][1]])
                        ts(S0dec[d][0][:], ST[:], eLC, None, ALU.mult, None, [B_ST, B_Ep], [S0dec[d][1]], eng="pool")
                    rounds.append(b1)

                    def b2(c=c):
                        bk, B_bk = G.next_bank()
                        for h in range(2):
                            Cp = cper[d][pz][c][h]
                            mm(B_bk, bk[:, h * 64:(h + 1) * 64], Cp["Pm"][0][:], Gt[d][0][:, h * 64:(h + 1) * 64], [Cp["Pm"][1], Gt[d][1]])
                        cp("dve", SAt[d][0][:], bk[:, 0:128], [B_bk], [SAt[d][1]])
                    rounds.append(b2)

                    def b3(c=c, ti=ti, cs_=cs_, eLC=eLC):
                        ST, B_ST = STs[d][st_idx[d] % 2]
                        STn, B_STn = STs[d][(st_idx[d] + 1) % 2]
                        st_idx[d] += 1
                        SA, B_SA = SAt[d]
                        bk, B_bk = G.next_bank()
                        for h in range(2):
                            hs = slice(h * 64, (h + 1) * 64)
                            Cp = cper[d][pz][c][h]
                            o = bk[:, h * 64:(h + 1) * 64]
                            mm(B_bk, o, rTt[hs, cs_], ST[hs, :], [B_rT, B_ST], start=True, stop=False)
                            mm(B_bk, o, Cp["RBT"][0][:], SA[:, h * 64:(h + 1) * 64], [Cp["RBT"][1], B_SA], start=False, stop=False)
                            mm(B_bk, o, Cp["RKT"][0][:], vtok[:, ti, h * 64:(h + 1) * 64], [Cp["RKT"][1], B_vt[ti]], start=False, stop=True)
                        bk2, B_bk2 = G.next_bank()
                        Cq = cpair[d][pz][c]
                        mm(B_bk2, bk2[:, 0:128], Cq["ktok"][0][:], vtok[:, ti, :], [Cq["ktok"][1], B_vt[ti]], start=True, stop=False)
                        mm(B_bk2, bk2[:, 0:128], Cq["btok"][0][:], SA[:], [Cq["btok"][1], B_SA], start=False, stop=True)
                        if ti not in ywritten:
                            ywritten.add(ti)
                            cp("act", ytok[:, ti, :], bk[:, 0:128], [B_bk], [B_y[ti]])
                        else:
                            tt(ytok[:, ti, :], bk[:, 0:128], ytok[:, ti, :], ALU.add, [B_bk, B_y[ti]], [B_y[ti]])
                        for h in range(2):
                            hs = slice(h * 64, (h + 1) * 64)
                            stt(STn[hs, :], bk2[hs, h * 64:(h + 1) * 64], eLC[hs, :], S0dec[d][0][hs, :], ALU.mult, ALU.add,
                                [B_bk2, B_Ep, S0dec[d][1]], [B_STn])
                    rounds.append(b3)
                return rounds

            def interleave(lists):
                items = []
                for li, l in enumerate(lists):
                    for i, fn in enumerate(l):
                        items.append(((i + 0.5) / len(l), li, i, fn))
                items.sort(key=lambda t: (t[0], t[1]))
                for _, _, _, fn in items:
                    fn()

            nsteps = 17
            for step in range(nsteps + 1):
                lists = []
                for d in range(2):
                    if step < nsteps:
                        lists.append(prep_rounds(d, step))
                    if step >= 1 and RW_STAGE >= 6:
                        lists.append(chain_rounds(d, step - 1))
                interleave(lists)

            if RW_STAGE < 7:
                continue
            for ti in range(NT):
                t0 = ti * 128
                sm, B_sm = small[ti % 4]
                y, B_yy = ytok[:, ti, :], B_y[ti]
                yt, B_yt = yst[ti % 2]
                y3 = y.rearrange("p (h c) -> p h c", h=2)
                cp("act", sm[:, 6:8], bon[:, ti, :], [B_bon[ti]], [B_sm])
                OP("dve", lambda e, y3=y3, sm=sm: e.tensor_reduce(out=sm[:, 0:2], in_=y3, axis=AX.X, op=ALU.add), reads=[B_yy], writes=[B_sm])
                ts(sm[:, 0:2], sm[:, 0:2], 1.0 / 64, None, ALU.mult, None, [B_sm], [B_sm])
                for h in range(2):
                    ts(yt[:, h * 64:(h + 1) * 64], y[:, h * 64:(h + 1) * 64], sm[:, h:h + 1], None, ALU.subtract, None, [B_yy, B_sm], [B_yt])
                sq, B_sq = gts[ti % 2]
                act(sq[:], yt[:], AF.Square, [B_yt], [B_sq])
                OP("dve", lambda e, sq=sq, sm=sm: e.tensor_reduce(out=sm[:, 2:4], in_=sq[:].rearrange("p (h c) -> p h c", h=2), axis=AX.X,
                                                                op=ALU.add), reads=[B_sq], writes=[B_sm])
                ts(sm[:, 2:4], sm[:, 2:4], 1.0 / 64, 64e-5, ALU.mult, ALU.add, [B_sm], [B_sm])
                act(sm[:, 2:4], sm[:, 2:4], AF.Sqrt, [B_sm], [B_sm])
                OP("dve", lambda e, sm=sm: e.reciprocal(out=sm[:, 4:6], in_=sm[:, 2:4]), reads=[B_sm], writes=[B_sm])
                for h in range(2):
                    hsl = slice(h * 64, (h + 1) * 64)
                    stt(yt[:, hsl], yt[:, hsl], sm[:, 4 + h:5 + h], lnxg[:, hsl], ALU.mult, ALU.mult, [B_yt, B_sm, B_lnx], [B_yt])
                tt(yt[:], yt[:], lnxb[:], ALU.add, [B_yt, B_lnx], [B_yt])
                for h in range(2):
                    hsl = slice(h * 64, (h + 1) * 64)
                    stt(yt[:, hsl], vtok[:, ti, hsl], sm[:, 6 + h:7 + h], yt[:, hsl], ALU.mult, ALU.add, [B_vt[ti], B_sm, B_yt], [B_yt])
                bk, B_bk = G.next_bank()
                tr(B_bk, bk[:, 0:128], yt[:], [B_yt])
                gt, B_gt = gts[ti % 2]
                P.dma("sp", gt[:], restT[R_RWG + ch0:R_RWG + ch0 + 128, t0:t0 + 128], B_gt, reads=[B_restT], writes=[B_gt])
                act(gt[:], gt[:], AF.Silu, [B_gt], [B_gt])
                ot, B_ot = ost[ti % 2]
                tt(ot[:], bk[:, 0:128], gt[:], ALU.mult, [B_bk, B_gt], [B_ot])
                P.dma("pool", G.brT[2 * W + ch0:2 * W + ch0 + 128, t0:t0 + 128], ot[:], G.B_brT, reads=[B_ot])
        P.barrier()


def mk_helpers(G):
    P = G.P

    class H:
        pass
    H_ = H()

    def OP(eng, fn, reads=(), writes=()):
        return P.op(eng, fn, reads=reads, writes=writes)

    def mm(bk, o, lhsT, rhs, reads, start=True, stop=True):
        OP("pe", lambda e: e.matmul(o, lhsT=lhsT, rhs=rhs, start=start, stop=stop), reads=reads, writes=[bk])

    def tr(bk, o, in_, reads):
        OP("pe", lambda e: e.transpose(o, in_, G.ident[:]), reads=list(reads) + [G.B_ident], writes=[bk])

    def act(o, i, func, reads, writes, **kw):
        OP("act", lambda e: e.activation(out=o, in_=i, func=func, **kw), reads=reads, writes=writes)

    def tt(o, a, b, op, reads, writes, eng="dve"):
        OP(eng, lambda e: e.tensor_tensor(out=o, in0=a, in1=b, op=op), reads=reads, writes=writes)

    def ts(o, a, s1, s2, op0, op1, reads, writes, eng="dve"):
        if s2 is None:
            OP(eng, lambda e: e.tensor_scalar(out=o, in0=a, scalar1=s1, scalar2=None, op0=op0), reads=reads, writes=writes)
        else:
            OP(eng, lambda e: e.tensor_scalar(out=o, in0=a, scalar1=s1, scalar2=s2, op0=op0, op1=op1), reads=reads, writes=writes)

    def stt(o, a, sc, b, op0, op1, reads, writes):
        OP("dve", lambda e: e.scalar_tensor_tensor(out=o, in0=a, scalar=sc, in1=b, op0=op0, op1=op1), reads=reads, writes=writes)

    def cp(eng, o, i, reads, writes):
        if eng == "act":
            act(o, i, AF.Copy, reads, writes)
        else:
            OP(eng, lambda e: e.tensor_copy(out=o, in_=i), reads=reads, writes=writes)

    def memset(eng, o, val, writes):
        OP(eng, lambda e: e.memset(o, val), writes=writes)
    return OP, mm, tr, act, tt, ts, stt, cp, memset


R_NAG, R_PU, R_PG, R_MERGE = 0, 1024, 2048, 7296
PADW = 8 + 256 + 16 + 4096 + 16
OFF_C, OFF_L = 8, 8 + 256 + 16


def host_pool_invcnt():
    out = np.zeros((4, S), np.float32)
    for g, win in enumerate((2, 4, 8, 16)):
        for (o, T) in ((0, NCTX), (NCTX, SEQ)):
            t = np.arange(T)
            lo = np.maximum(t - win // 2, 0)
            hi = np.minimum(t + win // 2, T)
            out[g, o:o + T] = 1.0 / (hi - lo)
    return out


def phase_pool(G, layer):
    nc, P = G.nc, G.P
    OP, mm, tr, act, tt, ts, stt, cp, memset = mk_helpers(G)
    restT, B_restT = G.restT, G.B_restT
    with contextlib.ExitStack() as ps:
        def sb(name, shape, dt=F32):
            return _sb(ps, nc, name, shape, dt)
        A = sb("pA", [128, PADW]); B_A = Buf("pA")
        Bt = sb("pB", [128, PADW]); B_B = Buf("pB")
        Ct = sb("pC", [128, PADW]); B_C = Buf("pC")
        inv = sb("pinv", [128, S]); B_inv = Buf("pinv")
        diff = [(sb("pdiff%d" % i, [128, S], BF16), Buf("pdiff%d" % i)) for i in range(2)]
        gate = sb("pgate", [128, S]); B_gate = Buf("pgate")
        pw = sb("ppw", [128, 2, 256], BF16); B_pw = Buf("ppw")
        psc = sb("ppsc", [128, 2]); B_psc = Buf("ppsc")
        stg = [(sb("pstg%d" % i, [128, 512], BF16), Buf("pstg%d" % i)) for i in range(2)]
        for t_, b_ in ((A, B_A), (Bt, B_B), (Ct, B_C)):
            memset("pool", t_[:], 0.0, [b_])

        def zero_gaps(t_, b_):
            memset("pool", t_[:, 0:OFF_C], 0.0, [b_])
            memset("pool", t_[:, OFF_C + 256:OFF_L], 0.0, [b_])
            memset("pool", t_[:, OFF_L + 4096:PADW], 0.0, [b_])

        R0, R1 = 4, PADW - 8
        for g in range(4):
            win = (2, 4, 8, 16)[g]
            P.dma("sp", inv[:], G.pool_invcnt[g].partition_broadcast(128), None, writes=[B_inv])
            P.dma("pool", pw[:], G.pool_w[layer, g].rearrange("(k p) d -> p k d", p=128), None, writes=[B_pw])
            P.dma("sp", psc[:], G.pool_scale[layer, g * 256:(g + 1) * 256].rearrange("(k p) -> p k", p=128), None, writes=[B_psc])
            for cbi in range(2):
                cb = g * 2 + cbi
                row = R_PU + cb * 128
                P.dma("sp", A[:, OFF_C:OFF_C + 256], restT[row:row + 128, 0:256], None, reads=[B_restT], writes=[B_A])
                P.dma("sp", A[:, OFF_L:OFF_L + 4096], restT[row:row + 128, 256:S], None, reads=[B_restT], writes=[B_A], nowait=True)
                tt(Bt[:, R0:R1], A[:, R0 - 1:R1 - 1], A[:, R0:R1], ALU.add, [B_A], [B_B])
                cur, B_cur = Bt, B_B
                oth, B_oth = Ct, B_C
                sh = 1
                w_ = 2
                while w_ < win:
                    tt(oth[:, R0:R1], cur[:, R0 - sh:R1 - sh], cur[:, R0 + sh:R1 + sh], ALU.add, [B_cur], [B_oth])
                    cur, B_cur, oth, B_oth = oth, B_oth, cur, B_cur
                    sh *= 2
                    w_ *= 2
                dt_, B_d = diff[cbi]
                for (po, so, T) in ((OFF_C, 0, 256), (OFF_L, 256, 4096)):
                    tt(cur[:, po:po + T], cur[:, po:po + T], inv[:, so:so + T], ALU.mult, [B_cur, B_inv], [B_cur])
                    tt(dt_[:, so:so + T], cur[:, po:po + T], A[:, po:po + T], ALU.subtract, [B_cur, B_A], [B_d])
            for dch in range(2):
                row = R_PG + g * 256 + dch * 128
                P.dma("sp", gate[:], restT[row:row + 128, :], None, reads=[B_restT], writes=[B_gate])
                act(gate[:], gate[:], AF.Silu, [B_gate], [B_gate])
                for bi, t0 in enumerate(range(0, S, 512)):
                    tw = min(512, S - t0)
                    bk, B_bk = G.next_bank()
                    for k in range(2):
                        mm(B_bk, bk[:, 0:tw], pw[:, k, dch * 128:(dch + 1) * 128], diff[k][0][:, t0:t0 + tw], [B_pw, diff[k][1]],
                           start=(k == 0), stop=(k == 1))
                    st_, B_st = stg[bi % 2]
                    stt(st_[:, 0:tw], bk[:, 0:tw], psc[:, dch:dch + 1], gate[:, t0:t0 + tw], ALU.mult, ALU.mult,
                        [B_bk, B_psc, B_gate], [B_st])
                    orow = W + g * 256 + dch * 128
                    P.dma("pool", G.brT[orow:orow + 128, t0:t0 + tw], st_[:, 0:tw], None, reads=[B_st], writes=[])
        P.barrier()


def host_na_table(na_rpb):
    L = na_rpb.shape[0]
    NEG = np.float32(-30000.0)
    col = np.arange(64)
    cs = np.clip(col - 8, 0, 48)
    cmask = (col[:, None] >= cs[None, :]) & (col[:, None] < cs[None, :] + 16)
    coff = np.clip(col[:, None] - col[None, :] + 15, 0, 30)
    tab = np.full((L, 16, 2, 64, 18, 64), NEG, np.float32)
    for par in range(2):
        for j in range(16):
            ro = j - 1 + par
            if 0 <= ro <= 14:
                g = na_rpb[:, :, ro][:, :, coff]
                tab[:, :, par, :, j, :] = np.where(cmask[None, None], g, NEG)
    tab[:, :, 1, :, 16, :] = np.where(cmask[None, None], na_rpb[:, :, 3][:, :, coff], NEG)
    tab[:, :, 0, :, 17, :] = np.where(cmask[None, None], na_rpb[:, :, 10][:, :, coff], NEG)
    return np.ascontiguousarray(tab.reshape(L, 16, 128, 18, 64))


def phase_na(G, layer, do_ctx=True):
    nc, P = G.nc, G.P
    OP, mm, tr, act, tt, ts, stt, cp, memset = mk_helpers(G)
    restT, B_restT = G.restT, G.B_restT
    qkT, B_qkT, vaug, B_vaug = G.qkT, G.B_qkT, G.vaug, G.B_vaug
    with contextlib.ExitStack() as ps:
        def sb(name, shape, dt=F32):
            return _sb(ps, nc, name, shape, dt)
        V = sb("naV", [128, NT, 16 * 65], BF16); B_V = Buf("naV")
        vsrc = vaug.rearrange("(n p) h e -> p n (h e)", p=128)
        for i in range(0, NT, 6):
            j = min(NT, i + 6)
            P.dma("sp", V[:, i:j, :], vsrc[:, i:j, :], None, reads=[B_vaug], writes=[B_V], nowait=(i > 0))
        qs = [(sb("naq%d" % i, [64, S], BF16), Buf("naq%d" % i)) for i in range(2)]
        ks = [(sb("nak%d" % i, [64, S], BF16), Buf("nak%d" % i)) for i in range(2)]
        tabs = [(sb("natab%d" % i, [128, 18, 64]), Buf("natab%d" % i)) for i in range(2)]
        ytok = sb("naytok", [128, NT, 64]); B_yt = [Buf("nay%d" % i) for i in range(NT)]
        gate = sb("nagate", [64, S]); B_gate = Buf("nagate")
        sc = [(sb("nasc%d" % i, [128, 5, 64]), Buf("nasc%d" % i)) for i in range(2)]
        PT = [[(sb("naPT%d%d" % (p_, i), [128, 7, 128], BF16), Buf("naPT%d%d" % (p_, i))) for i in range(2)] for p_ in range(2)]
        for p_ in range(2):
            for i in range(2):
                memset("pool", PT[p_][i][0][:], 0.0, [PT[p_][i][1]])
        rden = [(sb("narden%d" % i, [128, 1]), Buf("narden%d" % i)) for i in range(2)]
        ost = [(sb("naost%d" % i, [64, 512], BF16), Buf("naost%d" % i)) for i in range(2)]
        ptc = [(sb("naptc%d" % i, [128, 2, 128], BF16), Buf("naptc%d" % i)) for i in range(2)]

        for h in range(16):
            q, B_q = qs[h % 2]
            k, B_k = ks[h % 2]
            tab, B_tab = tabs[h % 2]
            P.dma("sp", q[:], qkT[h * 64:(h + 1) * 64, :], None, reads=[B_qkT], writes=[B_q])
            P.dma("sp", k[:], qkT[W + h * 64:W + (h + 1) * 64, :], None, reads=[B_qkT], writes=[B_k])
            P.dma("sp", tab[:], G.na_tab[layer, h], None, writes=[B_tab])
            P.dma("sp", gate[:], restT[R_NAG + h * 64:R_NAG + (h + 1) * 64, :], None, reads=[B_restT], writes=[B_gate])
            act(gate[:], gate[:], AF.Silu, [B_gate], [B_gate])
            vh = slice(h * 65, (h + 1) * 65)
            if do_ctx:
                for qt in range(2):
                    bk, B_bk = G.next_bank()
                    for kt in range(2):
                        mm(B_bk, bk[:, kt * 128:(kt + 1) * 128], k[:, kt * 128:(kt + 1) * 128], q[:, qt * 128:(qt + 1) * 128], [B_k, B_q])
                    pc, B_pc = ptc[qt]
                    act(pc[:].rearrange("p a b -> p (a b)"), bk[:, 0:256], AF.Exp, [B_bk], [B_pc], scale=0.125)
                    bk2, B_bk2 = G.next_bank()
                    for kt in range(2):
                        mm(B_bk2, bk2[:, 0:65], pc[:, kt, :], V[:, kt, vh], [B_pc, B_V], start=(kt == 0), stop=(kt == 1))
                    rd, B_rd = rden[qt]
                    OP("dve", lambda e, rd=rd, bk2=bk2: e.reciprocal(out=rd[:], in_=bk2[:, 64:65]), reads=[B_bk2], writes=[B_rd])
                    ts(ytok[:, qt, :], bk2[:, 0:64], rd[:, 0:1], None, ALU.mult, None, [B_bk2, B_rd], [B_yt[qt]])
            def emit_scores(rp):
                plan = []
                for par in range(2):
                    r = 2 * rp + par
                    r0 = min(max(r - 4, 0), 56)
                    t_lo = r0 // 2
                    t_hi = (r0 + 7) // 2
                    ntl = t_hi - t_lo + 1
                    bk, B_bk = G.next_bank()
                    qsl = q[:, 256 + r * 64:256 + (r + 1) * 64]
                    for m in range(ntl):
                        tk = 256 + (t_lo + m) * 128
                        mm(B_bk, bk[:, m * 64:(m + 1) * 64], k[:, tk:tk + 128], qsl, [B_k, B_q])
                    for m in range(2):
                        mm(B_bk, bk[:, (5 + m) * 64:(6 + m) * 64], k[:, m * 128:(m + 1) * 128], qsl, [B_k, B_q])
                    s_, B_s = sc[par]
                    j0 = 2 * t_lo - r + 8
                    bk3 = bk[:, 0:ntl * 64].rearrange("p (a b) -> p a b", b=64)
                    if ntl == 4:
                        stt(s_[:, 0:4, :], bk3, 0.125, tab[:, j0:j0 + 7:2, :], ALU.mult, ALU.add, [B_bk, B_tab], [B_s])
                    else:
                        stt(s_[:, 0:1, :], bk3[:, 0:1, :], 0.125, tab[:, 16:17, :], ALU.mult, ALU.add, [B_bk, B_tab], [B_s])
                        stt(s_[:, 1:4, :], bk3[:, 1:4, :], 0.125, tab[:, j0 + 2:j0 + 7:2, :], ALU.mult, ALU.add, [B_bk, B_tab], [B_s])
                        stt(s_[:, 4:5, :], bk3[:, 4:5, :], 0.125, tab[:, 17:18, :], ALU.mult, ALU.add, [B_bk, B_tab], [B_s])
                    pt, B_pt = PT[par][rp % 2]
                    qo = par * 64
                    act(pt[:, 0:ntl, qo:qo + 64], s_[:, 0:ntl, :], AF.Exp, [B_s], [B_pt])
                    act(pt[:, 5:7, qo:qo + 64], bk[:, 320:448].rearrange("p (a b) -> p a b", b=64), AF.Exp, [B_bk], [B_pt], scale=0.125)
                    plan.append((pt, B_pt, t_lo, ntl))
                return plan

            def emit_pv(rp, plan):
                bk2, B_bk2 = G.next_bank()
                nmm = 0
                tot = sum(p_[3] + 2 for p_ in plan)
                for (pt, B_pt, t_lo, ntl) in plan:
                    for m in range(ntl):
                        mm(B_bk2, bk2[:, 0:65], pt[:, m, :], V[:, 2 + t_lo + m, vh], [B_pt, B_V], start=(nmm == 0), stop=(nmm == tot - 1))
                        nmm += 1
                    for m in range(2):
                        mm(B_bk2, bk2[:, 0:65], pt[:, 5 + m, :], V[:, m, vh], [B_pt, B_V], start=(nmm == 0), stop=(nmm == tot - 1))
                        nmm += 1
                rd, B_rd = rden[rp % 2]
                OP("dve", lambda e, rd=rd, bk2=bk2: e.reciprocal(out=rd[:], in_=bk2[:, 64:65]), reads=[B_bk2], writes=[B_rd])
                ts(ytok[:, 2 + rp, :], bk2[:, 0:64], rd[:, 0:1], None, ALU.mult, None, [B_bk2, B_rd], [B_yt[2 + rp]])

            plans = {0: emit_scores(0)}
            for rp in range(32):
                if rp + 1 < 32:
                    plans[rp + 1] = emit_scores(rp + 1)
                emit_pv(rp, plans.pop(rp))
            t_first = 0 if do_ctx else 2
            for ti0 in range(t_first, NT, 4):
                n = min(4, NT - ti0)
                bk, B_bk = G.next_bank()
                for i in range(n):
                    tr(B_bk, bk[0:64, i * 128:(i + 1) * 128], ytok[:, ti0 + i, :], [B_yt[ti0 + i]])
                o_, B_o = ost[(ti0 // 4) % 2]
                tt(o_[:, 0:n * 128], bk[0:64, 0:n * 128], gate[:, ti0 * 128:(ti0 + n) * 128], ALU.mult, [B_bk, B_gate], [B_o])
                P.dma("pool", G.brT[h * 64:(h + 1) * 64, ti0 * 128:(ti0 + n) * 128], o_[:, 0:n * 128], None, reads=[B_o], writes=[])
        P.barrier()


def phase_out(G, layer, last):
    nc, P = G.nc, G.P
    OP, mm, tr, act, tt, ts, stt, cp, memset = mk_helpers(G)
    restT, B_restT = G.restT, G.B_restT
    mT, B_mT = G.mT, G.B_mT
    t_first = 2 if last else 0
    with contextlib.ExitStack() as ps:
        def sb(name, shape, dt=F32):
            return _sb(ps, nc, name, shape, dt)
        wbr = sb("wbr", [128, 24, D], BF16); B_wbrs = [Buf("wbr%d" % i) for i in range(4)]
        wsrc = G.w_branch[layer].rearrange("b (k p) d -> p (b k) d", p=128)
        for cb4 in range(4):
            csl = slice(cb4 * 512, (cb4 + 1) * 512)
            P.dma("pool", wbr[:, 0:12, csl], wsrc[:, 0:12, csl], None, writes=[B_wbrs[cb4]])
            P.dma("pool", wbr[:, 12:24, csl], wsrc[:, 12:24, csl], None, writes=[B_wbrs[cb4]], nowait=True)
        bts = [(sb("bT%d" % i, [128, 24, 512], BF16), Buf("bT%d" % i)) for i in range(2)]
        lgs = [(sb("lg%d" % i, [128, 512]), Buf("lg%d" % i)) for i in range(12)]
        macc = [(sb("macc%d" % i, [128, 512]), Buf("macc%d" % i)) for i in range(2)]
        mst = [(sb("mst%d" % i, [128, 512], BF16), Buf("mst%d" % i)) for i in range(2)]
        bsrc = G.brT.rearrange("(c p) t -> p c t", p=128)
        for bi, t0 in enumerate(range(t_first * 128, S, 512)):
            tw = min(512, S - t0)
            bt, B_bt = bts[bi % 2]
            P.dma("sp", bt[:, 0:12, 0:tw], bsrc[:, 0:12, t0:t0 + tw], None, reads=[G.B_brT], writes=[B_bt])
            P.dma("sp", bt[:, 12:24, 0:tw], bsrc[:, 12:24, t0:t0 + tw], None, reads=[G.B_brT], writes=[B_bt], nowait=True)
            for fo in range(16):
                ma, B_ma = macc[fo % 2]
                for kb in range(3):
                    lg, B_lg = lgs[(fo * 3 + kb) % 12]
                    row = R_MERGE + kb * D + fo * 128
                    P.dma("sp", lg[:, 0:tw], restT[row:row + 128, t0:t0 + tw], None, reads=[B_restT], writes=[B_lg])
                    act(lg[:, 0:tw], lg[:, 0:tw], AF.Sigmoid, [B_lg], [B_lg])
                    bk, B_bk = G.next_bank()
                    for kc in range(8):
                        mm(B_bk, bk[:, 0:tw], wbr[:, kb * 8 + kc, fo * 128:(fo + 1) * 128], bt[:, kb * 8 + kc, 0:tw], [B_wbrs[fo // 4], B_bt],
                           start=(kc == 0), stop=(kc == 7))
                    if kb == 0:
                        tt(ma[:, 0:tw], bk[:, 0:tw], lg[:, 0:tw], ALU.mult, [B_bk, B_lg], [B_ma])
                    else:
                        tt(lg[:, 0:tw], bk[:, 0:tw], lg[:, 0:tw], ALU.mult, [B_bk, B_lg], [B_lg])
                        if kb == 1:
                            tt(ma[:, 0:tw], ma[:, 0:tw], lg[:, 0:tw], ALU.add, [B_ma, B_lg], [B_ma], eng="pool")
                        else:
                            ms, B_ms = mst[fo % 2]
                            tt(ms[:, 0:tw], ma[:, 0:tw], lg[:, 0:tw], ALU.add, [B_ma, B_lg], [B_ms], eng="pool")
                            P.dma("pool", mT[fo * 128:(fo + 1) * 128, t0:t0 + tw], ms[:, 0:tw], None, reads=[B_ms], writes=[])
        P.barrier()
    with contextlib.ExitStack() as ps:
        def sb(name, shape, dt=F32):
            return _sb(ps, nc, name, shape, dt)
        wo = sb("wo", [128, 16, D], BF16); B_wos = [Buf("wo%d" % i) for i in range(4)]
        wsrc = G.w_out[layer].rearrange("(k p) d -> p k d", p=128)
        for cb4 in range(4):
            csl = slice(cb4 * 512, (cb4 + 1) * 512)
            P.dma("pool", wo[:, :, csl], wsrc[:, :, csl], None, writes=[B_wos[cb4]])
        gbc = sb("gbc", [128, 2, D]); B_gbc = Buf("gbc")
        P.dma("sp", gbc[:, 0, :], G.gate_d[layer, 0].partition_broadcast(128), None, reads=[G.B_gate_d], writes=[B_gbc])
        P.dma("sp", gbc[:, 1, :], G.gate_d[layer, 1].partition_broadcast(128), None, reads=[G.B_gate_d], writes=[B_gbc], nowait=True)
        if last:
            fg = sb("fg", [128, D]); B_fg = Buf("fg")
            P.dma("sp", fg[:], G.final_g.partition_broadcast(128), None, writes=[B_fg])
        mts = [(sb("mt%d" % i, [128, 16, 128], BF16), Buf("mt%d" % i)) for i in range(2)]
        xts = [(sb("xo%d" % i, [128, D]), Buf("xo%d" % i)) for i in range(2)]
        xns = [(sb("xnw%d" % i, [128, D]), Buf("xnw%d" % i)) for i in range(2)]
        junk = sb("ojunk", [128, D], BF16); B_junk = Buf("ojunk")
        stat = [(sb("ostat%d" % i, [128, 2]), Buf("ostat%d" % i)) for i in range(2)]
        msrc = mT.rearrange("(k p) t -> p k t", p=128)
        for ti in range(t_first, NT):
            mt, B_mt = mts[ti % 2]
            xt, B_xt = xts[ti % 2]
            xn, B_xn = xns[ti % 2]
            isctx = 1 if ti < 2 else 0
            P.dma("sp", mt[:], msrc[:, :, ti * 128:(ti + 1) * 128], None, reads=[B_mT], writes=[B_mt])
            P.dma("sp", xt[:], G.x_src(layer, ti), None, reads=[G.B_xs], writes=[B_xt])
            for cbk in range(4):
                bk, B_bk = G.next_bank()
                for kc in range(16):
                    mm(B_bk, bk[:, :], mt[:, kc, :], wo[:, kc, cbk * 512:(cbk + 1) * 512], [B_mt, B_wos[cbk]], start=(kc == 0), stop=(kc == 15))
                csl = slice(cbk * 512, (cbk + 1) * 512)
                tt(xn[:, csl], bk[:, :], gbc[:, isctx, csl], ALU.mult, [B_bk, B_gbc], [B_xn])
                tt(xn[:, csl], xn[:, csl], xt[:, csl], ALU.add, [B_xn, B_xt], [B_xn], eng="pool")
            if not last:
                P.dma("pool", G.xs[ti * 128:(ti + 1) * 128, :], xn[:], None, reads=[B_xn], writes=[])
            else:
                st, B_st = stat[ti % 2]
                act(junk[:], xn[:], AF.Square, [B_xn], [B_junk, B_st], scale=float(D) ** -0.5, accum_out=st[:, 0:1])
                ts(st[:, 0:1], st[:, 0:1], 1e-6, None, ALU.add, None, [B_st], [B_st])
                act(st[:, 0:1], st[:, 0:1], AF.Sqrt, [B_st], [B_st])
                OP("dve", lambda e, st=st: e.reciprocal(out=st[:, 1:2], in_=st[:, 0:1]), reads=[B_st], writes=[B_st])
                stt(xn[:], xn[:], st[:, 1:2], fg[:], ALU.mult, ALU.mult, [B_xn, B_st, B_fg], [B_xn])
                P.dma("pool", G.out[(ti - 2) * 128:(ti - 1) * 128, :], xn[:], None, reads=[B_xn], writes=[])
        P.barrier()


ALL_PHASES = ("p0", "p1", "rw", "pool", "na", "p3")


def build_program(n_layers=DEPTH, debug_outs=(), phases=ALL_PHASES, ext_in=(), only_layer=None):
    nc = bass.Bass("TRN2", target_bir_lowering=False)
    es = contextlib.ExitStack()
    with es:
        _build(nc, es, n_layers, debug_outs, phases, ext_in, only_layer)
    return nc


def _build(nc, es, n_layers, debug_outs, phases, ext_in, only_layer=None):
    P = Prog(nc, es)
    allow = es.enter_context(nc.allow_non_contiguous_dma(reason="small strided param loads"))

    def din(name, shape, dt=F32):
        return nc.dram_tensor(name, list(shape), dt, kind="ExternalInput").ap()

    def dscr(name, shape, dt=F32):
        kind = "ExternalOutput" if name in debug_outs else ("ExternalInput" if name in ext_in else "Internal")
        return nc.dram_tensor(name, list(shape), dt, kind=kind).ap()

    if "p0" in phases or "p1" in phases:
        x_in = din("x", [SEQ, D])
        ctx_in = din("ctx", [NCTX, D])
        c_in = din("c", [D])
        cctx_in = din("c_ctx", [D])
        norm_g = din("norm_g", [DEPTH, D])
        w_mod = din("w_mod", [DEPTH, D, 3 * D])
        b_mod = din("b_mod", [DEPTH, 3 * D])
        w_in = din("w_in", [DEPTH, D, D_IN])
    ident_in = din("ident", [128, 128])
    sel_in = din("sel", [2, 2, 128])
    out = nc.dram_tensor("out", [SEQ, D], F32, kind="ExternalOutput").ap()

    qkT = dscr("qkT", [2048, S], BF16)
    vaug = dscr("vaug", [S, 16, 65], BF16)
    restT = dscr("restT", [D_IN - 3072, S], F32)
    B_qkT, B_vaug, B_restT = Buf("qkT"), Buf("vaug"), Buf("restT")

    banks = []
    for i in range(8):
        t = es.enter_context(nc.psum_tensor("bank%d" % i, [128, 512], F32))
        banks.append((t, Buf("bank%d" % i, excl=True)))
    bank_rr = [0]

    def next_bank():
        b = banks[bank_rr[0] % 8]
        bank_rr[0] += 1
        return b

    ident = _sb(es, nc, "ident", [128, 128], F32)
    B_ident = Buf("ident")
    P.dma("sp", ident[:], ident_in[:, :], B_ident, writes=[B_ident])
    sel = _sb(es, nc, "sel", [2, 2, 128], F32)
    B_sel = Buf("sel")
    P.dma("sp", sel[:], sel_in[:, :, :], B_sel, writes=[B_sel])
    gs_col = _sb(es, nc, "gs_col", [128, 16, 2], F32)
    sh_col = _sb(es, nc, "sh_col", [128, 16, 2], F32)
    B_gs, B_sh = Buf("gs"), Buf("sh")
    gate_d = dscr("gate_d", [DEPTH, 2, D], F32)
    B_gate_d = Buf("gate_d")

    G = type("Ctx", (), {})()
    G.nc, G.P, G.next_bank, G.ident, G.B_ident, G.restT, G.B_restT = nc, P, next_bank, ident, B_ident, restT, B_restT
    G.din, G.dscr = din, dscr
    G.phases = phases
    setup_consts(G, es)
    xs = dscr("xs", [S, D], F32)
    G.xs, G.B_xs = xs, Buf("xs")
    G.qkT, G.B_qkT, G.vaug, G.B_vaug = qkT, B_qkT, vaug, B_vaug
    G.gate_d, G.B_gate_d = gate_d, B_gate_d
    G.out = out
    G.mT, G.B_mT = dscr("mT", [D, S], BF16), Buf("mT")
    if "pool" in phases:
        G.pool_invcnt = din("pool_invcnt", [4, S])
        G.pool_w = din("pool_w", [DEPTH, 4, 256, 256])
        G.pool_scale = din("pool_scale", [DEPTH, W])
    if "na" in phases:
        G.na_tab = din("na_tab", [DEPTH, 16, 128, 18, 64])
    if "p3" in phases:
        G.w_branch = din("w_branch", [DEPTH, 3, W, D])
        G.w_out = din("w_out", [DEPTH, D, D])
        G.final_g = din("final_g", [D])
        if "p1" not in phases:
            x_in = din("x", [SEQ, D])
            ctx_in = din("ctx", [NCTX, D])

        def x_src(layer, ti):
            if layer == 0:
                return ctx_in[ti * 128:(ti + 1) * 128, :] if ti < 2 else x_in[(ti - 2) * 128:(ti - 1) * 128, :]
            return xs[ti * 128:(ti + 1) * 128, :]
        G.x_src = x_src
    if "rw" in phases:
        G.rwpar = din("rwpar", [DEPTH, 8, 128, NPAR])
        G.rw_w2 = din("rw_w2", [DEPTH, 2, 64, W])
        G.rw_a2 = din("rw_a2", [DEPTH, 2, 64, W])
        G.lnx_g = din("rw_lnx_g", [DEPTH, W])
        G.lnx_b = din("rw_lnx_b", [DEPTH, W])
    brT = dscr("brT", [3 * W, S], BF16)
    G.brT, G.B_brT = brT, Buf("brT")
    for layer in range(n_layers):
        if only_layer is not None and layer != only_layer:
            continue
        if "p0" in G.phases:
            with contextlib.ExitStack() as ps:
                condT = _sb(ps, nc, "condT", [128, 16, 2], F32)
                B_cond = Buf("condT")
                P.dma("sp", condT[:, :, 0], c_in.rearrange("(k p) -> p k", p=128), B_cond, writes=[B_cond])
                P.dma("sp", condT[:, :, 1], cctx_in.rearrange("(k p) -> p k", p=128), B_cond, writes=[B_cond], nowait=True)
                scond = _sb(ps, nc, "scond", [128, 16, 2], F32)
                B_scond = Buf("scond")
                P.op("act", lambda e: e.activation(out=scond[:], in_=condT[:], func=AF.Silu),
                     reads=[B_cond], writes=[B_scond])
                gcol = _sb(ps, nc, "gcol", [128, 16], F32)
                B_gcol = Buf("gcol")
                P.dma("sp", gcol[:], norm_g[layer].rearrange("(k p) -> p k", p=128), B_gcol, writes=[B_gcol])
                modrow = _sb(ps, nc, "modrow", [2, 3 * D], F32)
                B_modrow = Buf("modrow")
                bmod2 = _sb(ps, nc, "bmod2", [2, 3 * D], F32)
                B_bmod = Buf("bmod2")
                P.dma("sp", bmod2[0:1, :], b_mod[layer:layer + 1, :], B_bmod, writes=[B_bmod])
                P.dma("sp", bmod2[1:2, :], b_mod[layer:layer + 1, :], B_bmod, writes=[B_bmod], nowait=True)
                wbufs = []
                for i in range(2):
                    wbufs.append((_sb(ps, nc, "wmod%d" % i, [128, 16, 512], F32), Buf("wmod%d" % i)))
                for cb in range(12):
                    wt, B_w = wbufs[cb % 2]
                    src = w_mod[layer, :, cb * 512:(cb + 1) * 512].rearrange("(k p) c -> p k c", p=128)
                    P.dma("sp", wt[:, 0:8, :], src[:, 0:8, :], B_w, writes=[B_w])
                    P.dma("sp", wt[:, 8:16, :], src[:, 8:16, :], B_w, writes=[B_w], nowait=True)
                    bk, B_bk = next_bank()
                    for k in range(16):
                        P.op("pe", lambda e, k=k, wt=wt, bk=bk: e.matmul(bk[0:2, :], lhsT=scond[:, k, :], rhs=wt[:, k, :],
                                                                        start=(k == 0), stop=(k == 15)),
                             reads=[B_scond, B_w], writes=[B_bk])
                    P.op("dve", lambda e, bk=bk, cb=cb: e.tensor_tensor(out=modrow[:, cb * 512:(cb + 1) * 512], in0=bk[0:2, :],
                                                                       in1=bmod2[:, cb * 512:(cb + 1) * 512], op=ALU.add),
                         reads=[B_bk, B_bmod], writes=[B_modrow])
                bk, B_bk = next_bank()
                for which in range(2):
                    for k in range(16):
                        c0 = which * D + k * 128
                        o0 = (which * 16 + k) * 2
                        P.op("pe", lambda e, c0=c0, o0=o0, bk=bk: e.matmul(bk[:, o0:o0 + 2], lhsT=modrow[0:2, c0:c0 + 128],
                                                                          rhs=ident[0:2, 0:2], start=True, stop=True),
                             reads=[B_modrow, B_ident], writes=[B_bk])
                P.op("act", lambda e, bk=bk: e.activation(out=sh_col[:].rearrange("p k t -> p (k t)"), in_=bk[:, 0:32], func=AF.Copy),
                     reads=[B_bk], writes=[B_sh])
                tmpc = _sb(ps, nc, "tmpc", [128, 16, 2], F32)
                B_tmpc = Buf("tmpc")
                P.op("dve", lambda e, bk=bk: e.tensor_scalar(out=tmpc[:].rearrange("p k t -> p (k t)"), in0=bk[:, 32:64], scalar1=1.0,
                                                            scalar2=None, op0=ALU.add),
                     reads=[B_bk], writes=[B_tmpc])
                for t in range(2):
                    P.op("dve", lambda e, t=t: e.tensor_tensor(out=gs_col[:, :, t], in0=tmpc[:, :, t], in1=gcol[:], op=ALU.mult),
                         reads=[B_tmpc, B_gcol], writes=[B_gs])
                P.dma("sp", gate_d[layer], modrow[0:2, 2 * D:3 * D], B_gate_d, reads=[B_modrow])
                P.barrier()

        if "p1" in G.phases:
            with contextlib.ExitStack() as ps:
                hT = _sb(ps, nc, "hT", [128, 16, S], BF16)
                B_hT = [Buf("hT%d" % i) for i in range(NT)]
                ps_outer = ps
                ps = contextlib.ExitStack()
                ps.__enter__()
                xts = [(_sb(ps, nc, "xt%d" % i, [128, D], F32), Buf("xt%d" % i)) for i in range(2)]
                xns = [(_sb(ps, nc, "xn%d" % i, [128, D], F32), Buf("xn%d" % i)) for i in range(2)]
                junk = _sb(ps, nc, "junk", [128, D], BF16)
                B_junk = Buf("junk")
                stat = [(_sb(ps, nc, "stat%d" % i, [128, 2], F32), Buf("stat%d" % i)) for i in range(2)]
                for ti in range(NT):
                    xt, B_xt = xts[ti % 2]
                    xn, B_xn = xns[ti % 2]
                    st, B_st = stat[ti % 2]
                    isctx = 1 if ti < 2 else 0
                    if layer == 0:
                        src = ctx_in[ti * 128:(ti + 1) * 128, :] if ti < 2 else x_in[(ti - 2) * 128:(ti - 1) * 128, :]
                    else:
                        src = xs[ti * 128:(ti + 1) * 128, :]
                    P.dma("sp", xt[:], src, B_xt, writes=[B_xt])
                    P.op("act", lambda e, xt=xt, st=st: e.activation(out=junk[:], in_=xt[:], func=AF.Square, scale=float(D) ** -0.5,
                                                                   accum_out=st[:, 0:1]),
                         reads=[B_xt], writes=[B_junk, B_st])
                    P.op("dve", lambda e, st=st: e.tensor_scalar(out=st[:, 0:1], in0=st[:, 0:1], scalar1=1e-6, scalar2=None,
                                                               op0=ALU.add),
                         reads=[B_st], writes=[B_st])
                    P.op("act", lambda e, st=st: e.activation(out=st[:, 0:1], in_=st[:, 0:1], func=AF.Sqrt),
                         reads=[B_st], writes=[B_st])
                    P.op("dve", lambda e, st=st: e.reciprocal(out=st[:, 1:2], in_=st[:, 0:1]),
                         reads=[B_st], writes=[B_st])
                    P.op("act", lambda e, xt=xt, xn=xn, st=st: e.activation(out=xn[:], in_=xt[:], func=AF.Copy, scale=st[:, 1:2]),
                         reads=[B_xt, B_st], writes=[B_xn])
                    for g in range(4):
                        bk, B_bk = next_bank()
                        for j in range(4):
                            k = g * 4 + j
                            P.op("pe", lambda e, k=k, j=j, bk=bk, xn=xn: e.transpose(bk[:, j * 128:(j + 1) * 128],
                                                                                   xn[:, k * 128:(k + 1) * 128], ident[:]),
                                 reads=[B_xn, B_ident], writes=[B_bk])
                        for j in range(4):
                            k = g * 4 + j
                            eng = "act" if (j % 2 == 0) else "dve"
                            o = hT[:, k, ti * 128:(ti + 1) * 128]
                            i_ = bk[:, j * 128:(j + 1) * 128]
                            if eng == "act":
                                P.op("act", lambda e, o=o, i_=i_, k=k, isctx=isctx: e.activation(
                                    out=o, in_=i_, func=AF.Identity, scale=gs_col[:, k, isctx:isctx + 1],
                                    bias=sh_col[:, k, isctx:isctx + 1]),
                                    reads=[B_bk, B_gs, B_sh], writes=[B_hT[ti]])
                            else:
                                P.op("dve", lambda e, o=o, i_=i_, k=k, isctx=isctx: e.tensor_scalar(
                                    out=o, in0=i_, scalar1=gs_col[:, k, isctx:isctx + 1], scalar2=sh_col[:, k, isctx:isctx + 1],
                                    op0=ALU.mult, op1=ALU.add),
                                    reads=[B_bk, B_gs, B_sh], writes=[B_hT[ti]])

                P.barrier()
                ps.__exit__(None, None, None)
                ps = contextlib.ExitStack()
                ps.__enter__()
                wbs = [(_sb(ps, nc, "win%d" % i, [128, 16, 512], BF16), Buf("win%d" % i)) for i in range(2)]
                ost = [(_sb(ps, nc, "ost%d" % i, [128, 512], F32), Buf("ost%d" % i)) for i in range(4)]
                ostb = [(_sb(ps, nc, "ostb%d" % i, [128, 512], BF16), Buf("ostb%d" % i)) for i in range(4)]
                vst = [(_sb(ps, nc, "vst%d" % i, [128, 8, 65], BF16), Buf("vst%d" % i)) for i in range(2)]
                for i in range(2):
                    P.op("pool", lambda e, i=i: e.memset(vst[i][0][:], 1.0), writes=[vst[i][1]])
                tblocks = [(i * 512, 512) for i in range(8)] + [(4096, 256)]
                n_cb = (D_IN + 511) // 512
                evac_rr = [0]
                sub_rr = [0]
                for cb in range(n_cb):
                    c0 = cb * 512
                    cw = min(512, D_IN - c0)
                    wt, B_w = wbs[cb % 2]
                    src = w_in[layer, :, c0:c0 + cw].rearrange("(k p) c -> p k c", p=128)
                    P.dma("pool", wt[:, 0:8, 0:cw], src[:, 0:8, :], B_w, writes=[B_w])
                    P.dma("pool", wt[:, 8:16, 0:cw], src[:, 8:16, :], B_w, writes=[B_w], nowait=True)
                    if 2048 <= c0 < 3072:
                        hg = (c0 - 2048) // 512
                        for ti in range(NT):
                            bk, B_bk = next_bank()
                            for k in range(16):
                                P.op("pe", lambda e, k=k, ti=ti, bk=bk, wt=wt: e.matmul(
                                    bk[:, :], lhsT=hT[:, k, ti * 128:(ti + 1) * 128], rhs=wt[:, k, :], start=(k == 0), stop=(k == 15)),
                                    reads=[B_hT[ti], B_w], writes=[B_bk])
                            vs, B_vs = vst[ti % 2]
                            eng = "act" if evac_rr[0] % 2 == 0 else "dve"
                            evac_rr[0] += 1
                            o = vs[:, :, 0:64]
                            i_ = bk[:, :].rearrange("p (h d) -> p h d", h=8)
                            if eng == "act":
                                P.op("act", lambda e, o=o, i_=i_: e.activation(out=o, in_=i_, func=AF.Copy), reads=[B_bk], writes=[B_vs])
                            else:
                                P.op("dve", lambda e, o=o, i_=i_: e.tensor_copy(out=o, in_=i_), reads=[B_bk], writes=[B_vs])
                            P.dma("sp", vaug[ti * 128:(ti + 1) * 128, hg * 8:(hg + 1) * 8, :], vs[:], B_vaug, reads=[B_vs], writes=[])
                        continue
                    for j in range(cw // 128):
                        is_qk = c0 < 2048
                        col = c0 + j * 128
                        ctx_needed = (1024 <= col < 2048) or (6144 <= col < 9216) or (10240 <= col < 10368)
                        tbl = tblocks if (layer < DEPTH - 1 or ctx_needed) else ([(256, 256)] + tblocks[1:])
                        for tb, (t0, tw) in enumerate(tbl):
                            bk, B_bk = next_bank()
                            for k in range(16):
                                P.op("pe", lambda e, k=k, j=j, bk=bk, wt=wt, t0=t0, tw=tw: e.matmul(
                                    bk[:, 0:tw], lhsT=wt[:, k, j * 128:(j + 1) * 128], rhs=hT[:, k, t0:t0 + tw],
                                    start=(k == 0), stop=(k == 15)),
                                    reads=[B_w] + B_hT[t0 // 128:(t0 + tw) // 128], writes=[B_bk])
                            stg, B_stg = (ostb if is_qk else ost)[sub_rr[0] % 4]
                            sub_rr[0] += 1
                            eng = "act" if evac_rr[0] % 2 == 0 else "dve"
                            evac_rr[0] += 1
                            o = stg[:, 0:tw]
                            i_ = bk[:, 0:tw]
                            if eng == "act":
                                P.op("act", lambda e, o=o, i_=i_: e.activation(out=o, in_=i_, func=AF.Copy), reads=[B_bk], writes=[B_stg])
                            else:
                                P.op("dve", lambda e, o=o, i_=i_: e.tensor_copy(out=o, in_=i_), reads=[B_bk], writes=[B_stg])
                            if is_qk:
                                P.dma("sp", qkT[col:col + 128, t0:t0 + tw], stg[:, 0:tw], B_qkT, reads=[B_stg])
                            else:
                                r0 = col - 3072
                                P.dma("sp", restT[r0:r0 + 128, t0:t0 + tw], stg[:, 0:tw], B_restT, reads=[B_stg])
                P.barrier()
                ps.__exit__(None, None, None)
                ps = ps_outer
                P.barrier()

        if "rw" in G.phases:
            phase_rwkv(G, layer)
        if "pool" in G.phases:
            phase_pool(G, layer)
        if "na" in G.phases:
            phase_na(G, layer, do_ctx=(layer < DEPTH - 1))
        if "p3" in G.phases:
            phase_out(G, layer, last=(layer == DEPTH - 1))

    P.barrier()
    print("inst counts", P.ninst, "nsem", P.nsem)


def kernel(**inputs):
    inp = {k: np.asarray(v) for k, v in inputs.items()}
    shared = dict(host_consts())
    for k in ("c_ctx", "norm_g", "w_mod", "b_mod", "w_in", "rw_w2", "rw_a2", "rw_lnx_g", "rw_lnx_b", "pool_w", "pool_scale",
              "w_branch", "w_out", "final_g"):
        shared[k] = np.ascontiguousarray(inp[k], dtype=np.float32)
    shared["rwpar"] = pack_rwpar(inp)
    shared["pool_invcnt"] = host_pool_invcnt()
    shared["na_tab"] = host_na_table(np.asarray(inp["na_rpb"], np.float32))
    nb = inp["x"].shape[0]
    in_maps = []
    for b in range(nb):
        m = dict(shared)
        m["x"] = np.ascontiguousarray(inp["x"][b], dtype=np.float32)
        m["ctx"] = np.ascontiguousarray(inp["ctx"][b], dtype=np.float32)
        m["c"] = np.ascontiguousarray(inp["c"][b], dtype=np.float32)
        in_maps.append(m)
    nc = build_program()
    res = run_bass_kernel_spmd(nc, in_maps, core_ids=list(range(nb)))
    return np.stack([np.asarray(res.results[b]["out"], dtype=np.float32) for b in range(nb)], axis=0)
```

```python
import contextlib
import os
import numpy as np
import concourse.bass as bass
import concourse.mybir as mybir
from concourse.bass_utils import run_bass_kernel_spmd

F32 = mybir.dt.float32
F32R = mybir.dt.float32r
BF16 = mybir.dt.bfloat16
AF = mybir.ActivationFunctionType
ALU = mybir.AluOpType
AX = mybir.AxisListType

D = 2048
SEQ = 4096
NCTX = 256
S = SEQ + NCTX
NT = S // 128
W = 1024
DEPTH = 2
D_IN = 16512
NCORES = 4
SAME_ENGINE_SYNC = bool(int(os.environ.get("SAME_ENGINE_SYNC", "1")))


class Ev:
    __slots__ = ("sem", "key", "val")

    def __init__(self, sem, key, val):
        self.sem, self.key, self.val = sem, key, val


class Buf:
    __slots__ = ("name", "w", "rs", "excl")

    def __init__(self, name, excl=False):
        self.name = name
        self.w = []
        self.rs = {}
        self.excl = excl


class Prog:
    N_LANES = {"sp": 24, "pool": 8, "act": 4}

    def __init__(self, nc, es):
        self.nc = nc
        self.es = es
        self.h = {"pe": nc.tensor, "act": nc.scalar, "dve": nc.vector, "pool": nc.gpsimd, "sp": nc.sync}
        self.sem = {e: es.enter_context(nc.semaphore("s_" + e)) for e in self.h}
        self.cnt = {e: 0 for e in self.h}
        self.seen = {e: {} for e in self.h}
        self.lanes = {}
        self.lane_rr = {}
        self.nsem = 0
        self.ninst = {e: 0 for e in self.h}

    def _lane(self, q):
        if q not in self.lanes:
            self.lanes[q] = []
            for i in range(self.N_LANES[q]):
                self.nsem += 1
                key = "d_%s%d" % (q, i)
                self.lanes[q].append([self.es.enter_context(self.nc.semaphore(key)), key, 0])
            self.lane_rr[q] = 0
        ln = self.lanes[q][self.lane_rr[q] % len(self.lanes[q])]
        self.lane_rr[q] += 1
        if ln[2] > 0:
            self._wait(q, Ev(ln[0], ln[1], ln[2]))
        return ln

    def _wait(self, eng, ev):
        if ev is None:
            return
        if ev.key == eng and not SAME_ENGINE_SYNC:
            return
        if ev.key == "pe" and eng == "pe":
            return
        if self.seen[eng].get(ev.key, 0) >= ev.val:
            return
        self.h[eng].wait_ge(ev.sem, ev.val)
        self.ninst[eng] += 1
        self.seen[eng][ev.key] = ev.val

    def _deps(self, eng, reads, writes):
        for b in reads:
            for ev in b.w:
                self._wait(eng, ev)
            if b.excl:
                for k, r in b.rs.items():
                    if k != eng:
                        self._wait(eng, r)
        for b in writes:
            for ev in b.w:
                self._wait(eng, ev)
            for r in b.rs.values():
                self._wait(eng, r)

    def op(self, eng, fn, reads=(), writes=()):
        self._deps(eng, reads, writes)
        inst = fn(self.h[eng])
        self.cnt[eng] += 1
        self.ninst[eng] += 1
        inst.then_inc(self.sem[eng], 1)
        ev = Ev(self.sem[eng], eng, self.cnt[eng])
        for b in reads:
            b.rs[eng] = ev
        for b in writes:
            b.w = [ev]
            b.rs = {}
        return ev

    def dma(self, q, out, in_, owner=None, reads=(), writes=(), nowait=False, **kw):
        if not nowait:
            self._deps(q, reads, writes)
        ln = self._lane(q)
        inst = self.h[q].dma_start(out=out, in_=in_, **kw)
        ln[2] += 16
        inst.then_inc(ln[0], 16)
        self.ninst[q] += 1
        ev = Ev(ln[0], ln[1], ln[2])
        for b in reads:
            b.rs[ln[1]] = ev
        for b in writes:
            if nowait:
                b.w = list(b.w) + [ev]
            else:
                b.w = [ev]
                b.rs = {}
        return ev

    def barrier(self):
        evs = [Ev(self.sem[e], e, self.cnt[e]) for e in self.h if self.cnt[e] > 0]
        for q, lanes in self.lanes.items():
            evs += [Ev(l[0], l[1], l[2]) for l in lanes if l[2] > 0]
        for e in self.h:
            for ev in evs:
                if ev.key == e:
                    if self.seen[e].get(e, 0) < ev.val and e != "sp":
                        self.h[e].wait_ge(ev.sem, ev.val)
                        self.seen[e][e] = ev.val
                    continue
                self._wait(e, ev)


_uid = [0]


def _rd(ap):
    try:
        if ap.dtype == F32R:
            return ap.bitcast(F32)
    except AttributeError:
        pass
    return ap


def _sb(es, nc, name, shape, dt):
    _uid[0] += 1
    return es.enter_context(nc.sbuf_tensor("sb%d_%s" % (_uid[0], name), list(shape), dt))


def host_consts():
    idx = np.arange(128)
    masks = np.stack([(idx[:, None] < idx[None, :]), (idx[:, None] <= idx[None, :]),
                      (idx[:, None] > idx[None, :]), (idx[:, None] >= idx[None, :])]).astype(np.float32)
    blockones = (idx[:, None] // 64 == idx[None, :] // 64).astype(np.float32)
    resetmask = np.ones((128, 256), np.float32)
    resetmask[:, 0] = 0.0
    resetmask[:, 128] = 0.0
    headsel = np.zeros((128, 2), np.float32)
    headsel[:64, 0] = 1.0
    headsel[64:, 1] = 1.0
    sel = np.zeros((2, 2, 128), np.float32)
    sel[0, 0] = 1
    sel[1, 1] = 1
    return {"ident": np.eye(128, dtype=np.float32), "sel": sel, "masks": masks, "blockones": blockones,
            "resetmask": resetmask, "headsel": headsel}


def pack_rwpar(inp):
    cols = [inp["rw_mu"][:, 0], inp["rw_mu"][:, 1], inp["rw_mu"][:, 2], inp["rw_k_k"], inp["rw_k_a"],
            inp["rw_r_k"].reshape(DEPTH, W), inp["rw_w0"][:, 0], inp["rw_w0"][:, 1], inp["rw_a0"][:, 0], inp["rw_a0"][:, 1],
            inp["rw_lnx_g"], inp["rw_lnx_b"]]
    a = np.stack([np.asarray(c, np.float32) for c in cols], axis=-1)
    return np.ascontiguousarray(a.reshape(DEPTH, 8, 128, NPAR))


def setup_consts(G, es):
    nc, P = G.nc, G.P
    masks_in = G.din("masks", [4, 128, 128])
    bo_in = G.din("blockones", [128, 128])
    rm_in = G.din("resetmask", [128, 256])
    hs_in = G.din("headsel", [128, 2])
    G.masks = _sb(es, nc, "masks", [128, 4, 128], F32)
    G.B_masks = Buf("masks")
    for i in range(4):
        P.dma("sp", G.masks[:, i, :], masks_in[i], G.B_masks, writes=[G.B_masks], nowait=(i > 0))
    G.blockones = _sb(es, nc, "blockones", [128, 128], F32)
    G.B_bo = Buf("blockones")
    P.dma("sp", G.blockones[:], bo_in[:, :], G.B_bo, writes=[G.B_bo])
    G.resetmask = _sb(es, nc, "resetmask", [128, 256], F32)
    G.B_rm = Buf("resetmask")
    P.dma("sp", G.resetmask[:], rm_in[:, :], G.B_rm, writes=[G.B_rm])
    G.headsel = _sb(es, nc, "headsel", [128, 2], F32)
    G.B_hs = Buf("headsel")
    P.dma("sp", G.headsel[:], hs_in[:, :], G.B_hs, writes=[G.B_hs])


NPAR = 12
R_RWR, R_RWK, R_RWV, R_RWG, R_LW, R_LA = 3072, 4096, 5120, 6144, 7168, 7232
LOGW_SCALE = -0.6065306597126334


def phase_rwkv(G, layer, do_ctx_out=True):
    nc, P = G.nc, G.P
    restT, B_restT = G.restT, G.B_restT
    rwpar = G.rwpar
    rw_w2, rw_a2 = G.rw_w2, G.rw_a2
    lnx_g, lnx_b = G.lnx_g, G.lnx_b
    masks, B_masks = G.masks, G.B_masks
    M_lt, M_le, M_gt, M_ge = (masks[:, i, :] for i in range(4))

    def OP(eng, fn, reads=(), writes=()):
        return P.op(eng, fn, reads=reads, writes=writes)

    def mm(bk, o, lhsT, rhs, reads, start=True, stop=True):
        OP("pe", lambda e: e.matmul(o, lhsT=lhsT, rhs=rhs, start=start, stop=stop), reads=reads, writes=[bk])

    def tr(bk, o, in_, reads):
        OP("pe", lambda e: e.transpose(o, _rd(in_), G.ident[:]), reads=list(reads) + [G.B_ident], writes=[bk])

    def act(o, i, func, reads, writes, **kw):
        kw = {k_: _rd(v_) for k_, v_ in kw.items()}
        OP("act", lambda e: e.activation(out=o, in_=_rd(i), func=func, **kw), reads=reads, writes=writes)

    def tt(o, a, b, op, reads, writes, eng="dve"):
        OP(eng, lambda e: e.tensor_tensor(out=o, in0=_rd(a), in1=_rd(b), op=op), reads=reads, writes=writes)

    def ts(o, a, s1, s2, op0, op1, reads, writes, eng="dve"):
        if s2 is None:
            OP(eng, lambda e: e.tensor_scalar(out=o, in0=_rd(a), scalar1=_rd(s1), scalar2=None, op0=op0), reads=reads, writes=writes)
        else:
            OP(eng, lambda e: e.tensor_scalar(out=o, in0=_rd(a), scalar1=_rd(s1), scalar2=_rd(s2), op0=op0, op1=op1), reads=reads, writes=writes)

    def stt(o, a, sc, b, op0, op1, reads, writes):
        OP("dve", lambda e: e.scalar_tensor_tensor(out=o, in0=_rd(a), scalar=_rd(sc), in1=_rd(b), op0=op0, op1=op1), reads=reads, writes=writes)

    def cp(eng, o, i, reads, writes):
        if eng == "act":
            act(o, i, AF.Copy, reads, writes)
        else:
            OP(eng, lambda e: e.tensor_copy(out=o, in_=_rd(i)), reads=reads, writes=writes)

    with contextlib.ExitStack() as ps:
        def sb(name, shape, dt=F32):
            return _sb(ps, nc, name, shape, dt)

        R32 = F32R if int(os.environ.get("RW_F32R", "0")) else F32
        lwla = sb("lwla", [128, S])
        B_lwla = Buf("lwla")
        P.dma("sp", lwla[:], restT[R_LW:R_LW + 128, :], B_lwla, reads=[B_restT], writes=[B_lwla])
        act(lwla[0:64, :], lwla[0:64, :], AF.Tanh, [B_lwla], [B_lwla])

        rT = sb("r", [128, S]); B_r = Buf("r")
        kT = sb("k", [128, S]); B_k = Buf("k")
        vT = sb("vkkn", [128, S]); B_v = Buf("vkkn")
        ytok = sb("ytok", [128, NT, 128]); B_y = [Buf("ytok%d" % i) for i in range(NT)]
        bon = sb("bon", [128, NT, 2]); B_bon = [Buf("bon%d" % i) for i in range(NT)]
        vtok = sb("vtok", [128, NT, 128], R32); B_vt = [Buf("vtok%d" % i) for i in range(NT)]
        par = sb("par", [128, NPAR + 8]); B_par = Buf("par")
        w2t = sb("w2t", [128, 2, 128]); B_w2 = Buf("w2t")
        lnxg = sb("lnxg", [128, 128]); lnxb = sb("lnxb", [128, 128]); B_lnx = Buf("lnx")
        rkblk = sb("rkblk", [128, 2], R32); B_rkblk = Buf("rkblk")
        nsum = ytok[:].rearrange("p t c -> p (t c)")
        GW = 256
        gtmp = [[(sb("gt%d_%d" % (d, i), [128, GW]), Buf("gt%d_%d" % (d, i))) for i in range(6)] for d in range(2)]
        gout = []
        for d in range(2):
            gd = []
            for pz in range(2):
                ar = sb("goAR%d_%d" % (d, pz), [128, 2, GW], R32)
                B_ar = Buf("goAR%d_%d" % (d, pz))
                tl = [(sb("go%d_%d_%d" % (d, pz, i), [128, GW], F32 if i == 0 else R32), Buf("go%d_%d_%d" % (d, pz, i))) for i in (0, 2, 3)]
                gd.append([tl[0], (ar[:, 0, :], B_ar), tl[1], tl[2], (ar[:, 1, :], B_ar), (ar, B_ar)])
            gout.append(gd)
        ukds = [(sb("ukd%d" % d, [128, GW], R32), Buf("ukd%d" % d)) for d in range(2)]
        def ctile(name, w=128, dt=None):
            return (sb(name, [128, w], R32 if dt is None else dt), Buf(name))
        cper = [[[[{n: ctile("c%s%d%d%d%d" % (n, d, pz, c, h)) for n in ("Pm", "BmT", "RBT", "RKT")} for h in range(2)]
                  for c in range(2)] for pz in range(2)] for d in range(2)]
        cpair = [[[{n: ctile("c%s%d%d%d" % (n, d, pz, c)) for n in ("btok", "ktok")} for c in range(2)]
                  for pz in range(2)] for d in range(2)]
        ctmp = [[[{n: ctile("t%s%d%d%d" % (n, d, c, h)) for n in ("Xa", "XTa", "Xb", "XTb", "Pb")} for h in range(2)]
                 for c in range(2)] for d in range(2)]
        STs = [[ctile("ST%d%d" % (d, i), 64) for i in range(2)] for d in range(2)]
        S0dec = [ctile("S0dec%d" % d, 64, F32) for d in range(2)]
        Gt = [ctile("G%d" % d) for d in range(2)]
        SAt = [ctile("SA%d" % d) for d in range(2)]
        yst = [ctmp[0][0][1]["Xa"], ctmp[0][0][1]["XTa"]]
        ost = [(sb("rwo%d" % i, [128, 128], BF16), Buf("rwo%d" % i)) for i in range(2)]
        gts = [(gtmp[0][0][0][:, 0:128], gtmp[0][0][1]), (gtmp[0][1][0][:, 0:128], gtmp[0][1][1])]
        small = [ctile("sm%d" % i, 8, F32) for i in range(4)]

        RW_STAGE = int(os.environ.get("RW_STAGE", "99"))
        RW_SUB = int(os.environ.get("RW_SUB", "99"))
        for hp in range(int(os.environ.get("RW_PAIRS", "8"))):
            ch0 = hp * 128
            P.dma("sp", par[:, 0:NPAR], rwpar[layer, hp], B_par, writes=[B_par])
            for d in range(2):
                P.dma("sp", w2t[0:64, d, :], rw_w2[layer, d, :, ch0:ch0 + 128], B_w2, writes=[B_w2], nowait=(d > 0))
                P.dma("sp", w2t[64:128, d, :], rw_a2[layer, d, :, ch0:ch0 + 128], B_w2, writes=[B_w2], nowait=True)
            P.dma("sp", lnxg[:], lnx_g[layer, ch0:ch0 + 128].partition_broadcast(128), B_lnx, writes=[B_lnx])
            P.dma("sp", lnxb[:], lnx_b[layer, ch0:ch0 + 128].partition_broadcast(128), B_lnx, writes=[B_lnx], nowait=True)
            ts(par[:, NPAR:NPAR + 3], par[:, 0:3], -1.0, 1.0, ALU.mult, ALU.add, [B_par], [B_par])
            ts(par[:, NPAR + 3:NPAR + 6], par[:, 0:3], 0.5, None, ALU.mult, None, [B_par], [B_par])
            ts(par[:, NPAR + 6:NPAR + 7], par[:, 4:5], -1.0, 1.0, ALU.mult, ALU.add, [B_par], [B_par])
            ts(rkblk[:], G.headsel[:], par[:, 5:6], None, ALU.mult, None, [B_par, G.B_hs], [B_rkblk])
            C_KK, C_KA, C_OMKA = par[:, 3:4], par[:, 4:5], par[:, NPAR + 6:NPAR + 7]

            for zi, (zt, B_z, row) in enumerate(((rT, B_r, R_RWR), (kT, B_k, R_RWK), (vT, B_v, R_RWV))):
                P.dma("sp", zt[:], restT[row + ch0:row + ch0 + 128, :], B_z, reads=[B_restT], writes=[B_z])
                tt(nsum[:, 1:S - 1], zt[:, 0:S - 2], zt[:, 2:S], ALU.add, [B_z], B_y)
                for (dst, srcc) in ((0, 1), (255, 254), (256, 257), (S - 1, S - 2)):
                    cp("dve", nsum[:, dst:dst + 1], zt[:, srcc:srcc + 1], [B_z], B_y)
                ts(nsum[:, :], nsum[:, :], par[:, NPAR + 3 + zi:NPAR + 4 + zi], None, ALU.mult, None, B_y + [B_par], B_y)
                stt(zt[:], zt[:], par[:, NPAR + zi:NPAR + 1 + zi], nsum[:, :], ALU.mult, ALU.add, [B_z, B_par] + B_y, [B_z])
            if RW_STAGE < 2:
                continue
            for ti in range(NT):
                bk, B_bk = G.next_bank()
                tr(B_bk, bk[:, 0:128], vT[:, ti * 128:(ti + 1) * 128], [B_v])
                cp("act" if ti % 2 == 0 else "dve", vtok[:, ti, :], bk[:, 0:128], [B_bk], [B_vt[ti]])
            act(nsum[:, :], kT[:], AF.Copy, [B_k, B_par], B_y, scale=C_KK)
            act(vT[:], nsum[:, :], AF.Square, B_y + B_vt, [B_v])
            for t0 in range(0, S, 512):
                tw = min(512, S - t0)
                bk, B_bk = G.next_bank()
                mm(B_bk, bk[:, 0:tw], G.blockones[:], vT[:, t0:t0 + tw], [G.B_bo, B_v])
                act(vT[:, t0:t0 + tw], bk[:, 0:tw], AF.Sqrt, [B_bk], [B_v])
            ts(vT[:], vT[:], 1e-12, None, ALU.max, None, [B_v], [B_v])
            OP("dve", lambda e: e.reciprocal(out=vT[:], in_=vT[:]), reads=[B_v], writes=[B_v])
            tt(vT[:], vT[:], nsum[:, :], ALU.mult, [B_v] + B_y, [B_v])
            kkn, B_kkn = vT, B_v

            if RW_STAGE < 3:
                continue
            order = {0: list(range(17)), 1: [0] + list(range(16, 0, -1))}
            st_idx = [0, 0]
            ywritten = set()
            bwritten = set()
            for d in range(2):
                if RW_SUB < -1:
                    break
                ts(STs[d][0][0][:], G.ident[:, 0:64], 0.0, None, ALU.mult, None, [G.B_ident], [STs[d][0][1]], eng="pool")

            def prep_rounds(d, step):
                g = order[d][step]
                pz = step % 2
                t0 = g * GW
                (sg, B_sg), (cs, B_cs), (tmp, B_tmp), (ad, B_ad), (kd, B_kd), (en, B_en) = gtmp[d]
                ukd, B_ukd = ukds[d]
                (Ep, B_Ep), (aTt, B_aT), (bTt, B_bT), (kTt, B_kT), (rTt, B_rT), (ARt, B_AR) = gout[d][pz]
                rounds = []

                def r0():
                    if RW_SUB < 0:
                        return
                    bk, B_bk = G.next_bank()
                    bk2, B_bk2 = G.next_bank()
                    mm(B_bk, bk[:, 0:GW], w2t[0:64, d, :], lwla[0:64, t0:t0 + GW], [B_w2, B_lwla])
                    mm(B_bk2, bk2[:, 0:GW], w2t[64:128, d, :], lwla[64:128, t0:t0 + GW], [B_w2, B_lwla])
                    act(sg[:], bk[:, 0:GW], AF.Sigmoid, [B_bk, B_par], [B_sg], bias=par[:, 6 + d:7 + d])
                    act(ad[:], bk2[:, 0:GW], AF.Sigmoid, [B_bk2, B_par], [B_ad], bias=par[:, 8 + d:9 + d])
                    if RW_SUB < 1:
                        return
                    OP("dve", lambda e: e.tensor_tensor_scan(out=cs[:], data0=G.resetmask[:], data1=sg[:], initial=0.0,
                                                            op0=ALU.mult, op1=ALU.add), reads=[B_sg, G.B_rm], writes=[B_cs])
                    if d == 1 and RW_SUB >= 2:
                        cs3 = cs[:].rearrange("p (c k) -> p c k", k=128)
                        tot = cs3[:, :, 127:128].to_broadcast([128, 2, 128])
                        tt(tmp[:].rearrange("p (c k) -> p c k", k=128), tot, cs3, ALU.subtract, [B_cs], [B_tmp])
                        tt(cs[:], tmp[:], sg[:], ALU.add, [B_tmp, B_sg], [B_cs])
                rounds.append(r0)
                if RW_SUB < 3:
                    return rounds

                def r1():
                    act(Ep[:], cs[:], AF.Exp, [B_cs], [B_Ep], scale=LOGW_SCALE)
                    act(en[:], cs[:], AF.Exp, [B_cs], [B_en], scale=-LOGW_SCALE)
                    tt(tmp[:], cs[:], sg[:], ALU.subtract, [B_cs, B_sg], [B_tmp])
                    act(tmp[:], tmp[:], AF.Exp, [B_tmp], [B_tmp], scale=LOGW_SCALE)
                    ts(kd[:], ad[:], C_KA, C_OMKA, ALU.mult, ALU.add, [B_ad, B_par], [B_kd])
                    tt(kd[:], kd[:], kT[:, t0:t0 + GW], ALU.mult, [B_kd, B_k], [B_kd])
                rounds.append(r1)
                if RW_SUB < 4:
                    return rounds

                def r2():
                    tt(ukd[:], rT[:, t0:t0 + GW], kd[:], ALU.mult, [B_r, B_kd], [B_ukd], eng="pool")
                    stt(aTt[:], kkn[:, t0:t0 + GW], -1.0, tmp[:], ALU.mult, ALU.mult, [B_kkn, B_tmp], [B_aT])
                    tt(bTt[:], kkn[:, t0:t0 + GW], ad[:], ALU.mult, [B_kkn, B_ad], [B_bT])
                    tt(bTt[:], bTt[:], en[:], ALU.mult, [B_bT, B_en], [B_bT])
                    tt(kTt[:], kd[:], en[:], ALU.mult, [B_kd, B_en], [B_kT])
                    tt(rTt[:], rT[:, t0:t0 + GW], Ep[:], ALU.mult, [B_r, B_Ep], [B_rT], eng="pool")
                rounds.append(r2)

                strict_st, incl_st = (M_lt, M_le) if d == 0 else (M_gt, M_ge)
                strict_ts = M_gt if d == 0 else M_lt

                def r3():
                    for c in range(2):
                        cs_ = slice(c * 128, (c + 1) * 128)
                        bkAs = [G.next_bank(), G.next_bank()]
                        for h in range(2):
                            hs = slice(h * 64, (h + 1) * 64)
                            bkA, B_A = bkAs[h]
                            mm(B_A, bkA[:, 0:256], bTt[hs, cs_], ARt[hs, :, cs_], [B_bT, B_AR])
                        for h in range(2):
                            T = ctmp[d][c][h]
                            bkA, B_A = bkAs[h]
                            tt(T["Xa"][0][:], bkA[:, 0:128], strict_st, ALU.mult,
                               [B_A, B_masks], [T["Xa"][1]])
                            tt(T["Pb"][0][:], T["Xa"][0][:], G.ident[:], ALU.add, [T["Xa"][1], G.B_ident], [T["Pb"][1]], eng="pool")
                        for h in range(2):
                            hs = slice(h * 64, (h + 1) * 64)
                            bkB, B_B = G.next_bank()
                            Cp = cper[d][pz][c][h]
                            mm(B_B, bkB[:, 0:256], kTt[hs, cs_], ARt[hs, :, cs_], [B_kT, B_AR])
                            tt(Cp["BmT"][0][:], bkB[:, 0:128], strict_st, ALU.mult, [B_B, B_masks], [Cp["BmT"][1]])
                            bkA_, B_A_ = bkAs[h]
                            tt(Cp["RBT"][0][:], bkA_[:, 128:256], incl_st, ALU.mult, [B_A_, B_masks], [Cp["RBT"][1]])
                            tt(Cp["RKT"][0][:], bkB[:, 128:256], incl_st, ALU.mult, [B_B, B_masks], [Cp["RKT"][1]])
                        bkC, B_C = G.next_bank()
                        tr(B_C, bkC[:, 0:128], bTt[:, cs_], [B_bT])
                        tr(B_C, bkC[:, 128:256], kTt[:, cs_], [B_kT])
                        for h in range(2):
                            T = ctmp[d][c][h]
                            tr(B_C, bkC[:, (2 + h) * 128:(3 + h) * 128], T["Xa"][0][:], [T["Xa"][1]])
                        cp("act", cpair[d][pz][c]["btok"][0][:], bkC[:, 0:128], [B_C], [cpair[d][pz][c]["btok"][1]])
                        cp("act", cpair[d][pz][c]["ktok"][0][:], bkC[:, 128:256], [B_C], [cpair[d][pz][c]["ktok"][1]])
                        for h in range(2):
                            T = ctmp[d][c][h]
                            cp("act", T["XTa"][0][:], bkC[:, (2 + h) * 128:(3 + h) * 128], [B_C], [T["XTa"][1]])
                if RW_STAGE < 4:
                    return rounds
                rounds.append(r3)

                def r3b():
                    bk, B_bk = G.next_bank()
                    for c in range(2):
                        mm(B_bk, bk[:, 2 * c:2 * c + 2], ukd[:, c * 128:(c + 1) * 128], rkblk[:], [B_ukd, B_rkblk])
                    for c in range(2):
                        ti = g * 2 + c
                        if ti not in bwritten:
                            bwritten.add(ti)
                            cp("act", bon[:, ti, :], bk[:, 2 * c:2 * c + 2], [B_bk], [B_bon[ti]])
                        else:
                            tt(bon[:, ti, :], bk[:, 2 * c:2 * c + 2], bon[:, ti, :], ALU.add, [B_bk, B_bon[ti]], [B_bon[ti]])
                rounds.append(r3b)

                def make_level(lvl):
                    def rl():
                        src, dst = ("a", "b") if lvl % 2 == 1 else ("b", "a")
                        last = (lvl == 6)
                        banks_ = []
                        for c in range(2):
                            bk, B_bk = G.next_bank()
                            banks_.append((bk, B_bk))
                            for h in range(2):
                                T = ctmp[d][c][h]
                                X, B_X = T["X" + src]
                                XT, B_XT = T["XT" + src]
                                if not last:
                                    mm(B_bk, bk[:, h * 128:(h + 1) * 128], XT[:], X[:], [B_X, B_XT])
                                else:
                                    mm(B_bk, bk[:, h * 128:(h + 1) * 128], X[:], XT[:], [B_X, B_XT])
                        for c in range(2):
                            bk, B_bk = banks_[c]
                            for h in range(2):
                                T = ctmp[d][c][h]
                                nm = ("X" if not last else "XT") + dst
                                cp("act" if c == 0 else "dve", T[nm][0][:], bk[:, h * 128:(h + 1) * 128], [B_bk], [T[nm][1]])

                    def rl1():
                        src, dst = ("a", "b") if lvl % 2 == 1 else ("b", "a")
                        if lvl == 6:
                            return
                        for c in range(2):
                            bk, B_bk = G.next_bank()
                            for h in range(2):
                                T = ctmp[d][c][h]
                                tr(B_bk, bk[:, h * 128:(h + 1) * 128], T["X" + dst][0][:], [T["X" + dst][1]])
                            for h in range(2):
                                T = ctmp[d][c][h]
                                cp("dve" if c == 0 else "act", T["XT" + dst][0][:], bk[:, h * 128:(h + 1) * 128], [B_bk], [T["XT" + dst][1]])

                    def rl2():
                        src, dst = ("a", "b") if lvl % 2 == 1 else ("b", "a")
                        if RW_SUB < 11:
                            return
                        for c in range(2):
                            bk, B_bk = G.next_bank()
                            for h in range(2):
                                T = ctmp[d][c][h]
                                Cp = cper[d][pz][c][h]
                                Pold, B_Pold = T["Pb"] if lvl % 2 == 1 else Cp["Pm"]
                                mm(B_bk, bk[:, h * 128:(h + 1) * 128], T["XT" + dst][0][:], Pold[:], [T["XT" + dst][1], B_Pold])
                            for h in range(2):
                                T = ctmp[d][c][h]
                                Cp = cper[d][pz][c][h]
                                Pold, B_Pold = T["Pb"] if lvl % 2 == 1 else Cp["Pm"]
                                Pnew, B_Pnew = Cp["Pm"] if lvl % 2 == 1 else T["Pb"]
                                tt(Pnew[:], bk[:, h * 128:(h + 1) * 128], Pold[:], ALU.add, [B_bk, B_Pold], [B_Pnew])
                    return (rl, rl1, rl2) if lvl < 6 else (rl, rl2)
                if RW_STAGE < 5:
                    return rounds
                for lvl in range(1, 1 + int(os.environ.get("RW_LVL", "6"))):
                    rounds.extend(make_level(lvl))
                def rfin():
                    for c in range(2):
                        for h in range(2):
                            T = ctmp[d][c][h]
                            Cp = cper[d][pz][c][h]
                            cp("pool", Cp["Pm"][0][:], T["Pb"][0][:], [T["Pb"][1]], [Cp["Pm"][1]])
                if RW_SUB >= 12:
                    rounds.append(rfin)
                return rounds

            def chain_rounds(d, step):
                g = order[d][step]
                pz = step % 2
                (Ep, B_Ep), (aTt, B_aT), (bTt, B_bT), (kTt, B_kT), (rTt, B_rT), (ARt, B_AR) = gout[d][pz]
                rounds = []
                corder = (0, 1) if d == 0 else (1, 0)
                for c in corder:
                    ti = g * 2 + c
                    cs_ = slice(c * 128, (c + 1) * 128)
                    ecol = c * 128 + (127 if d == 0 else 0)
                    eLC = Ep[:, ecol:ecol + 1]

                    def b1(c=c, ti=ti, cs_=cs_, eLC=eLC):
                        ST, B_ST = STs[d][st_idx[d] % 2]
                        bk, B_bk = G.next_bank()
                        for h in range(2):
                            hs = slice(h * 64, (h + 1) * 64)
                            Cp = cper[d][pz][c][h]
                            o = bk[:, h * 64:(h + 1) * 64]
                            mm(B_bk, o, Cp["BmT"][0][:], vtok[:, ti, h * 64:(h + 1) * 64], [Cp["BmT"][1], B_vt[ti]], start=True, stop=False)
                            mm(B_bk, o, aTt[hs, cs_], ST[hs, :], [B_aT, B_ST], start=False, stop=True)
                        cp("act", Gt[d][0][:], bk[:, 0:128], [B_bk], [Gt[d][1]])
                        ts(S0dec[d][0][:], ST[:], eLC, None, ALU.mult, None, [B_ST, B_Ep], [S0dec[d][1]], eng="pool")
                    rounds.append(b1)

                    def b2(c=c):
                        bk, B_bk = G.next_bank()
                        for h in range(2):
                            Cp = cper[d][pz][c][h]
                            mm(B_bk, bk[:, h * 64:(h + 1) * 64], Cp["Pm"][0][:], Gt[d][0][:, h * 64:(h + 1) * 64], [Cp["Pm"][1], Gt[d][1]])
                        cp("dve", SAt[d][0][:], bk[:, 0:128], [B_bk], [SAt[d][1]])
                    rounds.append(b2)

                    def b3(c=c, ti=ti, cs_=cs_, eLC=eLC):
                        ST, B_ST = STs[d][st_idx[d] % 2]
                        STn, B_STn = STs[d][(st_idx[d] + 1) % 2]
                        st_idx[d] += 1
                        SA, B_SA = SAt[d]
                        bk, B_bk = G.next_bank()
                        for h in range(2):
                            hs = slice(h * 64, (h + 1) * 64)
                            Cp = cper[d][pz][c][h]
                            o = bk[:, h * 64:(h + 1) * 64]
                            mm(B_bk, o, rTt[hs, cs_], ST[hs, :], [B_rT, B_ST], start=True, stop=False)
                            mm(B_bk, o, Cp["RBT"][0][:], SA[:, h * 64:(h + 1) * 64], [Cp["RBT"][1], B_SA], start=False, stop=False)
                            mm(B_bk, o, Cp["RKT"][0][:], vtok[:, ti, h * 64:(h + 1) * 64], [Cp["RKT"][1], B_vt[ti]], start=False, stop=True)
                        bk2, B_bk2 = G.next_bank()
                        Cq = cpair[d][pz][c]
                        mm(B_bk2, bk2[:, 0:128], Cq["ktok"][0][:], vtok[:, ti, :], [Cq["ktok"][1], B_vt[ti]], start=True, stop=False)
                        mm(B_bk2, bk2[:, 0:128], Cq["btok"][0][:], SA[:], [Cq["btok"][1], B_SA], start=False, stop=True)
                        if ti not in ywritten:
                            ywritten.add(ti)
                            cp("act", ytok[:, ti, :], bk[:, 0:128], [B_bk], [B_y[ti]])
                        else:
                            tt(ytok[:, ti, :], bk[:, 0:128], ytok[:, ti, :], ALU.add, [B_bk, B_y[ti]], [B_y[ti]])
                        for h in range(2):
                            hs = slice(h * 64, (h + 1) * 64)
                            stt(STn[hs, :], bk2[hs, h * 64:(h + 1) * 64], eLC[hs, :], S0dec[d][0][hs, :], ALU.mult, ALU.add,
                                [B_bk2, B_Ep, S0dec[d][1]], [B_STn])
                    rounds.append(b3)
                return rounds

            def interleave(lists):
                items = []
                for li, l in enumerate(lists):
                    for i, fn in enumerate(l):
                        items.append(((i + 0.5) / len(l), li, i, fn))
                items.sort(key=lambda t: (t[0], t[1]))
                for _, _, _, fn in items:
                    fn()

            nsteps = 17
            for step in range(nsteps + 1):
                lists = []
                for d in range(2):
                    if step < nsteps:
                        lists.append(prep_rounds(d, step))
                    if step >= 1 and RW_STAGE >= 6:
                        lists.append(chain_rounds(d, step - 1))
                interleave(lists)

            if RW_STAGE < 7:
                continue
            for ti in range(NT):
                t0 = ti * 128
                sm, B_sm = small[ti % 4]
                y, B_yy = ytok[:, ti, :], B_y[ti]
                yt, B_yt = yst[ti % 2]
                y3 = y.rearrange("p (h c) -> p h c", h=2)
                cp("act", sm[:, 6:8], bon[:, ti, :], [B_bon[ti]], [B_sm])
                OP("dve", lambda e, y3=y3, sm=sm: e.tensor_reduce(out=sm[:, 0:2], in_=y3, axis=AX.X, op=ALU.add), reads=[B_yy], writes=[B_sm])
                ts(sm[:, 0:2], sm[:, 0:2], 1.0 / 64, None, ALU.mult, None, [B_sm], [B_sm])
                for h in range(2):
                    ts(yt[:, h * 64:(h + 1) * 64], y[:, h * 64:(h + 1) * 64], sm[:, h:h + 1], None, ALU.subtract, None, [B_yy, B_sm], [B_yt])
                sq, B_sq = gts[ti % 2]
                act(sq[:], yt[:], AF.Square, [B_yt], [B_sq])
                OP("dve", lambda e, sq=sq, sm=sm: e.tensor_reduce(out=sm[:, 2:4], in_=sq[:].rearrange("p (h c) -> p h c", h=2), axis=AX.X,
                                                                op=ALU.add), reads=[B_sq], writes=[B_sm])
                ts(sm[:, 2:4], sm[:, 2:4], 1.0 / 64, 64e-5, ALU.mult, ALU.add, [B_sm], [B_sm])
                act(sm[:, 2:4], sm[:, 2:4], AF.Sqrt, [B_sm], [B_sm])
                OP("dve", lambda e, sm=sm: e.reciprocal(out=sm[:, 4:6], in_=sm[:, 2:4]), reads=[B_sm], writes=[B_sm])
                for h in range(2):
                    hsl = slice(h * 64, (h + 1) * 64)
                    stt(yt[:, hsl], yt[:, hsl], sm[:, 4 + h:5 + h], lnxg[:, hsl], ALU.mult, ALU.mult, [B_yt, B_sm, B_lnx], [B_yt])
                tt(yt[:], yt[:], lnxb[:], ALU.add, [B_yt, B_lnx], [B_yt])
                for h in range(2):
                    hsl = slice(h * 64, (h + 1) * 64)
                    stt(yt[:, hsl], vtok[:, ti, hsl], sm[:, 6 + h:7 + h], yt[:, hsl], ALU.mult, ALU.add, [B_vt[ti], B_sm, B_yt], [B_yt])
                bk, B_bk = G.next_bank()
                tr(B_bk, bk[:, 0:128], yt[:], [B_yt])
                gt, B_gt = gts[ti % 2]
                P.dma("sp", gt[:], restT[R_RWG + ch0:R_RWG + ch0 + 128, t0:t0 + 128], B_gt, reads=[B_restT], writes=[B_gt])
                act(gt[:], gt[:], AF.Silu, [B_gt], [B_gt])
                ot, B_ot = ost[ti % 2]
                tt(ot[:], bk[:, 0:128], gt[:], ALU.mult, [B_bk, B_gt], [B_ot])
                P.dma("pool", G.brT[2 * W + ch0:2 * W + ch0 + 128, t0:t0 + 128], ot[:], G.B_brT, reads=[B_ot])
        P.barrier()


def mk_helpers(G):
    P = G.P

    class H:
        pass
    H_ = H()

    def OP(eng, fn, reads=(), writes=()):
        return P.op(eng, fn, reads=reads, writes=writes)

    def mm(bk, o, lhsT, rhs, reads, start=True, stop=True):
        OP("pe", lambda e: e.matmul(o, lhsT=lhsT, rhs=rhs, start=start, stop=stop), reads=reads, writes=[bk])

    def tr(bk, o, in_, reads):
        OP("pe", lambda e: e.transpose(o, in_, G.ident[:]), reads=list(reads) + [G.B_ident], writes=[bk])

    def act(o, i, func, reads, writes, **kw):
        OP("act", lambda e: e.activation(out=o, in_=i, func=func, **kw), reads=reads, writes=writes)

    def tt(o, a, b, op, reads, writes, eng="dve"):
        OP(eng, lambda e: e.tensor_tensor(out=o, in0=a, in1=b, op=op), reads=reads, writes=writes)

    def ts(o, a, s1, s2, op0, op1, reads, writes, eng="dve"):
        if s2 is None:
            OP(eng, lambda e: e.tensor_scalar(out=o, in0=a, scalar1=s1, scalar2=None, op0=op0), reads=reads, writes=writes)
        else:
            OP(eng, lambda e: e.tensor_scalar(out=o, in0=a, scalar1=s1, scalar2=s2, op0=op0, op1=op1), reads=reads, writes=writes)

    def stt(o, a, sc, b, op0, op1, reads, writes):
        OP("dve", lambda e: e.scalar_tensor_tensor(out=o, in0=a, scalar=sc, in1=b, op0=op0, op1=op1), reads=reads, writes=writes)

    def cp(eng, o, i, reads, writes):
        if eng == "act":
            act(o, i, AF.Copy, reads, writes)
        else:
            OP(eng, lambda e: e.tensor_copy(out=o, in_=i), reads=reads, writes=writes)

    def memset(eng, o, val, writes):
        OP(eng, lambda e: e.memset(o, val), writes=writes)
    return OP, mm, tr, act, tt, ts, stt, cp, memset


R_NAG, R_PU, R_PG, R_MERGE = 0, 1024, 2048, 7296
PADW = 8 + 256 + 16 + 4096 + 16
OFF_C, OFF_L = 8, 8 + 256 + 16


def host_pool_invcnt():
    out = np.zeros((4, S), np.float32)
    for g, win in enumerate((2, 4, 8, 16)):
        for (o, T) in ((0, NCTX), (NCTX, SEQ)):
            t = np.arange(T)
            lo = np.maximum(t - win // 2, 0)
            hi = np.minimum(t + win // 2, T)
            out[g, o:o + T] = 1.0 / (hi - lo)
    return out


def phase_pool(G, layer):
    nc, P = G.nc, G.P
    OP, mm, tr, act, tt, ts, stt, cp, memset = mk_helpers(G)
    restT, B_restT = G.restT, G.B_restT
    with contextlib.ExitStack() as ps:
        def sb(name, shape, dt=F32):
            return _sb(ps, nc, name, shape, dt)
        A = sb("pA", [128, PADW]); B_A = Buf("pA")
        Bt = sb("pB", [128, PADW]); B_B = Buf("pB")
        Ct = sb("pC", [128, PADW]); B_C = Buf("pC")
        inv = sb("pinv", [128, S]); B_inv = Buf("pinv")
        diff = [(sb("pdiff%d" % i, [128, S], BF16), Buf("pdiff%d" % i)) for i in range(2)]
        gate = sb("pgate", [128, S]); B_gate = Buf("pgate")
        pw = sb("ppw", [128, 2, 256], BF16); B_pw = Buf("ppw")
        psc = sb("ppsc", [128, 2]); B_psc = Buf("ppsc")
        stg = [(sb("pstg%d" % i, [128, 512], BF16), Buf("pstg%d" % i)) for i in range(2)]
        for t_, b_ in ((A, B_A), (Bt, B_B), (Ct, B_C)):
            memset("pool", t_[:], 0.0, [b_])

        def zero_gaps(t_, b_):
            memset("pool", t_[:, 0:OFF_C], 0.0, [b_])
            memset("pool", t_[:, OFF_C + 256:OFF_L], 0.0, [b_])
            memset("pool", t_[:, OFF_L + 4096:PADW], 0.0, [b_])

        R0, R1 = 4, PADW - 8
        for g in range(4):
            win = (2, 4, 8, 16)[g]
            P.dma("sp", inv[:], G.pool_invcnt[g].partition_broadcast(128), None, writes=[B_inv])
            P.dma("pool", pw[:], G.pool_w[layer, g].rearrange("(k p) d -> p k d", p=128), None, writes=[B_pw])
            P.dma("sp", psc[:], G.pool_scale[layer, g * 256:(g + 1) * 256].rearrange("(k p) -> p k", p=128), None, writes=[B_psc])
            for cbi in range(2):
                cb = g * 2 + cbi
                row = R_PU + cb * 128
                P.dma("sp", A[:, OFF_C:OFF_C + 256], restT[row:row + 128, 0:256], None, reads=[B_restT], writes=[B_A])
                P.dma("sp", A[:, OFF_L:OFF_L + 4096], restT[row:row + 128, 256:S], None, reads=[B_restT], writes=[B_A], nowait=True)
                tt(Bt[:, R0:R1], A[:, R0 - 1:R1 - 1], A[:, R0:R1], ALU.add, [B_A], [B_B])
                cur, B_cur = Bt, B_B
                oth, B_oth = Ct, B_C
                sh = 1
                w_ = 2
                while w_ < win:
                    tt(oth[:, R0:R1], cur[:, R0 - sh:R1 - sh], cur[:, R0 + sh:R1 + sh], ALU.add, [B_cur], [B_oth])
                    cur, B_cur, oth, B_oth = oth, B_oth, cur, B_cur
                    sh *= 2
                    w_ *= 2
                dt_, B_d = diff[cbi]
                for (po, so, T) in ((OFF_C, 0, 256), (OFF_L, 256, 4096)):
                    tt(cur[:, po:po + T], cur[:, po:po + T], inv[:, so:so + T], ALU.mult, [B_cur, B_inv], [B_cur])
                    tt(dt_[:, so:so + T], cur[:, po:po + T], A[:, po:po + T], ALU.subtract, [B_cur, B_A], [B_d])
            for dch in range(2):
                row = R_PG + g * 256 + dch * 128
                P.dma("sp", gate[:], restT[row:row + 128, :], None, reads=[B_restT], writes=[B_gate])
                act(gate[:], gate[:], AF.Silu, [B_gate], [B_gate])
                for bi, t0 in enumerate(range(0, S, 512)):
                    tw = min(512, S - t0)
                    bk, B_bk = G.next_bank()
                    for k in range(2):
                        mm(B_bk, bk[:, 0:tw], pw[:, k, dch * 128:(dch + 1) * 128], diff[k][0][:, t0:t0 + tw], [B_pw, diff[k][1]],
                           start=(k == 0), stop=(k == 1))
                    st_, B_st = stg[bi % 2]
                    stt(st_[:, 0:tw], bk[:, 0:tw], psc[:, dch:dch + 1], gate[:, t0:t0 + tw], ALU.mult, ALU.mult,
                        [B_bk, B_psc, B_gate], [B_st])
                    orow = W + g * 256 + dch * 128
                    P.dma("pool", G.brT[orow:orow + 128, t0:t0 + tw], st_[:, 0:tw], None, reads=[B_st], writes=[])
        P.barrier()


def host_na_table(na_rpb):
    L = na_rpb.shape[0]
    NEG = np.float32(-30000.0)
    col = np.arange(64)
    cs = np.clip(col - 8, 0, 48)
    cmask = (col[:, None] >= cs[None, :]) & (col[:, None] < cs[None, :] + 16)
    coff = np.clip(col[:, None] - col[None, :] + 15, 0, 30)
    tab = np.full((L, 16, 2, 64, 18, 64), NEG, np.float32)
    for par in range(2):
        for j in range(16):
            ro = j - 1 + par
            if 0 <= ro <= 14:
                g = na_rpb[:, :, ro][:, :, coff]
                tab[:, :, par, :, j, :] = np.where(cmask[None, None], g, NEG)
    tab[:, :, 1, :, 16, :] = np.where(cmask[None, None], na_rpb[:, :, 3][:, :, coff], NEG)
    tab[:, :, 0, :, 17, :] = np.where(cmask[None, None], na_rpb[:, :, 10][:, :, coff], NEG)
    return np.ascontiguousarray(tab.reshape(L, 16, 128, 18, 64))


def phase_na(G, layer, do_ctx=True):
    nc, P = G.nc, G.P
    OP, mm, tr, act, tt, ts, stt, cp, memset = mk_helpers(G)
    restT, B_restT = G.restT, G.B_restT
    qkT, B_qkT, vaug, B_vaug = G.qkT, G.B_qkT, G.vaug, G.B_vaug
    with contextlib.ExitStack() as ps:
        def sb(name, shape, dt=F32):
            return _sb(ps, nc, name, shape, dt)
        V = sb("naV", [128, NT, 16 * 65], BF16); B_V = Buf("naV")
        vsrc = vaug.rearrange("(n p) h e -> p n (h e)", p=128)
        for i in range(0, NT, 6):
            j = min(NT, i + 6)
            P.dma("sp", V[:, i:j, :], vsrc[:, i:j, :], None, reads=[B_vaug], writes=[B_V], nowait=(i > 0))
        qs = [(sb("naq%d" % i, [64, S], BF16), Buf("naq%d" % i)) for i in range(2)]
        ks = [(sb("nak%d" % i, [64, S], BF16), Buf("nak%d" % i)) for i in range(2)]
        tabs = [(sb("natab%d" % i, [128, 18, 64]), Buf("natab%d" % i)) for i in range(2)]
        ytok = sb("naytok", [128, NT, 64]); B_yt = [Buf("nay%d" % i) for i in range(NT)]
        gate = sb("nagate", [64, S]); B_gate = Buf("nagate")
        sc = [(sb("nasc%d" % i, [128, 5, 64]), Buf("nasc%d" % i)) for i in range(2)]
        PT = [[(sb("naPT%d%d" % (p_, i), [128, 7, 128], BF16), Buf("naPT%d%d" % (p_, i))) for i in range(2)] for p_ in range(2)]
        for p_ in range(2):
            for i in range(2):
                memset("pool", PT[p_][i][0][:], 0.0, [PT[p_][i][1]])
        rden = [(sb("narden%d" % i, [128, 1]), Buf("narden%d" % i)) for i in range(2)]
        ost = [(sb("naost%d" % i, [64, 512], BF16), Buf("naost%d" % i)) for i in range(2)]
        ptc = [(sb("naptc%d" % i, [128, 2, 128], BF16), Buf("naptc%d" % i)) for i in range(2)]

        for h in range(16):
            q, B_q = qs[h % 2]
            k, B_k = ks[h % 2]
            tab, B_tab = tabs[h % 2]
            P.dma("sp", q[:], qkT[h * 64:(h + 1) * 64, :], None, reads=[B_qkT], writes=[B_q])
            P.dma("sp", k[:], qkT[W + h * 64:W + (h + 1) * 64, :], None, reads=[B_qkT], writes=[B_k])
            P.dma("sp", tab[:], G.na_tab[layer, h], None, writes=[B_tab])
            P.dma("sp", gate[:], restT[R_NAG + h * 64:R_NAG + (h + 1) * 64, :], None, reads=[B_restT], writes=[B_gate])
            act(gate[:], gate[:], AF.Silu, [B_gate], [B_gate])
            vh = slice(h * 65, (h + 1) * 65)
            if do_ctx:
                for qt in range(2):
                    bk, B_bk = G.next_bank()
                    for kt in range(2):
                        mm(B_bk, bk[:, kt * 128:(kt + 1) * 128], k[:, kt * 128:(kt + 1) * 128], q[:, qt * 128:(qt + 1) * 128], [B_k, B_q])
                    pc, B_pc = ptc[qt]
                    act(pc[:].rearrange("p a b -> p (a b)"), bk[:, 0:256], AF.Exp, [B_bk], [B_pc], scale=0.125)
                    bk2, B_bk2 = G.next_bank()
                    for kt in range(2):
                        mm(B_bk2, bk2[:, 0:65], pc[:, kt, :], V[:, kt, vh], [B_pc, B_V], start=(kt == 0), stop=(kt == 1))
                    rd, B_rd = rden[qt]
                    OP("dve", lambda e, rd=rd, bk2=bk2: e.reciprocal(out=rd[:], in_=bk2[:, 64:65]), reads=[B_bk2], writes=[B_rd])
                    ts(ytok[:, qt, :], bk2[:, 0:64], rd[:, 0:1], None, ALU.mult, None, [B_bk2, B_rd], [B_yt[qt]])
            def emit_scores(rp):
                plan = []
                for par in range(2):
                    r = 2 * rp + par
                    r0 = min(max(r - 4, 0), 56)
                    t_lo = r0 // 2
                    t_hi = (r0 + 7) // 2
                    ntl = t_hi - t_lo + 1
                    bk, B_bk = G.next_bank()
                    qsl = q[:, 256 + r * 64:256 + (r + 1) * 64]
                    for m in range(ntl):
                        tk = 256 + (t_lo + m) * 128
                        mm(B_bk, bk[:, m * 64:(m + 1) * 64], k[:, tk:tk + 128], qsl, [B_k, B_q])
                    for m in range(2):
                        mm(B_bk, bk[:, (5 + m) * 64:(6 + m) * 64], k[:, m * 128:(m + 1) * 128], qsl, [B_k, B_q])
                    s_, B_s = sc[par]
                    j0 = 2 * t_lo - r + 8
                    bk3 = bk[:, 0:ntl * 64].rearrange("p (a b) -> p a b", b=64)
                    if ntl == 4:
                        stt(s_[:, 0:4, :], bk3, 0.125, tab[:, j0:j0 + 7:2, :], ALU.mult, ALU.add, [B_bk, B_tab], [B_s])
                    else:
                        stt(s_[:, 0:1, :], bk3[:, 0:1, :], 0.125, tab[:, 16:17, :], ALU.mult, ALU.add, [B_bk, B_tab], [B_s])
                        stt(s_[:, 1:4, :], bk3[:, 1:4, :], 0.125, tab[:, j0 + 2:j0 + 7:2, :], ALU.mult, ALU.add, [B_bk, B_tab], [B_s])
                        stt(s_[:, 4:5, :], bk3[:, 4:5, :], 0.125, tab[:, 17:18, :], ALU.mult, ALU.add, [B_bk, B_tab], [B_s])
                    pt, B_pt = PT[par][rp % 2]
                    qo = par * 64
                    act(pt[:, 0:ntl, qo:qo + 64], s_[:, 0:ntl, :], AF.Exp, [B_s], [B_pt])
                    act(pt[:, 5:7, qo:qo + 64], bk[:, 320:448].rearrange("p (a b) -> p a b", b=64), AF.Exp, [B_bk], [B_pt], scale=0.125)
                    plan.append((pt, B_pt, t_lo, ntl))
                return plan

            def emit_pv(rp, plan):
                bk2, B_bk2 = G.next_bank()
                nmm = 0
                tot = sum(p_[3] + 2 for p_ in plan)
                for (pt, B_pt, t_lo, ntl) in plan:
                    for m in range(ntl):
                        mm(B_bk2, bk2[:, 0:65], pt[:, m, :], V[:, 2 + t_lo + m, vh], [B_pt, B_V], start=(nmm == 0), stop=(nmm == tot - 1))
                        nmm += 1
                    for m in range(2):
                        mm(B_bk2, bk2[:, 0:65], pt[:, 5 + m, :], V[:, m, vh], [B_pt, B_V], start=(nmm == 0), stop=(nmm == tot - 1))
                        nmm += 1
                rd, B_rd = rden[rp % 2]
                OP("dve", lambda e, rd=rd, bk2=bk2: e.reciprocal(out=rd[:], in_=bk2[:, 64:65]), reads=[B_bk2], writes=[B_rd])
                ts(ytok[:, 2 + rp, :], bk2[:, 0:64], rd[:, 0:1], None, ALU.mult, None, [B_bk2, B_rd], [B_yt[2 + rp]])

            plans = {0: emit_scores(0)}
            for rp in range(32):
                if rp + 1 < 32:
                    plans[rp + 1] = emit_scores(rp + 1)
                emit_pv(rp, plans.pop(rp))
            t_first = 0 if do_ctx else 2
            for ti0 in range(t_first, NT, 4):
                n = min(4, NT - ti0)
                bk, B_bk = G.next_bank()
                for i in range(n):
                    tr(B_bk, bk[0:64, i * 128:(i + 1) * 128], ytok[:, ti0 + i, :], [B_yt[ti0 + i]])
                o_, B_o = ost[(ti0 // 4) % 2]
                tt(o_[:, 0:n * 128], bk[0:64, 0:n * 128], gate[:, ti0 * 128:(ti0 + n) * 128], ALU.mult, [B_bk, B_gate], [B_o])
                P.dma("pool", G.brT[h * 64:(h + 1) * 64, ti0 * 128:(ti0 + n) * 128], o_[:, 0:n * 128], None, reads=[B_o], writes=[])
        P.barrier()


def phase_out(G, layer, last):
    nc, P = G.nc, G.P
    OP, mm, tr, act, tt, ts, stt, cp, memset = mk_helpers(G)
    restT, B_restT = G.restT, G.B_restT
    mT, B_mT = G.mT, G.B_mT
    t_first = 2 if last else 0
    with contextlib.ExitStack() as ps:
        def sb(name, shape, dt=F32):
            return _sb(ps, nc, name, shape, dt)
        wbr = sb("wbr", [128, 24, D], BF16); B_wbrs = [Buf("wbr%d" % i) for i in range(4)]
        wsrc = G.w_branch[layer].rearrange("b (k p) d -> p (b k) d", p=128)
        for cb4 in range(4):
            csl = slice(cb4 * 512, (cb4 + 1) * 512)
            P.dma("pool", wbr[:, 0:12, csl], wsrc[:, 0:12, csl], None, writes=[B_wbrs[cb4]])
            P.dma("pool", wbr[:, 12:24, csl], wsrc[:, 12:24, csl], None, writes=[B_wbrs[cb4]], nowait=True)
        bts = [(sb("bT%d" % i, [128, 24, 512], BF16), Buf("bT%d" % i)) for i in range(2)]
        lgs = [(sb("lg%d" % i, [128, 512]), Buf("lg%d" % i)) for i in range(12)]
        macc = [(sb("macc%d" % i, [128, 512]), Buf("macc%d" % i)) for i in range(2)]
        mst = [(sb("mst%d" % i, [128, 512], BF16), Buf("mst%d" % i)) for i in range(2)]
        bsrc = G.brT.rearrange("(c p) t -> p c t", p=128)
        for bi, t0 in enumerate(range(t_first * 128, S, 512)):
            tw = min(512, S - t0)
            bt, B_bt = bts[bi % 2]
            P.dma("sp", bt[:, 0:12, 0:tw], bsrc[:, 0:12, t0:t0 + tw], None, reads=[G.B_brT], writes=[B_bt])
            P.dma("sp", bt[:, 12:24, 0:tw], bsrc[:, 12:24, t0:t0 + tw], None, reads=[G.B_brT], writes=[B_bt], nowait=True)
            for fo in range(16):
                ma, B_ma = macc[fo % 2]
                for kb in range(3):
                    lg, B_lg = lgs[(fo * 3 + kb) % 12]
                    row = R_MERGE + kb * D + fo * 128
                    P.dma("sp", lg[:, 0:tw], restT[row:row + 128, t0:t0 + tw], None, reads=[B_restT], writes=[B_lg])
                    act(lg[:, 0:tw], lg[:, 0:tw], AF.Sigmoid, [B_lg], [B_lg])
                    bk, B_bk = G.next_bank()
                    for kc in range(8):
                        mm(B_bk, bk[:, 0:tw], wbr[:, kb * 8 + kc, fo * 128:(fo + 1) * 128], bt[:, kb * 8 + kc, 0:tw], [B_wbrs[fo // 4], B_bt],
                           start=(kc == 0), stop=(kc == 7))
                    if kb == 0:
                        tt(ma[:, 0:tw], bk[:, 0:tw], lg[:, 0:tw], ALU.mult, [B_bk, B_lg], [B_ma])
                    else:
                        tt(lg[:, 0:tw], bk[:, 0:tw], lg[:, 0:tw], ALU.mult, [B_bk, B_lg], [B_lg])
                        if kb == 1:
                            tt(ma[:, 0:tw], ma[:, 0:tw], lg[:, 0:tw], ALU.add, [B_ma, B_lg], [B_ma], eng="pool")
                        else:
                            ms, B_ms = mst[fo % 2]
                            tt(ms[:, 0:tw], ma[:, 0:tw], lg[:, 0:tw], ALU.add, [B_ma, B_lg], [B_ms], eng="pool")
                            P.dma("pool", mT[fo * 128:(fo + 1) * 128, t0:t0 + tw], ms[:, 0:tw], None, reads=[B_ms], writes=[])
        P.barrier()
    with contextlib.ExitStack() as ps:
        def sb(name, shape, dt=F32):
            return _sb(ps, nc, name, shape, dt)
        wo = sb("wo", [128, 16, D], BF16); B_wos = [Buf("wo%d" % i) for i in range(4)]
        wsrc = G.w_out[layer].rearrange("(k p) d -> p k d", p=128)
        for cb4 in range(4):
            csl = slice(cb4 * 512, (cb4 + 1) * 512)
            P.dma("pool", wo[:, :, csl], wsrc[:, :, csl], None, writes=[B_wos[cb4]])
        gbc = sb("gbc", [128, 2, D]); B_gbc = Buf("gbc")
        P.dma("sp", gbc[:, 0, :], G.gate_d[layer, 0].partition_broadcast(128), None, reads=[G.B_gate_d], writes=[B_gbc])
        P.dma("sp", gbc[:, 1, :], G.gate_d[layer, 1].partition_broadcast(128), None, reads=[G.B_gate_d], writes=[B_gbc], nowait=True)
        if last:
            fg = sb("fg", [128, D]); B_fg = Buf("fg")
            P.dma("sp", fg[:], G.final_g.partition_broadcast(128), None, writes=[B_fg])
        mts = [(sb("mt%d" % i, [128, 16, 128], BF16), Buf("mt%d" % i)) for i in range(2)]
        xts = [(sb("xo%d" % i, [128, D]), Buf("xo%d" % i)) for i in range(2)]
        xns = [(sb("xnw%d" % i, [128, D]), Buf("xnw%d" % i)) for i in range(2)]
        junk = sb("ojunk", [128, D], BF16); B_junk = Buf("ojunk")
        stat = [(sb("ostat%d" % i, [128, 2]), Buf("ostat%d" % i)) for i in range(2)]
        msrc = mT.rearrange("(k p) t -> p k t", p=128)
        for ti in range(t_first, NT):
            mt, B_mt = mts[ti % 2]
            xt, B_xt = xts[ti % 2]
            xn, B_xn = xns[ti % 2]
            isctx = 1 if ti < 2 else 0
            P.dma("sp", mt[:], msrc[:, :, ti * 128:(ti + 1) * 128], None, reads=[B_mT], writes=[B_mt])
            P.dma("sp", xt[:], G.x_src(layer, ti), None, reads=[G.B_xs], writes=[B_xt])
            for cbk in range(4):
                bk, B_bk = G.next_bank()
                for kc in range(16):
                    mm(B_bk, bk[:, :], mt[:, kc, :], wo[:, kc, cbk * 512:(cbk + 1) * 512], [B_mt, B_wos[cbk]], start=(kc == 0), stop=(kc == 15))
                csl = slice(cbk * 512, (cbk + 1) * 512)
                tt(xn[:, csl], bk[:, :], gbc[:, isctx, csl], ALU.mult, [B_bk, B_gbc], [B_xn])
                tt(xn[:, csl], xn[:, csl], xt[:, csl], ALU.add, [B_xn, B_xt], [B_xn], eng="pool")
            if not last:
                P.dma("pool", G.xs[ti * 128:(ti + 1) * 128, :], xn[:], None, reads=[B_xn], writes=[])
            else:
                st, B_st = stat[ti % 2]
                act(junk[:], xn[:], AF.Square, [B_xn], [B_junk, B_st], scale=float(D) ** -0.5, accum_out=st[:, 0:1])
                ts(st[:, 0:1], st[:, 0:1], 1e-6, None, ALU.add, None, [B_st], [B_st])
                act(st[:, 0:1], st[:, 0:1], AF.Sqrt, [B_st], [B_st])
                OP("dve", lambda e, st=st: e.reciprocal(out=st[:, 1:2], in_=st[:, 0:1]), reads=[B_st], writes=[B_st])
                stt(xn[:], xn[:], st[:, 1:2], fg[:], ALU.mult, ALU.mult, [B_xn, B_st, B_fg], [B_xn])
                P.dma("pool", G.out[(ti - 2) * 128:(ti - 1) * 128, :], xn[:], None, reads=[B_xn], writes=[])
        P.barrier()


ALL_PHASES = ("p0", "p1", "rw", "pool", "na", "p3")


def build_program(n_layers=DEPTH, debug_outs=(), phases=ALL_PHASES, ext_in=(), only_layer=None):
    nc = bass.Bass("TRN2", target_bir_lowering=False)
    es = contextlib.ExitStack()
    with es:
        _build(nc, es, n_layers, debug_outs, phases, ext_in, only_layer)
    return nc


def _build(nc, es, n_layers, debug_outs, phases, ext_in, only_layer=None):
    P = Prog(nc, es)
    allow = es.enter_context(nc.allow_non_contiguous_dma(reason="small strided param loads"))

    def din(name, shape, dt=F32):
        return nc.dram_tensor(name, list(shape), dt, kind="ExternalInput").ap()

    def dscr(name, shape, dt=F32):
        kind = "ExternalOutput" if name in debug_outs else ("ExternalInput" if name in ext_in else "Internal")
        return nc.dram_tensor(name, list(shape), dt, kind=kind).ap()

    if "p0" in phases or "p1" in phases:
        x_in = din("x", [SEQ, D])
        ctx_in = din("ctx", [NCTX, D])
        c_in = din("c", [D])
        cctx_in = din("c_ctx", [D])
        norm_g = din("norm_g", [DEPTH, D])
        w_mod = din("w_mod", [DEPTH, D, 3 * D])
        b_mod = din("b_mod", [DEPTH, 3 * D])
        w_in = din("w_in", [DEPTH, D, D_IN])
    ident_in = din("ident", [128, 128])
    sel_in = din("sel", [2, 2, 128])
    out = nc.dram_tensor("out", [SEQ, D], F32, kind="ExternalOutput").ap()

    qkT = dscr("qkT", [2048, S], BF16)
    vaug = dscr("vaug", [S, 16, 65], BF16)
    restT = dscr("restT", [D_IN - 3072, S], F32)
    B_qkT, B_vaug, B_restT = Buf("qkT"), Buf("vaug"), Buf("restT")

    banks = []
    for i in range(8):
        t = es.enter_context(nc.psum_tensor("bank%d" % i, [128, 512], F32))
        banks.append((t, Buf("bank%d" % i, excl=True)))
    bank_rr = [0]

    def next_bank():
        b = banks[bank_rr[0] % 8]
        bank_rr[0] += 1
        return b

    ident = _sb(es, nc, "ident", [128, 128], F32)
    B_ident = Buf("ident")
    P.dma("sp", ident[:], ident_in[:, :], B_ident, writes=[B_ident])
    sel = _sb(es, nc, "sel", [2, 2, 128], F32)
    B_sel = Buf("sel")
    P.dma("sp", sel[:], sel_in[:, :, :], B_sel, writes=[B_sel])
    gs_col = _sb(es, nc, "gs_col", [128, 16, 2], F32)
    sh_col = _sb(es, nc, "sh_col", [128, 16, 2], F32)
    B_gs, B_sh = Buf("gs"), Buf("sh")
    gate_d = dscr("gate_d", [DEPTH, 2, D], F32)
    B_gate_d = Buf("gate_d")

    G = type("Ctx", (), {})()
    G.nc, G.P, G.next_bank, G.ident, G.B_ident, G.restT, G.B_restT = nc, P, next_bank, ident, B_ident, restT, B_restT
    G.din, G.dscr = din, dscr
    G.phases = phases
    setup_consts(G, es)
    xs = dscr("xs", [S, D], F32)
    G.xs, G.B_xs = xs, Buf("xs")
    G.qkT, G.B_qkT, G.vaug, G.B_vaug = qkT, B_qkT, vaug, B_vaug
    G.gate_d, G.B_gate_d = gate_d, B_gate_d
    G.out = out
    G.mT, G.B_mT = dscr("mT", [D, S], BF16), Buf("mT")
    if "pool" in phases:
        G.pool_invcnt = din("pool_invcnt", [4, S])
        G.pool_w = din("pool_w", [DEPTH, 4, 256, 256])
        G.pool_scale = din("pool_scale", [DEPTH, W])
    if "na" in phases:
        G.na_tab = din("na_tab", [DEPTH, 16, 128, 18, 64])
    if "p3" in phases:
        G.w_branch = din("w_branch", [DEPTH, 3, W, D])
        G.w_out = din("w_out", [DEPTH, D, D])
        G.final_g = din("final_g", [D])
        if "p1" not in phases:
            x_in = din("x", [SEQ, D])
            ctx_in = din("ctx", [NCTX, D])

        def x_src(layer, ti):
            if layer == 0:
                return ctx_in[ti * 128:(ti + 1) * 128, :] if ti < 2 else x_in[(ti - 2) * 128:(ti - 1) * 128, :]
            return xs[ti * 128:(ti + 1) * 128, :]
        G.x_src = x_src
    if "rw" in phases:
        G.rwpar = din("rwpar", [DEPTH, 8, 128, NPAR])
        G.rw_w2 = din("rw_w2", [DEPTH, 2, 64, W])
        G.rw_a2 = din("rw_a2", [DEPTH, 2, 64, W])
        G.lnx_g = din("rw_lnx_g", [DEPTH, W])
        G.lnx_b = din("rw_lnx_b", [DEPTH, W])
    brT = dscr("brT", [3 * W, S], BF16)
    G.brT, G.B_brT = brT, Buf("brT")
    for layer in range(n_layers):
        if only_layer is not None and layer != only_layer:
            continue
        if "p0" in G.phases:
            with contextlib.ExitStack() as ps:
                condT = _sb(ps, nc, "condT", [128, 16, 2], F32)
                B_cond = Buf("condT")
                P.dma("sp", condT[:, :, 0], c_in.rearrange("(k p) -> p k", p=128), B_cond, writes=[B_cond])
                P.dma("sp", condT[:, :, 1], cctx_in.rearrange("(k p) -> p k", p=128), B_cond, writes=[B_cond], nowait=True)
                scond = _sb(ps, nc, "scond", [128, 16, 2], F32)
                B_scond = Buf("scond")
                P.op("act", lambda e: e.activation(out=scond[:], in_=condT[:], func=AF.Silu),
                     reads=[B_cond], writes=[B_scond])
                gcol = _sb(ps, nc, "gcol", [128, 16], F32)
                B_gcol = Buf("gcol")
                P.dma("sp", gcol[:], norm_g[layer].rearrange("(k p) -> p k", p=128), B_gcol, writes=[B_gcol])
                modrow = _sb(ps, nc, "modrow", [2, 3 * D], F32)
                B_modrow = Buf("modrow")
                bmod2 = _sb(ps, nc, "bmod2", [2, 3 * D], F32)
                B_bmod = Buf("bmod2")
                P.dma("sp", bmod2[0:1, :], b_mod[layer:layer + 1, :], B_bmod, writes=[B_bmod])
                P.dma("sp", bmod2[1:2, :], b_mod[layer:layer + 1, :], B_bmod, writes=[B_bmod], nowait=True)
                wbufs = []
                for i in range(2):
                    wbufs.append((_sb(ps, nc, "wmod%d" % i, [128, 16, 512], F32), Buf("wmod%d" % i)))
                for cb in range(12):
                    wt, B_w = wbufs[cb % 2]
                    src = w_mod[layer, :, cb * 512:(cb + 1) * 512].rearrange("(k p) c -> p k c", p=128)
                    P.dma("sp", wt[:, 0:8, :], src[:, 0:8, :], B_w, writes=[B_w])
                    P.dma("sp", wt[:, 8:16, :], src[:, 8:16, :], B_w, writes=[B_w], nowait=True)
                    bk, B_bk = next_bank()
                    for k in range(16):
                        P.op("pe", lambda e, k=k, wt=wt, bk=bk: e.matmul(bk[0:2, :], lhsT=scond[:, k, :], rhs=wt[:, k, :],
                                                                        start=(k == 0), stop=(k == 15)),
                             reads=[B_scond, B_w], writes=[B_bk])
                    P.op("dve", lambda e, bk=bk, cb=cb: e.tensor_tensor(out=modrow[:, cb * 512:(cb + 1) * 512], in0=bk[0:2, :],
                                                                       in1=bmod2[:, cb * 512:(cb + 1) * 512], op=ALU.add),
                         reads=[B_bk, B_bmod], writes=[B_modrow])
                bk, B_bk = next_bank()
                for which in range(2):
                    for k in range(16):
                        c0 = which * D + k * 128
                        o0 = (which * 16 + k) * 2
                        P.op("pe", lambda e, c0=c0, o0=o0, bk=bk: e.matmul(bk[:, o0:o0 + 2], lhsT=modrow[0:2, c0:c0 + 128],
                                                                          rhs=ident[0:2, 0:2], start=True, stop=True),
                             reads=[B_modrow, B_ident], writes=[B_bk])
                P.op("act", lambda e, bk=bk: e.activation(out=sh_col[:].rearrange("p k t -> p (k t)"), in_=bk[:, 0:32], func=AF.Copy),
                     reads=[B_bk], writes=[B_sh])
                tmpc = _sb(ps, nc, "tmpc", [128, 16, 2], F32)
                B_tmpc = Buf("tmpc")
                P.op("dve", lambda e, bk=bk: e.tensor_scalar(out=tmpc[:].rearrange("p k t -> p (k t)"), in0=bk[:, 32:64], scalar1=1.0,
                                                            scalar2=None, op0=ALU.add),
                     reads=[B_bk], writes=[B_tmpc])
                for t in range(2):
                    P.op("dve", lambda e, t=t: e.tensor_tensor(out=gs_col[:, :, t], in0=tmpc[:, :, t], in1=gcol[:], op=ALU.mult),
                         reads=[B_tmpc, B_gcol], writes=[B_gs])
                P.dma("sp", gate_d[layer], modrow[0:2, 2 * D:3 * D], B_gate_d, reads=[B_modrow])
                P.barrier()

        if "p1" in G.phases:
            with contextlib.ExitStack() as ps:
                hT = _sb(ps, nc, "hT", [128, 16, S], BF16)
                B_hT = [Buf("hT%d" % i) for i in range(NT)]
                ps_outer = ps
                ps = contextlib.ExitStack()
                ps.__enter__()
                xts = [(_sb(ps, nc, "xt%d" % i, [128, D], F32), Buf("xt%d" % i)) for i in range(2)]
                xns = [(_sb(ps, nc, "xn%d" % i, [128, D], F32), Buf("xn%d" % i)) for i in range(2)]
                junk = _sb(ps, nc, "junk", [128, D], BF16)
                B_junk = Buf("junk")
                stat = [(_sb(ps, nc, "stat%d" % i, [128, 2], F32), Buf("stat%d" % i)) for i in range(2)]
                for ti in range(NT):
                    xt, B_xt = xts[ti % 2]
                    xn, B_xn = xns[ti % 2]
                    st, B_st = stat[ti % 2]
                    isctx = 1 if ti < 2 else 0
                    if layer == 0:
                        src = ctx_in[ti * 128:(ti + 1) * 128, :] if ti < 2 else x_in[(ti - 2) * 128:(ti - 1) * 128, :]
                    else:
                        src = xs[ti * 128:(ti + 1) * 128, :]
                    P.dma("sp", xt[:], src, B_xt, writes=[B_xt])
                    P.op("act", lambda e, xt=xt, st=st: e.activation(out=junk[:], in_=xt[:], func=AF.Square, scale=float(D) ** -0.5,
                                                                   accum_out=st[:, 0:1]),
                         reads=[B_xt], writes=[B_junk, B_st])
                    P.op("dve", lambda e, st=st: e.tensor_scalar(out=st[:, 0:1], in0=st[:, 0:1], scalar1=1e-6, scalar2=None,
                                                               op0=ALU.add),
                         reads=[B_st], writes=[B_st])
                    P.op("act", lambda e, st=st: e.activation(out=st[:, 0:1], in_=st[:, 0:1], func=AF.Sqrt),
                         reads=[B_st], writes=[B_st])
                    P.op("dve", lambda e, st=st: e.reciprocal(out=st[:, 1:2], in_=st[:, 0:1]),
                         reads=[B_st], writes=[B_st])
                    P.op("act", lambda e, xt=xt, xn=xn, st=st: e.activation(out=xn[:], in_=xt[:], func=AF.Copy, scale=st[:, 1:2]),
                         reads=[B_xt, B_st], writes=[B_xn])
                    for g in range(4):
                        bk, B_bk = next_bank()
                        for j in range(4):
                            k = g * 4 + j
                            P.op("pe", lambda e, k=k, j=j, bk=bk, xn=xn: e.transpose(bk[:, j * 128:(j + 1) * 128],
                                                                                   xn[:, k * 128:(k + 1) * 128], ident[:]),
                                 reads=[B_xn, B_ident], writes=[B_bk])
                        for j in range(4):
                            k = g * 4 + j
                            eng = "act" if (j % 2 == 0) else "dve"
                            o = hT[:, k, ti * 128:(ti + 1) * 128]
                            i_ = bk[:, j * 128:(j + 1) * 128]
                            if eng == "act":
                                P.op("act", lambda e, o=o, i_=i_, k=k, isctx=isctx: e.activation(
                                    out=o, in_=i_, func=AF.Identity, scale=gs_col[:, k, isctx:isctx + 1],
                                    bias=sh_col[:, k, isctx:isctx + 1]),
                                    reads=[B_bk, B_gs, B_sh], writes=[B_hT[ti]])
                            else:
                                P.op("dve", lambda e, o=o, i_=i_, k=k, isctx=isctx: e.tensor_scalar(
                                    out=o, in0=i_, scalar1=gs_col[:, k, isctx:isctx + 1], scalar2=sh_col[:, k, isctx:isctx + 1],
                                    op0=ALU.mult, op1=ALU.add),
                                    reads=[B_bk, B_gs, B_sh], writes=[B_hT[ti]])

                P.barrier()
                ps.__exit__(None, None, None)
                ps = contextlib.ExitStack()
                ps.__enter__()
                wbs = [(_sb(ps, nc, "win%d" % i, [128, 16, 512], BF16), Buf("win%d" % i)) for i in range(2)]
                ost = [(_sb(ps, nc, "ost%d" % i, [128, 512], F32), Buf("ost%d" % i)) for i in range(4)]
                ostb = [(_sb(ps, nc, "ostb%d" % i, [128, 512], BF16), Buf("ostb%d" % i)) for i in range(4)]
                vst = [(_sb(ps, nc, "vst%d" % i, [128, 8, 65], BF16), Buf("vst%d" % i)) for i in range(2)]
                for i in range(2):
                    P.op("pool", lambda e, i=i: e.memset(vst[i][0][:], 1.0), writes=[vst[i][1]])
                tblocks = [(i * 512, 512) for i in range(8)] + [(4096, 256)]
                n_cb = (D_IN + 511) // 512
                evac_rr = [0]
                sub_rr = [0]
                for cb in range(n_cb):
                    c0 = cb * 512
                    cw = min(512, D_IN - c0)
                    wt, B_w = wbs[cb % 2]
                    src = w_in[layer, :, c0:c0 + cw].rearrange("(k p) c -> p k c", p=128)
                    P.dma("pool", wt[:, 0:8, 0:cw], src[:, 0:8, :], B_w, writes=[B_w])
                    P.dma("pool", wt[:, 8:16, 0:cw], src[:, 8:16, :], B_w, writes=[B_w], nowait=True)
                    if 2048 <= c0 < 3072:
                        hg = (c0 - 2048) // 512
                        for ti in range(NT):
                            bk, B_bk = next_bank()
                            for k in range(16):
                                P.op("pe", lambda e, k=k, ti=ti, bk=bk, wt=wt: e.matmul(
                                    bk[:, :], lhsT=hT[:, k, ti * 128:(ti + 1) * 128], rhs=wt[:, k, :], start=(k == 0), stop=(k == 15)),
                                    reads=[B_hT[ti], B_w], writes=[B_bk])
                            vs, B_vs = vst[ti % 2]
                            eng = "act" if evac_rr[0] % 2 == 0 else "dve"
                            evac_rr[0] += 1
                            o = vs[:, :, 0:64]
                            i_ = bk[:, :].rearrange("p (h d) -> p h d", h=8)
                            if eng == "act":
                                P.op("act", lambda e, o=o, i_=i_: e.activation(out=o, in_=i_, func=AF.Copy), reads=[B_bk], writes=[B_vs])
                            else:
                                P.op("dve", lambda e, o=o, i_=i_: e.tensor_copy(out=o, in_=i_), reads=[B_bk], writes=[B_vs])
                            P.dma("sp", vaug[ti * 128:(ti + 1) * 128, hg * 8:(hg + 1) * 8, :], vs[:], B_vaug, reads=[B_vs], writes=[])
                        continue
                    for j in range(cw // 128):
                        is_qk = c0 < 2048
                        col = c0 + j * 128
                        ctx_needed = (1024 <= col < 2048) or (6144 <= col < 9216) or (10240 <= col < 10368)
                        tbl = tblocks if (layer < DEPTH - 1 or ctx_needed) else ([(256, 256)] + tblocks[1:])
                        for tb, (t0, tw) in enumerate(tbl):
                            bk, B_bk = next_bank()
                            for k in range(16):
                                P.op("pe", lambda e, k=k, j=j, bk=bk, wt=wt, t0=t0, tw=tw: e.matmul(
                                    bk[:, 0:tw], lhsT=wt[:, k, j * 128:(j + 1) * 128], rhs=hT[:, k, t0:t0 + tw],
                                    start=(k == 0), stop=(k == 15)),
                                    reads=[B_w] + B_hT[t0 // 128:(t0 + tw) // 128], writes=[B_bk])
                            stg, B_stg = (ostb if is_qk else ost)[sub_rr[0] % 4]
                            sub_rr[0] += 1
                            eng = "act" if evac_rr[0] % 2 == 0 else "dve"
                            evac_rr[0] += 1
                            o = stg[:, 0:tw]
                            i_ = bk[:, 0:tw]
                            if eng == "act":
                                P.op("act", lambda e, o=o, i_=i_: e.activation(out=o, in_=i_, func=AF.Copy), reads=[B_bk], writes=[B_stg])
                            else:
                                P.op("dve", lambda e, o=o, i_=i_: e.tensor_copy(out=o, in_=i_), reads=[B_bk], writes=[B_stg])
                            if is_qk:
                                P.dma("sp", qkT[col:col + 128, t0:t0 + tw], stg[:, 0:tw], B_qkT, reads=[B_stg])
                            else:
                                r0 = col - 3072
                                P.dma("sp", restT[r0:r0 + 128, t0:t0 + tw], stg[:, 0:tw], B_restT, reads=[B_stg])
                P.barrier()
                ps.__exit__(None, None, None)
                ps = ps_outer
                P.barrier()

        if "rw" in G.phases:
            phase_rwkv(G, layer)
        if "pool" in G.phases:
            phase_pool(G, layer)
        if "na" in G.phases:
            phase_na(G, layer, do_ctx=(layer < DEPTH - 1))
        if "p3" in G.phases:
            phase_out(G, layer, last=(layer == DEPTH - 1))

    P.barrier()
    print("inst counts", P.ninst, "nsem", P.nsem)


def kernel(**inputs):
    inp = {k: np.asarray(v) for k, v in inputs.items()}
    shared = dict(host_consts())
    for k in ("c_ctx", "norm_g", "w_mod", "b_mod", "w_in", "rw_w2", "rw_a2", "rw_lnx_g", "rw_lnx_b", "pool_w", "pool_scale",
              "w_branch", "w_out", "final_g"):
        shared[k] = np.ascontiguousarray(inp[k], dtype=np.float32)
    shared["rwpar"] = pack_rwpar(inp)
    shared["pool_invcnt"] = host_pool_invcnt()
    shared["na_tab"] = host_na_table(np.asarray(inp["na_rpb"], np.float32))
    nb = inp["x"].shape[0]
    in_maps = []
    for b in range(nb):
        m = dict(shared)
        m["x"] = np.ascontiguousarray(inp["x"][b], dtype=np.float32)
        m["ctx"] = np.ascontiguousarray(inp["ctx"][b], dtype=np.float32)
        m["c"] = np.ascontiguousarray(inp["c"][b], dtype=np.float32)
        in_maps.append(m)
    nc = build_program()
    res = run_bass_kernel_spmd(nc, in_maps, core_ids=list(range(nb)))
    return np.stack([np.asarray(res.results[b]["out"], dtype=np.float32) for b in range(nb)], axis=0)
```
